# Optimizing a Trainium2 kernel written in Bass

```python
import math
import jax
import jax.numpy as jnp
from jax import lax
import numpy as np

D_MODEL = 1024
BATCH = 16
SEQ = 2048
DEPTH = 2
DEC_BATCH = 2
DEC_SEQ = 16384
PAST_LEN = 128

BRANCH_W = 512
N_BRANCH = 3
HY_W = BRANCH_W
HY_ORDER = 2
HY_SHORT = 3
HY_EMB = 33
HY_FILTER_HIDDEN = 64
HY_FAST_PCT = 0.3
HY_SLOW_PCT = 1.5
HY_TARGET = 1e-2
MLA_HEADS = 4
MLA_NOPE = 128
MLA_ROPE = 64
MLA_V = 128
MLA_Q_LORA = 256
MLA_KV_LORA = 128
ROPE_THETA = 10000.0
Q_BLOCK = 128
GDN_HEADS = 4
GDN_DK = 128
GDN_DV = 128
GDN_CONV = 3
GDN_CHUNK = 64
D_FF = -(-8 * D_MODEL // (3 * 256)) * 256
NORM_EPS = 1e-6

IN_SIZES = (
    (HY_ORDER + 1) * HY_W,
    MLA_Q_LORA,
    MLA_KV_LORA + MLA_ROPE,
    GDN_HEADS * (2 * GDN_DK + GDN_DV),
    GDN_HEADS * GDN_DV,
    2 * GDN_HEADS,
    2 * GDN_HEADS,
    N_BRANCH * D_MODEL,
)
D_IN = sum(IN_SIZES)

kernel_name = "hyena_mla_gdn_parallel_encoder"


def rms_norm(x, g):
    xf = x.astype(jnp.float32)
    y = xf * lax.rsqrt(jnp.mean(xf * xf, axis=-1, keepdims=True) + NORM_EPS)
    return (y * g.astype(jnp.float32)).astype(x.dtype)


def centred_dwconv(x, w):
    k = w.shape[0]
    p = k // 2
    L = x.shape[1]
    xp = jnp.pad(x, ((0, 0), (p, p), (0, 0)))
    out = xp[:, 0:L] * w[0]
    for i in range(1, k):
        out = out + xp[:, i:i + L] * w[i]
    return out


def hyena_filter_bank(L, w1, b1, freq, w2, b2, w3):
    f32 = jnp.float32
    t = jnp.linspace(0.0, 1.0, L, dtype=f32)[:, None]
    bands = (HY_EMB - 1) // 2
    wpos = (2.0 * math.pi / L) * jnp.arange(L, dtype=f32)[:, None]
    fr = jnp.linspace(1e-4, bands - 1, bands, dtype=f32)[None, :]
    feats = jnp.concatenate([t, jnp.cos(fr * wpos), -jnp.sin(fr * wpos)], axis=-1)
    freq = freq.astype(f32)
    h = jnp.sin(freq * (feats @ w1.astype(f32) + b1.astype(f32)))
    h = jnp.sin(freq * (h @ w2.astype(f32) + b2.astype(f32)))
    h = (h @ w3.astype(f32)).reshape(L, HY_ORDER, 2, HY_W)
    deltas = jnp.abs(jnp.linspace(math.log(HY_TARGET) / HY_SLOW_PCT,
                                  math.log(HY_TARGET) / HY_FAST_PCT, HY_W, dtype=f32))
    h = h * jnp.exp(-t * deltas)[:, None, None, :]
    fwd = h[:, :, 0]
    bwd = h[1:, :, 1][::-1]
    kern = jnp.concatenate([fwd, jnp.zeros((1, HY_ORDER, HY_W), f32), bwd], axis=0)
    kern = kern / jnp.sum(jnp.abs(kern), axis=0, keepdims=True)
    return jnp.fft.rfft(kern, axis=0)


def hyena_mixer(u, conv_w, conv_b, w1, b1, freq, w2, b2, w3, skip):
    L = u.shape[1]
    u = (centred_dwconv(u, conv_w) + conv_b).astype(jnp.float32)
    x1, x2, v = jnp.split(u, 3, axis=-1)
    kf = hyena_filter_bank(L, w1, b1, freq, w2, b2, w3)
    z = v
    for o, gate in enumerate((x1, x2)):
        zf = jnp.fft.rfft(z, n=2 * L, axis=1)
        conv = jnp.fft.irfft(zf * kf[None, :, o], n=2 * L, axis=1)[:, :L]
        z = gate * (conv + skip[o].astype(jnp.float32) * z)
    return z


def rope_tables(L):
    half = MLA_ROPE // 2
    inv = ROPE_THETA ** (-jnp.arange(half, dtype=jnp.float32) / half)
    ang = jnp.arange(L, dtype=jnp.float32)[:, None] * inv[None, :]
    return jnp.cos(ang), jnp.sin(ang)


def apply_rope(x, cos, sin):
    x1, x2 = jnp.split(x, 2, axis=-1)
    return jnp.concatenate([x1 * cos - x2 * sin, x2 * cos + x1 * sin], axis=-1).astype(x.dtype)


def mla_mixer(q_lat, kv_lat, q_norm, wq_b, kv_norm, wkv_b):
    B, L, _ = q_lat.shape
    q = (rms_norm(q_lat, q_norm) @ wq_b).reshape(B, L, MLA_HEADS, MLA_NOPE + MLA_ROPE)
    q_nope, q_pe = q[..., :MLA_NOPE], q[..., MLA_NOPE:]
    c_kv, k_pe = kv_lat[..., :MLA_KV_LORA], kv_lat[..., MLA_KV_LORA:]
    kv = (rms_norm(c_kv, kv_norm) @ wkv_b).reshape(B, L, MLA_HEADS, MLA_NOPE + MLA_V)
    k_nope, v = kv[..., :MLA_NOPE], kv[..., MLA_NOPE:]
    cos, sin = rope_tables(L)
    q_pe = apply_rope(q_pe, cos[:, None, :], sin[:, None, :])
    k_pe = apply_rope(k_pe, cos, sin)
    scale = (MLA_NOPE + MLA_ROPE) ** -0.5
    nblk = L // Q_BLOCK
    qn_b = jnp.moveaxis(q_nope.reshape(B, nblk, Q_BLOCK, MLA_HEADS, MLA_NOPE), 1, 0)
    qp_b = jnp.moveaxis(q_pe.reshape(B, nblk, Q_BLOCK, MLA_HEADS, MLA_ROPE), 1, 0)

    def attend(args):
        qn, qp = args
        s = (jnp.einsum('bqhd,bkhd->bhqk', qn, k_nope)
             + jnp.einsum('bqhr,bkr->bhqk', qp, k_pe))
        p = jax.nn.softmax(s.astype(jnp.float32) * scale, axis=-1).astype(v.dtype)
        return jnp.einsum('bhqk,bkhd->bqhd', p, v)

    o = lax.map(attend, (qn_b, qp_b))
    return jnp.moveaxis(o, 0, 1).reshape(B, L, MLA_HEADS * MLA_V)


def l2_normalise(x):
    return x * lax.rsqrt(jnp.sum(x * x, axis=-1, keepdims=True) + NORM_EPS)


def chunk_gated_delta(q, k, v, g, beta):
    B, L, H, DK = q.shape
    DV = v.shape[-1]
    C = GDN_CHUNK
    N = L // C

    def chunks(t):
        return jnp.swapaxes(t.reshape(B, N, C, H, t.shape[-1]), 2, 3)

    q, k, v = chunks(q), chunks(k), chunks(v)
    g = jnp.swapaxes(g.reshape(B, N, C, H), 2, 3)
    beta = jnp.swapaxes(beta.reshape(B, N, C, H), 2, 3)
    gc = jnp.cumsum(g, axis=-1)
    idx = jnp.arange(C)
    causal = idx[:, None] >= idx[None, :]
    strict = idx[:, None] > idx[None, :]
    decay_mask = jnp.exp(jnp.where(causal, gc[..., :, None] - gc[..., None, :], -jnp.inf))
    kb = k * beta[..., None]
    vb = v * beta[..., None]
    a_strict = jnp.where(strict, jnp.einsum('bnhid,bnhjd->bnhij', kb, k) * decay_mask, 0.0)
    m = a_strict + jnp.eye(C, dtype=q.dtype)
    rhs = jnp.concatenate([vb, kb * jnp.exp(gc)[..., None]], axis=-1)
    sol = lax.linalg.triangular_solve(m, rhs, left_side=True, lower=True, unit_diagonal=True)
    u, w = sol[..., :DV], sol[..., DV:]
    attn_qk = jnp.einsum('bnhid,bnhjd->bnhij', q, k) * decay_mask
    q_dec = q * jnp.exp(gc)[..., None]
    k_dec = k * jnp.exp(gc[..., -1:] - gc)[..., None]
    chunk_decay = jnp.exp(gc[..., -1])

    def step(S, xs):
        w_c, u_c, qd, kd, aqk, dec = xs
        v_new = u_c - jnp.einsum('bhcd,bhde->bhce', w_c, S)
        o = jnp.einsum('bhcd,bhde->bhce', qd, S) + jnp.einsum('bhij,bhje->bhie', aqk, v_new)
        S = S * dec[..., None, None] + jnp.einsum('bhcd,bhce->bhde', kd, v_new)
        return S, o

    xs = tuple(jnp.moveaxis(t, 1, 0) for t in (w, u, q_dec, k_dec, attn_qk, chunk_decay))
    S0 = jnp.zeros((B, H, DK, DV), q.dtype)
    _, o = lax.scan(step, S0, xs)
    return jnp.transpose(o, (1, 0, 3, 2, 4)).reshape(B, L, H, DV)


def gdn_mixer(qkv, z, b_raw, a_raw, conv_w, a_log, dt_bias, out_norm):
    B, L, _ = qkv.shape
    f32 = jnp.float32
    qkv = jax.nn.silu(centred_dwconv(qkv, conv_w).astype(f32))
    nqk = GDN_HEADS * GDN_DK
    q = l2_normalise(qkv[..., :nqk].reshape(B, L, GDN_HEADS, GDN_DK)) * (GDN_DK ** -0.5)
    k = l2_normalise(qkv[..., nqk:2 * nqk].reshape(B, L, GDN_HEADS, GDN_DK))
    v = qkv[..., 2 * nqk:].reshape(B, L, GDN_HEADS, GDN_DV)
    beta = jax.nn.sigmoid(b_raw.astype(f32)).reshape(B, L, 2, GDN_HEADS)
    g = (-jnp.exp(a_log.astype(f32))
         * jax.nn.softplus(a_raw.astype(f32).reshape(B, L, 2, GDN_HEADS) + dt_bias.astype(f32)))
    o_f = chunk_gated_delta(q, k, v, g[:, :, 0], beta[:, :, 0])
    flip = lambda t: jnp.flip(t, axis=1)
    o_b = flip(chunk_gated_delta(flip(q), flip(k), flip(v), flip(g[:, :, 1]), flip(beta[:, :, 1])))
    o = rms_norm(o_f + o_b, out_norm) * jax.nn.silu(z.astype(f32).reshape(B, L, GDN_HEADS, GDN_DV))
    return o.reshape(B, L, GDN_HEADS * GDN_DV)


def trunk_layer(x, norm_mix_pre, norm_mix_post, norm_ffn_pre, norm_ffn_post, w_in,
                hy_conv_w, hy_conv_b, hy_ffn_w1, hy_ffn_b1, hy_sin_freq, hy_ffn_w2, hy_ffn_b2,
                hy_ffn_w3, hy_skip, mla_q_norm, mla_wq_b, mla_kv_norm, mla_wkv_b,
                gdn_conv_w, gdn_a_log, gdn_dt_bias, gdn_out_norm,
                w_branch, w_out, w_gate, w_up, w_down):
    B, L, _ = x.shape
    dt = x.dtype
    h = rms_norm(x, norm_mix_pre)
    proj = h @ w_in
    points = [int(p) for p in np.cumsum(IN_SIZES)[:-1]]
    hy_in, mla_q, mla_kv, gdn_qkv, gdn_z, gdn_b, gdn_a, gate_logits = jnp.split(proj, points, axis=-1)
    o_hy = hyena_mixer(hy_in, hy_conv_w, hy_conv_b, hy_ffn_w1, hy_ffn_b1, hy_sin_freq,
                       hy_ffn_w2, hy_ffn_b2, hy_ffn_w3, hy_skip)
    o_mla = mla_mixer(mla_q, mla_kv, mla_q_norm, mla_wq_b, mla_kv_norm, mla_wkv_b)
    o_gdn = gdn_mixer(gdn_qkv, gdn_z, gdn_b, gdn_a, gdn_conv_w, gdn_a_log, gdn_dt_bias, gdn_out_norm)
    gates = jax.nn.sigmoid(gate_logits.astype(jnp.float32)).astype(dt).reshape(B, L, N_BRANCH, D_MODEL)
    branches = (o_hy, o_mla, o_gdn)
    merged = gates[:, :, 0] * (branches[0].astype(dt) @ w_branch[0])
    for i in range(1, N_BRANCH):
        merged = merged + gates[:, :, i] * (branches[i].astype(dt) @ w_branch[i])
    x = x + rms_norm(merged @ w_out, norm_mix_post)
    h = rms_norm(x, norm_ffn_pre)
    f = (jax.nn.silu(h @ w_gate) * (h @ w_up)) @ w_down
    return x + rms_norm(f, norm_ffn_post)


def setup_inputs(seed: int = 0) -> dict:
    key = jax.random.key(seed)
    ks = iter(jax.random.split(key, 40))
    f32 = jnp.float32

    def nrm(shape, scale):
        return scale * jax.random.normal(next(ks), shape, f32)

    def gain(shape):
        return 1.0 + 0.05 * jax.random.normal(next(ks), shape, f32)

    D = D_MODEL
    x_prompt = nrm((BATCH, SEQ, D), 1.0)
    x_sample = nrm((DEC_BATCH, DEC_SEQ, D), 1.0)
    a_log = jnp.log(jax.random.uniform(next(ks), (DEPTH, 2, GDN_HEADS), f32, 1.0, 16.0))
    dt0 = jnp.exp(jax.random.uniform(next(ks), (DEPTH, 2, GDN_HEADS), f32,
                                     math.log(1e-3), math.log(1e-1)))
    dt_bias = dt0 + jnp.log(-jnp.expm1(-dt0))
    return {
        "x_prompt": x_prompt,
        "x_sample": x_sample,
        "norm_mix_pre": gain((DEPTH, D)),
        "norm_mix_post": gain((DEPTH, D)),
        "norm_ffn_pre": gain((DEPTH, D)),
        "norm_ffn_post": gain((DEPTH, D)),
        "w_in": nrm((DEPTH, D, D_IN), D ** -0.5),
        "hy_conv_w": nrm((DEPTH, HY_SHORT, (HY_ORDER + 1) * HY_W), HY_SHORT ** -0.5),
        "hy_conv_b": nrm((DEPTH, (HY_ORDER + 1) * HY_W), 0.02),
        "hy_ffn_w1": nrm((DEPTH, HY_EMB, HY_FILTER_HIDDEN), HY_EMB ** -0.5),
        "hy_ffn_b1": nrm((DEPTH, HY_FILTER_HIDDEN), 0.02),
        "hy_sin_freq": gain((DEPTH, HY_FILTER_HIDDEN)),
        "hy_ffn_w2": nrm((DEPTH, HY_FILTER_HIDDEN, HY_FILTER_HIDDEN), HY_FILTER_HIDDEN ** -0.5),
        "hy_ffn_b2": nrm((DEPTH, HY_FILTER_HIDDEN), 0.02),
        "hy_ffn_w3": nrm((DEPTH, HY_FILTER_HIDDEN, HY_ORDER * 2 * HY_W), HY_FILTER_HIDDEN ** -0.5),
        "hy_skip": nrm((DEPTH, HY_ORDER, HY_W), 0.5),
        "mla_q_norm": gain((DEPTH, MLA_Q_LORA)),
        "mla_wq_b": nrm((DEPTH, MLA_Q_LORA, MLA_HEADS * (MLA_NOPE + MLA_ROPE)), MLA_Q_LORA ** -0.5),
        "mla_kv_norm": gain((DEPTH, MLA_KV_LORA)),
        "mla_wkv_b": nrm((DEPTH, MLA_KV_LORA, MLA_HEADS * (MLA_NOPE + MLA_V)), MLA_KV_LORA ** -0.5),
        "gdn_conv_w": nrm((DEPTH, GDN_CONV, GDN_HEADS * (2 * GDN_DK + GDN_DV)), GDN_CONV ** -0.5),
        "gdn_a_log": a_log,
        "gdn_dt_bias": dt_bias,
        "gdn_out_norm": gain((DEPTH, GDN_DV)),
        "w_branch": nrm((DEPTH, N_BRANCH, BRANCH_W, D), BRANCH_W ** -0.5),
        "w_out": nrm((DEPTH, D, D), D ** -0.5),
        "w_gate": nrm((DEPTH, D, D_FF), D ** -0.5),
        "w_up": nrm((DEPTH, D, D_FF), D ** -0.5),
        "w_down": nrm((DEPTH, D_FF, D), D_FF ** -0.5),
    }


def reference(x_prompt, x_sample, norm_mix_pre, norm_mix_post, norm_ffn_pre, norm_ffn_post, w_in,
              hy_conv_w, hy_conv_b, hy_ffn_w1, hy_ffn_b1, hy_sin_freq, hy_ffn_w2, hy_ffn_b2,
              hy_ffn_w3, hy_skip, mla_q_norm, mla_wq_b, mla_kv_norm, mla_wkv_b,
              gdn_conv_w, gdn_a_log, gdn_dt_bias, gdn_out_norm,
              w_branch, w_out, w_gate, w_up, w_down):
    weights = (norm_mix_pre, norm_mix_post, norm_ffn_pre, norm_ffn_post, w_in,
               hy_conv_w, hy_conv_b, hy_ffn_w1, hy_ffn_b1, hy_sin_freq, hy_ffn_w2, hy_ffn_b2,
               hy_ffn_w3, hy_skip, mla_q_norm, mla_wq_b, mla_kv_norm, mla_wkv_b,
               gdn_conv_w, gdn_a_log, gdn_dt_bias, gdn_out_norm,
               w_branch, w_out, w_gate, w_up, w_down)

    def run_trunk(x):
        for layer in range(DEPTH):
            x = trunk_layer(x, *[w[layer] for w in weights])
        return x

    y_prompt = run_trunk(x_prompt)
    y_sample = run_trunk(x_sample)
    return (y_prompt, y_sample)
```

```python
import math
from contextlib import ExitStack
import numpy as np
import ml_dtypes
import concourse.bass as bass
import concourse.mybir as mybir
from concourse.bass_utils import run_bass_kernel_spmd

F32 = mybir.dt.float32
BF16 = mybir.dt.bfloat16
AF = mybir.ActivationFunctionType
ALU = mybir.AluOpType
AX = mybir.AxisListType
AP = bass.AP

D = 1024
KC = 8
DEPTH = 2
D_IN = 7120
D_FF = 2816
NFF = 22
EPS = 1e-6
C_HY, C_QL, C_CKV, C_KPE, C_GQKV, C_GZ, C_GB, C_GA, C_GATE = 0, 1536, 1792, 1920, 1984, 3520, 4032, 4040, 4048
NDS = 8
HY_RMAX = 128
GDN_CUT = 0
C_CUT = 0


class Em:
    def __init__(s, nc, es):
        s.nc = nc
        s.E = {'pe': nc.tensor, 'act': nc.scalar, 'dve': nc.vector, 'pool': nc.gpsimd, 'sp': nc.sync}
        s.sem = {}
        s.cnt = {}
        for e in s.E:
            s.sem[e] = es.enter_context(nc.semaphore("c_" + e))
            s.cnt[e] = 0
        s.seen = {e: {} for e in s.E}
        s.W = {}
        s.R = {}
        s.dq = {}
        for q in ('sp', 'pool'):
            s.dq[q] = 0
            for i in range(NDS):
                s.sem[(q, i)] = es.enter_context(nc.semaphore(f"d_{q}{i}"))
                s.cnt[(q, i)] = 0
        s.psn = 0
        s.ninst = 0

    def _wait(s, e, sk, v):
        if v <= 0:
            return
        if s.seen[e].get(sk, 0) < v:
            s.E[e].wait_ge(s.sem[sk], v)
            s.seen[e][sk] = v
            s.ninst += 1

    def _deps(s, e, r, w, pe_acc=False):
        for k in r:
            for sk, v in s.W.get(k, {}).items():
                s._wait(e, sk, v)
        for k in w:
            for sk, v in s.W.get(k, {}).items():
                if pe_acc and sk == 'pe':
                    continue
                s._wait(e, sk, v)
            for sk, v in s.R.get(k, {}).items():
                s._wait(e, sk, v)

    def _mark(s, tok, r, w):
        sk, v = tok
        for k in w:
            s.W.setdefault(k, {})[sk] = v
            s.R[k] = {}
        for k in r:
            s.R.setdefault(k, {})[sk] = v

    def op(s, e, fn, r=(), w=()):
        s._deps(e, r, w, pe_acc=(e == 'pe'))
        ins = fn(s.E[e])
        ins.then_inc(s.sem[e], 1)
        s.cnt[e] += 1
        s.ninst += 1
        s._mark((e, s.cnt[e]), r, w)

    def dma(s, q, out, in_, r=(), w=(), slow=False):
        s._deps(q, r, w)
        i = s.dq[q] % NDS
        s.dq[q] += 1
        sk = (q, i)
        s._wait(q, sk, s.cnt[sk])
        if slow:
            s.E[q].dma_start(out=out, in_=in_, allow_slow_non_contiguous=True).then_inc(s.sem[sk], 16)
        else:
            s.E[q].dma_start(out=out, in_=in_).then_inc(s.sem[sk], 16)
        s.cnt[sk] += 16
        s.ninst += 1
        s._mark((sk, s.cnt[sk]), r, w)

    def barrier(s):
        for e in s.E:
            for sk, v in s.cnt.items():
                if sk != e:
                    s._wait(e, sk, v)
        s.W.clear()
        s.R.clear()


def V(ap, dims, off=0):
    a = ap.ap
    return AP(tensor=ap.tensor, offset=ap.offset + off, ap=[list(a[0])] + [[st, n] for st, n in dims])


def rsqrt(em, out, src, addc, rk, wk):
    em.op('act', lambda E: E.activation(out=out, in_=src, func=AF.Ln, bias=float(addc), scale=1.0), r=[rk], w=[wk])
    em.op('act', lambda E: E.activation(out=out, in_=out, func=AF.Exp, scale=-0.5), r=[wk], w=[wk])


def psplit(ap, p0, p1):
    a = [list(d) for d in ap.ap]
    off = ap.offset + p0 * a[0][0]
    a[0][1] = p1 - p0
    return AP(tensor=ap.tensor, offset=off, ap=a)


def bf(x):
    return np.asarray(x, np.float32).astype(ml_dtypes.bfloat16)


def make_consts(Ls):
    c = {}
    c["ident"] = np.eye(128, dtype=np.float32)
    c["identb"] = bf(np.eye(128))
    c["onesb"] = bf(np.ones((128, 128)))
    c["ones"] = np.ones((128, 128), np.float32)
    i = np.arange(128)
    c["U_le"] = (i[:, None] <= i[None, :]).astype(np.float32)
    c["U_ge"] = (i[:, None] >= i[None, :]).astype(np.float32)
    c["SL"] = (i[None, :] < i[:, None]).astype(np.float32)
    c["SU"] = (i[None, :] > i[:, None]).astype(np.float32)
    Rm = np.zeros((64, 64), np.float32)
    for d in range(32):
        Rm[d + 32, d] = -1.0
        Rm[d, d + 32] = 1.0
    c["rope_R"] = bf(Rm)
    for L in sorted(set(Ls)):
        half = 32
        inv = 10000.0 ** (-np.arange(half, dtype=np.float32) / half)
        ang = np.arange(L, dtype=np.float32)[:, None] * inv[None, :]
        cs = np.cos(ang).astype(np.float32).T
        sn = np.sin(ang).astype(np.float32).T
        c[f"ropecos{L}"] = np.concatenate([cs, cs], 0).astype(np.float32)
        c[f"ropesin{L}"] = np.concatenate([sn, sn], 0).astype(np.float32)
        N = 2 * L
        N1 = N // 128
        P1 = L // 128
        R1 = min(N1, HY_RMAX)
        nch1 = max(1, N1 // HY_RMAX)
        s1 = np.arange(N1, dtype=np.float64)
        f1 = np.arange(N1, dtype=np.float64)
        a = 2 * np.pi * np.outer(s1, f1) / N1
        F1 = np.concatenate([np.cos(a), -np.sin(a)], 1)
        c[f"hyF1_{L}"] = bf(F1.reshape(nch1, R1, 2 * N1).transpose(1, 0, 2))
        s2 = np.arange(128, dtype=np.float64)
        a = 2 * np.pi * np.outer(s2, f1) / N
        c[f"hyTwr_{L}"] = np.cos(a).astype(np.float32)
        c[f"hyTwi_{L}"] = (-np.sin(a)).astype(np.float32)
        aT = a.T
        c[f"hyTwTr_{L}"] = np.cos(aT).reshape(nch1, R1, 128).transpose(1, 0, 2).astype(np.float32)
        c[f"hyTwTi_{L}"] = np.sin(aT).reshape(nch1, R1, 128).transpose(1, 0, 2).astype(np.float32)
        t1 = np.arange(P1, dtype=np.float64)
        a = 2 * np.pi * np.outer(f1, t1) / N1
        c[f"hyG1r_{L}"] = bf((np.cos(a) / N).reshape(nch1, R1, P1).transpose(1, 0, 2))
        c[f"hyG1i_{L}"] = bf((-np.sin(a) / N).reshape(nch1, R1, P1).transpose(1, 0, 2))
        sidx = np.arange(N)
        lag = np.where(sidx < L, sidx, N - sidx)
        lag[L] = 0
        tgrid = np.linspace(0.0, 1.0, L, dtype=np.float32)
        bands = 16
        wpos = (2.0 * math.pi / L) * np.arange(L, dtype=np.float32)
        fr = np.linspace(1e-4, bands - 1, bands, dtype=np.float32)
        feats = np.concatenate([tgrid[:, None], np.cos(fr[None, :] * wpos[:, None]), -np.sin(fr[None, :] * wpos[:, None])], -1)
        c[f"hyfeat_{L}"] = np.ascontiguousarray(feats[lag].T.astype(np.float32))
        c[f"hytneg_{L}"] = np.ascontiguousarray((-tgrid[lag]).reshape(N1, 128).T.astype(np.float32))
    a = 2 * np.pi * np.outer(np.arange(128.0), np.arange(128.0)) / 128
    c["hyF2r"] = bf(np.cos(a))
    c["hyF2i"] = bf(-np.sin(a))
    c["hyF2in"] = bf(np.sin(a))
    c["hyG2a"] = bf(np.concatenate([np.cos(a), np.sin(a)], 1))
    c["hyG2b"] = bf(np.concatenate([-np.sin(a), np.cos(a)], 1))
    deltas = np.abs(np.linspace(math.log(1e-2) / 1.5, math.log(1e-2) / 0.3, 512, dtype=np.float32))
    c["hydelta"] = np.ascontiguousarray(np.broadcast_to(deltas[None, :], (128, 512))).astype(np.float32)
    return c


class Ctx:
    pass


def dcol(ap_dram, off, n, nparts=128, pstride=1, cstride=128):
    return AP(tensor=ap_dram.tensor, offset=ap_dram.offset + off, ap=[[pstride, nparts], [cstride, n], [1, 1]])


def drow_bc(ap_dram, off, n, nparts=128):
    return AP(tensor=ap_dram.tensor, offset=ap_dram.offset + off, ap=[[0, nparts], [1, n]])


def build(seq_lens, debug=False, stages=None, mixers=None):
    nc = bass.Bass("TRN2", target_bir_lowering=False)
    Ttot = sum(seq_lens)
    Lmax = max(seq_lens)
    seq_off = [sum(seq_lens[:i]) for i in range(len(seq_lens))]
    Ls = sorted(set(seq_lens))
    g = Ctx()
    g.nc = nc
    g.uidc = [0]

    def un(name):
        g.uidc[0] += 1
        return f"{name}_{g.uidc[0]}"
    g.un = un
    ext = {}

    def xin(name, shape, dt=F32):
        ext[name] = nc.dram_tensor(name, list(shape), dt, kind="ExternalInput").ap()
        return ext[name]

    xin("x", [Ttot, D])
    for nm, shp in (("norm_mix_pre", [DEPTH, D]), ("norm_mix_post", [DEPTH, D]), ("norm_ffn_pre", [DEPTH, D]),
                    ("norm_ffn_post", [DEPTH, D]), ("w_in", [DEPTH, D, D_IN]), ("hy_conv_w", [DEPTH, 3, 1536]),
                    ("hy_conv_b", [DEPTH, 1536]), ("hy_ffn_w1", [DEPTH, 33, 64]), ("hy_ffn_b1", [DEPTH, 64]),
                    ("hy_sin_freq", [DEPTH, 64]), ("hy_ffn_w2", [DEPTH, 64, 64]), ("hy_ffn_b2", [DEPTH, 64]),
                    ("hy_ffn_w3", [DEPTH, 64, 2048]), ("hy_skip", [DEPTH, 2, 512]), ("mla_q_norm", [DEPTH, 256]),
                    ("mla_wq_b", [DEPTH, 256, 768]), ("mla_kv_norm", [DEPTH, 128]), ("mla_wkv_b", [DEPTH, 128, 1024]),
                    ("gdn_conv_w", [DEPTH, 3, 1536]), ("gdn_a_log", [DEPTH, 8]), ("gdn_dt_bias", [DEPTH, 8]),
                    ("gdn_out_norm", [DEPTH, 128]), ("w_branch", [DEPTH, 1536, D]), ("w_out", [DEPTH, D, D]),
                    ("w_gate", [DEPTH, D, D_FF]), ("w_up", [DEPTH, D, D_FF]), ("w_down", [DEPTH, D_FF, D])):
        xin(nm, shp)
    consts = make_consts(seq_lens)
    for k, v in consts.items():
        xin("c_" + k, v.shape, BF16 if v.dtype == ml_dtypes.bfloat16 else F32)
    y = nc.dram_tensor("y", [Ttot, D], F32, kind="ExternalOutput").ap()

    def scratch(name, shape, dt):
        kind = "ExternalOutput" if debug else "Internal"
        return nc.dram_tensor(name, list(shape), dt, kind=kind).ap()

    XT = scratch("XT", [D, Ttot], F32)
    WTOK = [scratch(f"WTOK{l}", [6, 128, KC, 3, 512], BF16) for l in range(DEPTH)]
    WFM = [scratch(f"WFM{l}", [32, 128, KC, 128], BF16) for l in range(DEPTH)]
    WBR = [scratch(f"WBR{l}", [8, 128, 12, 128], BF16) for l in range(DEPTH)]
    WOUT = [scratch(f"WOUT{l}", [8, 128, KC, 128], BF16) for l in range(DEPTH)]
    WG = [scratch(f"WG{l}", [NFF, 128, KC, 128], BF16) for l in range(DEPTH)]
    WU = [scratch(f"WU{l}", [NFF, 128, KC, 128], BF16) for l in range(DEPTH)]
    WD = [scratch(f"WD{l}", [8, 128, NFF, 128], BF16) for l in range(DEPTH)]
    PH = scratch("PH", [Lmax, 1536], BF16)
    PG = scratch("PG", [Lmax, 1536], BF16)
    BD = scratch("BD", [Lmax, 16], F32)
    QT = scratch("QT", [4, 192, Lmax], BF16)
    KTn = scratch("KTn", [4, 128, Lmax], BF16)
    KTp = scratch("KTp", [64, Lmax], BF16)
    VS = scratch("VS", [Lmax, 512], BF16)
    ZT = scratch("ZT", [512, Lmax], BF16)
    GT = scratch("GT", [3072, Lmax], BF16)
    OT = scratch("OT", [1536, Lmax], BF16)
    OF = scratch("OF", [512, Lmax], F32)
    Z2 = scratch("Z2", [Lmax, 512], BF16)
    KTd = {L_: scratch(f"KTd{L_}", [2 * L_, 1024], BF16) for L_ in Ls}
    KFq = {L_: scratch(f"KF{L_}", [2, 512 // min(64, 4096 // (L_ // 64)), max(1, min(64, 4096 // (L_ // 64)) * (L_ // 64) // 512), 128, 2, 512], BF16)
           for L_ in Ls}
    filt_done = {}

    es = ExitStack()
    with es:
        em = Em(nc, es)

        def sb(name, shape, dt, st=es):
            return st.enter_context(nc.sbuf_tensor(g.un(name), list(shape), dt))

        PS = [es.enter_context(nc.psum_tensor(f"ps{i}", [128, 512], F32)) for i in range(8)]

        def psum(banks=range(8)):
            banks = list(banks)
            b = banks[em.psn % len(banks)]
            em.psn += 1
            return PS[b], ('ps', b)

        ident = sb("ident", [128, 128], F32)
        identb = sb("identb", [128, 128], BF16)
        onesb = sb("onesb", [128, 128], BF16)
        gains = sb("gains", [128, 4, DEPTH, KC], F32)
        em.dma('sp', ident[:], ext["c_ident"][:, :], w=['ident'])
        em.dma('sp', identb[:], ext["c_identb"][:, :], w=['identb'])
        em.dma('sp', onesb[:], ext["c_onesb"][:, :], w=['onesb'])
        for i, nm in enumerate(("norm_mix_pre", "norm_mix_post", "norm_ffn_pre", "norm_ffn_post")):
            for l in range(DEPTH):
                em.dma('sp', gains[:, i, l, :], dcol(ext[nm], l * D, KC), w=['gains'], slow=True)
        em.barrier()

        g.__dict__.update(locals())
        if stages is None or 'prep' in stages:
            stage_prep(g)
        for l in range(DEPTH):
            for si, L in enumerate(seq_lens):
                if stages is None or ('A', l) in stages:
                    stage_A(g, l, si)
                if stages is None or ('B', l) in stages:
                    stage_B(g, l, si)
                if stages is None or ('C', l) in stages:
                    stage_C(g, l, si)
        if stages is None or 'out' in stages:
            stage_out(g)
        em.barrier()
        print("emitted instructions:", em.ninst, {k: v for k, v in em.cnt.items() if isinstance(k, str)})
    return nc, consts


class Rot:
    def __init__(s, tiles, name):
        s.t = tiles
        s.name = name
        s.i = 0

    def next(s):
        j = s.i % len(s.t)
        s.i += 1
        return s.t[j], (s.name, j)


def stage_prep(g):
    nc, em, ext = g.nc, g.em, g.ext
    with ExitStack() as st:
        def sb(name, shape, dt):
            return st.enter_context(nc.sbuf_tensor(g.un(name), list(shape), dt))
        wld = Rot([sb(f"p_wld{i}", [128, 512], F32) for i in range(4)], "p_wld")
        cwt = Rot([sb(f"p_cw{i}", [128, 3, 512], F32) for i in range(2)], "p_cw")
        stg = Rot([sb(f"p_stg{i}", [128, 24 * 512], BF16) for i in range(2)], "p_stg")
        engs = ['dve', 'pool', 'act']
        ei = [0]

        def cast(out, in_, r, w, mul=None):
            e = engs[ei[0] % 3]
            ei[0] += 1
            if mul is not None:
                if e == 'act':
                    e = 'dve'
                em.op(e, lambda E: E.tensor_tensor(out, in_, mul, ALU.mult), r=r, w=w)
            elif e == 'act':
                em.op(e, lambda E: E.copy(out, in_), r=r, w=w)
            else:
                em.op(e, lambda E: E.tensor_copy(out, in_), r=r, w=w)

        def prep_fm(src, row0, nkc, col0, W, dst, t0):
            sg, sk = stg.next()
            for kc in range(nkc):
                wt, wk = wld.next()
                em.dma('sp', wt[:, 0:W], src[row0 + kc * 128: row0 + (kc + 1) * 128, col0:col0 + W], w=[wk])
                cast(sg[:, kc * W:(kc + 1) * W], wt[:, 0:W], [wk], [sk])
            nt = W // 128
            d = dst
            dap = AP(tensor=d.tensor, offset=d.offset + t0 * 128 * nkc * 128,
                     ap=[[nkc * 128, 128], [128, nkc], [128 * nkc * 128, nt], [1, 128]])
            sap = V(sg[:, 0:1], [(W, nkc), (128, nt), (1, 128)])
            em.dma('pool', dap, sap, r=[sk], w=[('dram', d.tensor.name)])

        for l in range(DEPTH):
            w_in = ext["w_in"][l]
            for grp in range(6):
                col0 = (C_HY + 512 * grp) if grp < 3 else (C_GQKV + 512 * (grp - 3))
                cwn = "hy_conv_w" if grp < 3 else "gdn_conv_w"
                cw, cwk = cwt.next()
                cw_src = ext[cwn]
                em.dma('sp', cw[:], AP(tensor=cw_src.tensor, offset=cw_src.offset + l * 3 * 1536 + 512 * (grp % 3),
                                       ap=[[0, 128], [1536, 3], [1, 512]]), w=[cwk])
                sg, sk = stg.next()
                for kc in range(KC):
                    wt, wk = wld.next()
                    em.dma('sp', wt[:], w_in[kc * 128:(kc + 1) * 128, col0:col0 + 512], w=[wk])
                    for s_ in range(3):
                        o = (kc * 3 + s_) * 512
                        cast(sg[:, o:o + 512], wt[:], [wk, cwk], [sk], mul=cw[:, s_, :])
                em.dma('pool', g.WTOK[l][grp].rearrange("p k s c -> p (k s c)"), sg[:, 0:KC * 3 * 512], r=[sk],
                       w=[('dram', f"WTOK{l}")])
            prep_fm(w_in, 0, KC, C_QL, 512, g.WFM[l], 0)
            prep_fm(w_in, 0, KC, C_GZ, 512, g.WFM[l], 4)
            for i in range(6):
                prep_fm(w_in, 0, KC, C_GATE + 512 * i, 512, g.WFM[l], 8 + 4 * i)
            for i in range(2):
                prep_fm(ext["w_branch"][l], 0, 12, 512 * i, 512, g.WBR[l], 4 * i)
                prep_fm(ext["w_out"][l], 0, KC, 512 * i, 512, g.WOUT[l], 4 * i)
                prep_fm(ext["w_down"][l], 0, NFF, 512 * i, 512, g.WD[l], 4 * i)
            for i in range(11):
                prep_fm(ext["w_gate"][l], 0, KC, 256 * i, 256, g.WG[l], 2 * i)
                prep_fm(ext["w_up"][l], 0, KC, 256 * i, 256, g.WU[l], 2 * i)
        xld = Rot([sb(f"p_x{i}", [128, D], F32) for i in range(2)], "p_x")
        xts = Rot([sb(f"p_xt{i}", [128, KC, 128], F32) for i in range(2)], "p_xt")
        for t in range(g.Ttot // 128):
            xt_, xk = xld.next()
            em.dma('sp', xt_[:], ext["x"][t * 128:(t + 1) * 128, :], w=[xk])
            xo, xok = xts.next()
            for hlf in range(2):
                ps, pk = g.psum()
                for c in range(4):
                    kc = hlf * 4 + c
                    em.op('pe', lambda E: E.transpose(ps[:, c * 128:(c + 1) * 128], xt_[:, kc * 128:(kc + 1) * 128],
                                                      g.ident[:]), r=[xk, 'ident'], w=[pk])
                e = 'dve' if hlf == 0 else 'act'
                dst = xo[:, hlf * 4:(hlf + 1) * 4, :]
                src = ps[:].rearrange("p (c t) -> p c t", c=4)
                if e == 'dve':
                    em.op(e, lambda E: E.tensor_copy(dst, src), r=[pk], w=[xok])
                else:
                    em.op(e, lambda E: E.copy(dst, src), r=[pk], w=[xok])
            dap = AP(tensor=g.XT.tensor, offset=g.XT.offset + t * 128, ap=[[g.Ttot, 128], [128 * g.Ttot, KC], [1, 128]])
            em.dma('pool', dap, xo[:], r=[xok], w=[('dram', 'XT')])
    em.barrier()


def stage_A(g, l, si):
    nc, em, ext = g.nc, g.em, g.ext
    L = g.seq_lens[si]
    t0 = g.seq_off[si]
    Ttot = g.Ttot
    nb = L // 512
    QSCALE = 192.0 ** -0.5
    with ExitStack() as st:
        def sb(name, shape, dt):
            return st.enter_context(nc.sbuf_tensor(g.un(name), list(shape), dt))
        g32 = sb("a_g32", [128, KC], F32)
        hb = sb("a_hb", [128, 1536], F32)
        tmpf = sb("a_tmpf", [128, 2 * 768], F32)
        wbd = sb("a_wbd", [128, KC, 16], BF16)
        wqb = sb("a_wqb", [128, 2, 768], BF16)
        wkn = sb("a_wkn", [128, 4, 128], BF16)
        wv = sb("a_wv", [128, 4, 128], BF16)
        gq = sb("a_gq", [128, 2], F32)
        gkv = sb("a_gkv", [128, 1], F32)
        ropeR = sb("a_ropeR", [64, 64], BF16)
        em.op('dve', lambda E: E.tensor_scalar(g32[:], g.gains[:, 0, l, :], 32.0, None, ALU.mult), r=['gains'], w=['g32'])
        em.dma('sp', hb[:], drow_bc(ext["hy_conv_b"], l * 1536, 1536), w=['hb'])
        em.dma('sp', ropeR[:], ext["c_rope_R"][:, :], w=['ropeR'])
        w_in = ext["w_in"][l]
        em.dma('sp', V(tmpf[:, 0:1], [(16, KC), (1, 16)]),
               AP(tensor=w_in.tensor, offset=w_in.offset + C_GB, ap=[[D_IN, 128], [128 * D_IN, KC], [1, 16]]), w=['tmpf'])
        em.op('dve', lambda E: E.tensor_copy(wbd[:].rearrange("p k c -> p (k c)"), tmpf[:, 0:KC * 16]), r=['tmpf'], w=['wbd'])
        wq = ext["mla_wq_b"][l]
        em.dma('sp', V(tmpf[:, 0:1], [(768, 2), (1, 768)]),
               AP(tensor=wq.tensor, offset=wq.offset, ap=[[768, 128], [128 * 768, 2], [1, 768]]), w=['tmpf'], r=['tmpf'])
        em.op('dve', lambda E: E.tensor_copy(wqb[:].rearrange("p k c -> p (k c)"), tmpf[:, 0:1536]), r=['tmpf'], w=['wqb'])
        wkv = ext["mla_wkv_b"][l]
        em.dma('sp', tmpf[:, 0:1024], wkv[:, :], w=['tmpf'], r=['tmpf'])
        em.op('dve', lambda E: E.tensor_copy(wkn[:], V(tmpf[:, 0:1], [(256, 4), (1, 128)])), r=['tmpf'], w=['wkn'])
        em.op('dve', lambda E: E.tensor_copy(wv[:], V(tmpf[:, 0:1], [(256, 4), (1, 128)], off=128)), r=['tmpf'], w=['wv'])
        em.dma('sp', gq[:], dcol(ext["mla_q_norm"], l * 256, 2), w=['gq'], slow=True)
        em.dma('sp', gkv[:], dcol(ext["mla_kv_norm"], l * 128, 1), w=['gkv'], slow=True)
        em.op('dve', lambda E: E.tensor_scalar(gq[:], gq[:], 16.0, None, ALU.mult), r=['gq'], w=['gq'])
        em.op('dve', lambda E: E.tensor_scalar(gkv[:], gkv[:], math.sqrt(128.0), None, ALU.mult), r=['gkv'], w=['gkv'])
        xT = Rot([sb(f"a_xT{i}", [128, KC, 512], F32) for i in range(2)], "a_xT")
        xh = Rot([sb(f"a_xh{i}", [128, KC, 2], F32) for i in range(2)], "a_xh")
        sq = Rot([sb(f"a_sq{i}", [128, KC, 512], BF16) for i in range(1)], "a_sq")
        sqh = sb("a_sqh", [128, KC, 2], BF16)
        rstd = Rot([sb(f"a_rstd{i}", [128, 512], F32) for i in range(2)], "a_rstd")
        rsth = sb("a_rsth", [128, 2], F32)
        xhg = sb("a_xhg", [128, KC, 2], F32)
        hT = Rot([sb(f"a_hT{i}", [128, KC, 514], BF16) for i in range(2)], "a_hT")
        wtk = Rot([sb(f"a_wtk{i}", [128, KC, 3, 512], BF16) for i in range(2)], "a_wtk")
        wfm = Rot([sb(f"a_wfm{i}", [128, 4, KC, 128], BF16) for i in range(2)], "a_wfm")
        stg = Rot([sb(f"a_stg{i}", [128, 4, 512], BF16) for i in range(3)], "a_stg")
        bds = Rot([sb(f"a_bds{i}", [128, 4, 16], F32) for i in range(2)], "a_bds")
        ql = sb("a_ql", [128, 2, 512], F32)
        ck = sb("a_ck", [128, 512], F32)
        kpe = sb("a_kpe", [64, 512], BF16)
        qln = sb("a_qln", [128, 2, 512], BF16)
        ckn = sb("a_ckn", [128, 512], BF16)
        qpe = Rot([sb(f"a_qpe{i}", [64, 512], BF16) for i in range(2)], "a_qpe")
        rt1 = Rot([sb(f"a_rt1{i}", [64, 512], F32) for i in range(2)], "a_rt1")
        rt2 = Rot([sb(f"a_rt2{i}", [64, 512], F32) for i in range(2)], "a_rt2")
        cosT = Rot([sb(f"a_cos{i}", [64, 512], F32) for i in range(2)], "a_cos")
        sinT = Rot([sb(f"a_sin{i}", [64, 512], F32) for i in range(2)], "a_sin")
        rcos, rsin = ext[f"c_ropecos{L}"], ext[f"c_ropesin{L}"]
        eng2 = ['dve', 'pool']

        def rope_store(src_bf, srck, cs, csk, sn, snk, dst_ap, dkey):
            ps, pk = g.psum()
            em.op('pe', lambda E: E.matmul(ps[0:64, :], ropeR[:], src_bf, start=True, stop=True), r=[srck, 'ropeR'], w=[pk])
            a, ak = rt1.next()
            b_, bk = rt2.next()
            em.op('pool', lambda E: E.tensor_tensor(a[:], src_bf, cs[:], ALU.mult), r=[srck, csk], w=[ak])
            em.op('dve', lambda E: E.tensor_tensor(b_[:], ps[0:64, :], sn[:], ALU.mult), r=[pk, snk], w=[bk])
            o, ok = qpe.next()
            em.op('dve', lambda E: E.tensor_tensor(o[:], a[:], b_[:], ALU.add), r=[ak, bk], w=[ok])
            em.dma('pool', dst_ap, o[:], r=[ok], w=[dkey])

        for b in range(nb):
            c0 = t0 + b * 512
            s0 = b * 512
            x_, xk = xT.next()
            em.dma('sp', x_[:], AP(tensor=g.XT.tensor, offset=g.XT.offset + c0, ap=[[Ttot, 128], [128 * Ttot, KC], [1, 512]]),
                   r=[('dram', 'XT')], w=[xk])
            xh_, xhk = xh.next()
            for side, col in ((0, c0 - 1), (1, c0 + 512)):
                if (side == 0 and b == 0) or (side == 1 and b == nb - 1):
                    em.op('pool', lambda E: E.memset(xh_[:, :, side:side + 1], 0.0), w=[xhk])
                else:
                    em.dma('sp', xh_[:, :, side:side + 1],
                           AP(tensor=g.XT.tensor, offset=g.XT.offset + col, ap=[[Ttot, 128], [128 * Ttot, KC], [1, 1]]),
                           r=[('dram', 'XT')], w=[xhk], slow=True)
            cs, csk = cosT.next()
            sn, snk = sinT.next()
            em.dma('sp', cs[:], rcos[:, s0:s0 + 512], w=[csk])
            em.dma('sp', sn[:], rsin[:, s0:s0 + 512], w=[snk])
            sq_, sqk = sq.next()
            em.op('act', lambda E: E.activation(out=sq_[:], in_=x_[:], func=AF.Square), r=[xk], w=[sqk])
            ps, pk = g.psum()
            for kc in range(KC):
                em.op('pe', lambda E: E.matmul(ps[:], g.onesb[:], sq_[:, kc, :], start=(kc == 0), stop=(kc == KC - 1)),
                      r=[sqk, 'onesb'], w=[pk])
            rs, rsk = rstd.next()
            rsqrt(em, rs[:], ps[:], D * EPS, pk, rsk)
            h_, hk = hT.next()
            for kc in range(KC):
                em.op('dve', lambda E: E.scalar_tensor_tensor(h_[:, kc, 1:513], x_[:, kc, :], g32[:, kc:kc + 1], rs[:],
                                                                     ALU.mult, ALU.mult), r=[xk, rsk, 'g32'], w=[hk])
            em.op('act', lambda E: E.activation(out=sqh[:], in_=xh_[:], func=AF.Square), r=[xhk], w=['sqh'])
            ps2, pk2 = g.psum()
            for kc in range(KC):
                em.op('pe', lambda E: E.matmul(ps2[:, 0:2], g.onesb[:], sqh[:, kc, :], start=(kc == 0), stop=(kc == KC - 1)),
                      r=['sqh', 'onesb'], w=[pk2])
            rsqrt(em, rsth[:], ps2[:, 0:2], D * EPS, pk2, 'rsth')
            em.op('pool', lambda E: E.tensor_tensor(xhg[:], xh_[:], V(g32[:, 0:1], [(1, KC), (0, 2)]), ALU.mult),
                  r=[xhk, 'g32'], w=['xhg'])
            em.op('pool', lambda E: E.tensor_tensor(V(h_[:, 0, 0:1], [(514, KC), (513, 2)]), xhg[:],
                                                    V(rsth[:, 0:1], [(0, KC), (1, 2)]), ALU.mult),
                  r=['xhg', 'rsth'], w=[hk])
            for grp in range(6):
                w_, wk = wtk.next()
                em.dma('sp', w_[:].rearrange("p k s c -> p (k s c)"), g.WTOK[l][grp].rearrange("p k s c -> p (k s c)"),
                       r=[('dram', f"WTOK{l}")], w=[wk])
                sg, sgk = stg.next()
                for tt in range(4):
                    ps, pk = g.psum()
                    n = 0
                    for kc in range(KC):
                        for s_ in range(3):
                            lo = 1 + tt * 128 + (s_ - 1)
                            em.op('pe', lambda E: E.matmul(ps[:], h_[:, kc, lo:lo + 128], w_[:, kc, s_, :],
                                                           start=(n == 0), stop=(n == 23)), r=[hk, wk], w=[pk])
                            n += 1
                    if grp < 3:
                        em.op('dve', lambda E: E.tensor_tensor(sg[:, tt, :], ps[:], hb[:, grp * 512:(grp + 1) * 512], ALU.add),
                              r=[pk, 'hb'], w=[sgk])
                    else:
                        em.op('act', lambda E: E.activation(out=sg[:, tt, :], in_=ps[:], func=AF.Silu), r=[pk], w=[sgk])
                dst = g.PH if grp < 3 else g.PG
                em.dma('pool', AP(tensor=dst.tensor, offset=dst.offset + s0 * 1536 + (grp % 3) * 512,
                                  ap=[[1536, 128], [128 * 1536, 4], [1, 512]]), sg[:], r=[sgk],
                       w=[('dram', 'PH' if grp < 3 else 'PG')])
            ps, pk = g.psum()
            for tt in range(4):
                for kc in range(KC):
                    em.op('pe', lambda E: E.matmul(ps[:, tt * 16:(tt + 1) * 16], h_[:, kc, 1 + tt * 128:1 + (tt + 1) * 128],
                                                   wbd[:, kc, :], start=(kc == 0), stop=(kc == KC - 1)), r=[hk, 'wbd'], w=[pk])
            bd_, bdk = bds.next()
            em.op('dve', lambda E: E.tensor_copy(bd_[:].rearrange("p t c -> p (t c)"), ps[:, 0:64]), r=[pk], w=[bdk])
            em.dma('pool', AP(tensor=g.BD.tensor, offset=g.BD.offset + s0 * 16, ap=[[16, 128], [128 * 16, 4], [1, 16]]),
                   bd_[:], r=[bdk], w=[('dram', 'BD')])
            for tg in range(8):
                wf, wfk = wfm.next()
                em.dma('sp', wf[:].rearrange("p t k c -> p t (k c)"),
                       AP(tensor=g.WFM[l].tensor, offset=g.WFM[l].offset + tg * 4 * 128 * KC * 128,
                          ap=[[KC * 128, 128], [128 * KC * 128, 4], [1, KC * 128]]), r=[('dram', f"WFM{l}")], w=[wfk])
                if tg >= 1:
                    sg, sgk = stg.next()
                for ti in range(4):
                    M = 64 if (tg == 0 and ti == 3) else 128
                    ps, pk = g.psum()
                    for kc in range(KC):
                        em.op('pe', lambda E: E.matmul(ps[0:M, :], wf[:, ti, kc, 0:M], h_[:, kc, 1:513],
                                                       start=(kc == 0), stop=(kc == KC - 1)), r=[hk, wfk], w=[pk])
                    if tg == 0:
                        if ti < 2:
                            em.op('dve', lambda E: E.tensor_copy(ql[:, ti, :], ps[:]), r=[pk], w=['ql'])
                        elif ti == 2:
                            em.op('dve', lambda E: E.tensor_copy(ck[:], ps[:]), r=[pk], w=['ck'])
                        else:
                            em.op('act', lambda E: E.copy(kpe[:], ps[0:64, :]), r=[pk], w=['kpe'])
                    elif tg == 1:
                        em.op('act', lambda E: E.activation(out=sg[:, ti, :], in_=ps[:], func=AF.Silu), r=[pk], w=[sgk])
                    else:
                        em.op('act', lambda E: E.activation(out=sg[:, ti, :], in_=ps[:], func=AF.Sigmoid), r=[pk], w=[sgk])
                if tg == 1:
                    em.dma('pool', AP(tensor=g.ZT.tensor, offset=g.ZT.offset + s0, ap=[[g.Lmax, 128], [128 * g.Lmax, 4], [1, 512]]),
                           sg[:], r=[sgk], w=[('dram', 'ZT')])
                elif tg >= 2:
                    em.dma('pool', AP(tensor=g.GT.tensor, offset=g.GT.offset + (tg - 2) * 512 * g.Lmax + s0,
                                      ap=[[g.Lmax, 128], [128 * g.Lmax, 4], [1, 512]]), sg[:], r=[sgk], w=[('dram', 'GT')])
            sq_, sqk = sq.next()
            em.op('act', lambda E: E.activation(out=sq_[:, 0:2, :], in_=ql[:], func=AF.Square), r=['ql'], w=[sqk])
            em.op('act', lambda E: E.activation(out=sq_[:, 2, :], in_=ck[:], func=AF.Square), r=['ck'], w=[sqk])
            ps, pk = g.psum()
            for c in range(2):
                em.op('pe', lambda E: E.matmul(ps[:], g.onesb[:], sq_[:, c, :], start=(c == 0), stop=(c == 1)), r=[sqk, 'onesb'], w=[pk])
            rs, rsk = rstd.next()
            rsqrt(em, rs[:], ps[:], 256 * EPS, pk, rsk)
            for c in range(2):
                em.op('dve', lambda E: E.scalar_tensor_tensor(qln[:, c, :], ql[:, c, :], gq[:, c:c + 1], rs[:], ALU.mult, ALU.mult),
                      r=['ql', rsk, 'gq'], w=['qln'])
            ps, pk = g.psum()
            em.op('pe', lambda E: E.matmul(ps[:], g.onesb[:], sq_[:, 2, :], start=True, stop=True), r=[sqk, 'onesb'], w=[pk])
            rs, rsk = rstd.next()
            rsqrt(em, rs[:], ps[:], 128 * EPS, pk, rsk)
            em.op('dve', lambda E: E.scalar_tensor_tensor(ckn[:], ck[:], gkv[:, 0:1], rs[:], ALU.mult, ALU.mult),
                  r=['ck', rsk, 'gkv'], w=['ckn'])
            for h in range(4):
                ps, pk = g.psum()
                for c in range(2):
                    em.op('pe', lambda E: E.matmul(ps[:], wqb[:, c, h * 192:h * 192 + 128], qln[:, c, :], start=(c == 0), stop=(c == 1)),
                          r=['wqb', 'qln'], w=[pk])
                sg, sgk = stg.next()
                em.op('act', lambda E: E.activation(out=sg[:, 0, :], in_=ps[:], func=AF.Copy, scale=QSCALE), r=[pk], w=[sgk])
                em.dma('pool', g.QT[h, 0:128, s0:s0 + 512], sg[:, 0, :], r=[sgk], w=[('dram', 'QT')])
                ps, pk = g.psum()
                for c in range(2):
                    em.op('pe', lambda E: E.matmul(ps[0:64, :], wqb[:, c, h * 192 + 128:h * 192 + 192], qln[:, c, :],
                                                   start=(c == 0), stop=(c == 1)), r=['wqb', 'qln'], w=[pk])
                qp, qpk = qpe.next()
                em.op('act', lambda E: E.activation(out=qp[:], in_=ps[0:64, :], func=AF.Copy, scale=QSCALE), r=[pk], w=[qpk])
                rope_store(qp[:], qpk, cs, csk, sn, snk, g.QT[h, 128:192, s0:s0 + 512], ('dram', 'QT'))
                ps, pk = g.psum()
                em.op('pe', lambda E: E.matmul(ps[:], wkn[:, h, :], ckn[:], start=True, stop=True), r=['wkn', 'ckn'], w=[pk])
                em.op('dve', lambda E: E.tensor_copy(sg[:, 1, :], ps[:]), r=[pk], w=[sgk])
                em.dma('pool', g.KTn[h, :, s0:s0 + 512], sg[:, 1, :], r=[sgk], w=[('dram', 'KTn')])
            rope_store(kpe[:], 'kpe', cs, csk, sn, snk, g.KTp[:, s0:s0 + 512], ('dram', 'KTp'))
            sg, sgk = stg.next()
            for tt in range(4):
                ps, pk = g.psum()
                em.op('pe', lambda E: E.matmul(ps[:], ckn[:, tt * 128:(tt + 1) * 128], wv[:].rearrange("p h c -> p (h c)"),
                                               start=True, stop=True), r=['ckn', 'wv'], w=[pk])
                em.op('act' if tt % 2 else 'dve', lambda E: (E.copy if tt % 2 else E.tensor_copy)(sg[:, tt, :], ps[:]), r=[pk], w=[sgk])
            em.dma('pool', AP(tensor=g.VS.tensor, offset=g.VS.offset + s0 * 512, ap=[[512, 128], [128 * 512, 4], [1, 512]]),
                   sg[:], r=[sgk], w=[('dram', 'VS')])
    em.barrier()


def mixer_mla(g, l, si):
    nc, em = g.nc, g.em
    L = g.seq_lens[si]
    Lmax = g.Lmax
    nq, nk = L // 512, L // 128
    NCH = max(1, min(8, nk // 4))
    kpc = nk // NCH
    with ExitStack() as st:
        def sb(name, shape, dt):
            return st.enter_context(nc.sbuf_tensor(g.un(name), list(shape), dt))
        Kp = sb("m_Kp", [64, L], BF16)
        Kn = sb("m_Kn", [128, L], BF16)
        Vh = sb("m_Vh", [128, nk, 128], BF16)
        qn = Rot([sb(f"m_qn{i}", [128, 512], BF16) for i in range(2)], "m_qn")
        qp = Rot([sb(f"m_qp{i}", [64, 512], BF16) for i in range(2)], "m_qp")
        PT = Rot([sb(f"m_PT{i}", [128, 512], BF16) for i in range(4)], "m_PT")
        rs = Rot([sb(f"m_rs{i}", [128, 512], F32) for i in range(2)], "m_rs")
        ob = Rot([sb(f"m_ob{i}", [128, 512], BF16) for i in range(2)], "m_ob")
        acc = Rot([sb(f"m_acc{i}", [128, 512], F32) for i in range(2)], "m_acc")
        ones32 = sb("m_ones", [128, 128], F32)
        em.dma('sp', ones32[:], g.ext["c_ones"][:, :], w=['ones32'])
        for ci in range(NCH):
            em.dma('sp', Kp[:, ci * kpc * 128:(ci + 1) * kpc * 128], g.KTp[:, ci * kpc * 128:(ci + 1) * kpc * 128],
                   r=[('dram', 'KTp')], w=[('Kp', ci)])
        nacc = 0
        for h in range(4):
            for ci in range(NCH):
                a, b_ = ci * kpc * 128, (ci + 1) * kpc * 128
                em.dma('sp', Kn[:, a:b_], g.KTn[h, :, a:b_], r=[('dram', 'KTn')], w=[('Kn', ci)])
                em.dma('sp', Vh[:, ci * kpc:(ci + 1) * kpc, :],
                       AP(tensor=g.VS.tensor, offset=g.VS.offset + a * 512 + h * 128, ap=[[512, 128], [128 * 512, kpc], [1, 128]]),
                       r=[('dram', 'VS')], w=[('Vh', ci)])
            for qb in range(nq):
                q0 = qb * 512
                qn_, qnk = qn.next()
                qp_, qpk = qp.next()
                em.dma('sp', qn_[:], g.QT[h, 0:128, q0:q0 + 512], r=[('dram', 'QT')], w=[qnk])
                em.dma('sp', qp_[:], g.QT[h, 128:192, q0:q0 + 512], r=[('dram', 'QT')], w=[qpk])
                po, pok = g.psum([4, 6][nacc % 2:nacc % 2 + 1])
                pz, pzk = g.psum([5, 7][nacc % 2:nacc % 2 + 1])
                ac, ack = acc.next()
                nacc += 1

                def s_tile(kt):
                    ci = kt // kpc
                    ps, pk = g.psum(range(4))
                    em.op('pe', lambda E: E.matmul(ps[:], Kn[:, kt * 128:(kt + 1) * 128], qn_[:], start=True, stop=False),
                          r=[('Kn', ci), qnk], w=[pk])
                    em.op('pe', lambda E: E.matmul(ps[:], Kp[:, kt * 128:(kt + 1) * 128], qp_[:], start=False, stop=True),
                          r=[('Kp', ci), qpk], w=[pk])
                    return ps, pk

                cur = s_tile(0)
                for kt in range(nk):
                    ci = kt // kpc
                    nxt = s_tile(kt + 1) if kt + 1 < nk else None
                    ps, pk = cur
                    pt, ptk = PT.next()
                    em.op('act', lambda E: E.activation(out=pt[:], in_=ps[:], func=AF.Exp), r=[pk], w=[ptk])
                    em.op('pe', lambda E: E.matmul(po[:], Vh[:, kt, :], pt[:], start=(kt == 0), stop=(kt == nk - 1)),
                          r=[('Vh', ci), ptk], w=[pok])
                    if kt == 0:
                        em.op('pool', lambda E: E.tensor_copy(ac[:], pt[:]), r=[ptk], w=[ack])
                    else:
                        em.op('pool', lambda E: E.tensor_tensor(ac[:], ac[:], pt[:], ALU.add), r=[ptk, ack], w=[ack])
                    cur = nxt
                em.op('pe', lambda E: E.matmul(pz[:], ones32[:], ac[:], start=True, stop=True), r=['ones32', ack], w=[pzk])
                r_, rk = rs.next()
                em.op('dve', lambda E: E.reciprocal(r_[:], pz[:]), r=[pzk], w=[rk])
                o_, ok = ob.next()
                em.op('dve', lambda E: E.tensor_tensor(o_[:], po[:], r_[:], ALU.mult), r=[pok, rk], w=[ok])
                em.dma('pool', g.OT[512 + h * 128:512 + (h + 1) * 128, q0:q0 + 512], o_[:], r=[ok], w=[('dram', 'OT')])
    em.barrier()


def stage_B(g, l, si):
    if g.mixers is None or 'mla' in g.mixers:
        mixer_mla(g, l, si)
    if g.mixers is None or 'gdn' in g.mixers:
        mixer_gdn(g, l, si)
    if g.mixers is None or 'hy' in g.mixers:
        mixer_hyena(g, l, si)


def mixer_gdn(g, l, si):
    nc, em, ext = g.nc, g.em, g.ext
    L = g.seq_lens[si]
    Lmax = g.Lmax
    nt = L // 128
    with ExitStack() as st:
        def sb(name, shape, dt):
            return st.enter_context(nc.sbuf_tensor(g.un(name), list(shape), dt))

        def T(name, shape, dt, n=2):
            return Rot([sb(f"g_{name}{i}", shape, dt) for i in range(n)], "g_" + name)
        cU = {0: sb("g_Ule", [128, 128], F32), 1: sb("g_Uge", [128, 128], F32)}
        cS = {0: sb("g_SL", [128, 128], F32), 1: sb("g_SU", [128, 128], F32)}
        ones32 = sb("g_ones", [128, 128], F32)
        negA = sb("g_negA", [128, 8], F32)
        dtb = sb("g_dtb", [128, 8], F32)
        gno = sb("g_gno", [128, 1], F32)
        em.dma('sp', cU[0][:], ext["c_U_le"][:, :], w=['cU0'])
        em.dma('sp', cU[1][:], ext["c_U_ge"][:, :], w=['cU1'])
        em.dma('sp', cS[0][:], ext["c_SL"][:, :], w=['cS0'])
        em.dma('sp', cS[1][:], ext["c_SU"][:, :], w=['cS1'])
        em.dma('sp', ones32[:], ext["c_ones"][:, :], w=['ones32'])
        em.dma('sp', negA[:], drow_bc(ext["gdn_a_log"], l * 8, 8), w=['negA'])
        em.dma('sp', dtb[:], drow_bc(ext["gdn_dt_bias"], l * 8, 8), w=['dtb'])
        em.dma('sp', gno[:], dcol(ext["gdn_out_norm"], l * 128, 1), w=['gno'], slow=True)
        em.op('act', lambda E: E.activation(out=negA[:], in_=negA[:], func=AF.Exp), r=['negA'], w=['negA'])
        em.op('dve', lambda E: E.tensor_scalar(negA[:], negA[:], -1.0, None, ALU.mult), r=['negA'], w=['negA'])
        em.op('dve', lambda E: E.tensor_scalar(gno[:], gno[:], math.sqrt(128.0), None, ALU.mult), r=['gno'], w=['gno'])
        maskT = {0: (cU[0], 'cU0'), 1: (cU[1], 'cU1')}
        qkv = T("qkv", [128, 1536], BF16)
        bd = T("bd", [128, 16], F32)
        sqf = T("sqf", [128, 1024], F32, 1)
        ss8 = T("ss8", [128, 8], F32)
        qkn = T("qkn", [128, 8, 128], BF16)
        qkT = T("qkT", [128, 8, 128], BF16)
        sm = T("sm", [128, 32], F32)
        gU = T("gU", [128, 4, 128], F32)
        gcc = T("gcc", [128, 8], F32)
        dif = T("dif", [128, 4, 128], F32)
        difT = T("difT", [128, 4, 128], F32)
        DmS = T("DmS", [128, 4, 128], F32)
        Bk = T("B", [128, 4, 128], F32, 3)
        Xk = T("X", [128, 4, 128], F32, 3)
        Pk = T("P", [128, 4, 128], F32, 3)
        IBk = T("IB", [128, 4, 128], F32, 3)
        attnT = T("attnT", [128, 4, 128], BF16)
        vb = T("vb", [128, 4, 128], F32)
        rr = T("rr", [128, 4, 128], F32)
        t1b = T("t1b", [128, 4, 128], F32)
        Slo = T("Slo", [128, 4, 128], BF16)
        Erow = T("Erow", [128, 4, 128], F32)
        qdT = T("qdT", [128, 4, 128], BF16)
        kdec = T("kdec", [128, 4, 128], BF16)
        vnew = T("vnew", [128, 4, 128], BF16)
        Sb = T("Sb", [128, 4, 128], BF16)
        S32 = sb("g_S32", [128, 4, 128], F32)
        of32 = T("of32", [128, 4, 128], F32)
        osum = T("osum", [128, 4, 128], F32)
        osq = T("osq", [128, 4, 128], BF16)
        orst = T("orst", [128, 4, 128], F32)
        zT = T("zT", [128, 4, 128], BF16)
        oout = T("oout", [128, 4, 128], BF16)
        identb4 = V(g.ident[:, 0:1], [(0, 4), (1, 128)])

        def f2(t):
            return t[:].rearrange("p h c -> p (h c)")

        def mm4(ps, lhs, lk, rhs, rk, pk, start=True, stop=True):
            for h in range(4):
                em.op('pe', lambda E: E.matmul(ps[:, h * 128:(h + 1) * 128], lhs[:, h, :], rhs[:, h, :], start=start, stop=stop),
                      r=[lk, rk], w=[pk])

        for d in range(2):
            U, Uk = cU[d], f'cU{d}'
            MS, MSk = cS[d], f'cS{d}'
            MT, MTk = maskT[d]
            em.op('pool', lambda E: E.memset(S32[:], 0.0), w=['S32'])
            sb_, sbk = Sb.next()
            em.op('pool', lambda E: E.memset(sb_[:], 0.0), w=[sbk])
            sl_, slk = Slo.next()
            em.op('pool', lambda E: E.memset(sl_[:], 0.0), w=[slk])
            order = range(nt) if d == 0 else range(nt - 1, -1, -1)
            for n in order:
                r0 = n * 128
                qkv_, qk_ = qkv.next()
                em.dma('sp', qkv_[:], g.PG[r0:r0 + 128, :], r=[('dram', 'PG')], w=[qk_])
                bd_, bdk = bd.next()
                em.dma('sp', bd_[:], g.BD[r0:r0 + 128, :], r=[('dram', 'BD')], w=[bdk])
                sq_, sqk = sqf.next()
                em.op('pool', lambda E: E.tensor_tensor(sq_[:], qkv_[:, 0:1024], qkv_[:, 0:1024], ALU.mult), r=[qk_], w=[sqk])
                s8, s8k = ss8.next()
                em.op('dve', lambda E: E.tensor_reduce(s8[:], sq_[:].rearrange("p (h c) -> p h c", h=8), AX.X, ALU.add), r=[sqk], w=[s8k])
                rsqrt(em, s8[:], s8[:], EPS, s8k, s8k)
                em.op('dve', lambda E: E.tensor_scalar(s8[:, 0:4], s8[:, 0:4], 128.0 ** -0.5, None, ALU.mult), r=[s8k], w=[s8k])
                qn_, qnk = qkn.next()
                em.op('dve', lambda E: E.tensor_tensor(qn_[:], qkv_[:, 0:1024].rearrange("p (h c) -> p h c", h=8),
                                                       V(s8[:, 0:1], [(1, 8), (0, 128)]), ALU.mult), r=[qk_, s8k], w=[qnk])
                qT_, qTk = qkT.next()
                for hh in range(2):
                    ps, pk = g.psum()
                    for h in range(4):
                        em.op('pe', lambda E: E.matmul(ps[:, h * 128:(h + 1) * 128], qn_[:, hh * 4 + h, :], g.identb[:], start=True, stop=True),
                              r=[qnk, 'identb'], w=[pk])
                    dst = qT_[:, hh * 4:(hh + 1) * 4, :].rearrange("p h c -> p (h c)")
                    if hh == 0:
                        em.op('act', lambda E: E.copy(dst, ps[:]), r=[pk], w=[qTk])
                    else:
                        em.op('dve', lambda E: E.tensor_copy(dst, ps[:]), r=[pk], w=[qTk])
                qT4, kT4, kn4 = qT_[:, 0:4], qT_[:, 4:8], qn_[:, 4:8]
                psG, pGk = g.psum()
                mm4(psG, kT4, qTk, kT4, qTk, pGk)
                psQ, pQk = g.psum()
                mm4(psQ, kT4, qTk, qT4, qTk, pQk)
                if GDN_CUT == 1:
                    continue
                sm_, smk = sm.next()
                beta, nbeta, gtok, tmp4, ecol, bg, ed, dec = (sm_[:, i * 4:(i + 1) * 4] for i in range(8))
                em.op('act', lambda E: E.activation(out=beta, in_=bd_[:, d * 4:d * 4 + 4], func=AF.Exp, scale=-1.0), r=[bdk], w=[smk])
                em.op('dve', lambda E: E.tensor_scalar(beta, beta, 1.0, None, ALU.add), r=[smk], w=[smk])
                em.op('dve', lambda E: E.reciprocal(beta, beta), r=[smk], w=[smk])
                em.op('dve', lambda E: E.tensor_scalar(nbeta, beta, -1.0, None, ALU.mult), r=[smk], w=[smk])
                em.op('dve', lambda E: E.tensor_tensor(tmp4, bd_[:, 8 + d * 4:12 + d * 4], dtb[:, d * 4:d * 4 + 4], ALU.add), r=[bdk, 'dtb'], w=[smk])
                em.op('act', lambda E: E.activation(out=tmp4, in_=tmp4, func=AF.Exp), r=[smk], w=[smk])
                em.op('act', lambda E: E.activation(out=tmp4, in_=tmp4, func=AF.Ln, bias=1.0, scale=1.0), r=[smk], w=[smk])
                em.op('dve', lambda E: E.tensor_tensor(gtok, tmp4, negA[:, d * 4:d * 4 + 4], ALU.mult), r=[smk, 'negA'], w=[smk])
                if GDN_CUT == 2:
                    continue
                gU_, gUk = gU.next()
                for h in range(4):
                    em.op('pool' if h % 2 else 'dve', lambda E: E.tensor_scalar(gU_[:, h, :], U[:], gtok[:, h:h + 1], None, ALU.mult),
                          r=[Uk, smk], w=[gUk])
                psR, pRk = g.psum()
                em.op('pe', lambda E: E.matmul(psR[:], ones32[:], f2(gU_), start=True, stop=True), r=['ones32', gUk], w=[pRk])
                psC, pCk = g.psum()
                em.op('pe', lambda E: E.matmul(psC[:, 0:4], U[:], gtok, start=True, stop=True), r=[Uk, smk], w=[pCk])
                em.op('pe', lambda E: E.matmul(psC[:, 4:8], ones32[:], gtok, start=True, stop=True), r=['ones32', smk], w=[pCk])
                gcc_, gck = gcc.next()
                em.op('dve', lambda E: E.tensor_copy(gcc_[:], psC[:, 0:8]), r=[pCk], w=[gck])
                dif_, difk = dif.next()
                dT_, dTk = difT.next()
                for h in range(4):
                    em.op('dve', lambda E: E.tensor_scalar(dif_[:, h, :], psR[:, h * 128:(h + 1) * 128], gcc_[:, h:h + 1], 0.0,
                                                           ALU.subtract, ALU.max), r=[pRk, gck], w=[difk])
                    em.op('dve', lambda E: E.tensor_scalar(dT_[:, h, :], psR[:, h * 128:(h + 1) * 128], gcc_[:, h:h + 1], 0.0,
                                                           ALU.subtract, ALU.min), r=[pRk, gck], w=[dTk])
                em.op('act', lambda E: E.activation(out=dif_[:], in_=dif_[:], func=AF.Exp, scale=-1.0), r=[difk], w=[difk])
                em.op('act', lambda E: E.activation(out=dT_[:], in_=dT_[:], func=AF.Exp), r=[dTk], w=[dTk])
                er_, erk = Erow.next()
                em.op('act', lambda E: E.activation(out=f2(er_), in_=psR[:], func=AF.Exp), r=[pRk, difk, dTk], w=[erk])
                if GDN_CUT == 3:
                    continue
                ds_, dsk = DmS.next()
                for h in range(4):
                    em.op('dve', lambda E: E.scalar_tensor_tensor(ds_[:, h, :], dif_[:, h, :], nbeta[:, h:h + 1], MS[:], ALU.mult, ALU.mult),
                          r=[difk, smk, MSk], w=[dsk])
                B0, B0k = Bk.next()
                em.op('dve', lambda E: E.tensor_tensor(f2(B0), psG[:], f2(ds_), ALU.mult), r=[pGk, dsk], w=[B0k])
                if GDN_CUT == 31:
                    continue
                em.op('pool', lambda E: E.tensor_tensor(dT_[:], dT_[:], V(MT[:, 0:1], [(0, 4), (1, 128)]), ALU.mult), r=[dTk, MTk], w=[dTk])
                at_, atk = attnT.next()
                em.op('dve', lambda E: E.tensor_tensor(f2(at_), psQ[:], f2(dT_), ALU.mult), r=[pQk, dTk], w=[atk])
                if GDN_CUT == 32:
                    continue
                psX, pXk = g.psum()
                for h in range(4):
                    em.op('pe', lambda E: E.matmul(psX[:, h * 128:(h + 1) * 128], B0[:, h, :], g.ident[:], start=True, stop=True),
                          r=[B0k, 'ident'], w=[pXk])
                if GDN_CUT == 33:
                    continue
                X0, X0k = Xk.next()
                em.op('dve', lambda E: E.tensor_copy(f2(X0), psX[:]), r=[pXk], w=[X0k])
                if GDN_CUT == 34:
                    continue
                P0, P0k = Pk.next()
                em.op('dve', lambda E: E.tensor_tensor(P0[:], psX[:].rearrange("p (h c) -> p h c", h=4), identb4, ALU.add),
                      r=[pXk, 'ident'], w=[P0k])
                if GDN_CUT == 4:
                    continue
                Bc, Bck, Xc, Xck, Pc, Pck = B0, B0k, X0, X0k, P0, P0k
                for k in range(1, 7):
                    if k < 6:
                        psX, pXk = g.psum()
                        mm4(psX, Bc, Bck, Xc, Xck, pXk)
                    psB, pBk = g.psum()
                    mm4(psB, Xc, Xck, Bc, Bck, pBk)
                    if k < 6:
                        Xn, Xnk = Xk.next()
                        em.op('dve', lambda E: E.tensor_copy(f2(Xn), psX[:]), r=[pXk], w=[Xnk])
                        Bn, Bnk = Bk.next()
                        em.op('dve', lambda E: E.tensor_copy(f2(Bn), psB[:]), r=[pBk], w=[Bnk])
                    IB, IBk_ = IBk.next()
                    em.op('dve', lambda E: E.tensor_tensor(IB[:], psB[:].rearrange("p (h c) -> p h c", h=4), identb4, ALU.add),
                          r=[pBk, 'ident'], w=[IBk_])
                    psP, pPk = g.psum()
                    mm4(psP, IB, IBk_, Pc, Pck, pPk)
                    Pn, Pnk = Pk.next()
                    em.op('dve', lambda E: E.tensor_copy(f2(Pn), psP[:]), r=[pPk], w=[Pnk])
                    Pc, Pck = Pn, Pnk
                    if k < 6:
                        Bc, Bck, Xc, Xck = Bn, Bnk, Xn, Xnk
                TmT, Tk = Pc, Pck
                if GDN_CUT == 5:
                    continue
                em.op('act', lambda E: E.activation(out=ecol, in_=gcc_[:, 0:4], func=AF.Exp), r=[gck], w=[smk])
                em.op('dve', lambda E: E.tensor_tensor(bg, beta, ecol, ALU.mult), r=[smk], w=[smk])
                em.op('dve', lambda E: E.tensor_tensor(ed, gcc_[:, 4:8], gcc_[:, 0:4], ALU.subtract), r=[gck], w=[smk])
                em.op('act', lambda E: E.activation(out=ed, in_=ed, func=AF.Exp), r=[smk], w=[smk])
                em.op('act', lambda E: E.activation(out=dec, in_=gcc_[:, 4:8], func=AF.Exp), r=[gck], w=[smk])
                vb_, vbk = vb.next()
                em.op('pool', lambda E: E.tensor_tensor(vb_[:], qkv_[:, 1024:1536].rearrange("p (h c) -> p h c", h=4),
                                                        V(beta[:, 0:1], [(1, 4), (0, 128)]), ALU.mult), r=[qk_, smk], w=[vbk])
                kd_, kdk = kdec.next()
                em.op('pool', lambda E: E.tensor_tensor(kd_[:], kn4, V(ed[:, 0:1], [(1, 4), (0, 128)]), ALU.mult), r=[qnk, smk], w=[kdk])
                qd_, qdk = qdT.next()
                em.op('dve', lambda E: E.tensor_tensor(qd_[:], qT4, er_[:], ALU.mult), r=[qTk, erk], w=[qdk])
                if GDN_CUT == 6:
                    continue
                psV, pVk = g.psum()
                for h in range(4):
                    em.op('pe', lambda E: E.matmul(psV[:, h * 128:(h + 1) * 128], kT4[:, h, :], sb_[:, h, :], start=True, stop=False),
                          r=[qTk, sbk], w=[pVk])
                    em.op('pe', lambda E: E.matmul(psV[:, h * 128:(h + 1) * 128], kT4[:, h, :], sl_[:, h, :], start=False, stop=True),
                          r=[qTk, slk], w=[pVk])
                t1_, t1k = t1b.next()
                em.op('dve', lambda E: E.tensor_tensor(t1_[:], psV[:].rearrange("p (h c) -> p h c", h=4), V(bg[:, 0:1], [(1, 4), (0, 128)]), ALU.mult),
                      r=[pVk, smk], w=[t1k])
                r_, rk_ = rr.next()
                em.op('pool', lambda E: E.tensor_tensor(r_[:], vb_[:], t1_[:], ALU.subtract), r=[vbk, t1k], w=[rk_])
                psN2, pN2k = g.psum()
                mm4(psN2, TmT, Tk, r_, rk_, pN2k)
                vn_, vnk = vnew.next()
                em.op('act', lambda E: E.copy(f2(vn_), psN2[:]), r=[pN2k], w=[vnk])
                psO, pOk = g.psum()
                for h in range(4):
                    em.op('pe', lambda E: E.matmul(psO[:, h * 128:(h + 1) * 128], sb_[:, h, :], qd_[:, h, :], start=True, stop=False),
                          r=[sbk, qdk], w=[pOk])
                    em.op('pe', lambda E: E.matmul(psO[:, h * 128:(h + 1) * 128], vn_[:, h, :], at_[:, h, :], start=False, stop=True),
                          r=[vnk, atk], w=[pOk])
                psS, pSk = g.psum()
                mm4(psS, kd_, kdk, vn_, vnk, pSk)
                em.op('pool', lambda E: E.tensor_tensor(S32[:], S32[:], V(dec[:, 0:1], [(1, 4), (0, 128)]), ALU.mult), r=['S32', smk], w=['S32'])
                em.op('dve', lambda E: E.tensor_tensor(f2(S32), f2(S32), psS[:], ALU.add), r=['S32', pSk], w=['S32'])
                sb_, sbk = Sb.next()
                em.op('act', lambda E: E.copy(sb_[:], S32[:]), r=['S32'], w=[sbk])
                sl_, slk = Slo.next()
                em.op('pool', lambda E: E.tensor_tensor(sl_[:], S32[:], sb_[:], ALU.subtract), r=['S32', sbk], w=[slk])
                if GDN_CUT == 7:
                    continue
                ofap = AP(tensor=g.OF.tensor, offset=g.OF.offset + r0, ap=[[Lmax, 128], [128 * Lmax, 4], [1, 128]])
                if d == 0:
                    of_, ofk = of32.next()
                    em.op('dve', lambda E: E.tensor_copy(f2(of_), psO[:]), r=[pOk], w=[ofk])
                    em.dma('pool', ofap, of_[:], r=[ofk], w=[('dram', 'OF')])
                else:
                    of_, ofk = of32.next()
                    em.dma('sp', of_[:], ofap, r=[('dram', 'OF')], w=[ofk])
                    z_, zk = zT.next()
                    em.dma('sp', z_[:], AP(tensor=g.ZT.tensor, offset=g.ZT.offset + r0, ap=[[Lmax, 128], [128 * Lmax, 4], [1, 128]]),
                           r=[('dram', 'ZT')], w=[zk])
                    os_, osk = osum.next()
                    em.op('dve', lambda E: E.tensor_tensor(f2(os_), psO[:], f2(of_), ALU.add), r=[pOk, ofk], w=[osk])
                    oq_, oqk = osq.next()
                    em.op('act', lambda E: E.activation(out=oq_[:], in_=os_[:], func=AF.Square), r=[osk], w=[oqk])
                    psN, pNk = g.psum()
                    em.op('pe', lambda E: E.matmul(psN[:], g.onesb[:], f2(oq_), start=True, stop=True), r=['onesb', oqk], w=[pNk])
                    or_, ork = orst.next()
                    rsqrt(em, f2(or_), psN[:], 128 * EPS, pNk, ork)
                    em.op('dve', lambda E: E.tensor_tensor(or_[:], or_[:], os_[:], ALU.mult), r=[ork, osk], w=[ork])
                    oo_, ook = oout.next()
                    em.op('dve', lambda E: E.scalar_tensor_tensor(oo_[:], or_[:], gno[:, 0:1], z_[:], ALU.mult, ALU.mult),
                          r=[ork, 'gno', zk], w=[ook])
                    em.dma('pool', AP(tensor=g.OT.tensor, offset=g.OT.offset + 1024 * Lmax + r0, ap=[[Lmax, 128], [128 * Lmax, 4], [1, 128]]),
                           oo_[:], r=[ook], w=[('dram', 'OT')])
            em.barrier()
    em.barrier()


def mixer_hyena(g, l, si):
    nc, em, ext = g.nc, g.em, g.ext
    L = g.seq_lens[si]
    Lmax = g.Lmax
    N = 2 * L
    N1 = N // 128
    P1 = L // 128
    R1 = min(N1, HY_RMAX)
    nch1 = max(1, N1 // HY_RMAX)
    CG = min(64, 4096 // N1)
    CS = 64
    ncol = CG * N1
    nchunk = ncol // 512
    cpb = 512 // N1
    nb1 = min(CG, 512 // (2 * N1))
    KF = g.KFq[L]
    KTd = g.KTd[L]
    TWO_PI = 2.0 * math.pi
    with ExitStack() as st:
        def sb(name, shape, dt):
            return st.enter_context(nc.sbuf_tensor(g.un(name), list(shape), dt))

        def T(name, shape, dt, n=2):
            return Rot([sb(f"h_{name}{i}", shape, dt) for i in range(n)], "h_" + name)
        F1 = sb("h_F1", [R1, nch1, 2 * N1], BF16)
        Twr = sb("h_Twr", [128, N1], F32)
        Twi = sb("h_Twi", [128, N1], F32)
        TwTr = sb("h_TwTr", [R1, nch1, 128], F32)
        TwTi = sb("h_TwTi", [R1, nch1, 128], F32)
        G1r = sb("h_G1r", [R1, nch1, P1], BF16)
        G1i = sb("h_G1i", [R1, nch1, P1], BF16)
        F2r = sb("h_F2r", [128, 128], BF16)
        F2i = sb("h_F2i", [128, 128], BF16)
        F2in = sb("h_F2in", [128, 128], BF16)
        G2a = sb("h_G2a", [128, 256], BF16)
        G2b = sb("h_G2b", [128, 256], BF16)
        skip = sb("h_skip", [128, 2, 512], F32)
        for t_, nm in ((F1, f"hyF1_{L}"), (TwTr, f"hyTwTr_{L}"), (TwTi, f"hyTwTi_{L}"), (G1r, f"hyG1r_{L}"), (G1i, f"hyG1i_{L}")):
            em.dma('sp', t_[:], ext["c_" + nm][:, :, :], w=['hconst'])
        for t_, nm in ((Twr, f"hyTwr_{L}"), (Twi, f"hyTwi_{L}"), (F2r, "hyF2r"), (F2i, "hyF2i"), (F2in, "hyF2in"), (G2a, "hyG2a"), (G2b, "hyG2b")):
            em.dma('sp', t_[:], ext["c_" + nm][:, :], w=['hconst'])
        em.dma('sp', skip[:].rearrange("p o c -> p (o c)"), drow_bc(ext["hy_skip"], l * 1024, 1024), w=['hconst'])
        HC = 'hconst'
        Apr = sb("h_Apr", [128, CG, N1], BF16)
        Api = sb("h_Api", [128, CG, N1], BF16)
        Zkr = sb("h_Zkr", [128, CG, N1], BF16)
        Zki = sb("h_Zki", [128, CG, N1], BF16)
        Bpr = sb("h_Bpr", [R1, nch1, CG, 128], BF16)
        Bpi = sb("h_Bpi", [R1, nch1, CG, 128], BF16)
        tP = T("tP", [128, 512], F32, 2)
        tQ = T("tQ", [128, 512], F32, 2)
        kfc = T("kfc", [128, 2, 512], BF16, 2)

        def cmul_tw(ps, pk, M, units, width, tr, ti, outr_fn, outi_fn, okeys):
            n = units * 2 * width
            src = V(ps[0:M, 0:1], [(2 * width, units), (width, 2), (1, width)])
            trb = V(tr, [(0, units), (0, 2), (1, width)])
            tib = V(ti, [(0, units), (0, 2), (1, width)])
            p_, pk_ = tP.next()
            q_, qk_ = tQ.next()
            pv = V(p_[0:M, 0:1], [(2 * width, units), (width, 2), (1, width)])
            qv = V(q_[0:M, 0:1], [(2 * width, units), (width, 2), (1, width)])
            em.op('dve', lambda E: E.tensor_tensor(pv, src, trb, ALU.mult), r=[pk, HC], w=[pk_])
            em.op('dve', lambda E: E.tensor_tensor(qv, src, tib, ALU.mult), r=[pk, HC], w=[qk_])
            pre = V(p_[0:M, 0:1], [(2 * width, units), (1, width)])
            pim = V(p_[0:M, 0:1], [(2 * width, units), (1, width)], off=width)
            qre = V(q_[0:M, 0:1], [(2 * width, units), (1, width)])
            qim = V(q_[0:M, 0:1], [(2 * width, units), (1, width)], off=width)
            em.op('pool', lambda E: E.tensor_tensor(outr_fn, pre, qim, ALU.subtract), r=[pk_, qk_], w=[okeys[0]])
            em.op('pool', lambda E: E.tensor_tensor(outi_fn, qre, pim, ALU.add), r=[pk_, qk_], w=[okeys[1]])

        def fwd_fft(lhs_fn, kchs):
            for c0 in range(0, CG, nb1):
                ps, pk = g.psum()
                for u in range(nb1):
                    for kc in range(kchs):
                        lap, lk = lhs_fn(c0 + u, kc)
                        K_ = lap.shape[0]
                        em.op('pe', lambda E: E.matmul(ps[:, u * 2 * N1:(u + 1) * 2 * N1], lap, F1[0:K_, kc, :],
                                                       start=(kc == 0), stop=(kc == kchs - 1)), r=[lk, HC], w=[pk])
                cmul_tw(ps, pk, 128, nb1, N1, Twr[:, :], Twi[:, :], Apr[:, c0:c0 + nb1, :], Api[:, c0:c0 + nb1, :], ['Apr', 'Api'])

        def stage3(j):
            cols = slice(j * 512, (j + 1) * 512)
            ar = Apr[:].rearrange("p c f -> p (c f)")[:, cols]
            ai = Api[:].rearrange("p c f -> p (c f)")[:, cols]
            pzr, pzrk = g.psum()
            em.op('pe', lambda E: E.matmul(pzr[:], F2r[:], ar, start=True, stop=False), r=['Apr', HC], w=[pzrk])
            em.op('pe', lambda E: E.matmul(pzr[:], F2in[:], ai, start=False, stop=True), r=['Api', HC], w=[pzrk])
            pzi, pzik = g.psum()
            em.op('pe', lambda E: E.matmul(pzi[:], F2i[:], ar, start=True, stop=False), r=['Apr', HC], w=[pzik])
            em.op('pe', lambda E: E.matmul(pzi[:], F2r[:], ai, start=False, stop=True), r=['Api', HC], w=[pzik])
            return pzr, pzrk, pzi, pzik

        if g.filt_done.get(L) != l:
            g.filt_done[L] = l
            with ExitStack() as st2:
                def sb2(name, shape, dt):
                    return st2.enter_context(nc.sbuf_tensor(g.un(name), list(shape), dt))
                w1 = sb2("hf_w1", [33, 64], F32)
                w2 = sb2("hf_w2", [64, 64], F32)
                w3 = sb2("hf_w3", [64, 2048], F32)
                pc = sb2("hf_pc", [64, 8], F32)
                dl = sb2("hf_dl", [128, 512], F32)
                tneg = sb2("hf_tneg", [128, N1], F32)
                ones32 = sb2("hf_ones", [128, 128], F32)
                invn = sb2("hf_invn", [128, 2, 512], F32)
                em.dma('sp', w1[:], ext["hy_ffn_w1"][l], w=['fw'])
                em.dma('sp', w2[:], ext["hy_ffn_w2"][l], w=['fw'])
                em.dma('sp', w3[:], ext["hy_ffn_w3"][l], w=['fw'])
                em.dma('sp', pc[:, 0:1], dcol(ext["hy_ffn_b1"], l * 64, 1, nparts=64), w=['fpc'], slow=True)
                em.dma('sp', pc[:, 1:2], dcol(ext["hy_sin_freq"], l * 64, 1, nparts=64), w=['fpc'], slow=True)
                em.dma('sp', pc[:, 2:3], dcol(ext["hy_ffn_b2"], l * 64, 1, nparts=64), w=['fpc'], slow=True)
                em.dma('sp', dl[:], ext["c_hydelta"][:, :], w=['fw'])
                em.dma('sp', tneg[:], ext[f"c_hytneg_{L}"][:, :], w=['fw'])
                em.dma('sp', ones32[:], ext["c_ones"][:, :], w=['fw'])
                em.op('dve', lambda E: E.tensor_scalar(pc[:, 1:2], pc[:, 1:2], 1.0 / TWO_PI, None, ALU.mult), r=['fpc'], w=['fpc'])
                em.op('pool', lambda E: E.memset(pc[:, 3:4], -math.pi), w=['fpc'])
                ft = Rot([sb2(f"hf_ft{i}", [33, 512], F32) for i in range(2)], "hf_ft")
                hh1 = Rot([sb2(f"hf_h1{i}", [64, 512], F32) for i in range(2)], "hf_h1")
                hh2 = Rot([sb2(f"hf_h2{i}", [64, 512], F32) for i in range(2)], "hf_h2")
                dk = Rot([sb2(f"hf_dk{i}", [128, 512], F32) for i in range(2)], "hf_dk")
                kf32 = Rot([sb2(f"hf_k32{i}", [128, 512], F32) for i in range(2)], "hf_k32")
                kab = Rot([sb2(f"hf_kab{i}", [128, 512], F32) for i in range(2)], "hf_kab")
                kbf = Rot([sb2(f"hf_kbf{i}", [128, 2, 512], BF16) for i in range(2)], "hf_kbf")
                feat = ext[f"c_hyfeat_{L}"]
                msk = sb2("hf_msk", [64, 512], F32)

                def sin_layer(ps, pk, bcol, out, ok):
                    em.op('dve', lambda E: E.tensor_scalar(out, ps, pc[:, bcol:bcol + 1], pc[:, 1:2], ALU.add, ALU.mult), r=[pk, 'fpc'], w=[ok])
                    for _ in range(2):
                        em.op('dve', lambda E: E.tensor_scalar(msk[:], out, 0.5, None, ALU.is_gt), r=[ok], w=['msk'])
                        em.op('dve', lambda E: E.tensor_tensor(out, out, msk[:], ALU.subtract), r=[ok, 'msk'], w=[ok])
                        em.op('dve', lambda E: E.tensor_scalar(msk[:], out, -0.5, None, ALU.is_lt), r=[ok], w=['msk'])
                        em.op('dve', lambda E: E.tensor_tensor(out, out, msk[:], ALU.add), r=[ok, 'msk'], w=[ok])
                    em.op('act', lambda E: E.activation(out=out, in_=out, func=AF.Sin, scale=TWO_PI), r=[ok], w=[ok])

                pn = [g.psum([6]), g.psum([7])]
                ntile_tot = N // 128
                for blk in range(N // 512):
                    f_, fk = ft.next()
                    em.dma('sp', f_[:], feat[:, blk * 512:(blk + 1) * 512], w=[fk])
                    ps, pk = g.psum(range(6))
                    em.op('pe', lambda E: E.matmul(ps[0:64, :], w1[:], f_[:], start=True, stop=True), r=['fw', fk], w=[pk])
                    h1, h1k = hh1.next()
                    sin_layer(ps[0:64, :], pk, 0, h1[:], h1k)
                    ps, pk = g.psum(range(6))
                    em.op('pe', lambda E: E.matmul(ps[0:64, :], w2[:], h1[:], start=True, stop=True), r=['fw', h1k], w=[pk])
                    h2, h2k = hh2.next()
                    sin_layer(ps[0:64, :], pk, 2, h2[:], h2k)
                    for tt in range(4):
                        ti_ = blk * 4 + tt
                        dirn = 0 if ti_ * 128 < L else 1
                        d_, dkk = dk.next()
                        em.op('act', lambda E: E.activation(out=d_[:], in_=dl[:], func=AF.Exp, scale=tneg[:, ti_:ti_ + 1]), r=['fw'], w=[dkk])
                        kb_, kbk = kbf.next()
                        for o in range(2):
                            ps, pk = g.psum(range(6))
                            col0 = o * 1024 + dirn * 512
                            em.op('pe', lambda E: E.matmul(ps[:], h2[:, tt * 128:(tt + 1) * 128], w3[:, col0:col0 + 512], start=True, stop=True),
                                  r=[h2k, 'fw'], w=[pk])
                            k_, kk = kf32.next()
                            em.op('dve', lambda E: E.tensor_tensor(k_[:], ps[:], d_[:], ALU.mult), r=[pk, dkk], w=[kk])
                            if ti_ * 128 == L:
                                em.op('pool', lambda E: E.memset(k_[0:1, :], 0.0), w=[kk])
                            a_, ak = kab.next()
                            em.op('act', lambda E: E.activation(out=a_[:], in_=k_[:], func=AF.Abs), r=[kk], w=[ak])
                            em.op('pe', lambda E: E.matmul(pn[o][0][:], ones32[:], a_[:], start=(ti_ == 0), stop=(ti_ == ntile_tot - 1)),
                                  r=['fw', ak], w=[pn[o][1]])
                            em.op('act', lambda E: E.copy(kb_[:, o, :], k_[:]), r=[kk], w=[kbk])
                        em.dma('pool', KTd[ti_ * 128:(ti_ + 1) * 128, :], kb_[:].rearrange("p o c -> p (o c)"), r=[kbk], w=[('dram', 'KTd')])
                for o in range(2):
                    em.op('dve', lambda E: E.reciprocal(invn[:, o, :], pn[o][0][:]), r=[pn[o][1]], w=['invn'])
                ktd = Rot([sb2(f"hf_ktd{i}", [R1, nch1, 128, CG], BF16) for i in range(2)], "hf_ktd")
                for o in range(2):
                    for gi in range(512 // CG):
                        cbase = gi * CG
                        kt_, ktk = ktd.next()
                        for kc in range(nch1):
                            em.dma('sp', kt_[:, kc], AP(tensor=KTd.tensor, offset=KTd.offset + kc * R1 * 128 * 1024 + o * 512 + cbase,
                                                        ap=[[128 * 1024, R1], [1024, 128], [1, CG]]), r=[('dram', 'KTd')], w=[ktk])
                        fwd_fft(lambda c, kc: (V(kt_[:, kc, 0, 0:1], [(CG, 128)], off=c), ktk), nch1)
                        for j in range(nchunk):
                            pzr, pzrk, pzi, pzik = stage3(j)
                            kc_, kck = kfc.next()
                            inb = V(invn[:, o, cbase + j * cpb:cbase + j * cpb + 1], [(1, cpb), (0, N1)])
                            em.op('dve', lambda E: E.tensor_tensor(kc_[:, 0, :].rearrange("p (c f) -> p c f", c=cpb),
                                                                   pzr[:].rearrange("p (c f) -> p c f", c=cpb), inb, ALU.mult),
                                  r=[pzrk, 'invn'], w=[kck])
                            em.op('dve', lambda E: E.tensor_tensor(kc_[:, 1, :].rearrange("p (c f) -> p c f", c=cpb),
                                                                   pzi[:].rearrange("p (c f) -> p c f", c=cpb), inb, ALU.mult),
                                  r=[pzik, 'invn'], w=[kck])
                            em.dma('pool', KF[o, gi, j].rearrange("p a c -> p (a c)"), kc_[:].rearrange("p a c -> p (a c)"), r=[kck],
                                   w=[('dram', 'KF')])
            em.barrier()
        xs = {nm: T(nm, [P1, 128, CS], BF16, 1) for nm in ("x1s", "x2s", "vs")}
        z1 = sb("h_z1", [P1, 128, CG], BF16)
        z2s = sb("h_z2s", [P1, 128, CS], BF16)
        gt1 = T("gt1", [P1, 128, 4], F32, 2)
        gt2 = T("gt2", [P1, 128, 4], F32, 2)
        for sl in range(512 // CS):
            tiles = {}
            for i, nm in enumerate(("x1s", "x2s", "vs")):
                t_, tk = xs[nm].next()
                src_ = AP(tensor=g.PH.tensor, offset=g.PH.offset + i * 512 + sl * CS, ap=[[128 * 1536, P1], [1536, 128], [1, CS]])
                nsp = 2 if P1 >= 128 else 1
                for sp_ in range(nsp):
                    a0, a1 = sp_ * P1 // nsp, (sp_ + 1) * P1 // nsp
                    em.dma('sp', psplit(t_[:], a0, a1), psplit(src_, a0, a1), r=[('dram', 'PH')], w=[tk])
                tiles[nm] = (t_, tk)
            for sg_ in range(CS // CG):
                gi = sl * (CS // CG) + sg_
                cb = sg_ * CG
                cglob = gi * CG
                zsrc, zk = tiles["vs"]
                zoff, zstride = cb, CS
                for o in range(2):
                    gate, gk = tiles["x1s" if o == 0 else "x2s"]
                    zt, ztk, zo, zs_ = zsrc, zk, zoff, zstride
                    fwd_fft(lambda c, kc: (V(zt[:, 0, 0:1], [(zs_, 128)], off=zo + c), ztk), 1)
                    for j in range(nchunk):
                        pzr, pzrk, pzi, pzik = stage3(j)
                        kc_, kck = kfc.next()
                        em.dma('sp', kc_[:].rearrange("p a c -> p (a c)"), KF[o, gi, j].rearrange("p a c -> p (a c)"), r=[('dram', 'KF')], w=[kck])
                        cols = slice(j * 512, (j + 1) * 512)
                        zr_out = Zkr[:].rearrange("p c f -> p (c f)")[:, cols]
                        zi_out = Zki[:].rearrange("p c f -> p (c f)")[:, cols]
                        a_, ak = tP.next()
                        b_, bk = tQ.next()
                        em.op('dve', lambda E: E.tensor_tensor(a_[:], pzr[:], kc_[:, 0, :], ALU.mult), r=[pzrk, kck], w=[ak])
                        em.op('dve', lambda E: E.tensor_tensor(b_[:], pzi[:], kc_[:, 1, :], ALU.mult), r=[pzik, kck], w=[bk])
                        em.op('pool', lambda E: E.tensor_tensor(zr_out, a_[:], b_[:], ALU.subtract), r=[ak, bk], w=['Zkr'])
                        a_, ak = tP.next()
                        b_, bk = tQ.next()
                        em.op('dve', lambda E: E.tensor_tensor(a_[:], pzr[:], kc_[:, 1, :], ALU.mult), r=[pzrk, kck], w=[ak])
                        em.op('dve', lambda E: E.tensor_tensor(b_[:], pzi[:], kc_[:, 0, :], ALU.mult), r=[pzik, kck], w=[bk])
                        em.op('pool', lambda E: E.tensor_tensor(zi_out, a_[:], b_[:], ALU.add), r=[ak, bk], w=['Zki'])
                    for fc in range(nch1):
                        for c0 in range(0, CG, 2):
                            ps, pk = g.psum()
                            for u in range(2):
                                c = c0 + u
                                em.op('pe', lambda E: E.matmul(ps[0:R1, u * 256:(u + 1) * 256], Zkr[:, c, fc * R1:(fc + 1) * R1], G2a[:],
                                                               start=True, stop=False), r=['Zkr', HC], w=[pk])
                                em.op('pe', lambda E: E.matmul(ps[0:R1, u * 256:(u + 1) * 256], Zki[:, c, fc * R1:(fc + 1) * R1], G2b[:],
                                                               start=False, stop=True), r=['Zki', HC], w=[pk])
                            cmul_tw(ps, pk, R1, 2, 128, TwTr[:, fc, :], TwTi[:, fc, :], Bpr[:, fc, c0:c0 + 2, :], Bpi[:, fc, c0:c0 + 2, :],
                                    ['Bpr', 'Bpi'])
                    zn_, znk = (z1, 'z1') if o == 0 else (z2s, 'z2s')
                    zc0 = 0 if o == 0 else cb
                    for c0 in range(0, CG, 4):
                        ps, pk = g.psum()
                        n = 0
                        for fc in range(nch1):
                            for (gm, bp, bpk) in ((G1r, Bpr, 'Bpr'), (G1i, Bpi, 'Bpi')):
                                em.op('pe', lambda E: E.matmul(ps[0:P1, :], gm[:, fc, :], bp[:, fc, c0:c0 + 4, :].rearrange("p c t -> p (c t)"),
                                                               start=(n == 0), stop=(n == 2 * nch1 - 1)), r=[bpk, HC], w=[pk])
                                n += 1
                        yv = V(ps[0:P1, 0:1], [(1, 128), (128, 4)])
                        zv = V(zt[:, 0, 0:1], [(zs_, 128), (1, 4)], off=zo + c0)
                        gv = gate[:, :, cb + c0:cb + c0 + 4]
                        skb = V(skip[0:P1, o, cglob + c0:cglob + c0 + 1], [(0, 128), (1, 4)])
                        t1_, t1k = gt1.next()
                        em.op('pool', lambda E: E.tensor_tensor(t1_[:], zv, skb, ALU.mult), r=[ztk, HC], w=[t1k])
                        t2_, t2k = gt2.next()
                        em.op('dve', lambda E: E.tensor_tensor(t2_[:], yv, t1_[:], ALU.add), r=[pk, t1k], w=[t2k])
                        em.op('pool', lambda E: E.tensor_tensor(zn_[:, :, zc0 + c0:zc0 + c0 + 4], t2_[:], gv, ALU.mult), r=[t2k, gk], w=[znk])
                    if o == 0:
                        zsrc, zk, zoff, zstride = z1, 'z1', 0, CG
            dst_ = AP(tensor=g.Z2.tensor, offset=g.Z2.offset + sl * CS, ap=[[128 * 512, P1], [512, 128], [1, CS]])
            nsp = 2 if P1 >= 128 else 1
            for sp_ in range(nsp):
                a0, a1 = sp_ * P1 // nsp, (sp_ + 1) * P1 // nsp
                em.dma('pool', psplit(dst_, a0, a1), psplit(z2s[:], a0, a1), r=['z2s'], w=[('dram', 'Z2')])
        em.barrier()
        zl = T("zl", [128, 512], BF16, 2)
        zo_ = T("zo", [128, 4, 128], BF16, 2)
        for t in range(L // 128):
            a_, ak = zl.next()
            em.dma('sp', a_[:], g.Z2[t * 128:(t + 1) * 128, :], r=[('dram', 'Z2')], w=[ak])
            ps, pk = g.psum()
            for c in range(4):
                em.op('pe', lambda E: E.matmul(ps[:, c * 128:(c + 1) * 128], a_[:, c * 128:(c + 1) * 128], g.identb[:], start=True, stop=True),
                      r=[ak, 'identb'], w=[pk])
            o_, ok = zo_.next()
            em.op('act' if t % 2 else 'dve', lambda E: (E.copy if t % 2 else E.tensor_copy)(o_[:].rearrange("p c t -> p (c t)"), ps[:]),
                  r=[pk], w=[ok])
            em.dma('pool', AP(tensor=g.OT.tensor, offset=g.OT.offset + t * 128, ap=[[Lmax, 128], [128 * Lmax, 4], [1, 128]]), o_[:],
                   r=[ok], w=[('dram', 'OT')])
    em.barrier()


def stage_C(g, l, si):
    nc, em, ext = g.nc, g.em, g.ext
    L = g.seq_lens[si]
    t0 = g.seq_off[si]
    Ttot, Lmax = g.Ttot, g.Lmax
    nb = L // 512
    with ExitStack() as st:
        def sb(name, shape, dt):
            return st.enter_context(nc.sbuf_tensor(g.un(name), list(shape), dt))
        g32 = sb("c_g32", [128, 3, KC], F32)
        em.op('dve', lambda E: E.tensor_scalar(g32[:], g.gains[:, 1:4, l, :], 32.0, None, ALU.mult), r=['gains'], w=['g32'])
        xT = sb("c_xT", [128, KC, 512], F32)
        yT = sb("c_yT", [128, KC, 512], F32)
        OTb = sb("c_OTb", [128, 12, 512], BF16)
        Gj = Rot([sb(f"c_Gj{i}", [128, 3, 512], BF16) for i in range(2)], "c_Gj")
        wbr = Rot([sb(f"c_wbr{i}", [128, 12, 128], BF16) for i in range(2)], "c_wbr")
        wout = Rot([sb(f"c_wout{i}", [128, KC, 128], BF16) for i in range(2)], "c_wout")
        wgu = Rot([sb(f"c_wgu{i}", [128, 2, KC, 128], BF16) for i in range(3)], "c_wgu")
        wd = Rot([sb(f"c_wd{i}", [128, NFF, 128], BF16) for i in range(2)], "c_wd")
        tmp = Rot([sb(f"c_tmp{i}", [128, 512], F32) for i in range(4)], "c_tmp")
        sgt = Rot([sb(f"c_sgt{i}", [128, 512], BF16) for i in range(2)], "c_sgt")
        mT = sb("c_mT", [128, KC, 512], BF16)
        sqb = sb("c_sqb", [128, KC, 512], BF16)
        h2T = sb("c_h2T", [128, KC, 512], BF16)
        actT = sb("c_actT", [128, NFF, 512], BF16)
        rstd = Rot([sb(f"c_rstd{i}", [128, 512], F32) for i in range(2)], "c_rstd")
        eng2 = ['dve', 'pool']

        def norm_rstd(sqk_):
            ps, pk = g.psum()
            for kc in range(KC):
                em.op('pe', lambda E: E.matmul(ps[:], g.onesb[:], sqb[:, kc, :], start=(kc == 0), stop=(kc == KC - 1)),
                      r=[sqk_, 'onesb'], w=[pk])
            rs, rsk = rstd.next()
            rsqrt(em, rs[:], ps[:], D * EPS, pk, rsk)
            return rs, rsk

        for b in range(nb):
            c0 = t0 + b * 512
            s0 = b * 512
            em.dma('sp', xT[:], AP(tensor=g.XT.tensor, offset=g.XT.offset + c0, ap=[[Ttot, 128], [128 * Ttot, KC], [1, 512]]),
                   r=[('dram', 'XT')], w=['xT'])
            em.dma('sp', OTb[:], AP(tensor=g.OT.tensor, offset=g.OT.offset + s0, ap=[[Lmax, 128], [128 * Lmax, 12], [1, 512]]),
                   r=[('dram', 'OT')], w=['OTb'])
            for j in range(8):
                w_, wk = wbr.next()
                em.dma('sp', w_[:].rearrange("p k c -> p (k c)"), g.WBR[l][j].rearrange("p k c -> p (k c)"),
                       r=[('dram', f"WBR{l}")], w=[wk])
                gj, gk = Gj.next()
                em.dma('sp', gj[:], AP(tensor=g.GT.tensor, offset=g.GT.offset + j * 128 * Lmax + s0,
                                       ap=[[Lmax, 128], [1024 * Lmax, 3], [1, 512]]), r=[('dram', 'GT')], w=[gk])
                ts = []
                for i in range(3):
                    ps, pk = g.psum()
                    for kc in range(4):
                        em.op('pe', lambda E: E.matmul(ps[:], w_[:, i * 4 + kc, :], OTb[:, i * 4 + kc, :], start=(kc == 0), stop=(kc == 3)),
                              r=[wk, 'OTb'], w=[pk])
                    t_, tk = tmp.next()
                    em.op('dve', lambda E: E.tensor_tensor(t_[:], ps[:], gj[:, i, :], ALU.mult), r=[pk, gk], w=[tk])
                    ts.append((t_, tk))
                em.op('pool', lambda E: E.tensor_tensor(ts[0][0][:], ts[0][0][:], ts[1][0][:], ALU.add), r=[ts[0][1], ts[1][1]], w=[ts[0][1]])
                em.op('pool', lambda E: E.tensor_tensor(mT[:, j, :], ts[0][0][:], ts[2][0][:], ALU.add), r=[ts[0][1], ts[2][1]], w=['mT'])
            if C_CUT == 1:
                continue
            for j in range(8):
                w_, wk = wout.next()
                em.dma('sp', w_[:].rearrange("p k c -> p (k c)"), g.WOUT[l][j].rearrange("p k c -> p (k c)"),
                       r=[('dram', f"WOUT{l}")], w=[wk])
                ps, pk = g.psum()
                for kc in range(KC):
                    em.op('pe', lambda E: E.matmul(ps[:], w_[:, kc, :], mT[:, kc, :], start=(kc == 0), stop=(kc == KC - 1)),
                          r=[wk, 'mT'], w=[pk])
                em.op('dve', lambda E: E.tensor_copy(yT[:, j, :], ps[:]), r=[pk], w=['yT'])
                em.op('act', lambda E: E.activation(out=sqb[:, j, :], in_=yT[:, j, :], func=AF.Square), r=['yT'], w=['sqb'])
            rs, rsk = norm_rstd('sqb')
            for j in range(8):
                t_, tk = tmp.next()
                em.op('dve', lambda E: E.scalar_tensor_tensor(t_[:], yT[:, j, :], g32[:, 0, j:j + 1], rs[:], ALU.mult, ALU.mult),
                      r=['yT', rsk, 'g32'], w=[tk])
                em.op('pool', lambda E: E.tensor_tensor(xT[:, j, :], t_[:], xT[:, j, :], ALU.add), r=[tk, 'xT'], w=['xT'])
            if C_CUT == 2:
                continue
            em.op('act', lambda E: E.activation(out=sqb[:], in_=xT[:], func=AF.Square), r=['xT'], w=['sqb'])
            rs, rsk = norm_rstd('sqb')
            for j in range(8):
                em.op('dve', lambda E: E.scalar_tensor_tensor(h2T[:, j, :], xT[:, j, :], g32[:, 1, j:j + 1], rs[:], ALU.mult, ALU.mult),
                      r=['xT', rsk, 'g32'], w=['h2T'])
            for f in range(NFF):
                w_, wk = wgu.next()
                em.dma('sp', w_[:, 0].rearrange("p k c -> p (k c)"), g.WG[l][f].rearrange("p k c -> p (k c)"),
                       r=[('dram', f"WG{l}")], w=[wk])
                em.dma('sp', w_[:, 1].rearrange("p k c -> p (k c)"), g.WU[l][f].rearrange("p k c -> p (k c)"),
                       r=[('dram', f"WU{l}")], w=[wk])
                psg, pgk = g.psum()
                for kc in range(KC):
                    em.op('pe', lambda E: E.matmul(psg[:], w_[:, 0, kc, :], h2T[:, kc, :], start=(kc == 0), stop=(kc == KC - 1)),
                          r=[wk, 'h2T'], w=[pgk])
                psu, puk = g.psum()
                for kc in range(KC):
                    em.op('pe', lambda E: E.matmul(psu[:], w_[:, 1, kc, :], h2T[:, kc, :], start=(kc == 0), stop=(kc == KC - 1)),
                          r=[wk, 'h2T'], w=[puk])
                sg, sgk = sgt.next()
                em.op('act', lambda E: E.activation(out=sg[:], in_=psg[:], func=AF.Silu), r=[pgk], w=[sgk])
                em.op('dve', lambda E: E.tensor_tensor(actT[:, f, :], psu[:], sg[:], ALU.mult), r=[puk, sgk], w=['actT'])
            for j in range(8):
                w_, wk = wd.next()
                em.dma('sp', w_[:].rearrange("p k c -> p (k c)"), g.WD[l][j].rearrange("p k c -> p (k c)"),
                       r=[('dram', f"WD{l}")], w=[wk])
                ps, pk = g.psum()
                for f in range(NFF):
                    em.op('pe', lambda E: E.matmul(ps[:], w_[:, f, :], actT[:, f, :], start=(f == 0), stop=(f == NFF - 1)),
                          r=[wk, 'actT'], w=[pk])
                em.op('dve', lambda E: E.tensor_copy(yT[:, j, :], ps[:]), r=[pk], w=['yT'])
                em.op('act', lambda E: E.activation(out=sqb[:, j, :], in_=yT[:, j, :], func=AF.Square), r=['yT'], w=['sqb'])
            rs, rsk = norm_rstd('sqb')
            for j in range(8):
                t_, tk = tmp.next()
                em.op('dve', lambda E: E.scalar_tensor_tensor(t_[:], yT[:, j, :], g32[:, 2, j:j + 1], rs[:], ALU.mult, ALU.mult),
                      r=['yT', rsk, 'g32'], w=[tk])
                em.op('pool', lambda E: E.tensor_tensor(xT[:, j, :], t_[:], xT[:, j, :], ALU.add), r=[tk, 'xT'], w=['xT'])
            em.dma('pool', AP(tensor=g.XT.tensor, offset=g.XT.offset + c0, ap=[[Ttot, 128], [128 * Ttot, KC], [1, 512]]), xT[:],
                   r=['xT'], w=[('dram', 'XT')])
    em.barrier()


def stage_out(g):
    nc, em = g.nc, g.em
    with ExitStack() as st:
        def sb(name, shape, dt):
            return st.enter_context(nc.sbuf_tensor(g.un(name), list(shape), dt))
        xin_ = Rot([sb(f"o_x{i}", [128, KC, 128], F32) for i in range(2)], "o_x")
        xo = Rot([sb(f"o_y{i}", [128, D], F32) for i in range(2)], "o_y")
        for t in range(g.Ttot // 128):
            x_, xk = xin_.next()
            em.dma('sp', x_[:], AP(tensor=g.XT.tensor, offset=g.XT.offset + t * 128, ap=[[g.Ttot, 128], [128 * g.Ttot, KC], [1, 128]]),
                   r=[('dram', 'XT')], w=[xk])
            o_, ok = xo.next()
            for hlf in range(2):
                ps, pk = g.psum()
                for c in range(4):
                    kc = hlf * 4 + c
                    em.op('pe', lambda E: E.transpose(ps[:, c * 128:(c + 1) * 128], x_[:, kc, :], g.ident[:]), r=[xk, 'ident'], w=[pk])
                if hlf == 0:
                    em.op('dve', lambda E: E.tensor_copy(o_[:, 0:512], ps[:]), r=[pk], w=[ok])
                else:
                    em.op('dve', lambda E: E.tensor_copy(o_[:, 512:1024], ps[:]), r=[pk], w=[ok])
            em.dma('pool', g.y[t * 128:(t + 1) * 128, :], o_[:], r=[ok], w=[('dram', 'y')])
    em.barrier()


def make_in_map(x, w, consts):
    m = {"x": np.ascontiguousarray(x, np.float32)}
    for k, v in w.items():
        v = np.asarray(v, np.float32)
        if k in ("gdn_a_log", "gdn_dt_bias"):
            v = v.reshape(DEPTH, 8)
        if k == "w_branch":
            v = v.reshape(DEPTH, 1536, D)
        m[k] = np.ascontiguousarray(v)
    for k, v in consts.items():
        m["c_" + k] = v
    return m


_CACHE = {}


def kernel(x_prompt, x_sample, **w):
    x_prompt = np.asarray(x_prompt, np.float32)
    x_sample = np.asarray(x_sample, np.float32)
    n_cores = 8
    Bp, Lp, _ = x_prompt.shape
    Bs, Ls_, _ = x_sample.shape
    per = Bp // n_cores
    seq_lens = [Lp] * per + [Ls_]
    key = tuple(seq_lens)
    if key not in _CACHE:
        _CACHE[key] = build(seq_lens)
    nc, consts = _CACHE[key]
    in_maps = []
    for c in range(n_cores):
        xs = [x_prompt[c * per + j] for j in range(per)] + [x_sample[c % Bs]]
        in_maps.append(make_in_map(np.concatenate(xs, 0), w, consts))
    res = run_bass_kernel_spmd(nc, in_maps, core_ids=list(range(n_cores)))
    y_prompt = np.empty_like(x_prompt)
    y_sample = np.empty_like(x_sample)
    for c in range(n_cores):
        y = np.asarray(res.results[c]["y"], np.float32)
        for j in range(per):
            y_prompt[c * per + j] = y[j * Lp:(j + 1) * Lp]
        if c < Bs:
            y_sample[c] = y[per * Lp:per * Lp + Ls_]
    return (y_prompt, y_sample)
```

```python
import math
from contextlib import ExitStack
import numpy as np
import ml_dtypes
import concourse.bass as bass
import concourse.mybir as mybir
from concourse.bass_utils import run_bass_kernel_spmd

F32 = mybir.dt.float32
BF16 = mybir.dt.bfloat16
AF = mybir.ActivationFunctionType
ALU = mybir.AluOpType
AX = mybir.AxisListType
AP = bass.AP

D = 1024
KC = 8
DEPTH = 2
D_IN = 7120
D_FF = 2816
NFF = 22
EPS = 1e-6
C_HY, C_QL, C_CKV, C_KPE, C_GQKV, C_GZ, C_GB, C_GA, C_GATE = 0, 1536, 1792, 1920, 1984, 3520, 4032, 4040, 4048
NDS = 8
HY_RMAX = 128
GDN_CUT = 0
C_CUT = 0


class Em:
    def __init__(s, nc, es):
        s.nc = nc
        s.E = {'pe': nc.tensor, 'act': nc.scalar, 'dve': nc.vector, 'pool': nc.gpsimd, 'sp': nc.sync}
        s.sem = {}
        s.cnt = {}
        for e in s.E:
            s.sem[e] = es.enter_context(nc.semaphore("c_" + e))
            s.cnt[e] = 0
        s.seen = {e: {} for e in s.E}
        s.W = {}
        s.R = {}
        s.dq = {}
        for q in ('sp', 'pool'):
            s.dq[q] = 0
            for i in range(NDS):
                s.sem[(q, i)] = es.enter_context(nc.semaphore(f"d_{q}{i}"))
                s.cnt[(q, i)] = 0
        s.psn = 0
        s.ninst = 0

    def _wait(s, e, sk, v):
        if v <= 0:
            return
        if s.seen[e].get(sk, 0) < v:
            s.E[e].wait_ge(s.sem[sk], v)
            s.seen[e][sk] = v
            s.ninst += 1

    def _deps(s, e, r, w, pe_acc=False):
        for k in r:
            for sk, v in s.W.get(k, {}).items():
                s._wait(e, sk, v)
        for k in w:
            for sk, v in s.W.get(k, {}).items():
                if pe_acc and sk == 'pe':
                    continue
                s._wait(e, sk, v)
            for sk, v in s.R.get(k, {}).items():
                s._wait(e, sk, v)

    def _mark(s, tok, r, w):
        sk, v = tok
        for k in w:
            s.W.setdefault(k, {})[sk] = v
            s.R[k] = {}
        for k in r:
            s.R.setdefault(k, {})[sk] = v

    def op(s, e, fn, r=(), w=()):
        s._deps(e, r, w, pe_acc=(e == 'pe'))
        ins = fn(s.E[e])
        ins.then_inc(s.sem[e], 1)
        s.cnt[e] += 1
        s.ninst += 1
        s._mark((e, s.cnt[e]), r, w)

    def dma(s, q, out, in_, r=(), w=(), slow=False):
        s._deps(q, r, w)
        i = s.dq[q] % NDS
        s.dq[q] += 1
        sk = (q, i)
        s._wait(q, sk, s.cnt[sk])
        if slow:
            s.E[q].dma_start(out=out, in_=in_, allow_slow_non_contiguous=True).then_inc(s.sem[sk], 16)
        else:
            s.E[q].dma_start(out=out, in_=in_).then_inc(s.sem[sk], 16)
        s.cnt[sk] += 16
        s.ninst += 1
        s._mark((sk, s.cnt[sk]), r, w)

    def barrier(s):
        for e in s.E:
            for sk, v in s.cnt.items():
                if sk != e:
                    s._wait(e, sk, v)
        s.W.clear()
        s.R.clear()


def V(ap, dims, off=0):
    a = ap.ap
    return AP(tensor=ap.tensor, offset=ap.offset + off, ap=[list(a[0])] + [[st, n] for st, n in dims])


def rsqrt(em, out, src, addc, rk, wk):
    em.op('act', lambda E: E.activation(out=out, in_=src, func=AF.Ln, bias=float(addc), scale=1.0), r=[rk], w=[wk])
    em.op('act', lambda E: E.activation(out=out, in_=out, func=AF.Exp, scale=-0.5), r=[wk], w=[wk])


def psplit(ap, p0, p1):
    a = [list(d) for d in ap.ap]
    off = ap.offset + p0 * a[0][0]
    a[0][1] = p1 - p0
    return AP(tensor=ap.tensor, offset=off, ap=a)


def bf(x):
    return np.asarray(x, np.float32).astype(ml_dtypes.bfloat16)


def make_consts(Ls):
    c = {}
    c["ident"] = np.eye(128, dtype=np.float32)
    c["identb"] = bf(np.eye(128))
    c["onesb"] = bf(np.ones((128, 128)))
    c["ones"] = np.ones((128, 128), np.float32)
    i = np.arange(128)
    c["U_le"] = (i[:, None] <= i[None, :]).astype(np.float32)
    c["U_ge"] = (i[:, None] >= i[None, :]).astype(np.float32)
    c["SL"] = (i[None, :] < i[:, None]).astype(np.float32)
    c["SU"] = (i[None, :] > i[:, None]).astype(np.float32)
    Rm = np.zeros((64, 64), np.float32)
    for d in range(32):
        Rm[d + 32, d] = -1.0
        Rm[d, d + 32] = 1.0
    c["rope_R"] = bf(Rm)
    for L in sorted(set(Ls)):
        half = 32
        inv = 10000.0 ** (-np.arange(half, dtype=np.float32) / half)
        ang = np.arange(L, dtype=np.float32)[:, None] * inv[None, :]
        cs = np.cos(ang).astype(np.float32).T
        sn = np.sin(ang).astype(np.float32).T
        c[f"ropecos{L}"] = np.concatenate([cs, cs], 0).astype(np.float32)
        c[f"ropesin{L}"] = np.concatenate([sn, sn], 0).astype(np.float32)
        N = 2 * L
        N1 = N // 128
        P1 = L // 128
        R1 = min(N1, HY_RMAX)
        nch1 = max(1, N1 // HY_RMAX)
        s1 = np.arange(N1, dtype=np.float64)
        f1 = np.arange(N1, dtype=np.float64)
        a = 2 * np.pi * np.outer(s1, f1) / N1
        F1 = np.concatenate([np.cos(a), -np.sin(a)], 1)
        c[f"hyF1_{L}"] = bf(F1.reshape(nch1, R1, 2 * N1).transpose(1, 0, 2))
        s2 = np.arange(128, dtype=np.float64)
        a = 2 * np.pi * np.outer(s2, f1) / N
        c[f"hyTwr_{L}"] = np.cos(a).astype(np.float32)
        c[f"hyTwi_{L}"] = (-np.sin(a)).astype(np.float32)
        aT = a.T
        c[f"hyTwTr_{L}"] = np.cos(aT).reshape(nch1, R1, 128).transpose(1, 0, 2).astype(np.float32)
        c[f"hyTwTi_{L}"] = np.sin(aT).reshape(nch1, R1, 128).transpose(1, 0, 2).astype(np.float32)
        t1 = np.arange(P1, dtype=np.float64)
        a = 2 * np.pi * np.outer(f1, t1) / N1
        c[f"hyG1r_{L}"] = bf((np.cos(a) / N).reshape(nch1, R1, P1).transpose(1, 0, 2))
        c[f"hyG1i_{L}"] = bf((-np.sin(a) / N).reshape(nch1, R1, P1).transpose(1, 0, 2))
        sidx = np.arange(N)
        lag = np.where(sidx < L, sidx, N - sidx)
        lag[L] = 0
        tgrid = np.linspace(0.0, 1.0, L, dtype=np.float32)
        bands = 16
        wpos = (2.0 * math.pi / L) * np.arange(L, dtype=np.float32)
        fr = np.linspace(1e-4, bands - 1, bands, dtype=np.float32)
        feats = np.concatenate([tgrid[:, None], np.cos(fr[None, :] * wpos[:, None]), -np.sin(fr[None, :] * wpos[:, None])], -1)
        c[f"hyfeat_{L}"] = np.ascontiguousarray(feats[lag].T.astype(np.float32))
        c[f"hytneg_{L}"] = np.ascontiguousarray((-tgrid[lag]).reshape(N1, 128).T.astype(np.float32))
    a = 2 * np.pi * np.outer(np.arange(128.0), np.arange(128.0)) / 128
    c["hyF2r"] = bf(np.cos(a))
    c["hyF2i"] = bf(-np.sin(a))
    c["hyF2in"] = bf(np.sin(a))
    c["hyG2a"] = bf(np.concatenate([np.cos(a), np.sin(a)], 1))
    c["hyG2b"] = bf(np.concatenate([-np.sin(a), np.cos(a)], 1))
    deltas = np.abs(np.linspace(math.log(1e-2) / 1.5, math.log(1e-2) / 0.3, 512, dtype=np.float32))
    c["hydelta"] = np.ascontiguousarray(np.broadcast_to(deltas[None, :], (128, 512))).astype(np.float32)
    return c


class Ctx:
    pass


def dcol(ap_dram, off, n, nparts=128, pstride=1, cstride=128):
    return AP(tensor=ap_dram.tensor, offset=ap_dram.offset + off, ap=[[pstride, nparts], [cstride, n], [1, 1]])


def drow_bc(ap_dram, off, n, nparts=128):
    return AP(tensor=ap_dram.tensor, offset=ap_dram.offset + off, ap=[[0, nparts], [1, n]])


def build(seq_lens, debug=False, stages=None, mixers=None):
    nc = bass.Bass("TRN2", target_bir_lowering=False)
    Ttot = sum(seq_lens)
    Lmax = max(seq_lens)
    seq_off = [sum(seq_lens[:i]) for i in range(len(seq_lens))]
    Ls = sorted(set(seq_lens))
    g = Ctx()
    g.nc = nc
    g.uidc = [0]

    def un(name):
        g.uidc[0] += 1
        return f"{name}_{g.uidc[0]}"
    g.un = un
    ext = {}

    def xin(name, shape, dt=F32):
        ext[name] = nc.dram_tensor(name, list(shape), dt, kind="ExternalInput").ap()
        return ext[name]

    xin("x", [Ttot, D])
    for nm, shp in (("norm_mix_pre", [DEPTH, D]), ("norm_mix_post", [DEPTH, D]), ("norm_ffn_pre", [DEPTH, D]),
                    ("norm_ffn_post", [DEPTH, D]), ("w_in", [DEPTH, D, D_IN]), ("hy_conv_w", [DEPTH, 3, 1536]),
                    ("hy_conv_b", [DEPTH, 1536]), ("hy_ffn_w1", [DEPTH, 33, 64]), ("hy_ffn_b1", [DEPTH, 64]),
                    ("hy_sin_freq", [DEPTH, 64]), ("hy_ffn_w2", [DEPTH, 64, 64]), ("hy_ffn_b2", [DEPTH, 64]),
                    ("hy_ffn_w3", [DEPTH, 64, 2048]), ("hy_skip", [DEPTH, 2, 512]), ("mla_q_norm", [DEPTH, 256]),
                    ("mla_wq_b", [DEPTH, 256, 768]), ("mla_kv_norm", [DEPTH, 128]), ("mla_wkv_b", [DEPTH, 128, 1024]),
                    ("gdn_conv_w", [DEPTH, 3, 1536]), ("gdn_a_log", [DEPTH, 8]), ("gdn_dt_bias", [DEPTH, 8]),
                    ("gdn_out_norm", [DEPTH, 128]), ("w_branch", [DEPTH, 1536, D]), ("w_out", [DEPTH, D, D]),
                    ("w_gate", [DEPTH, D, D_FF]), ("w_up", [DEPTH, D, D_FF]), ("w_down", [DEPTH, D_FF, D])):
        xin(nm, shp)
    consts = make_consts(seq_lens)
    for k, v in consts.items():
        xin("c_" + k, v.shape, BF16 if v.dtype == ml_dtypes.bfloat16 else F32)
    y = nc.dram_tensor("y", [Ttot, D], F32, kind="ExternalOutput").ap()

    def scratch(name, shape, dt):
        kind = "ExternalOutput" if debug else "Internal"
        return nc.dram_tensor(name, list(shape), dt, kind=kind).ap()

    XT = scratch("XT", [D, Ttot], F32)
    WTOK = [scratch(f"WTOK{l}", [6, 128, KC, 3, 512], BF16) for l in range(DEPTH)]
    WFM = [scratch(f"WFM{l}", [32, 128, KC, 128], BF16) for l in range(DEPTH)]
    WBR = [scratch(f"WBR{l}", [8, 128, 12, 128], BF16) for l in range(DEPTH)]
    WOUT = [scratch(f"WOUT{l}", [8, 128, KC, 128], BF16) for l in range(DEPTH)]
    WG = [scratch(f"WG{l}", [NFF, 128, KC, 128], BF16) for l in range(DEPTH)]
    WU = [scratch(f"WU{l}", [NFF, 128, KC, 128], BF16) for l in range(DEPTH)]
    WD = [scratch(f"WD{l}", [8, 128, NFF, 128], BF16) for l in range(DEPTH)]
    PH = scratch("PH", [Lmax, 1536], BF16)
    PG = scratch("PG", [Lmax, 1536], BF16)
    BD = scratch("BD", [Lmax, 16], F32)
    QT = scratch("QT", [4, 192, Lmax], BF16)
    KTn = scratch("KTn", [4, 128, Lmax], BF16)
    KTp = scratch("KTp", [64, Lmax], BF16)
    VS = scratch("VS", [Lmax, 512], BF16)
    ZT = scratch("ZT", [512, Lmax], BF16)
    GT = scratch("GT", [3072, Lmax], BF16)
    OT = scratch("OT", [1536, Lmax], BF16)
    OF = scratch("OF", [512, Lmax], F32)
    Z2 = scratch("Z2", [Lmax, 512], BF16)
    KTd = {L_: scratch(f"KTd{L_}", [2 * L_, 1024], BF16) for L_ in Ls}
    KFq = {L_: scratch(f"KF{L_}", [2, 512 // min(64, 4096 // (L_ // 64)), max(1, min(64, 4096 // (L_ // 64)) * (L_ // 64) // 512), 128, 2, 512], BF16)
           for L_ in Ls}
    filt_done = {}

    es = ExitStack()
    with es:
        em = Em(nc, es)

        def sb(name, shape, dt, st=es):
            return st.enter_context(nc.sbuf_tensor(g.un(name), list(shape), dt))

        PS = [es.enter_context(nc.psum_tensor(f"ps{i}", [128, 512], F32)) for i in range(8)]

        def psum(banks=range(8)):
            banks = list(banks)
            b = banks[em.psn % len(banks)]
            em.psn += 1
            return PS[b], ('ps', b)

        ident = sb("ident", [128, 128], F32)
        identb = sb("identb", [128, 128], BF16)
        onesb = sb("onesb", [128, 128], BF16)
        gains = sb("gains", [128, 4, DEPTH, KC], F32)
        em.dma('sp', ident[:], ext["c_ident"][:, :], w=['ident'])
        em.dma('sp', identb[:], ext["c_identb"][:, :], w=['identb'])
        em.dma('sp', onesb[:], ext["c_onesb"][:, :], w=['onesb'])
        for i, nm in enumerate(("norm_mix_pre", "norm_mix_post", "norm_ffn_pre", "norm_ffn_post")):
            for l in range(DEPTH):
                em.dma('sp', gains[:, i, l, :], dcol(ext[nm], l * D, KC), w=['gains'], slow=True)
        em.barrier()

        g.__dict__.update(locals())
        if stages is None or 'prep' in stages:
            stage_prep(g)
        for l in range(DEPTH):
            for si, L in enumerate(seq_lens):
                if stages is None or ('A', l) in stages:
                    stage_A(g, l, si)
                if stages is None or ('B', l) in stages:
                    stage_B(g, l, si)
                if stages is None or ('C', l) in stages:
                    stage_C(g, l, si)
        if stages is None or 'out' in stages:
            stage_out(g)
        em.barrier()
        print("emitted instructions:", em.ninst, {k: v for k, v in em.cnt.items() if isinstance(k, str)})
    return nc, consts


class Rot:
    def __init__(s, tiles, name):
        s.t = tiles
        s.name = name
        s.i = 0

    def next(s):
        j = s.i % len(s.t)
        s.i += 1
        return s.t[j], (s.name, j)


def stage_prep(g):
    nc, em, ext = g.nc, g.em, g.ext
    with ExitStack() as st:
        def sb(name, shape, dt):
            return st.enter_context(nc.sbuf_tensor(g.un(name), list(shape), dt))
        wld = Rot([sb(f"p_wld{i}", [128, 512], F32) for i in range(4)], "p_wld")
        cwt = Rot([sb(f"p_cw{i}", [128, 3, 512], F32) for i in range(2)], "p_cw")
        stg = Rot([sb(f"p_stg{i}", [128, 24 * 512], BF16) for i in range(2)], "p_stg")
        engs = ['dve', 'pool', 'act']
        ei = [0]

        def cast(out, in_, r, w, mul=None):
            e = engs[ei[0] % 3]
            ei[0] += 1
            if mul is not None:
                if e == 'act':
                    e = 'dve'
                em.op(e, lambda E: E.tensor_tensor(out, in_, mul, ALU.mult), r=r, w=w)
            elif e == 'act':
                em.op(e, lambda E: E.copy(out, in_), r=r, w=w)
            else:
                em.op(e, lambda E: E.tensor_copy(out, in_), r=r, w=w)

        def prep_fm(src, row0, nkc, col0, W, dst, t0):
            sg, sk = stg.next()
            for kc in range(nkc):
                wt, wk = wld.next()
                em.dma('sp', wt[:, 0:W], src[row0 + kc * 128: row0 + (kc + 1) * 128, col0:col0 + W], w=[wk])
                cast(sg[:, kc * W:(kc + 1) * W], wt[:, 0:W], [wk], [sk])
            nt = W // 128
            d = dst
            dap = AP(tensor=d.tensor, offset=d.offset + t0 * 128 * nkc * 128,
                     ap=[[nkc * 128, 128], [128, nkc], [128 * nkc * 128, nt], [1, 128]])
            sap = V(sg[:, 0:1], [(W, nkc), (128, nt), (1, 128)])
            em.dma('pool', dap, sap, r=[sk], w=[('dram', d.tensor.name)])

        for l in range(DEPTH):
            w_in = ext["w_in"][l]
            for grp in range(6):
                col0 = (C_HY + 512 * grp) if grp < 3 else (C_GQKV + 512 * (grp - 3))
                cwn = "hy_conv_w" if grp < 3 else "gdn_conv_w"
                cw, cwk = cwt.next()
                cw_src = ext[cwn]
                em.dma('sp', cw[:], AP(tensor=cw_src.tensor, offset=cw_src.offset + l * 3 * 1536 + 512 * (grp % 3),
                                       ap=[[0, 128], [1536, 3], [1, 512]]), w=[cwk])
                sg, sk = stg.next()
                for kc in range(KC):
                    wt, wk = wld.next()
                    em.dma('sp', wt[:], w_in[kc * 128:(kc + 1) * 128, col0:col0 + 512], w=[wk])
                    for s_ in range(3):
                        o = (kc * 3 + s_) * 512
                        cast(sg[:, o:o + 512], wt[:], [wk, cwk], [sk], mul=cw[:, s_, :])
                em.dma('pool', g.WTOK[l][grp].rearrange("p k s c -> p (k s c)"), sg[:, 0:KC * 3 * 512], r=[sk],
                       w=[('dram', f"WTOK{l}")])
            prep_fm(w_in, 0, KC, C_QL, 512, g.WFM[l], 0)
            prep_fm(w_in, 0, KC, C_GZ, 512, g.WFM[l], 4)
            for i in range(6):
                prep_fm(w_in, 0, KC, C_GATE + 512 * i, 512, g.WFM[l], 8 + 4 * i)
            for i in range(2):
                prep_fm(ext["w_branch"][l], 0, 12, 512 * i, 512, g.WBR[l], 4 * i)
                prep_fm(ext["w_out"][l], 0, KC, 512 * i, 512, g.WOUT[l], 4 * i)
                prep_fm(ext["w_down"][l], 0, NFF, 512 * i, 512, g.WD[l], 4 * i)
            for i in range(11):
                prep_fm(ext["w_gate"][l], 0, KC, 256 * i, 256, g.WG[l], 2 * i)
                prep_fm(ext["w_up"][l], 0, KC, 256 * i, 256, g.WU[l], 2 * i)
        xld = Rot([sb(f"p_x{i}", [128, D], F32) for i in range(2)], "p_x")
        xts = Rot([sb(f"p_xt{i}", [128, KC, 128], F32) for i in range(2)], "p_xt")
        for t in range(g.Ttot // 128):
            xt_, xk = xld.next()
            em.dma('sp', xt_[:], ext["x"][t * 128:(t + 1) * 128, :], w=[xk])
            xo, xok = xts.next()
            for hlf in range(2):
                ps, pk = g.psum()
                for c in range(4):
                    kc = hlf * 4 + c
                    em.op('pe', lambda E: E.transpose(ps[:, c * 128:(c + 1) * 128], xt_[:, kc * 128:(kc + 1) * 128],
                                                      g.ident[:]), r=[xk, 'ident'], w=[pk])
                e = 'dve' if hlf == 0 else 'act'
                dst = xo[:, hlf * 4:(hlf + 1) * 4, :]
                src = ps[:].rearrange("p (c t) -> p c t", c=4)
                if e == 'dve':
                    em.op(e, lambda E: E.tensor_copy(dst, src), r=[pk], w=[xok])
                else:
                    em.op(e, lambda E: E.copy(dst, src), r=[pk], w=[xok])
            dap = AP(tensor=g.XT.tensor, offset=g.XT.offset + t * 128, ap=[[g.Ttot, 128], [128 * g.Ttot, KC], [1, 128]])
            em.dma('pool', dap, xo[:], r=[xok], w=[('dram', 'XT')])
    em.barrier()


def stage_A(g, l, si):
    nc, em, ext = g.nc, g.em, g.ext
    L = g.seq_lens[si]
    t0 = g.seq_off[si]
    Ttot = g.Ttot
    nb = L // 512
    QSCALE = 192.0 ** -0.5
    with ExitStack() as st:
        def sb(name, shape, dt):
            return st.enter_context(nc.sbuf_tensor(g.un(name), list(shape), dt))
        g32 = sb("a_g32", [128, KC], F32)
        hb = sb("a_hb", [128, 1536], F32)
        tmpf = sb("a_tmpf", [128, 2 * 768], F32)
        wbd = sb("a_wbd", [128, KC, 16], BF16)
        wqb = sb("a_wqb", [128, 2, 768], BF16)
        wkn = sb("a_wkn", [128, 4, 128], BF16)
        wv = sb("a_wv", [128, 4, 128], BF16)
        gq = sb("a_gq", [128, 2], F32)
        gkv = sb("a_gkv", [128, 1], F32)
        ropeR = sb("a_ropeR", [64, 64], BF16)
        em.op('dve', lambda E: E.tensor_scalar(g32[:], g.gains[:, 0, l, :], 32.0, None, ALU.mult), r=['gains'], w=['g32'])
        em.dma('sp', hb[:], drow_bc(ext["hy_conv_b"], l * 1536, 1536), w=['hb'])
        em.dma('sp', ropeR[:], ext["c_rope_R"][:, :], w=['ropeR'])
        w_in = ext["w_in"][l]
        em.dma('sp', V(tmpf[:, 0:1], [(16, KC), (1, 16)]),
               AP(tensor=w_in.tensor, offset=w_in.offset + C_GB, ap=[[D_IN, 128], [128 * D_IN, KC], [1, 16]]), w=['tmpf'])
        em.op('dve', lambda E: E.tensor_copy(wbd[:].rearrange("p k c -> p (k c)"), tmpf[:, 0:KC * 16]), r=['tmpf'], w=['wbd'])
        wq = ext["mla_wq_b"][l]
        em.dma('sp', V(tmpf[:, 0:1], [(768, 2), (1, 768)]),
               AP(tensor=wq.tensor, offset=wq.offset, ap=[[768, 128], [128 * 768, 2], [1, 768]]), w=['tmpf'], r=['tmpf'])
        em.op('dve', lambda E: E.tensor_copy(wqb[:].rearrange("p k c -> p (k c)"), tmpf[:, 0:1536]), r=['tmpf'], w=['wqb'])
        wkv = ext["mla_wkv_b"][l]
        em.dma('sp', tmpf[:, 0:1024], wkv[:, :], w=['tmpf'], r=['tmpf'])
        em.op('dve', lambda E: E.tensor_copy(wkn[:], V(tmpf[:, 0:1], [(256, 4), (1, 128)])), r=['tmpf'], w=['wkn'])
        em.op('dve', lambda E: E.tensor_copy(wv[:], V(tmpf[:, 0:1], [(256, 4), (1, 128)], off=128)), r=['tmpf'], w=['wv'])
        em.dma('sp', gq[:], dcol(ext["mla_q_norm"], l * 256, 2), w=['gq'], slow=True)
        em.dma('sp', gkv[:], dcol(ext["mla_kv_norm"], l * 128, 1), w=['gkv'], slow=True)
        em.op('dve', lambda E: E.tensor_scalar(gq[:], gq[:], 16.0, None, ALU.mult), r=['gq'], w=['gq'])
        em.op('dve', lambda E: E.tensor_scalar(gkv[:], gkv[:], math.sqrt(128.0), None, ALU.mult), r=['gkv'], w=['gkv'])
        xT = Rot([sb(f"a_xT{i}", [128, KC, 512], F32) for i in range(2)], "a_xT")
        xh = Rot([sb(f"a_xh{i}", [128, KC, 2], F32) for i in range(2)], "a_xh")
        sq = Rot([sb(f"a_sq{i}", [128, KC, 512], BF16) for i in range(1)], "a_sq")
        sqh = sb("a_sqh", [128, KC, 2], BF16)
        rstd = Rot([sb(f"a_rstd{i}", [128, 512], F32) for i in range(2)], "a_rstd")
        rsth = sb("a_rsth", [128, 2], F32)
        xhg = sb("a_xhg", [128, KC, 2], F32)
        hT = Rot([sb(f"a_hT{i}", [128, KC, 514], BF16) for i in range(2)], "a_hT")
        wtk = Rot([sb(f"a_wtk{i}", [128, KC, 3, 512], BF16) for i in range(2)], "a_wtk")
        wfm = Rot([sb(f"a_wfm{i}", [128, 4, KC, 128], BF16) for i in range(2)], "a_wfm")
        stg = Rot([sb(f"a_stg{i}", [128, 4, 512], BF16) for i in range(3)], "a_stg")
        bds = Rot([sb(f"a_bds{i}", [128, 4, 16], F32) for i in range(2)], "a_bds")
        ql = sb("a_ql", [128, 2, 512], F32)
        ck = sb("a_ck", [128, 512], F32)
        kpe = sb("a_kpe", [64, 512], BF16)
        qln = sb("a_qln", [128, 2, 512], BF16)
        ckn = sb("a_ckn", [128, 512], BF16)
        qpe = Rot([sb(f"a_qpe{i}", [64, 512], BF16) for i in range(2)], "a_qpe")
        rt1 = Rot([sb(f"a_rt1{i}", [64, 512], F32) for i in range(2)], "a_rt1")
        rt2 = Rot([sb(f"a_rt2{i}", [64, 512], F32) for i in range(2)], "a_rt2")
        cosT = Rot([sb(f"a_cos{i}", [64, 512], F32) for i in range(2)], "a_cos")
        sinT = Rot([sb(f"a_sin{i}", [64, 512], F32) for i in range(2)], "a_sin")
        rcos, rsin = ext[f"c_ropecos{L}"], ext[f"c_ropesin{L}"]
        eng2 = ['dve', 'pool']

        def rope_store(src_bf, srck, cs, csk, sn, snk, dst_ap, dkey):
            ps, pk = g.psum()
            em.op('pe', lambda E: E.matmul(ps[0:64, :], ropeR[:], src_bf, start=True, stop=True), r=[srck, 'ropeR'], w=[pk])
            a, ak = rt1.next()
            b_, bk = rt2.next()
            em.op('pool', lambda E: E.tensor_tensor(a[:], src_bf, cs[:], ALU.mult), r=[srck, csk], w=[ak])
            em.op('dve', lambda E: E.tensor_tensor(b_[:], ps[0:64, :], sn[:], ALU.mult), r=[pk, snk], w=[bk])
            o, ok = qpe.next()
            em.op('dve', lambda E: E.tensor_tensor(o[:], a[:], b_[:], ALU.add), r=[ak, bk], w=[ok])
            em.dma('pool', dst_ap, o[:], r=[ok], w=[dkey])

        for b in range(nb):
            c0 = t0 + b * 512
            s0 = b * 512
            x_, xk = xT.next()
            em.dma('sp', x_[:], AP(tensor=g.XT.tensor, offset=g.XT.offset + c0, ap=[[Ttot, 128], [128 * Ttot, KC], [1, 512]]),
                   r=[('dram', 'XT')], w=[xk])
            xh_, xhk = xh.next()
            for side, col in ((0, c0 - 1), (1, c0 + 512)):
                if (side == 0 and b == 0) or (side == 1 and b == nb - 1):
                    em.op('pool', lambda E: E.memset(xh_[:, :, side:side + 1], 0.0), w=[xhk])
                else:
                    em.dma('sp', xh_[:, :, side:side + 1],
                           AP(tensor=g.XT.tensor, offset=g.XT.offset + col, ap=[[Ttot, 128], [128 * Ttot, KC], [1, 1]]),
                           r=[('dram', 'XT')], w=[xhk], slow=True)
            cs, csk = cosT.next()
            sn, snk = sinT.next()
            em.dma('sp', cs[:], rcos[:, s0:s0 + 512], w=[csk])
            em.dma('sp', sn[:], rsin[:, s0:s0 + 512], w=[snk])
            sq_, sqk = sq.next()
            em.op('act', lambda E: E.activation(out=sq_[:], in_=x_[:], func=AF.Square), r=[xk], w=[sqk])
            ps, pk = g.psum()
            for kc in range(KC):
                em.op('pe', lambda E: E.matmul(ps[:], g.onesb[:], sq_[:, kc, :], start=(kc == 0), stop=(kc == KC - 1)),
                      r=[sqk, 'onesb'], w=[pk])
            rs, rsk = rstd.next()
            rsqrt(em, rs[:], ps[:], D * EPS, pk, rsk)
            h_, hk = hT.next()
            for kc in range(KC):
                em.op('dve', lambda E: E.scalar_tensor_tensor(h_[:, kc, 1:513], x_[:, kc, :], g32[:, kc:kc + 1], rs[:],
                                                                     ALU.mult, ALU.mult), r=[xk, rsk, 'g32'], w=[hk])
            em.op('act', lambda E: E.activation(out=sqh[:], in_=xh_[:], func=AF.Square), r=[xhk], w=['sqh'])
            ps2, pk2 = g.psum()
            for kc in range(KC):
                em.op('pe', lambda E: E.matmul(ps2[:, 0:2], g.onesb[:], sqh[:, kc, :], start=(kc == 0), stop=(kc == KC - 1)),
                      r=['sqh', 'onesb'], w=[pk2])
            rsqrt(em, rsth[:], ps2[:, 0:2], D * EPS, pk2, 'rsth')
            em.op('pool', lambda E: E.tensor_tensor(xhg[:], xh_[:], V(g32[:, 0:1], [(1, KC), (0, 2)]), ALU.mult),
                  r=[xhk, 'g32'], w=['xhg'])
            em.op('pool', lambda E: E.tensor_tensor(V(h_[:, 0, 0:1], [(514, KC), (513, 2)]), xhg[:],
                                                    V(rsth[:, 0:1], [(0, KC), (1, 2)]), ALU.mult),
                  r=['xhg', 'rsth'], w=[hk])
            for grp in range(6):
                w_, wk = wtk.next()
                em.dma('sp', w_[:].rearrange("p k s c -> p (k s c)"), g.WTOK[l][grp].rearrange("p k s c -> p (k s c)"),
                       r=[('dram', f"WTOK{l}")], w=[wk])
                sg, sgk = stg.next()
                for tt in range(4):
                    ps, pk = g.psum()
                    n = 0
                    for kc in range(KC):
                        for s_ in range(3):
                            lo = 1 + tt * 128 + (s_ - 1)
                            em.op('pe', lambda E: E.matmul(ps[:], h_[:, kc, lo:lo + 128], w_[:, kc, s_, :],
                                                           start=(n == 0), stop=(n == 23)), r=[hk, wk], w=[pk])
                            n += 1
                    if grp < 3:
                        em.op('dve', lambda E: E.tensor_tensor(sg[:, tt, :], ps[:], hb[:, grp * 512:(grp + 1) * 512], ALU.add),
                              r=[pk, 'hb'], w=[sgk])
                    else:
                        em.op('act', lambda E: E.activation(out=sg[:, tt, :], in_=ps[:], func=AF.Silu), r=[pk], w=[sgk])
                dst = g.PH if grp < 3 else g.PG
                em.dma('pool', AP(tensor=dst.tensor, offset=dst.offset + s0 * 1536 + (grp % 3) * 512,
                                  ap=[[1536, 128], [128 * 1536, 4], [1, 512]]), sg[:], r=[sgk],
                       w=[('dram', 'PH' if grp < 3 else 'PG')])
            ps, pk = g.psum()
            for tt in range(4):
                for kc in range(KC):
                    em.op('pe', lambda E: E.matmul(ps[:, tt * 16:(tt + 1) * 16], h_[:, kc, 1 + tt * 128:1 + (tt + 1) * 128],
                                                   wbd[:, kc, :], start=(kc == 0), stop=(kc == KC - 1)), r=[hk, 'wbd'], w=[pk])
            bd_, bdk = bds.next()
            em.op('dve', lambda E: E.tensor_copy(bd_[:].rearrange("p t c -> p (t c)"), ps[:, 0:64]), r=[pk], w=[bdk])
            em.dma('pool', AP(tensor=g.BD.tensor, offset=g.BD.offset + s0 * 16, ap=[[16, 128], [128 * 16, 4], [1, 16]]),
                   bd_[:], r=[bdk], w=[('dram', 'BD')])
            for tg in range(8):
                wf, wfk = wfm.next()
                em.dma('sp', wf[:].rearrange("p t k c -> p t (k c)"),
                       AP(tensor=g.WFM[l].tensor, offset=g.WFM[l].offset + tg * 4 * 128 * KC * 128,
                          ap=[[KC * 128, 128], [128 * KC * 128, 4], [1, KC * 128]]), r=[('dram', f"WFM{l}")], w=[wfk])
                if tg >= 1:
                    sg, sgk = stg.next()
                for ti in range(4):
                    M = 64 if (tg == 0 and ti == 3) else 128
                    ps, pk = g.psum()
                    for kc in range(KC):
                        em.op('pe', lambda E: E.matmul(ps[0:M, :], wf[:, ti, kc, 0:M], h_[:, kc, 1:513],
                                                       start=(kc == 0), stop=(kc == KC - 1)), r=[hk, wfk], w=[pk])
                    if tg == 0:
                        if ti < 2:
                            em.op('dve', lambda E: E.tensor_copy(ql[:, ti, :], ps[:]), r=[pk], w=['ql'])
                        elif ti == 2:
                            em.op('dve', lambda E: E.tensor_copy(ck[:], ps[:]), r=[pk], w=['ck'])
                        else:
                            em.op('act', lambda E: E.copy(kpe[:], ps[0:64, :]), r=[pk], w=['kpe'])
                    elif tg == 1:
                        em.op('act', lambda E: E.activation(out=sg[:, ti, :], in_=ps[:], func=AF.Silu), r=[pk], w=[sgk])
                    else:
                        em.op('act', lambda E: E.activation(out=sg[:, ti, :], in_=ps[:], func=AF.Sigmoid), r=[pk], w=[sgk])
                if tg == 1:
                    em.dma('pool', AP(tensor=g.ZT.tensor, offset=g.ZT.offset + s0, ap=[[g.Lmax, 128], [128 * g.Lmax, 4], [1, 512]]),
                           sg[:], r=[sgk], w=[('dram', 'ZT')])
                elif tg >= 2:
                    em.dma('pool', AP(tensor=g.GT.tensor, offset=g.GT.offset + (tg - 2) * 512 * g.Lmax + s0,
                                      ap=[[g.Lmax, 128], [128 * g.Lmax, 4], [1, 512]]), sg[:], r=[sgk], w=[('dram', 'GT')])
            sq_, sqk = sq.next()
            em.op('act', lambda E: E.activation(out=sq_[:, 0:2, :], in_=ql[:], func=AF.Square), r=['ql'], w=[sqk])
            em.op('act', lambda E: E.activation(out=sq_[:, 2, :], in_=ck[:], func=AF.Square), r=['ck'], w=[sqk])
            ps, pk = g.psum()
            for c in range(2):
                em.op('pe', lambda E: E.matmul(ps[:], g.onesb[:], sq_[:, c, :], start=(c == 0), stop=(c == 1)), r=[sqk, 'onesb'], w=[pk])
            rs, rsk = rstd.next()
            rsqrt(em, rs[:], ps[:], 256 * EPS, pk, rsk)
            for c in range(2):
                em.op('dve', lambda E: E.scalar_tensor_tensor(qln[:, c, :], ql[:, c, :], gq[:, c:c + 1], rs[:], ALU.mult, ALU.mult),
                      r=['ql', rsk, 'gq'], w=['qln'])
            ps, pk = g.psum()
            em.op('pe', lambda E: E.matmul(ps[:], g.onesb[:], sq_[:, 2, :], start=True, stop=True), r=[sqk, 'onesb'], w=[pk])
            rs, rsk = rstd.next()
            rsqrt(em, rs[:], ps[:], 128 * EPS, pk, rsk)
            em.op('dve', lambda E: E.scalar_tensor_tensor(ckn[:], ck[:], gkv[:, 0:1], rs[:], ALU.mult, ALU.mult),
                  r=['ck', rsk, 'gkv'], w=['ckn'])
            for h in range(4):
                ps, pk = g.psum()
                for c in range(2):
                    em.op('pe', lambda E: E.matmul(ps[:], wqb[:, c, h * 192:h * 192 + 128], qln[:, c, :], start=(c == 0), stop=(c == 1)),
                          r=['wqb', 'qln'], w=[pk])
                sg, sgk = stg.next()
                em.op('act', lambda E: E.activation(out=sg[:, 0, :], in_=ps[:], func=AF.Copy, scale=QSCALE), r=[pk], w=[sgk])
                em.dma('pool', g.QT[h, 0:128, s0:s0 + 512], sg[:, 0, :], r=[sgk], w=[('dram', 'QT')])
                ps, pk = g.psum()
                for c in range(2):
                    em.op('pe', lambda E: E.matmul(ps[0:64, :], wqb[:, c, h * 192 + 128:h * 192 + 192], qln[:, c, :],
                                                   start=(c == 0), stop=(c == 1)), r=['wqb', 'qln'], w=[pk])
                qp, qpk = qpe.next()
                em.op('act', lambda E: E.activation(out=qp[:], in_=ps[0:64, :], func=AF.Copy, scale=QSCALE), r=[pk], w=[qpk])
                rope_store(qp[:], qpk, cs, csk, sn, snk, g.QT[h, 128:192, s0:s0 + 512], ('dram', 'QT'))
                ps, pk = g.psum()
                em.op('pe', lambda E: E.matmul(ps[:], wkn[:, h, :], ckn[:], start=True, stop=True), r=['wkn', 'ckn'], w=[pk])
                em.op('dve', lambda E: E.tensor_copy(sg[:, 1, :], ps[:]), r=[pk], w=[sgk])
                em.dma('pool', g.KTn[h, :, s0:s0 + 512], sg[:, 1, :], r=[sgk], w=[('dram', 'KTn')])
            rope_store(kpe[:], 'kpe', cs, csk, sn, snk, g.KTp[:, s0:s0 + 512], ('dram', 'KTp'))
            sg, sgk = stg.next()
            for tt in range(4):
                ps, pk = g.psum()
                em.op('pe', lambda E: E.matmul(ps[:], ckn[:, tt * 128:(tt + 1) * 128], wv[:].rearrange("p h c -> p (h c)"),
                                               start=True, stop=True), r=['ckn', 'wv'], w=[pk])
                em.op('act' if tt % 2 else 'dve', lambda E: (E.copy if tt % 2 else E.tensor_copy)(sg[:, tt, :], ps[:]), r=[pk], w=[sgk])
            em.dma('pool', AP(tensor=g.VS.tensor, offset=g.VS.offset + s0 * 512, ap=[[512, 128], [128 * 512, 4], [1, 512]]),
                   sg[:], r=[sgk], w=[('dram', 'VS')])
    em.barrier()


def mixer_mla(g, l, si):
    nc, em = g.nc, g.em
    L = g.seq_lens[si]
    Lmax = g.Lmax
    nq, nk = L // 512, L // 128
    NCH = max(1, min(8, nk // 4))
    kpc = nk // NCH
    with ExitStack() as st:
        def sb(name, shape, dt):
            return st.enter_context(nc.sbuf_tensor(g.un(name), list(shape), dt))
        Kp = sb("m_Kp", [64, L], BF16)
        Kn = sb("m_Kn", [128, L], BF16)
        Vh = sb("m_Vh", [128, nk, 128], BF16)
        qn = Rot([sb(f"m_qn{i}", [128, 512], BF16) for i in range(2)], "m_qn")
        qp = Rot([sb(f"m_qp{i}", [64, 512], BF16) for i in range(2)], "m_qp")
        PT = Rot([sb(f"m_PT{i}", [128, 512], BF16) for i in range(4)], "m_PT")
        rs = Rot([sb(f"m_rs{i}", [128, 512], F32) for i in range(2)], "m_rs")
        ob = Rot([sb(f"m_ob{i}", [128, 512], BF16) for i in range(2)], "m_ob")
        acc = Rot([sb(f"m_acc{i}", [128, 512], F32) for i in range(2)], "m_acc")
        ones32 = sb("m_ones", [128, 128], F32)
        em.dma('sp', ones32[:], g.ext["c_ones"][:, :], w=['ones32'])
        for ci in range(NCH):
            em.dma('sp', Kp[:, ci * kpc * 128:(ci + 1) * kpc * 128], g.KTp[:, ci * kpc * 128:(ci + 1) * kpc * 128],
                   r=[('dram', 'KTp')], w=[('Kp', ci)])
        nacc = 0
        for h in range(4):
            for ci in range(NCH):
                a, b_ = ci * kpc * 128, (ci + 1) * kpc * 128
                em.dma('sp', Kn[:, a:b_], g.KTn[h, :, a:b_], r=[('dram', 'KTn')], w=[('Kn', ci)])
                em.dma('sp', Vh[:, ci * kpc:(ci + 1) * kpc, :],
                       AP(tensor=g.VS.tensor, offset=g.VS.offset + a * 512 + h * 128, ap=[[512, 128], [128 * 512, kpc], [1, 128]]),
                       r=[('dram', 'VS')], w=[('Vh', ci)])
            for qb in range(nq):
                q0 = qb * 512
                qn_, qnk = qn.next()
                qp_, qpk = qp.next()
                em.dma('sp', qn_[:], g.QT[h, 0:128, q0:q0 + 512], r=[('dram', 'QT')], w=[qnk])
                em.dma('sp', qp_[:], g.QT[h, 128:192, q0:q0 + 512], r=[('dram', 'QT')], w=[qpk])
                po, pok = g.psum([4, 6][nacc % 2:nacc % 2 + 1])
                pz, pzk = g.psum([5, 7][nacc % 2:nacc % 2 + 1])
                ac, ack = acc.next()
                nacc += 1

                def s_tile(kt):
                    ci = kt // kpc
                    ps, pk = g.psum(range(4))
                    em.op('pe', lambda E: E.matmul(ps[:], Kn[:, kt * 128:(kt + 1) * 128], qn_[:], start=True, stop=False),
                          r=[('Kn', ci), qnk], w=[pk])
                    em.op('pe', lambda E: E.matmul(ps[:], Kp[:, kt * 128:(kt + 1) * 128], qp_[:], start=False, stop=True),
                          r=[('Kp', ci), qpk], w=[pk])
                    return ps, pk

                cur = s_tile(0)
                for kt in range(nk):
                    ci = kt // kpc
                    nxt = s_tile(kt + 1) if kt + 1 < nk else None
                    ps, pk = cur
                    pt, ptk = PT.next()
                    em.op('act', lambda E: E.activation(out=pt[:], in_=ps[:], func=AF.Exp), r=[pk], w=[ptk])
                    em.op('pe', lambda E: E.matmul(po[:], Vh[:, kt, :], pt[:], start=(kt == 0), stop=(kt == nk - 1)),
                          r=[('Vh', ci), ptk], w=[pok])
                    if kt == 0:
                        em.op('pool', lambda E: E.tensor_copy(ac[:], pt[:]), r=[ptk], w=[ack])
                    else:
                        em.op('pool', lambda E: E.tensor_tensor(ac[:], ac[:], pt[:], ALU.add), r=[ptk, ack], w=[ack])
                    cur = nxt
                em.op('pe', lambda E: E.matmul(pz[:], ones32[:], ac[:], start=True, stop=True), r=['ones32', ack], w=[pzk])
                r_, rk = rs.next()
                em.op('dve', lambda E: E.reciprocal(r_[:], pz[:]), r=[pzk], w=[rk])
                o_, ok = ob.next()
                em.op('dve', lambda E: E.tensor_tensor(o_[:], po[:], r_[:], ALU.mult), r=[pok, rk], w=[ok])
                em.dma('pool', g.OT[512 + h * 128:512 + (h + 1) * 128, q0:q0 + 512], o_[:], r=[ok], w=[('dram', 'OT')])
    em.barrier()


def stage_B(g, l, si):
    if g.mixers is None or 'mla' in g.mixers:
        mixer_mla(g, l, si)
    if g.mixers is None or 'gdn' in g.mixers:
        mixer_gdn(g, l, si)
    if g.mixers is None or 'hy' in g.mixers:
        mixer_hyena(g, l, si)


def mixer_gdn(g, l, si):
    nc, em, ext = g.nc, g.em, g.ext
    L = g.seq_lens[si]
    Lmax = g.Lmax
    nt = L // 128
    with ExitStack() as st:
        def sb(name, shape, dt):
            return st.enter_context(nc.sbuf_tensor(g.un(name), list(shape), dt))

        def T(name, shape, dt, n=2):
            return Rot([sb(f"g_{name}{i}", shape, dt) for i in range(n)], "g_" + name)
        cU = {0: sb("g_Ule", [128, 128], F32), 1: sb("g_Uge", [128, 128], F32)}
        cS = {0: sb("g_SL", [128, 128], F32), 1: sb("g_SU", [128, 128], F32)}
        ones32 = sb("g_ones", [128, 128], F32)
        negA = sb("g_negA", [128, 8], F32)
        dtb = sb("g_dtb", [128, 8], F32)
        gno = sb("g_gno", [128, 1], F32)
        em.dma('sp', cU[0][:], ext["c_U_le"][:, :], w=['cU0'])
        em.dma('sp', cU[1][:], ext["c_U_ge"][:, :], w=['cU1'])
        em.dma('sp', cS[0][:], ext["c_SL"][:, :], w=['cS0'])
        em.dma('sp', cS[1][:], ext["c_SU"][:, :], w=['cS1'])
        em.dma('sp', ones32[:], ext["c_ones"][:, :], w=['ones32'])
        em.dma('sp', negA[:], drow_bc(ext["gdn_a_log"], l * 8, 8), w=['negA'])
        em.dma('sp', dtb[:], drow_bc(ext["gdn_dt_bias"], l * 8, 8), w=['dtb'])
        em.dma('sp', gno[:], dcol(ext["gdn_out_norm"], l * 128, 1), w=['gno'], slow=True)
        em.op('act', lambda E: E.activation(out=negA[:], in_=negA[:], func=AF.Exp), r=['negA'], w=['negA'])
        em.op('dve', lambda E: E.tensor_scalar(negA[:], negA[:], -1.0, None, ALU.mult), r=['negA'], w=['negA'])
        em.op('dve', lambda E: E.tensor_scalar(gno[:], gno[:], math.sqrt(128.0), None, ALU.mult), r=['gno'], w=['gno'])
        maskT = {0: (cU[0], 'cU0'), 1: (cU[1], 'cU1')}
        qkv = T("qkv", [128, 1536], BF16)
        bd = T("bd", [128, 16], F32)
        sqf = T("sqf", [128, 1024], F32, 2)
        ss8 = T("ss8", [128, 8], F32)
        qkn = T("qkn", [128, 8, 128], BF16)
        qkT = T("qkT", [128, 8, 128], BF16)
        sm = T("sm", [128, 32], F32)
        gU = T("gU", [128, 4, 128], F32)
        gcc = T("gcc", [128, 8], F32)
        dif = T("dif", [128, 4, 128], F32)
        difT = T("difT", [128, 4, 128], F32)
        DmS = T("DmS", [128, 4, 128], F32)
        Bk = T("B", [128, 4, 128], F32, 6)
        Xk = T("X", [128, 4, 128], F32, 6)
        Pk = T("P", [128, 4, 128], F32, 6)
        IBk = T("IB", [128, 4, 128], F32, 6)
        attnT = T("attnT", [128, 4, 128], BF16)
        vb = T("vb", [128, 4, 128], F32)
        rr = T("rr", [128, 4, 128], F32)
        t1b = T("t1b", [128, 4, 128], F32)
        Erow = T("Erow", [128, 4, 128], F32)
        qdT = T("qdT", [128, 4, 128], BF16)
        kdec = T("kdec", [128, 4, 128], BF16)
        vnew = T("vnew", [128, 4, 128], BF16)
        of32 = T("of32", [128, 4, 128], F32)
        osum = T("osum", [128, 4, 128], F32)
        osq = T("osq", [128, 4, 128], BF16)
        orst = T("orst", [128, 4, 128], F32)
        zT = T("zT", [128, 4, 128], BF16)
        oout = T("oout", [128, 4, 128], BF16)
        identb4 = V(g.ident[:, 0:1], [(0, 4), (1, 128)])

        def f2(t):
            return t[:].rearrange("p h c -> p (h c)")

        def mm4(ps, lhs, lk, rhs, rk, pk, start=True, stop=True):
            for h in range(4):
                em.op('pe', lambda E: E.matmul(ps[:, h * 128:(h + 1) * 128], lhs[:, h, :], rhs[:, h, :], start=start, stop=stop),
                      r=[lk, rk], w=[pk])

        S32d = {d: sb(f"g_S32d{d}", [128, 4, 128], F32) for d in range(2)}
        Sbd = {d: T(f"Sbd{d}", [128, 4, 128], BF16) for d in range(2)}
        Slod = {d: T(f"Slod{d}", [128, 4, 128], BF16) for d in range(2)}
        state = {}
        for d in range(2):
            em.op('pool', lambda E: E.memset(S32d[d][:], 0.0), w=[f'S32_{d}'])
            a_, ak_ = Sbd[d].next()
            em.op('pool', lambda E: E.memset(a_[:], 0.0), w=[ak_])
            b_, bk_ = Slod[d].next()
            em.op('pool', lambda E: E.memset(b_[:], 0.0), w=[bk_])
            state[d] = (a_, ak_, b_, bk_)

        def body(d, n, first):
            U, Uk = cU[d], f'cU{d}'
            MS, MSk = cS[d], f'cS{d}'
            MT, MTk = maskT[d]
            S32_, S32k = S32d[d], f'S32_{d}'
            sb_, sbk, sl_, slk = state[d]
            r0 = n * 128
            qkv_, qk_ = qkv.next()
            em.dma('sp', qkv_[:], g.PG[r0:r0 + 128, :], r=[('dram', 'PG')], w=[qk_])
            bd_, bdk = bd.next()
            em.dma('sp', bd_[:], g.BD[r0:r0 + 128, :], r=[('dram', 'BD')], w=[bdk])
            sq_, sqk = sqf.next()
            em.op('pool', lambda E: E.tensor_tensor(sq_[:], qkv_[:, 0:1024], qkv_[:, 0:1024], ALU.mult), r=[qk_], w=[sqk])
            s8, s8k = ss8.next()
            em.op('dve', lambda E: E.tensor_reduce(s8[:], sq_[:].rearrange("p (h c) -> p h c", h=8), AX.X, ALU.add), r=[sqk], w=[s8k])
            rsqrt(em, s8[:], s8[:], EPS, s8k, s8k)
            em.op('dve', lambda E: E.tensor_scalar(s8[:, 0:4], s8[:, 0:4], 128.0 ** -0.5, None, ALU.mult), r=[s8k], w=[s8k])
            qn_, qnk = qkn.next()
            em.op('dve', lambda E: E.tensor_tensor(qn_[:], qkv_[:, 0:1024].rearrange("p (h c) -> p h c", h=8),
                                                   V(s8[:, 0:1], [(1, 8), (0, 128)]), ALU.mult), r=[qk_, s8k], w=[qnk])
            qT_, qTk = qkT.next()
            for hh in range(2):
                ps, pk = g.psum()
                for h in range(4):
                    em.op('pe', lambda E: E.matmul(ps[:, h * 128:(h + 1) * 128], qn_[:, hh * 4 + h, :], g.identb[:], start=True, stop=True),
                          r=[qnk, 'identb'], w=[pk])
                dst = qT_[:, hh * 4:(hh + 1) * 4, :].rearrange("p h c -> p (h c)")
                if hh == 0:
                    em.op('act', lambda E: E.copy(dst, ps[:]), r=[pk], w=[qTk])
                else:
                    em.op('dve', lambda E: E.tensor_copy(dst, ps[:]), r=[pk], w=[qTk])
            qT4, kT4, kn4 = qT_[:, 0:4], qT_[:, 4:8], qn_[:, 4:8]
            yield
            psG, pGk = g.psum()
            mm4(psG, kT4, qTk, kT4, qTk, pGk)
            psQ, pQk = g.psum()
            mm4(psQ, kT4, qTk, qT4, qTk, pQk)
            if GDN_CUT == 1000 + 1:
                pass
            yield
            sm_, smk = sm.next()
            beta, nbeta, gtok, tmp4, ecol, bg, ed, dec = (sm_[:, i * 4:(i + 1) * 4] for i in range(8))
            em.op('act', lambda E: E.activation(out=beta, in_=bd_[:, d * 4:d * 4 + 4], func=AF.Exp, scale=-1.0), r=[bdk], w=[smk])
            em.op('dve', lambda E: E.tensor_scalar(beta, beta, 1.0, None, ALU.add), r=[smk], w=[smk])
            em.op('dve', lambda E: E.reciprocal(beta, beta), r=[smk], w=[smk])
            em.op('dve', lambda E: E.tensor_scalar(nbeta, beta, -1.0, None, ALU.mult), r=[smk], w=[smk])
            em.op('dve', lambda E: E.tensor_tensor(tmp4, bd_[:, 8 + d * 4:12 + d * 4], dtb[:, d * 4:d * 4 + 4], ALU.add), r=[bdk, 'dtb'], w=[smk])
            em.op('act', lambda E: E.activation(out=tmp4, in_=tmp4, func=AF.Exp), r=[smk], w=[smk])
            em.op('act', lambda E: E.activation(out=tmp4, in_=tmp4, func=AF.Ln, bias=1.0, scale=1.0), r=[smk], w=[smk])
            em.op('dve', lambda E: E.tensor_tensor(gtok, tmp4, negA[:, d * 4:d * 4 + 4], ALU.mult), r=[smk, 'negA'], w=[smk])
            if GDN_CUT == 1000 + 2:
                pass
            yield
            gU_, gUk = gU.next()
            for h in range(4):
                em.op('pool' if h % 2 else 'dve', lambda E: E.tensor_scalar(gU_[:, h, :], U[:], gtok[:, h:h + 1], None, ALU.mult),
                      r=[Uk, smk], w=[gUk])
            psR, pRk = g.psum()
            em.op('pe', lambda E: E.matmul(psR[:], ones32[:], f2(gU_), start=True, stop=True), r=['ones32', gUk], w=[pRk])
            psC, pCk = g.psum()
            em.op('pe', lambda E: E.matmul(psC[:, 0:4], U[:], gtok, start=True, stop=True), r=[Uk, smk], w=[pCk])
            em.op('pe', lambda E: E.matmul(psC[:, 4:8], ones32[:], gtok, start=True, stop=True), r=['ones32', smk], w=[pCk])
            gcc_, gck = gcc.next()
            em.op('dve', lambda E: E.tensor_copy(gcc_[:], psC[:, 0:8]), r=[pCk], w=[gck])
            yield
            dif_, difk = dif.next()
            dT_, dTk = difT.next()
            for h in range(4):
                em.op('dve', lambda E: E.tensor_scalar(dif_[:, h, :], psR[:, h * 128:(h + 1) * 128], gcc_[:, h:h + 1], 0.0,
                                                       ALU.subtract, ALU.max), r=[pRk, gck], w=[difk])
                em.op('dve', lambda E: E.tensor_scalar(dT_[:, h, :], psR[:, h * 128:(h + 1) * 128], gcc_[:, h:h + 1], 0.0,
                                                       ALU.subtract, ALU.min), r=[pRk, gck], w=[dTk])
            em.op('act', lambda E: E.activation(out=dif_[:], in_=dif_[:], func=AF.Exp, scale=-1.0), r=[difk], w=[difk])
            em.op('act', lambda E: E.activation(out=dT_[:], in_=dT_[:], func=AF.Exp), r=[dTk], w=[dTk])
            yield
            er_, erk = Erow.next()
            em.op('act', lambda E: E.activation(out=f2(er_), in_=psR[:], func=AF.Exp), r=[pRk, difk, dTk], w=[erk])
            if GDN_CUT == 1000 + 3:
                pass
            yield
            ds_, dsk = DmS.next()
            for h in range(4):
                em.op('dve', lambda E: E.scalar_tensor_tensor(ds_[:, h, :], dif_[:, h, :], nbeta[:, h:h + 1], MS[:], ALU.mult, ALU.mult),
                      r=[difk, smk, MSk], w=[dsk])
            B0, B0k = Bk.next()
            em.op('dve', lambda E: E.tensor_tensor(f2(B0), psG[:], f2(ds_), ALU.mult), r=[pGk, dsk], w=[B0k])
            if GDN_CUT == 1000 + 31:
                pass
            em.op('pool', lambda E: E.tensor_tensor(dT_[:], dT_[:], V(MT[:, 0:1], [(0, 4), (1, 128)]), ALU.mult), r=[dTk, MTk], w=[dTk])
            at_, atk = attnT.next()
            em.op('dve', lambda E: E.tensor_tensor(f2(at_), psQ[:], f2(dT_), ALU.mult), r=[pQk, dTk], w=[atk])
            if GDN_CUT == 1000 + 32:
                pass
            yield
            psX, pXk = g.psum()
            for h in range(4):
                em.op('pe', lambda E: E.matmul(psX[:, h * 128:(h + 1) * 128], B0[:, h, :], g.ident[:], start=True, stop=True),
                      r=[B0k, 'ident'], w=[pXk])
            if GDN_CUT == 1000 + 33:
                pass
            X0, X0k = Xk.next()
            em.op('dve', lambda E: E.tensor_copy(f2(X0), psX[:]), r=[pXk], w=[X0k])
            if GDN_CUT == 1000 + 34:
                pass
            P0, P0k = Pk.next()
            em.op('dve', lambda E: E.tensor_tensor(P0[:], psX[:].rearrange("p (h c) -> p h c", h=4), identb4, ALU.add),
                  r=[pXk, 'ident'], w=[P0k])
            if GDN_CUT == 1000 + 4:
                pass
            Bc, Bck, Xc, Xck, Pc, Pck = B0, B0k, X0, X0k, P0, P0k
            for k in range(1, 7):
                if k < 6:
                    psX, pXk = g.psum()
                    mm4(psX, Bc, Bck, Xc, Xck, pXk)
                yield
                psB, pBk = g.psum()
                mm4(psB, Xc, Xck, Bc, Bck, pBk)
                if k < 6:
                    Xn, Xnk = Xk.next()
                    em.op('dve', lambda E: E.tensor_copy(f2(Xn), psX[:]), r=[pXk], w=[Xnk])
                    Bn, Bnk = Bk.next()
                    em.op('dve', lambda E: E.tensor_copy(f2(Bn), psB[:]), r=[pBk], w=[Bnk])
                yield
                IB, IBk_ = IBk.next()
                em.op('dve', lambda E: E.tensor_tensor(IB[:], psB[:].rearrange("p (h c) -> p h c", h=4), identb4, ALU.add),
                      r=[pBk, 'ident'], w=[IBk_])
                yield
                psP, pPk = g.psum()
                mm4(psP, IB, IBk_, Pc, Pck, pPk)
                yield
                Pn, Pnk = Pk.next()
                em.op('dve', lambda E: E.tensor_copy(f2(Pn), psP[:]), r=[pPk], w=[Pnk])
                Pc, Pck = Pn, Pnk
                if k < 6:
                    Bc, Bck, Xc, Xck = Bn, Bnk, Xn, Xnk
            TmT, Tk = Pc, Pck
            if GDN_CUT == 1000 + 5:
                pass
            yield
            em.op('act', lambda E: E.activation(out=ecol, in_=gcc_[:, 0:4], func=AF.Exp), r=[gck], w=[smk])
            em.op('dve', lambda E: E.tensor_tensor(bg, beta, ecol, ALU.mult), r=[smk], w=[smk])
            em.op('dve', lambda E: E.tensor_tensor(ed, gcc_[:, 4:8], gcc_[:, 0:4], ALU.subtract), r=[gck], w=[smk])
            em.op('act', lambda E: E.activation(out=ed, in_=ed, func=AF.Exp), r=[smk], w=[smk])
            em.op('act', lambda E: E.activation(out=dec, in_=gcc_[:, 4:8], func=AF.Exp), r=[gck], w=[smk])
            vb_, vbk = vb.next()
            em.op('pool', lambda E: E.tensor_tensor(vb_[:], qkv_[:, 1024:1536].rearrange("p (h c) -> p h c", h=4),
                                                    V(beta[:, 0:1], [(1, 4), (0, 128)]), ALU.mult), r=[qk_, smk], w=[vbk])
            kd_, kdk = kdec.next()
            em.op('pool', lambda E: E.tensor_tensor(kd_[:], kn4, V(ed[:, 0:1], [(1, 4), (0, 128)]), ALU.mult), r=[qnk, smk], w=[kdk])
            qd_, qdk = qdT.next()
            em.op('dve', lambda E: E.tensor_tensor(qd_[:], qT4, er_[:], ALU.mult), r=[qTk, erk], w=[qdk])
            if GDN_CUT == 1000 + 6:
                pass
            yield
            psV, pVk = g.psum()
            for h in range(4):
                em.op('pe', lambda E: E.matmul(psV[:, h * 128:(h + 1) * 128], kT4[:, h, :], sb_[:, h, :], start=True, stop=False),
                      r=[qTk, sbk], w=[pVk])
                em.op('pe', lambda E: E.matmul(psV[:, h * 128:(h + 1) * 128], kT4[:, h, :], sl_[:, h, :], start=False, stop=True),
                      r=[qTk, slk], w=[pVk])
            yield
            t1_, t1k = t1b.next()
            em.op('dve', lambda E: E.tensor_tensor(t1_[:], psV[:].rearrange("p (h c) -> p h c", h=4), V(bg[:, 0:1], [(1, 4), (0, 128)]), ALU.mult),
                  r=[pVk, smk], w=[t1k])
            r_, rk_ = rr.next()
            em.op('pool', lambda E: E.tensor_tensor(r_[:], vb_[:], t1_[:], ALU.subtract), r=[vbk, t1k], w=[rk_])
            yield
            psN2, pN2k = g.psum()
            mm4(psN2, TmT, Tk, r_, rk_, pN2k)
            yield
            vn_, vnk = vnew.next()
            em.op('act', lambda E: E.copy(f2(vn_), psN2[:]), r=[pN2k], w=[vnk])
            yield
            psO, pOk = g.psum()
            for h in range(4):
                em.op('pe', lambda E: E.matmul(psO[:, h * 128:(h + 1) * 128], sb_[:, h, :], qd_[:, h, :], start=True, stop=False),
                      r=[sbk, qdk], w=[pOk])
                em.op('pe', lambda E: E.matmul(psO[:, h * 128:(h + 1) * 128], vn_[:, h, :], at_[:, h, :], start=False, stop=True),
                      r=[vnk, atk], w=[pOk])
            yield
            psS, pSk = g.psum()
            mm4(psS, kd_, kdk, vn_, vnk, pSk)
            em.op('pool', lambda E: E.tensor_tensor(S32_[:], S32_[:], V(dec[:, 0:1], [(1, 4), (0, 128)]), ALU.mult), r=[S32k, smk], w=[S32k])
            em.op('dve', lambda E: E.tensor_tensor(f2(S32_), f2(S32_), psS[:], ALU.add), r=[S32k, pSk], w=[S32k])
            sb_, sbk = Sbd[d].next()
            em.op('act', lambda E: E.copy(sb_[:], S32_[:]), r=[S32k], w=[sbk])
            sl_, slk = Slod[d].next()
            em.op('pool', lambda E: E.tensor_tensor(sl_[:], S32_[:], sb_[:], ALU.subtract), r=[S32k, sbk], w=[slk])
            if GDN_CUT == 1000 + 7:
                pass
            yield
            ofap = AP(tensor=g.OF.tensor, offset=g.OF.offset + r0, ap=[[Lmax, 128], [128 * Lmax, 4], [1, 128]])
            if first:
                of_, ofk = of32.next()
                em.op('dve', lambda E: E.tensor_copy(f2(of_), psO[:]), r=[pOk], w=[ofk])
                em.dma('pool', ofap, of_[:], r=[ofk], w=[('dram', 'OF')])
            else:
                of_, ofk = of32.next()
                em.dma('sp', of_[:], ofap, r=[('dram', 'OF')], w=[ofk])
                z_, zk = zT.next()
                em.dma('sp', z_[:], AP(tensor=g.ZT.tensor, offset=g.ZT.offset + r0, ap=[[Lmax, 128], [128 * Lmax, 4], [1, 128]]),
                       r=[('dram', 'ZT')], w=[zk])
                os_, osk = osum.next()
                em.op('dve', lambda E: E.tensor_tensor(f2(os_), psO[:], f2(of_), ALU.add), r=[pOk, ofk], w=[osk])
                oq_, oqk = osq.next()
                em.op('act', lambda E: E.activation(out=oq_[:], in_=os_[:], func=AF.Square), r=[osk], w=[oqk])
                psN, pNk = g.psum()
                em.op('pe', lambda E: E.matmul(psN[:], g.onesb[:], f2(oq_), start=True, stop=True), r=['onesb', oqk], w=[pNk])
                or_, ork = orst.next()
                rsqrt(em, f2(or_), psN[:], 128 * EPS, pNk, ork)
                em.op('dve', lambda E: E.tensor_tensor(or_[:], or_[:], os_[:], ALU.mult), r=[ork, osk], w=[ork])
                oo_, ook = oout.next()
                em.op('dve', lambda E: E.scalar_tensor_tensor(oo_[:], or_[:], gno[:, 0:1], z_[:], ALU.mult, ALU.mult),
                      r=[ork, 'gno', zk], w=[ook])
                em.dma('pool', AP(tensor=g.OT.tensor, offset=g.OT.offset + 1024 * Lmax + r0, ap=[[Lmax, 128], [128 * Lmax, 4], [1, 128]]),
                       oo_[:], r=[ook], w=[('dram', 'OT')])
            state[d] = (sb_, sbk, sl_, slk)

        for idx in range(nt):
            first = idx < nt - 1 - idx
            gens = [body(0, idx, first), body(1, nt - 1 - idx, first)]
            while gens:
                for gen in list(gens):
                    try:
                        next(gen)
                    except StopIteration:
                        gens.remove(gen)
        em.barrier()
    em.barrier()


def mixer_hyena(g, l, si):
    nc, em, ext = g.nc, g.em, g.ext
    L = g.seq_lens[si]
    Lmax = g.Lmax
    N = 2 * L
    N1 = N // 128
    P1 = L // 128
    R1 = min(N1, HY_RMAX)
    nch1 = max(1, N1 // HY_RMAX)
    CG = min(64, 4096 // N1)
    CS = 64
    ncol = CG * N1
    nchunk = ncol // 512
    cpb = 512 // N1
    nb1 = min(CG, 512 // (2 * N1))
    KF = g.KFq[L]
    KTd = g.KTd[L]
    TWO_PI = 2.0 * math.pi
    with ExitStack() as st:
        def sb(name, shape, dt):
            return st.enter_context(nc.sbuf_tensor(g.un(name), list(shape), dt))

        def T(name, shape, dt, n=2):
            return Rot([sb(f"h_{name}{i}", shape, dt) for i in range(n)], "h_" + name)
        F1 = sb("h_F1", [R1, nch1, 2 * N1], BF16)
        Twr = sb("h_Twr", [128, N1], F32)
        Twi = sb("h_Twi", [128, N1], F32)
        TwTr = sb("h_TwTr", [R1, nch1, 128], F32)
        TwTi = sb("h_TwTi", [R1, nch1, 128], F32)
        G1r = sb("h_G1r", [R1, nch1, P1], BF16)
        G1i = sb("h_G1i", [R1, nch1, P1], BF16)
        F2r = sb("h_F2r", [128, 128], BF16)
        F2i = sb("h_F2i", [128, 128], BF16)
        F2in = sb("h_F2in", [128, 128], BF16)
        G2a = sb("h_G2a", [128, 256], BF16)
        G2b = sb("h_G2b", [128, 256], BF16)
        skip = sb("h_skip", [128, 2, 512], F32)
        for t_, nm in ((F1, f"hyF1_{L}"), (TwTr, f"hyTwTr_{L}"), (TwTi, f"hyTwTi_{L}"), (G1r, f"hyG1r_{L}"), (G1i, f"hyG1i_{L}")):
            em.dma('sp', t_[:], ext["c_" + nm][:, :, :], w=['hconst'])
        for t_, nm in ((Twr, f"hyTwr_{L}"), (Twi, f"hyTwi_{L}"), (F2r, "hyF2r"), (F2i, "hyF2i"), (F2in, "hyF2in"), (G2a, "hyG2a"), (G2b, "hyG2b")):
            em.dma('sp', t_[:], ext["c_" + nm][:, :], w=['hconst'])
        em.dma('sp', skip[:].rearrange("p o c -> p (o c)"), drow_bc(ext["hy_skip"], l * 1024, 1024), w=['hconst'])
        HC = 'hconst'
        Apr = sb("h_Apr", [128, CG, N1], BF16)
        Api = sb("h_Api", [128, CG, N1], BF16)
        Zkr = sb("h_Zkr", [128, CG, N1], BF16)
        Zki = sb("h_Zki", [128, CG, N1], BF16)
        Bpr = sb("h_Bpr", [R1, nch1, CG, 128], BF16)
        Bpi = sb("h_Bpi", [R1, nch1, CG, 128], BF16)
        tP = T("tP", [128, 512], F32, 2)
        tQ = T("tQ", [128, 512], F32, 2)
        kfc = T("kfc", [128, 2, 512], BF16, 2)

        def cmul_tw(ps, pk, M, units, width, tr, ti, outr_fn, outi_fn, okeys):
            n = units * 2 * width
            src = V(ps[0:M, 0:1], [(2 * width, units), (width, 2), (1, width)])
            trb = V(tr, [(0, units), (0, 2), (1, width)])
            tib = V(ti, [(0, units), (0, 2), (1, width)])
            p_, pk_ = tP.next()
            q_, qk_ = tQ.next()
            pv = V(p_[0:M, 0:1], [(2 * width, units), (width, 2), (1, width)])
            qv = V(q_[0:M, 0:1], [(2 * width, units), (width, 2), (1, width)])
            em.op('dve', lambda E: E.tensor_tensor(pv, src, trb, ALU.mult), r=[pk, HC], w=[pk_])
            em.op('dve', lambda E: E.tensor_tensor(qv, src, tib, ALU.mult), r=[pk, HC], w=[qk_])
            pre = V(p_[0:M, 0:1], [(2 * width, units), (1, width)])
            pim = V(p_[0:M, 0:1], [(2 * width, units), (1, width)], off=width)
            qre = V(q_[0:M, 0:1], [(2 * width, units), (1, width)])
            qim = V(q_[0:M, 0:1], [(2 * width, units), (1, width)], off=width)
            em.op('pool', lambda E: E.tensor_tensor(outr_fn, pre, qim, ALU.subtract), r=[pk_, qk_], w=[okeys[0]])
            em.op('pool', lambda E: E.tensor_tensor(outi_fn, qre, pim, ALU.add), r=[pk_, qk_], w=[okeys[1]])

        def fwd_fft(lhs_fn, kchs):
            for c0 in range(0, CG, nb1):
                ps, pk = g.psum()
                for u in range(nb1):
                    for kc in range(kchs):
                        lap, lk = lhs_fn(c0 + u, kc)
                        K_ = lap.shape[0]
                        em.op('pe', lambda E: E.matmul(ps[:, u * 2 * N1:(u + 1) * 2 * N1], lap, F1[0:K_, kc, :],
                                                       start=(kc == 0), stop=(kc == kchs - 1)), r=[lk, HC], w=[pk])
                cmul_tw(ps, pk, 128, nb1, N1, Twr[:, :], Twi[:, :], Apr[:, c0:c0 + nb1, :], Api[:, c0:c0 + nb1, :], ['Apr', 'Api'])

        def stage3(j):
            cols = slice(j * 512, (j + 1) * 512)
            ar = Apr[:].rearrange("p c f -> p (c f)")[:, cols]
            ai = Api[:].rearrange("p c f -> p (c f)")[:, cols]
            pzr, pzrk = g.psum()
            em.op('pe', lambda E: E.matmul(pzr[:], F2r[:], ar, start=True, stop=False), r=['Apr', HC], w=[pzrk])
            em.op('pe', lambda E: E.matmul(pzr[:], F2in[:], ai, start=False, stop=True), r=['Api', HC], w=[pzrk])
            pzi, pzik = g.psum()
            em.op('pe', lambda E: E.matmul(pzi[:], F2i[:], ar, start=True, stop=False), r=['Apr', HC], w=[pzik])
            em.op('pe', lambda E: E.matmul(pzi[:], F2r[:], ai, start=False, stop=True), r=['Api', HC], w=[pzik])
            return pzr, pzrk, pzi, pzik

        if g.filt_done.get(L) != l:
            g.filt_done[L] = l
            with ExitStack() as st2:
                def sb2(name, shape, dt):
                    return st2.enter_context(nc.sbuf_tensor(g.un(name), list(shape), dt))
                w1 = sb2("hf_w1", [33, 64], F32)
                w2 = sb2("hf_w2", [64, 64], F32)
                w3 = sb2("hf_w3", [64, 2048], F32)
                pc = sb2("hf_pc", [64, 8], F32)
                dl = sb2("hf_dl", [128, 512], F32)
                tneg = sb2("hf_tneg", [128, N1], F32)
                ones32 = sb2("hf_ones", [128, 128], F32)
                invn = sb2("hf_invn", [128, 2, 512], F32)
                em.dma('sp', w1[:], ext["hy_ffn_w1"][l], w=['fw'])
                em.dma('sp', w2[:], ext["hy_ffn_w2"][l], w=['fw'])
                em.dma('sp', w3[:], ext["hy_ffn_w3"][l], w=['fw'])
                em.dma('sp', pc[:, 0:1], dcol(ext["hy_ffn_b1"], l * 64, 1, nparts=64), w=['fpc'], slow=True)
                em.dma('sp', pc[:, 1:2], dcol(ext["hy_sin_freq"], l * 64, 1, nparts=64), w=['fpc'], slow=True)
                em.dma('sp', pc[:, 2:3], dcol(ext["hy_ffn_b2"], l * 64, 1, nparts=64), w=['fpc'], slow=True)
                em.dma('sp', dl[:], ext["c_hydelta"][:, :], w=['fw'])
                em.dma('sp', tneg[:], ext[f"c_hytneg_{L}"][:, :], w=['fw'])
                em.dma('sp', ones32[:], ext["c_ones"][:, :], w=['fw'])
                em.op('dve', lambda E: E.tensor_scalar(pc[:, 1:2], pc[:, 1:2], 1.0 / TWO_PI, None, ALU.mult), r=['fpc'], w=['fpc'])
                em.op('pool', lambda E: E.memset(pc[:, 3:4], -math.pi), w=['fpc'])
                ft = Rot([sb2(f"hf_ft{i}", [33, 512], F32) for i in range(2)], "hf_ft")
                hh1 = Rot([sb2(f"hf_h1{i}", [64, 512], F32) for i in range(2)], "hf_h1")
                hh2 = Rot([sb2(f"hf_h2{i}", [64, 512], F32) for i in range(2)], "hf_h2")
                dk = Rot([sb2(f"hf_dk{i}", [128, 512], F32) for i in range(2)], "hf_dk")
                kf32 = Rot([sb2(f"hf_k32{i}", [128, 512], F32) for i in range(2)], "hf_k32")
                kab = Rot([sb2(f"hf_kab{i}", [128, 512], F32) for i in range(2)], "hf_kab")
                kbf = Rot([sb2(f"hf_kbf{i}", [128, 2, 512], BF16) for i in range(2)], "hf_kbf")
                feat = ext[f"c_hyfeat_{L}"]
                msk = sb2("hf_msk", [64, 512], F32)

                def sin_layer(ps, pk, bcol, out, ok):
                    em.op('dve', lambda E: E.tensor_scalar(out, ps, pc[:, bcol:bcol + 1], pc[:, 1:2], ALU.add, ALU.mult), r=[pk, 'fpc'], w=[ok])
                    for _ in range(2):
                        em.op('dve', lambda E: E.tensor_scalar(msk[:], out, 0.5, None, ALU.is_gt), r=[ok], w=['msk'])
                        em.op('dve', lambda E: E.tensor_tensor(out, out, msk[:], ALU.subtract), r=[ok, 'msk'], w=[ok])
                        em.op('dve', lambda E: E.tensor_scalar(msk[:], out, -0.5, None, ALU.is_lt), r=[ok], w=['msk'])
                        em.op('dve', lambda E: E.tensor_tensor(out, out, msk[:], ALU.add), r=[ok, 'msk'], w=[ok])
                    em.op('act', lambda E: E.activation(out=out, in_=out, func=AF.Sin, scale=TWO_PI), r=[ok], w=[ok])

                pn = [g.psum([6]), g.psum([7])]
                ntile_tot = N // 128
                for blk in range(N // 512):
                    f_, fk = ft.next()
                    em.dma('sp', f_[:], feat[:, blk * 512:(blk + 1) * 512], w=[fk])
                    ps, pk = g.psum(range(6))
                    em.op('pe', lambda E: E.matmul(ps[0:64, :], w1[:], f_[:], start=True, stop=True), r=['fw', fk], w=[pk])
                    h1, h1k = hh1.next()
                    sin_layer(ps[0:64, :], pk, 0, h1[:], h1k)
                    ps, pk = g.psum(range(6))
                    em.op('pe', lambda E: E.matmul(ps[0:64, :], w2[:], h1[:], start=True, stop=True), r=['fw', h1k], w=[pk])
                    h2, h2k = hh2.next()
                    sin_layer(ps[0:64, :], pk, 2, h2[:], h2k)
                    for tt in range(4):
                        ti_ = blk * 4 + tt
                        dirn = 0 if ti_ * 128 < L else 1
                        d_, dkk = dk.next()
                        em.op('act', lambda E: E.activation(out=d_[:], in_=dl[:], func=AF.Exp, scale=tneg[:, ti_:ti_ + 1]), r=['fw'], w=[dkk])
                        kb_, kbk = kbf.next()
                        for o in range(2):
                            ps, pk = g.psum(range(6))
                            col0 = o * 1024 + dirn * 512
                            em.op('pe', lambda E: E.matmul(ps[:], h2[:, tt * 128:(tt + 1) * 128], w3[:, col0:col0 + 512], start=True, stop=True),
                                  r=[h2k, 'fw'], w=[pk])
                            k_, kk = kf32.next()
                            em.op('dve', lambda E: E.tensor_tensor(k_[:], ps[:], d_[:], ALU.mult), r=[pk, dkk], w=[kk])
                            if ti_ * 128 == L:
                                em.op('pool', lambda E: E.memset(k_[0:1, :], 0.0), w=[kk])
                            a_, ak = kab.next()
                            em.op('act', lambda E: E.activation(out=a_[:], in_=k_[:], func=AF.Abs), r=[kk], w=[ak])
                            em.op('pe', lambda E: E.matmul(pn[o][0][:], ones32[:], a_[:], start=(ti_ == 0), stop=(ti_ == ntile_tot - 1)),
                                  r=['fw', ak], w=[pn[o][1]])
                            em.op('act', lambda E: E.copy(kb_[:, o, :], k_[:]), r=[kk], w=[kbk])
                        em.dma('pool', KTd[ti_ * 128:(ti_ + 1) * 128, :], kb_[:].rearrange("p o c -> p (o c)"), r=[kbk], w=[('dram', 'KTd')])
                for o in range(2):
                    em.op('dve', lambda E: E.reciprocal(invn[:, o, :], pn[o][0][:]), r=[pn[o][1]], w=['invn'])
                ktd = Rot([sb2(f"hf_ktd{i}", [R1, nch1, 128, CG], BF16) for i in range(2)], "hf_ktd")
                for o in range(2):
                    for gi in range(512 // CG):
                        cbase = gi * CG
                        kt_, ktk = ktd.next()
                        for kc in range(nch1):
                            em.dma('sp', kt_[:, kc], AP(tensor=KTd.tensor, offset=KTd.offset + kc * R1 * 128 * 1024 + o * 512 + cbase,
                                                        ap=[[128 * 1024, R1], [1024, 128], [1, CG]]), r=[('dram', 'KTd')], w=[ktk])
                        fwd_fft(lambda c, kc: (V(kt_[:, kc, 0, 0:1], [(CG, 128)], off=c), ktk), nch1)
                        for j in range(nchunk):
                            pzr, pzrk, pzi, pzik = stage3(j)
                            kc_, kck = kfc.next()
                            inb = V(invn[:, o, cbase + j * cpb:cbase + j * cpb + 1], [(1, cpb), (0, N1)])
                            em.op('dve', lambda E: E.tensor_tensor(kc_[:, 0, :].rearrange("p (c f) -> p c f", c=cpb),
                                                                   pzr[:].rearrange("p (c f) -> p c f", c=cpb), inb, ALU.mult),
                                  r=[pzrk, 'invn'], w=[kck])
                            em.op('dve', lambda E: E.tensor_tensor(kc_[:, 1, :].rearrange("p (c f) -> p c f", c=cpb),
                                                                   pzi[:].rearrange("p (c f) -> p c f", c=cpb), inb, ALU.mult),
                                  r=[pzik, 'invn'], w=[kck])
                            em.dma('pool', KF[o, gi, j].rearrange("p a c -> p (a c)"), kc_[:].rearrange("p a c -> p (a c)"), r=[kck],
                                   w=[('dram', 'KF')])
            em.barrier()
        xs = {nm: T(nm, [P1, 128, CS], BF16, 1) for nm in ("x1s", "x2s", "vs")}
        z1 = sb("h_z1", [P1, 128, CG], BF16)
        z2s = sb("h_z2s", [P1, 128, CS], BF16)
        gt1 = T("gt1", [P1, 128, 4], F32, 2)
        gt2 = T("gt2", [P1, 128, 4], F32, 2)
        for sl in range(512 // CS):
            tiles = {}
            for i, nm in enumerate(("x1s", "x2s", "vs")):
                t_, tk = xs[nm].next()
                src_ = AP(tensor=g.PH.tensor, offset=g.PH.offset + i * 512 + sl * CS, ap=[[128 * 1536, P1], [1536, 128], [1, CS]])
                nsp = 2 if P1 >= 128 else 1
                for sp_ in range(nsp):
                    a0, a1 = sp_ * P1 // nsp, (sp_ + 1) * P1 // nsp
                    em.dma('sp', psplit(t_[:], a0, a1), psplit(src_, a0, a1), r=[('dram', 'PH')], w=[tk])
                tiles[nm] = (t_, tk)
            for sg_ in range(CS // CG):
                gi = sl * (CS // CG) + sg_
                cb = sg_ * CG
                cglob = gi * CG
                zsrc, zk = tiles["vs"]
                zoff, zstride = cb, CS
                for o in range(2):
                    gate, gk = tiles["x1s" if o == 0 else "x2s"]
                    zt, ztk, zo, zs_ = zsrc, zk, zoff, zstride
                    fwd_fft(lambda c, kc: (V(zt[:, 0, 0:1], [(zs_, 128)], off=zo + c), ztk), 1)
                    for j in range(nchunk):
                        pzr, pzrk, pzi, pzik = stage3(j)
                        kc_, kck = kfc.next()
                        em.dma('sp', kc_[:].rearrange("p a c -> p (a c)"), KF[o, gi, j].rearrange("p a c -> p (a c)"), r=[('dram', 'KF')], w=[kck])
                        cols = slice(j * 512, (j + 1) * 512)
                        zr_out = Zkr[:].rearrange("p c f -> p (c f)")[:, cols]
                        zi_out = Zki[:].rearrange("p c f -> p (c f)")[:, cols]
                        a_, ak = tP.next()
                        b_, bk = tQ.next()
                        em.op('dve', lambda E: E.tensor_tensor(a_[:], pzr[:], kc_[:, 0, :], ALU.mult), r=[pzrk, kck], w=[ak])
                        em.op('dve', lambda E: E.tensor_tensor(b_[:], pzi[:], kc_[:, 1, :], ALU.mult), r=[pzik, kck], w=[bk])
                        em.op('pool', lambda E: E.tensor_tensor(zr_out, a_[:], b_[:], ALU.subtract), r=[ak, bk], w=['Zkr'])
                        a_, ak = tP.next()
                        b_, bk = tQ.next()
                        em.op('dve', lambda E: E.tensor_tensor(a_[:], pzr[:], kc_[:, 1, :], ALU.mult), r=[pzrk, kck], w=[ak])
                        em.op('dve', lambda E: E.tensor_tensor(b_[:], pzi[:], kc_[:, 0, :], ALU.mult), r=[pzik, kck], w=[bk])
                        em.op('pool', lambda E: E.tensor_tensor(zi_out, a_[:], b_[:], ALU.add), r=[ak, bk], w=['Zki'])
                    for fc in range(nch1):
                        for c0 in range(0, CG, 2):
                            ps, pk = g.psum()
                            for u in range(2):
                                c = c0 + u
                                em.op('pe', lambda E: E.matmul(ps[0:R1, u * 256:(u + 1) * 256], Zkr[:, c, fc * R1:(fc + 1) * R1], G2a[:],
                                                               start=True, stop=False), r=['Zkr', HC], w=[pk])
                                em.op('pe', lambda E: E.matmul(ps[0:R1, u * 256:(u + 1) * 256], Zki[:, c, fc * R1:(fc + 1) * R1], G2b[:],
                                                               start=False, stop=True), r=['Zki', HC], w=[pk])
                            cmul_tw(ps, pk, R1, 2, 128, TwTr[:, fc, :], TwTi[:, fc, :], Bpr[:, fc, c0:c0 + 2, :], Bpi[:, fc, c0:c0 + 2, :],
                                    ['Bpr', 'Bpi'])
                    zn_, znk = (z1, 'z1') if o == 0 else (z2s, 'z2s')
                    zc0 = 0 if o == 0 else cb
                    for c0 in range(0, CG, 4):
                        ps, pk = g.psum()
                        n = 0
                        for fc in range(nch1):
                            for (gm, bp, bpk) in ((G1r, Bpr, 'Bpr'), (G1i, Bpi, 'Bpi')):
                                em.op('pe', lambda E: E.matmul(ps[0:P1, :], gm[:, fc, :], bp[:, fc, c0:c0 + 4, :].rearrange("p c t -> p (c t)"),
                                                               start=(n == 0), stop=(n == 2 * nch1 - 1)), r=[bpk, HC], w=[pk])
                                n += 1
                        yv = V(ps[0:P1, 0:1], [(1, 128), (128, 4)])
                        zv = V(zt[:, 0, 0:1], [(zs_, 128), (1, 4)], off=zo + c0)
                        gv = gate[:, :, cb + c0:cb + c0 + 4]
                        skb = V(skip[0:P1, o, cglob + c0:cglob + c0 + 1], [(0, 128), (1, 4)])
                        t1_, t1k = gt1.next()
                        em.op('pool', lambda E: E.tensor_tensor(t1_[:], zv, skb, ALU.mult), r=[ztk, HC], w=[t1k])
                        t2_, t2k = gt2.next()
                        em.op('dve', lambda E: E.tensor_tensor(t2_[:], yv, t1_[:], ALU.add), r=[pk, t1k], w=[t2k])
                        em.op('pool', lambda E: E.tensor_tensor(zn_[:, :, zc0 + c0:zc0 + c0 + 4], t2_[:], gv, ALU.mult), r=[t2k, gk], w=[znk])
                    if o == 0:
                        zsrc, zk, zoff, zstride = z1, 'z1', 0, CG
            dst_ = AP(tensor=g.Z2.tensor, offset=g.Z2.offset + sl * CS, ap=[[128 * 512, P1], [512, 128], [1, CS]])
            nsp = 2 if P1 >= 128 else 1
            for sp_ in range(nsp):
                a0, a1 = sp_ * P1 // nsp, (sp_ + 1) * P1 // nsp
                em.dma('pool', psplit(dst_, a0, a1), psplit(z2s[:], a0, a1), r=['z2s'], w=[('dram', 'Z2')])
        em.barrier()
        zl = T("zl", [128, 512], BF16, 2)
        zo_ = T("zo", [128, 4, 128], BF16, 2)
        for t in range(L // 128):
            a_, ak = zl.next()
            em.dma('sp', a_[:], g.Z2[t * 128:(t + 1) * 128, :], r=[('dram', 'Z2')], w=[ak])
            ps, pk = g.psum()
            for c in range(4):
                em.op('pe', lambda E: E.matmul(ps[:, c * 128:(c + 1) * 128], a_[:, c * 128:(c + 1) * 128], g.identb[:], start=True, stop=True),
                      r=[ak, 'identb'], w=[pk])
            o_, ok = zo_.next()
            em.op('act' if t % 2 else 'dve', lambda E: (E.copy if t % 2 else E.tensor_copy)(o_[:].rearrange("p c t -> p (c t)"), ps[:]),
                  r=[pk], w=[ok])
            em.dma('pool', AP(tensor=g.OT.tensor, offset=g.OT.offset + t * 128, ap=[[Lmax, 128], [128 * Lmax, 4], [1, 128]]), o_[:],
                   r=[ok], w=[('dram', 'OT')])
    em.barrier()


def stage_C(g, l, si):
    nc, em, ext = g.nc, g.em, g.ext
    L = g.seq_lens[si]
    t0 = g.seq_off[si]
    Ttot, Lmax = g.Ttot, g.Lmax
    nb = L // 512
    with ExitStack() as st:
        def sb(name, shape, dt):
            return st.enter_context(nc.sbuf_tensor(g.un(name), list(shape), dt))
        g32 = sb("c_g32", [128, 3, KC], F32)
        em.op('dve', lambda E: E.tensor_scalar(g32[:], g.gains[:, 1:4, l, :], 32.0, None, ALU.mult), r=['gains'], w=['g32'])
        xT = sb("c_xT", [128, KC, 512], F32)
        yT = sb("c_yT", [128, KC, 512], F32)
        OTb = sb("c_OTb", [128, 12, 512], BF16)
        Gj = Rot([sb(f"c_Gj{i}", [128, 3, 512], BF16) for i in range(2)], "c_Gj")
        wbr = Rot([sb(f"c_wbr{i}", [128, 12, 128], BF16) for i in range(2)], "c_wbr")
        wout = Rot([sb(f"c_wout{i}", [128, KC, 128], BF16) for i in range(2)], "c_wout")
        wgu = Rot([sb(f"c_wgu{i}", [128, 2, KC, 128], BF16) for i in range(3)], "c_wgu")
        wd = Rot([sb(f"c_wd{i}", [128, NFF, 128], BF16) for i in range(2)], "c_wd")
        tmp = Rot([sb(f"c_tmp{i}", [128, 512], F32) for i in range(4)], "c_tmp")
        sgt = Rot([sb(f"c_sgt{i}", [128, 512], BF16) for i in range(2)], "c_sgt")
        mT = sb("c_mT", [128, KC, 512], BF16)
        sqb = sb("c_sqb", [128, KC, 512], BF16)
        h2T = sb("c_h2T", [128, KC, 512], BF16)
        actT = sb("c_actT", [128, NFF, 512], BF16)
        rstd = Rot([sb(f"c_rstd{i}", [128, 512], F32) for i in range(2)], "c_rstd")
        eng2 = ['dve', 'pool']

        def norm_rstd(sqk_):
            ps, pk = g.psum()
            for kc in range(KC):
                em.op('pe', lambda E: E.matmul(ps[:], g.onesb[:], sqb[:, kc, :], start=(kc == 0), stop=(kc == KC - 1)),
                      r=[sqk_, 'onesb'], w=[pk])
            rs, rsk = rstd.next()
            rsqrt(em, rs[:], ps[:], D * EPS, pk, rsk)
            return rs, rsk

        for b in range(nb):
            c0 = t0 + b * 512
            s0 = b * 512
            em.dma('sp', xT[:], AP(tensor=g.XT.tensor, offset=g.XT.offset + c0, ap=[[Ttot, 128], [128 * Ttot, KC], [1, 512]]),
                   r=[('dram', 'XT')], w=['xT'])
            em.dma('sp', OTb[:], AP(tensor=g.OT.tensor, offset=g.OT.offset + s0, ap=[[Lmax, 128], [128 * Lmax, 12], [1, 512]]),
                   r=[('dram', 'OT')], w=['OTb'])
            for j in range(8):
                w_, wk = wbr.next()
                em.dma('sp', w_[:].rearrange("p k c -> p (k c)"), g.WBR[l][j].rearrange("p k c -> p (k c)"),
                       r=[('dram', f"WBR{l}")], w=[wk])
                gj, gk = Gj.next()
                em.dma('sp', gj[:], AP(tensor=g.GT.tensor, offset=g.GT.offset + j * 128 * Lmax + s0,
                                       ap=[[Lmax, 128], [1024 * Lmax, 3], [1, 512]]), r=[('dram', 'GT')], w=[gk])
                ts = []
                for i in range(3):
                    ps, pk = g.psum()
                    for kc in range(4):
                        em.op('pe', lambda E: E.matmul(ps[:], w_[:, i * 4 + kc, :], OTb[:, i * 4 + kc, :], start=(kc == 0), stop=(kc == 3)),
                              r=[wk, 'OTb'], w=[pk])
                    t_, tk = tmp.next()
                    em.op('dve', lambda E: E.tensor_tensor(t_[:], ps[:], gj[:, i, :], ALU.mult), r=[pk, gk], w=[tk])
                    ts.append((t_, tk))
                em.op('pool', lambda E: E.tensor_tensor(ts[0][0][:], ts[0][0][:], ts[1][0][:], ALU.add), r=[ts[0][1], ts[1][1]], w=[ts[0][1]])
                em.op('pool', lambda E: E.tensor_tensor(mT[:, j, :], ts[0][0][:], ts[2][0][:], ALU.add), r=[ts[0][1], ts[2][1]], w=['mT'])
            if C_CUT == 1:
                continue
            for j in range(8):
                w_, wk = wout.next()
                em.dma('sp', w_[:].rearrange("p k c -> p (k c)"), g.WOUT[l][j].rearrange("p k c -> p (k c)"),
                       r=[('dram', f"WOUT{l}")], w=[wk])
                ps, pk = g.psum()
                for kc in range(KC):
                    em.op('pe', lambda E: E.matmul(ps[:], w_[:, kc, :], mT[:, kc, :], start=(kc == 0), stop=(kc == KC - 1)),
                          r=[wk, 'mT'], w=[pk])
                em.op('dve', lambda E: E.tensor_copy(yT[:, j, :], ps[:]), r=[pk], w=['yT'])
                em.op('act', lambda E: E.activation(out=sqb[:, j, :], in_=yT[:, j, :], func=AF.Square), r=['yT'], w=['sqb'])
            rs, rsk = norm_rstd('sqb')
            for j in range(8):
                t_, tk = tmp.next()
                em.op('dve', lambda E: E.scalar_tensor_tensor(t_[:], yT[:, j, :], g32[:, 0, j:j + 1], rs[:], ALU.mult, ALU.mult),
                      r=['yT', rsk, 'g32'], w=[tk])
                em.op('pool', lambda E: E.tensor_tensor(xT[:, j, :], t_[:], xT[:, j, :], ALU.add), r=[tk, 'xT'], w=['xT'])
            if C_CUT == 2:
                continue
            em.op('act', lambda E: E.activation(out=sqb[:], in_=xT[:], func=AF.Square), r=['xT'], w=['sqb'])
            rs, rsk = norm_rstd('sqb')
            for j in range(8):
                em.op('dve', lambda E: E.scalar_tensor_tensor(h2T[:, j, :], xT[:, j, :], g32[:, 1, j:j + 1], rs[:], ALU.mult, ALU.mult),
                      r=['xT', rsk, 'g32'], w=['h2T'])
            for f in range(NFF):
                w_, wk = wgu.next()
                em.dma('sp', w_[:, 0].rearrange("p k c -> p (k c)"), g.WG[l][f].rearrange("p k c -> p (k c)"),
                       r=[('dram', f"WG{l}")], w=[wk])
                em.dma('sp', w_[:, 1].rearrange("p k c -> p (k c)"), g.WU[l][f].rearrange("p k c -> p (k c)"),
                       r=[('dram', f"WU{l}")], w=[wk])
                psg, pgk = g.psum()
                for kc in range(KC):
                    em.op('pe', lambda E: E.matmul(psg[:], w_[:, 0, kc, :], h2T[:, kc, :], start=(kc == 0), stop=(kc == KC - 1)),
                          r=[wk, 'h2T'], w=[pgk])
                psu, puk = g.psum()
                for kc in range(KC):
                    em.op('pe', lambda E: E.matmul(psu[:], w_[:, 1, kc, :], h2T[:, kc, :], start=(kc == 0), stop=(kc == KC - 1)),
                          r=[wk, 'h2T'], w=[puk])
                sg, sgk = sgt.next()
                em.op('act', lambda E: E.activation(out=sg[:], in_=psg[:], func=AF.Silu), r=[pgk], w=[sgk])
                em.op('dve', lambda E: E.tensor_tensor(actT[:, f, :], psu[:], sg[:], ALU.mult), r=[puk, sgk], w=['actT'])
            for j in range(8):
                w_, wk = wd.next()
                em.dma('sp', w_[:].rearrange("p k c -> p (k c)"), g.WD[l][j].rearrange("p k c -> p (k c)"),
                       r=[('dram', f"WD{l}")], w=[wk])
                ps, pk = g.psum()
                for f in range(NFF):
                    em.op('pe', lambda E: E.matmul(ps[:], w_[:, f, :], actT[:, f, :], start=(f == 0), stop=(f == NFF - 1)),
                          r=[wk, 'actT'], w=[pk])
                em.op('dve', lambda E: E.tensor_copy(yT[:, j, :], ps[:]), r=[pk], w=['yT'])
                em.op('act', lambda E: E.activation(out=sqb[:, j, :], in_=yT[:, j, :], func=AF.Square), r=['yT'], w=['sqb'])
            rs, rsk = norm_rstd('sqb')
            for j in range(8):
                t_, tk = tmp.next()
                em.op('dve', lambda E: E.scalar_tensor_tensor(t_[:], yT[:, j, :], g32[:, 2, j:j + 1], rs[:], ALU.mult, ALU.mult),
                      r=['yT', rsk, 'g32'], w=[tk])
                em.op('pool', lambda E: E.tensor_tensor(xT[:, j, :], t_[:], xT[:, j, :], ALU.add), r=[tk, 'xT'], w=['xT'])
            em.dma('pool', AP(tensor=g.XT.tensor, offset=g.XT.offset + c0, ap=[[Ttot, 128], [128 * Ttot, KC], [1, 512]]), xT[:],
                   r=['xT'], w=[('dram', 'XT')])
    em.barrier()


def stage_out(g):
    nc, em = g.nc, g.em
    with ExitStack() as st:
        def sb(name, shape, dt):
            return st.enter_context(nc.sbuf_tensor(g.un(name), list(shape), dt))
        xin_ = Rot([sb(f"o_x{i}", [128, KC, 128], F32) for i in range(2)], "o_x")
        xo = Rot([sb(f"o_y{i}", [128, D], F32) for i in range(2)], "o_y")
        for t in range(g.Ttot // 128):
            x_, xk = xin_.next()
            em.dma('sp', x_[:], AP(tensor=g.XT.tensor, offset=g.XT.offset + t * 128, ap=[[g.Ttot, 128], [128 * g.Ttot, KC], [1, 128]]),
                   r=[('dram', 'XT')], w=[xk])
            o_, ok = xo.next()
            for hlf in range(2):
                ps, pk = g.psum()
                for c in range(4):
                    kc = hlf * 4 + c
                    em.op('pe', lambda E: E.transpose(ps[:, c * 128:(c + 1) * 128], x_[:, kc, :], g.ident[:]), r=[xk, 'ident'], w=[pk])
                if hlf == 0:
                    em.op('dve', lambda E: E.tensor_copy(o_[:, 0:512], ps[:]), r=[pk], w=[ok])
                else:
                    em.op('dve', lambda E: E.tensor_copy(o_[:, 512:1024], ps[:]), r=[pk], w=[ok])
            em.dma('pool', g.y[t * 128:(t + 1) * 128, :], o_[:], r=[ok], w=[('dram', 'y')])
    em.barrier()


def make_in_map(x, w, consts):
    m = {"x": np.ascontiguousarray(x, np.float32)}
    for k, v in w.items():
        v = np.asarray(v, np.float32)
        if k in ("gdn_a_log", "gdn_dt_bias"):
            v = v.reshape(DEPTH, 8)
        if k == "w_branch":
            v = v.reshape(DEPTH, 1536, D)
        m[k] = np.ascontiguousarray(v)
    for k, v in consts.items():
        m["c_" + k] = v
    return m


_CACHE = {}


def kernel(x_prompt, x_sample, **w):
    x_prompt = np.asarray(x_prompt, np.float32)
    x_sample = np.asarray(x_sample, np.float32)
    n_cores = 8
    Bp, Lp, _ = x_prompt.shape
    Bs, Ls_, _ = x_sample.shape
    per = Bp // n_cores
    seq_lens = [Lp] * per + [Ls_]
    key = tuple(seq_lens)
    if key not in _CACHE:
        _CACHE[key] = build(seq_lens)
    nc, consts = _CACHE[key]
    in_maps = []
    for c in range(n_cores):
        xs = [x_prompt[c * per + j] for j in range(per)] + [x_sample[c % Bs]]
        in_maps.append(make_in_map(np.concatenate(xs, 0), w, consts))
    res = run_bass_kernel_spmd(nc, in_maps, core_ids=list(range(n_cores)))
    y_prompt = np.empty_like(x_prompt)
    y_sample = np.empty_like(x_sample)
    for c in range(n_cores):
        y = np.asarray(res.results[c]["y"], np.float32)
        for j in range(per):
            y_prompt[c * per + j] = y[j * Lp:(j + 1) * Lp]
        if c < Bs:
            y_sample[c] = y[per * Lp:per * Lp + Ls_]
    return (y_prompt, y_sample)
```

```python
import math
from contextlib import ExitStack
import numpy as np
import ml_dtypes
import concourse.bass as bass
import concourse.mybir as mybir
from concourse.bass_utils import run_bass_kernel_spmd

F32 = mybir.dt.float32
BF16 = mybir.dt.bfloat16
AF = mybir.ActivationFunctionType
ALU = mybir.AluOpType
AX = mybir.AxisListType
AP = bass.AP

D = 1024
KC = 8
DEPTH = 2
D_IN = 7120
D_FF = 2816
NFF = 22
EPS = 1e-6
C_HY, C_QL, C_CKV, C_KPE, C_GQKV, C_GZ, C_GB, C_GA, C_GATE = 0, 1536, 1792, 1920, 1984, 3520, 4032, 4040, 4048
NDS = 8
HY_RMAX = 128
GDN_CUT = 0
C_CUT = 0


class Em:
    def __init__(s, nc, es):
        s.nc = nc
        s.E = {'pe': nc.tensor, 'act': nc.scalar, 'dve': nc.vector, 'pool': nc.gpsimd, 'sp': nc.sync}
        s.sem = {}
        s.cnt = {}
        for e in s.E:
            s.sem[e] = es.enter_context(nc.semaphore("c_" + e))
            s.cnt[e] = 0
        s.seen = {e: {} for e in s.E}
        s.W = {}
        s.R = {}
        s.dq = {}
        for q in ('sp', 'pool'):
            s.dq[q] = 0
            for i in range(NDS):
                s.sem[(q, i)] = es.enter_context(nc.semaphore(f"d_{q}{i}"))
                s.cnt[(q, i)] = 0
        s.psn = 0
        s.ninst = 0

    def _wait(s, e, sk, v):
        if v <= 0:
            return
        if s.seen[e].get(sk, 0) < v:
            s.E[e].wait_ge(s.sem[sk], v)
            s.seen[e][sk] = v
            s.ninst += 1

    def _deps(s, e, r, w, pe_acc=False):
        for k in r:
            for sk, v in s.W.get(k, {}).items():
                s._wait(e, sk, v)
        for k in w:
            for sk, v in s.W.get(k, {}).items():
                if pe_acc and sk == 'pe':
                    continue
                s._wait(e, sk, v)
            for sk, v in s.R.get(k, {}).items():
                s._wait(e, sk, v)

    def _mark(s, tok, r, w):
        sk, v = tok
        for k in w:
            s.W.setdefault(k, {})[sk] = v
            s.R[k] = {}
        for k in r:
            s.R.setdefault(k, {})[sk] = v

    def op(s, e, fn, r=(), w=()):
        s._deps(e, r, w, pe_acc=(e == 'pe'))
        ins = fn(s.E[e])
        ins.then_inc(s.sem[e], 1)
        s.cnt[e] += 1
        s.ninst += 1
        s._mark((e, s.cnt[e]), r, w)

    def dma(s, q, out, in_, r=(), w=(), slow=False):
        s._deps(q, r, w)
        i = s.dq[q] % NDS
        s.dq[q] += 1
        sk = (q, i)
        s._wait(q, sk, s.cnt[sk])
        if slow:
            s.E[q].dma_start(out=out, in_=in_, allow_slow_non_contiguous=True).then_inc(s.sem[sk], 16)
        else:
            s.E[q].dma_start(out=out, in_=in_).then_inc(s.sem[sk], 16)
        s.cnt[sk] += 16
        s.ninst += 1
        s._mark((sk, s.cnt[sk]), r, w)

    def barrier(s):
        for e in s.E:
            for sk, v in s.cnt.items():
                if sk != e:
                    s._wait(e, sk, v)
        s.W.clear()
        s.R.clear()


def V(ap, dims, off=0):
    a = ap.ap
    return AP(tensor=ap.tensor, offset=ap.offset + off, ap=[list(a[0])] + [[st, n] for st, n in dims])


def rsqrt(em, out, src, addc, rk, wk):
    em.op('act', lambda E: E.activation(out=out, in_=src, func=AF.Ln, bias=float(addc), scale=1.0), r=[rk], w=[wk])
    em.op('act', lambda E: E.activation(out=out, in_=out, func=AF.Exp, scale=-0.5), r=[wk], w=[wk])


def psplit(ap, p0, p1):
    a = [list(d) for d in ap.ap]
    off = ap.offset + p0 * a[0][0]
    a[0][1] = p1 - p0
    return AP(tensor=ap.tensor, offset=off, ap=a)


def bf(x):
    return np.asarray(x, np.float32).astype(ml_dtypes.bfloat16)


def make_consts(Ls):
    c = {}
    c["ident"] = np.eye(128, dtype=np.float32)
    c["identb"] = bf(np.eye(128))
    c["onesb"] = bf(np.ones((128, 128)))
    c["ones"] = np.ones((128, 128), np.float32)
    i = np.arange(128)
    c["U_le"] = (i[:, None] <= i[None, :]).astype(np.float32)
    c["U_ge"] = (i[:, None] >= i[None, :]).astype(np.float32)
    c["SL"] = (i[None, :] < i[:, None]).astype(np.float32)
    c["SU"] = (i[None, :] > i[:, None]).astype(np.float32)
    Rm = np.zeros((64, 64), np.float32)
    for d in range(32):
        Rm[d + 32, d] = -1.0
        Rm[d, d + 32] = 1.0
    c["rope_R"] = bf(Rm)
    for L in sorted(set(Ls)):
        half = 32
        inv = 10000.0 ** (-np.arange(half, dtype=np.float32) / half)
        ang = np.arange(L, dtype=np.float32)[:, None] * inv[None, :]
        cs = np.cos(ang).astype(np.float32).T
        sn = np.sin(ang).astype(np.float32).T
        c[f"ropecos{L}"] = np.concatenate([cs, cs], 0).astype(np.float32)
        c[f"ropesin{L}"] = np.concatenate([sn, sn], 0).astype(np.float32)
        N = 2 * L
        N1 = N // 128
        P1 = L // 128
        R1 = min(N1, HY_RMAX)
        nch1 = max(1, N1 // HY_RMAX)
        s1 = np.arange(N1, dtype=np.float64)
        f1 = np.arange(N1, dtype=np.float64)
        a = 2 * np.pi * np.outer(s1, f1) / N1
        F1 = np.concatenate([np.cos(a), -np.sin(a)], 1)
        c[f"hyF1_{L}"] = bf(F1.reshape(nch1, R1, 2 * N1).transpose(1, 0, 2))
        s2 = np.arange(128, dtype=np.float64)
        a = 2 * np.pi * np.outer(s2, f1) / N
        c[f"hyTwr_{L}"] = np.cos(a).astype(np.float32)
        c[f"hyTwi_{L}"] = (-np.sin(a)).astype(np.float32)
        aT = a.T
        c[f"hyTwTr_{L}"] = np.cos(aT).reshape(nch1, R1, 128).transpose(1, 0, 2).astype(np.float32)
        c[f"hyTwTi_{L}"] = np.sin(aT).reshape(nch1, R1, 128).transpose(1, 0, 2).astype(np.float32)
        t1 = np.arange(P1, dtype=np.float64)
        a = 2 * np.pi * np.outer(f1, t1) / N1
        c[f"hyG1r_{L}"] = bf((np.cos(a) / N).reshape(nch1, R1, P1).transpose(1, 0, 2))
        c[f"hyG1i_{L}"] = bf((-np.sin(a) / N).reshape(nch1, R1, P1).transpose(1, 0, 2))
        sidx = np.arange(N)
        lag = np.where(sidx < L, sidx, N - sidx)
        lag[L] = 0
        tgrid = np.linspace(0.0, 1.0, L, dtype=np.float32)
        bands = 16
        wpos = (2.0 * math.pi / L) * np.arange(L, dtype=np.float32)
        fr = np.linspace(1e-4, bands - 1, bands, dtype=np.float32)
        feats = np.concatenate([tgrid[:, None], np.cos(fr[None, :] * wpos[:, None]), -np.sin(fr[None, :] * wpos[:, None])], -1)
        c[f"hyfeat_{L}"] = np.ascontiguousarray(feats[lag].T.astype(np.float32))
        c[f"hytneg_{L}"] = np.ascontiguousarray((-tgrid[lag]).reshape(N1, 128).T.astype(np.float32))
    a = 2 * np.pi * np.outer(np.arange(128.0), np.arange(128.0)) / 128
    c["hyF2r"] = bf(np.cos(a))
    c["hyF2i"] = bf(-np.sin(a))
    c["hyF2in"] = bf(np.sin(a))
    c["hyG2a"] = bf(np.concatenate([np.cos(a), np.sin(a)], 1))
    c["hyG2b"] = bf(np.concatenate([-np.sin(a), np.cos(a)], 1))
    deltas = np.abs(np.linspace(math.log(1e-2) / 1.5, math.log(1e-2) / 0.3, 512, dtype=np.float32))
    c["hydelta"] = np.ascontiguousarray(np.broadcast_to(deltas[None, :], (128, 512))).astype(np.float32)
    return c


class Ctx:
    pass


def dcol(ap_dram, off, n, nparts=128, pstride=1, cstride=128):
    return AP(tensor=ap_dram.tensor, offset=ap_dram.offset + off, ap=[[pstride, nparts], [cstride, n], [1, 1]])


def drow_bc(ap_dram, off, n, nparts=128):
    return AP(tensor=ap_dram.tensor, offset=ap_dram.offset + off, ap=[[0, nparts], [1, n]])


def build(seq_lens, debug=False, stages=None, mixers=None):
    nc = bass.Bass("TRN2", target_bir_lowering=False)
    Ttot = sum(seq_lens)
    Lmax = max(seq_lens)
    seq_off = [sum(seq_lens[:i]) for i in range(len(seq_lens))]
    Ls = sorted(set(seq_lens))
    g = Ctx()
    g.nc = nc
    g.uidc = [0]

    def un(name):
        g.uidc[0] += 1
        return f"{name}_{g.uidc[0]}"
    g.un = un
    ext = {}

    def xin(name, shape, dt=F32):
        ext[name] = nc.dram_tensor(name, list(shape), dt, kind="ExternalInput").ap()
        return ext[name]

    xin("x", [Ttot, D])
    for nm, shp in (("norm_mix_pre", [DEPTH, D]), ("norm_mix_post", [DEPTH, D]), ("norm_ffn_pre", [DEPTH, D]),
                    ("norm_ffn_post", [DEPTH, D]), ("w_in", [DEPTH, D, D_IN]), ("hy_conv_w", [DEPTH, 3, 1536]),
                    ("hy_conv_b", [DEPTH, 1536]), ("hy_ffn_w1", [DEPTH, 33, 64]), ("hy_ffn_b1", [DEPTH, 64]),
                    ("hy_sin_freq", [DEPTH, 64]), ("hy_ffn_w2", [DEPTH, 64, 64]), ("hy_ffn_b2", [DEPTH, 64]),
                    ("hy_ffn_w3", [DEPTH, 64, 2048]), ("hy_skip", [DEPTH, 2, 512]), ("mla_q_norm", [DEPTH, 256]),
                    ("mla_wq_b", [DEPTH, 256, 768]), ("mla_kv_norm", [DEPTH, 128]), ("mla_wkv_b", [DEPTH, 128, 1024]),
                    ("gdn_conv_w", [DEPTH, 3, 1536]), ("gdn_a_log", [DEPTH, 8]), ("gdn_dt_bias", [DEPTH, 8]),
                    ("gdn_out_norm", [DEPTH, 128]), ("w_branch", [DEPTH, 1536, D]), ("w_out", [DEPTH, D, D]),
                    ("w_gate", [DEPTH, D, D_FF]), ("w_up", [DEPTH, D, D_FF]), ("w_down", [DEPTH, D_FF, D])):
        xin(nm, shp)
    consts = make_consts(seq_lens)
    for k, v in consts.items():
        xin("c_" + k, v.shape, BF16 if v.dtype == ml_dtypes.bfloat16 else F32)
    y = nc.dram_tensor("y", [Ttot, D], F32, kind="ExternalOutput").ap()

    def scratch(name, shape, dt):
        kind = "ExternalOutput" if debug else "Internal"
        return nc.dram_tensor(name, list(shape), dt, kind=kind).ap()

    XT = scratch("XT", [D, Ttot], F32)
    WTOK = [scratch(f"WTOK{l}", [6, 128, KC, 3, 512], BF16) for l in range(DEPTH)]
    WFM = [scratch(f"WFM{l}", [32, 128, KC, 128], BF16) for l in range(DEPTH)]
    WBR = [scratch(f"WBR{l}", [8, 128, 12, 128], BF16) for l in range(DEPTH)]
    WOUT = [scratch(f"WOUT{l}", [8, 128, KC, 128], BF16) for l in range(DEPTH)]
    WG = [scratch(f"WG{l}", [NFF, 128, KC, 128], BF16) for l in range(DEPTH)]
    WU = [scratch(f"WU{l}", [NFF, 128, KC, 128], BF16) for l in range(DEPTH)]
    WD = [scratch(f"WD{l}", [8, 128, NFF, 128], BF16) for l in range(DEPTH)]
    PH = scratch("PH", [Lmax, 1536], BF16)
    PG = scratch("PG", [Lmax, 1536], BF16)
    BD = scratch("BD", [Lmax, 16], F32)
    QT = scratch("QT", [4, 192, Lmax], BF16)
    KTn = scratch("KTn", [4, 128, Lmax], BF16)
    KTp = scratch("KTp", [64, Lmax], BF16)
    VS = scratch("VS", [Lmax, 512], BF16)
    ZT = scratch("ZT", [512, Lmax], BF16)
    GT = scratch("GT", [3072, Lmax], BF16)
    OT = scratch("OT", [1536, Lmax], BF16)
    OF = scratch("OF", [512, Lmax], F32)
    Z2 = scratch("Z2", [Lmax, 512], BF16)
    KTd = {L_: scratch(f"KTd{L_}", [2 * L_, 1024], BF16) for L_ in Ls}
    KFq = {L_: scratch(f"KF{L_}", [2, 512 // min(64, 4096 // (L_ // 64)), max(1, min(64, 4096 // (L_ // 64)) * (L_ // 64) // 512), 128, 2, 512], BF16)
           for L_ in Ls}
    filt_done = {}

    es = ExitStack()
    with es:
        em = Em(nc, es)

        def sb(name, shape, dt, st=es):
            return st.enter_context(nc.sbuf_tensor(g.un(name), list(shape), dt))

        PS = [es.enter_context(nc.psum_tensor(f"ps{i}", [128, 512], F32)) for i in range(8)]

        def psum(banks=range(8)):
            banks = list(banks)
            b = banks[em.psn % len(banks)]
            em.psn += 1
            return PS[b], ('ps', b)

        ident = sb("ident", [128, 128], F32)
        identb = sb("identb", [128, 128], BF16)
        onesb = sb("onesb", [128, 128], BF16)
        gains = sb("gains", [128, 4, DEPTH, KC], F32)
        em.dma('sp', ident[:], ext["c_ident"][:, :], w=['ident'])
        em.dma('sp', identb[:], ext["c_identb"][:, :], w=['identb'])
        em.dma('sp', onesb[:], ext["c_onesb"][:, :], w=['onesb'])
        for i, nm in enumerate(("norm_mix_pre", "norm_mix_post", "norm_ffn_pre", "norm_ffn_post")):
            for l in range(DEPTH):
                em.dma('sp', gains[:, i, l, :], dcol(ext[nm], l * D, KC), w=['gains'], slow=True)
        em.barrier()

        g.__dict__.update(locals())
        if stages is None or 'prep' in stages:
            stage_prep(g)
        for l in range(DEPTH):
            for si, L in enumerate(seq_lens):
                if stages is None or ('A', l) in stages:
                    stage_A(g, l, si)
                if stages is None or ('B', l) in stages:
                    stage_B(g, l, si)
                if stages is None or ('C', l) in stages:
                    stage_C(g, l, si)
        if stages is None or 'out' in stages:
            stage_out(g)
        em.barrier()
        print("emitted instructions:", em.ninst, {k: v for k, v in em.cnt.items() if isinstance(k, str)})
    return nc, consts


class Rot:
    def __init__(s, tiles, name):
        s.t = tiles
        s.name = name
        s.i = 0

    def next(s):
        j = s.i % len(s.t)
        s.i += 1
        return s.t[j], (s.name, j)


def stage_prep(g):
    nc, em, ext = g.nc, g.em, g.ext
    with ExitStack() as st:
        def sb(name, shape, dt):
            return st.enter_context(nc.sbuf_tensor(g.un(name), list(shape), dt))
        wld = Rot([sb(f"p_wld{i}", [128, 512], F32) for i in range(4)], "p_wld")
        cwt = Rot([sb(f"p_cw{i}", [128, 3, 512], F32) for i in range(2)], "p_cw")
        stg = Rot([sb(f"p_stg{i}", [128, 24 * 512], BF16) for i in range(2)], "p_stg")
        engs = ['dve', 'pool', 'act']
        ei = [0]

        def cast(out, in_, r, w, mul=None):
            e = engs[ei[0] % 3]
            ei[0] += 1
            if mul is not None:
                if e == 'act':
                    e = 'dve'
                em.op(e, lambda E: E.tensor_tensor(out, in_, mul, ALU.mult), r=r, w=w)
            elif e == 'act':
                em.op(e, lambda E: E.copy(out, in_), r=r, w=w)
            else:
                em.op(e, lambda E: E.tensor_copy(out, in_), r=r, w=w)

        def prep_fm(src, row0, nkc, col0, W, dst, t0):
            sg, sk = stg.next()
            for kc in range(nkc):
                wt, wk = wld.next()
                em.dma('sp', wt[:, 0:W], src[row0 + kc * 128: row0 + (kc + 1) * 128, col0:col0 + W], w=[wk])
                cast(sg[:, kc * W:(kc + 1) * W], wt[:, 0:W], [wk], [sk])
            nt = W // 128
            d = dst
            dap = AP(tensor=d.tensor, offset=d.offset + t0 * 128 * nkc * 128,
                     ap=[[nkc * 128, 128], [128, nkc], [128 * nkc * 128, nt], [1, 128]])
            sap = V(sg[:, 0:1], [(W, nkc), (128, nt), (1, 128)])
            em.dma('pool', dap, sap, r=[sk], w=[('dram', d.tensor.name)])

        for l in range(DEPTH):
            w_in = ext["w_in"][l]
            for grp in range(6):
                col0 = (C_HY + 512 * grp) if grp < 3 else (C_GQKV + 512 * (grp - 3))
                cwn = "hy_conv_w" if grp < 3 else "gdn_conv_w"
                cw, cwk = cwt.next()
                cw_src = ext[cwn]
                em.dma('sp', cw[:], AP(tensor=cw_src.tensor, offset=cw_src.offset + l * 3 * 1536 + 512 * (grp % 3),
                                       ap=[[0, 128], [1536, 3], [1, 512]]), w=[cwk])
                sg, sk = stg.next()
                for kc in range(KC):
                    wt, wk = wld.next()
                    em.dma('sp', wt[:], w_in[kc * 128:(kc + 1) * 128, col0:col0 + 512], w=[wk])
                    for s_ in range(3):
                        o = (kc * 3 + s_) * 512
                        cast(sg[:, o:o + 512], wt[:], [wk, cwk], [sk], mul=cw[:, s_, :])
                em.dma('pool', g.WTOK[l][grp].rearrange("p k s c -> p (k s c)"), sg[:, 0:KC * 3 * 512], r=[sk],
                       w=[('dram', f"WTOK{l}")])
            prep_fm(w_in, 0, KC, C_QL, 512, g.WFM[l], 0)
            prep_fm(w_in, 0, KC, C_GZ, 512, g.WFM[l], 4)
            for i in range(6):
                prep_fm(w_in, 0, KC, C_GATE + 512 * i, 512, g.WFM[l], 8 + 4 * i)
            for i in range(2):
                prep_fm(ext["w_branch"][l], 0, 12, 512 * i, 512, g.WBR[l], 4 * i)
                prep_fm(ext["w_out"][l], 0, KC, 512 * i, 512, g.WOUT[l], 4 * i)
                prep_fm(ext["w_down"][l], 0, NFF, 512 * i, 512, g.WD[l], 4 * i)
            for i in range(11):
                prep_fm(ext["w_gate"][l], 0, KC, 256 * i, 256, g.WG[l], 2 * i)
                prep_fm(ext["w_up"][l], 0, KC, 256 * i, 256, g.WU[l], 2 * i)
        xld = Rot([sb(f"p_x{i}", [128, D], F32) for i in range(2)], "p_x")
        xts = Rot([sb(f"p_xt{i}", [128, KC, 128], F32) for i in range(2)], "p_xt")
        for t in range(g.Ttot // 128):
            xt_, xk = xld.next()
            em.dma('sp', xt_[:], ext["x"][t * 128:(t + 1) * 128, :], w=[xk])
            xo, xok = xts.next()
            for hlf in range(2):
                ps, pk = g.psum()
                for c in range(4):
                    kc = hlf * 4 + c
                    em.op('pe', lambda E: E.transpose(ps[:, c * 128:(c + 1) * 128], xt_[:, kc * 128:(kc + 1) * 128],
                                                      g.ident[:]), r=[xk, 'ident'], w=[pk])
                e = 'dve' if hlf == 0 else 'act'
                dst = xo[:, hlf * 4:(hlf + 1) * 4, :]
                src = ps[:].rearrange("p (c t) -> p c t", c=4)
                if e == 'dve':
                    em.op(e, lambda E: E.tensor_copy(dst, src), r=[pk], w=[xok])
                else:
                    em.op(e, lambda E: E.copy(dst, src), r=[pk], w=[xok])
            dap = AP(tensor=g.XT.tensor, offset=g.XT.offset + t * 128, ap=[[g.Ttot, 128], [128 * g.Ttot, KC], [1, 128]])
            em.dma('pool', dap, xo[:], r=[xok], w=[('dram', 'XT')])
    em.barrier()


def stage_A(g, l, si):
    nc, em, ext = g.nc, g.em, g.ext
    L = g.seq_lens[si]
    t0 = g.seq_off[si]
    Ttot = g.Ttot
    nb = L // 512
    QSCALE = 192.0 ** -0.5
    with ExitStack() as st:
        def sb(name, shape, dt):
            return st.enter_context(nc.sbuf_tensor(g.un(name), list(shape), dt))
        g32 = sb("a_g32", [128, KC], F32)
        hb = sb("a_hb", [128, 1536], F32)
        tmpf = sb("a_tmpf", [128, 2 * 768], F32)
        wbd = sb("a_wbd", [128, KC, 16], BF16)
        wqb = sb("a_wqb", [128, 2, 768], BF16)
        wkn = sb("a_wkn", [128, 4, 128], BF16)
        wv = sb("a_wv", [128, 4, 128], BF16)
        gq = sb("a_gq", [128, 2], F32)
        gkv = sb("a_gkv", [128, 1], F32)
        ropeR = sb("a_ropeR", [64, 64], BF16)
        em.op('dve', lambda E: E.tensor_scalar(g32[:], g.gains[:, 0, l, :], 32.0, None, ALU.mult), r=['gains'], w=['g32'])
        em.dma('sp', hb[:], drow_bc(ext["hy_conv_b"], l * 1536, 1536), w=['hb'])
        em.dma('sp', ropeR[:], ext["c_rope_R"][:, :], w=['ropeR'])
        w_in = ext["w_in"][l]
        em.dma('sp', V(tmpf[:, 0:1], [(16, KC), (1, 16)]),
               AP(tensor=w_in.tensor, offset=w_in.offset + C_GB, ap=[[D_IN, 128], [128 * D_IN, KC], [1, 16]]), w=['tmpf'])
        em.op('dve', lambda E: E.tensor_copy(wbd[:].rearrange("p k c -> p (k c)"), tmpf[:, 0:KC * 16]), r=['tmpf'], w=['wbd'])
        wq = ext["mla_wq_b"][l]
        em.dma('sp', V(tmpf[:, 0:1], [(768, 2), (1, 768)]),
               AP(tensor=wq.tensor, offset=wq.offset, ap=[[768, 128], [128 * 768, 2], [1, 768]]), w=['tmpf'], r=['tmpf'])
        em.op('dve', lambda E: E.tensor_copy(wqb[:].rearrange("p k c -> p (k c)"), tmpf[:, 0:1536]), r=['tmpf'], w=['wqb'])
        wkv = ext["mla_wkv_b"][l]
        em.dma('sp', tmpf[:, 0:1024], wkv[:, :], w=['tmpf'], r=['tmpf'])
        em.op('dve', lambda E: E.tensor_copy(wkn[:], V(tmpf[:, 0:1], [(256, 4), (1, 128)])), r=['tmpf'], w=['wkn'])
        em.op('dve', lambda E: E.tensor_copy(wv[:], V(tmpf[:, 0:1], [(256, 4), (1, 128)], off=128)), r=['tmpf'], w=['wv'])
        em.dma('sp', gq[:], dcol(ext["mla_q_norm"], l * 256, 2), w=['gq'], slow=True)
        em.dma('sp', gkv[:], dcol(ext["mla_kv_norm"], l * 128, 1), w=['gkv'], slow=True)
        em.op('dve', lambda E: E.tensor_scalar(gq[:], gq[:], 16.0, None, ALU.mult), r=['gq'], w=['gq'])
        em.op('dve', lambda E: E.tensor_scalar(gkv[:], gkv[:], math.sqrt(128.0), None, ALU.mult), r=['gkv'], w=['gkv'])
        xT = Rot([sb(f"a_xT{i}", [128, KC, 512], F32) for i in range(2)], "a_xT")
        xh = Rot([sb(f"a_xh{i}", [128, KC, 2], F32) for i in range(2)], "a_xh")
        sq = Rot([sb(f"a_sq{i}", [128, KC, 512], BF16) for i in range(1)], "a_sq")
        sqh = sb("a_sqh", [128, KC, 2], BF16)
        rstd = Rot([sb(f"a_rstd{i}", [128, 512], F32) for i in range(2)], "a_rstd")
        rsth = sb("a_rsth", [128, 2], F32)
        xhg = sb("a_xhg", [128, KC, 2], F32)
        hT = Rot([sb(f"a_hT{i}", [128, KC, 514], BF16) for i in range(2)], "a_hT")
        wtk = Rot([sb(f"a_wtk{i}", [128, KC, 3, 512], BF16) for i in range(2)], "a_wtk")
        wfm = Rot([sb(f"a_wfm{i}", [128, 4, KC, 128], BF16) for i in range(2)], "a_wfm")
        stg = Rot([sb(f"a_stg{i}", [128, 4, 512], BF16) for i in range(3)], "a_stg")
        bds = Rot([sb(f"a_bds{i}", [128, 4, 16], F32) for i in range(2)], "a_bds")
        ql = sb("a_ql", [128, 2, 512], F32)
        ck = sb("a_ck", [128, 512], F32)
        kpe = sb("a_kpe", [64, 512], BF16)
        qln = sb("a_qln", [128, 2, 512], BF16)
        ckn = sb("a_ckn", [128, 512], BF16)
        qpe = Rot([sb(f"a_qpe{i}", [64, 512], BF16) for i in range(2)], "a_qpe")
        rt1 = Rot([sb(f"a_rt1{i}", [64, 512], F32) for i in range(2)], "a_rt1")
        rt2 = Rot([sb(f"a_rt2{i}", [64, 512], F32) for i in range(2)], "a_rt2")
        cosT = Rot([sb(f"a_cos{i}", [64, 512], F32) for i in range(2)], "a_cos")
        sinT = Rot([sb(f"a_sin{i}", [64, 512], F32) for i in range(2)], "a_sin")
        rcos, rsin = ext[f"c_ropecos{L}"], ext[f"c_ropesin{L}"]
        eng2 = ['dve', 'pool']

        def rope_store(src_bf, srck, cs, csk, sn, snk, dst_ap, dkey):
            ps, pk = g.psum()
            em.op('pe', lambda E: E.matmul(ps[0:64, :], ropeR[:], src_bf, start=True, stop=True), r=[srck, 'ropeR'], w=[pk])
            a, ak = rt1.next()
            b_, bk = rt2.next()
            em.op('pool', lambda E: E.tensor_tensor(a[:], src_bf, cs[:], ALU.mult), r=[srck, csk], w=[ak])
            em.op('dve', lambda E: E.tensor_tensor(b_[:], ps[0:64, :], sn[:], ALU.mult), r=[pk, snk], w=[bk])
            o, ok = qpe.next()
            em.op('dve', lambda E: E.tensor_tensor(o[:], a[:], b_[:], ALU.add), r=[ak, bk], w=[ok])
            em.dma('pool', dst_ap, o[:], r=[ok], w=[dkey])

        for b in range(nb):
            c0 = t0 + b * 512
            s0 = b * 512
            x_, xk = xT.next()
            em.dma('sp', x_[:], AP(tensor=g.XT.tensor, offset=g.XT.offset + c0, ap=[[Ttot, 128], [128 * Ttot, KC], [1, 512]]),
                   r=[('dram', 'XT')], w=[xk])
            xh_, xhk = xh.next()
            for side, col in ((0, c0 - 1), (1, c0 + 512)):
                if (side == 0 and b == 0) or (side == 1 and b == nb - 1):
                    em.op('pool', lambda E: E.memset(xh_[:, :, side:side + 1], 0.0), w=[xhk])
                else:
                    em.dma('sp', xh_[:, :, side:side + 1],
                           AP(tensor=g.XT.tensor, offset=g.XT.offset + col, ap=[[Ttot, 128], [128 * Ttot, KC], [1, 1]]),
                           r=[('dram', 'XT')], w=[xhk], slow=True)
            cs, csk = cosT.next()
            sn, snk = sinT.next()
            em.dma('sp', cs[:], rcos[:, s0:s0 + 512], w=[csk])
            em.dma('sp', sn[:], rsin[:, s0:s0 + 512], w=[snk])
            sq_, sqk = sq.next()
            em.op('act', lambda E: E.activation(out=sq_[:], in_=x_[:], func=AF.Square), r=[xk], w=[sqk])
            ps, pk = g.psum()
            for kc in range(KC):
                em.op('pe', lambda E: E.matmul(ps[:], g.onesb[:], sq_[:, kc, :], start=(kc == 0), stop=(kc == KC - 1)),
                      r=[sqk, 'onesb'], w=[pk])
            rs, rsk = rstd.next()
            rsqrt(em, rs[:], ps[:], D * EPS, pk, rsk)
            h_, hk = hT.next()
            for kc in range(KC):
                em.op('dve', lambda E: E.scalar_tensor_tensor(h_[:, kc, 1:513], x_[:, kc, :], g32[:, kc:kc + 1], rs[:],
                                                                     ALU.mult, ALU.mult), r=[xk, rsk, 'g32'], w=[hk])
            em.op('act', lambda E: E.activation(out=sqh[:], in_=xh_[:], func=AF.Square), r=[xhk], w=['sqh'])
            ps2, pk2 = g.psum()
            for kc in range(KC):
                em.op('pe', lambda E: E.matmul(ps2[:, 0:2], g.onesb[:], sqh[:, kc, :], start=(kc == 0), stop=(kc == KC - 1)),
                      r=['sqh', 'onesb'], w=[pk2])
            rsqrt(em, rsth[:], ps2[:, 0:2], D * EPS, pk2, 'rsth')
            em.op('pool', lambda E: E.tensor_tensor(xhg[:], xh_[:], V(g32[:, 0:1], [(1, KC), (0, 2)]), ALU.mult),
                  r=[xhk, 'g32'], w=['xhg'])
            em.op('pool', lambda E: E.tensor_tensor(V(h_[:, 0, 0:1], [(514, KC), (513, 2)]), xhg[:],
                                                    V(rsth[:, 0:1], [(0, KC), (1, 2)]), ALU.mult),
                  r=['xhg', 'rsth'], w=[hk])
            for grp in range(6):
                w_, wk = wtk.next()
                em.dma('sp', w_[:].rearrange("p k s c -> p (k s c)"), g.WTOK[l][grp].rearrange("p k s c -> p (k s c)"),
                       r=[('dram', f"WTOK{l}")], w=[wk])
                sg, sgk = stg.next()
                for tt in range(4):
                    ps, pk = g.psum()
                    n = 0
                    for kc in range(KC):
                        for s_ in range(3):
                            lo = 1 + tt * 128 + (s_ - 1)
                            em.op('pe', lambda E: E.matmul(ps[:], h_[:, kc, lo:lo + 128], w_[:, kc, s_, :],
                                                           start=(n == 0), stop=(n == 23)), r=[hk, wk], w=[pk])
                            n += 1
                    if grp < 3:
                        em.op('dve', lambda E: E.tensor_tensor(sg[:, tt, :], ps[:], hb[:, grp * 512:(grp + 1) * 512], ALU.add),
                              r=[pk, 'hb'], w=[sgk])
                    else:
                        em.op('act', lambda E: E.activation(out=sg[:, tt, :], in_=ps[:], func=AF.Silu), r=[pk], w=[sgk])
                dst = g.PH if grp < 3 else g.PG
                em.dma('pool', AP(tensor=dst.tensor, offset=dst.offset + s0 * 1536 + (grp % 3) * 512,
                                  ap=[[1536, 128], [128 * 1536, 4], [1, 512]]), sg[:], r=[sgk],
                       w=[('dram', 'PH' if grp < 3 else 'PG')])
            ps, pk = g.psum()
            for tt in range(4):
                for kc in range(KC):
                    em.op('pe', lambda E: E.matmul(ps[:, tt * 16:(tt + 1) * 16], h_[:, kc, 1 + tt * 128:1 + (tt + 1) * 128],
                                                   wbd[:, kc, :], start=(kc == 0), stop=(kc == KC - 1)), r=[hk, 'wbd'], w=[pk])
            bd_, bdk = bds.next()
            em.op('dve', lambda E: E.tensor_copy(bd_[:].rearrange("p t c -> p (t c)"), ps[:, 0:64]), r=[pk], w=[bdk])
            em.dma('pool', AP(tensor=g.BD.tensor, offset=g.BD.offset + s0 * 16, ap=[[16, 128], [128 * 16, 4], [1, 16]]),
                   bd_[:], r=[bdk], w=[('dram', 'BD')])
            for tg in range(8):
                wf, wfk = wfm.next()
                em.dma('sp', wf[:].rearrange("p t k c -> p t (k c)"),
                       AP(tensor=g.WFM[l].tensor, offset=g.WFM[l].offset + tg * 4 * 128 * KC * 128,
                          ap=[[KC * 128, 128], [128 * KC * 128, 4], [1, KC * 128]]), r=[('dram', f"WFM{l}")], w=[wfk])
                if tg >= 1:
                    sg, sgk = stg.next()
                for ti in range(4):
                    M = 64 if (tg == 0 and ti == 3) else 128
                    ps, pk = g.psum()
                    for kc in range(KC):
                        em.op('pe', lambda E: E.matmul(ps[0:M, :], wf[:, ti, kc, 0:M], h_[:, kc, 1:513],
                                                       start=(kc == 0), stop=(kc == KC - 1)), r=[hk, wfk], w=[pk])
                    if tg == 0:
                        if ti < 2:
                            em.op('dve', lambda E: E.tensor_copy(ql[:, ti, :], ps[:]), r=[pk], w=['ql'])
                        elif ti == 2:
                            em.op('dve', lambda E: E.tensor_copy(ck[:], ps[:]), r=[pk], w=['ck'])
                        else:
                            em.op('act', lambda E: E.copy(kpe[:], ps[0:64, :]), r=[pk], w=['kpe'])
                    elif tg == 1:
                        em.op('act', lambda E: E.activation(out=sg[:, ti, :], in_=ps[:], func=AF.Silu), r=[pk], w=[sgk])
                    else:
                        em.op('act', lambda E: E.activation(out=sg[:, ti, :], in_=ps[:], func=AF.Sigmoid), r=[pk], w=[sgk])
                if tg == 1:
                    em.dma('pool', AP(tensor=g.ZT.tensor, offset=g.ZT.offset + s0, ap=[[g.Lmax, 128], [128 * g.Lmax, 4], [1, 512]]),
                           sg[:], r=[sgk], w=[('dram', 'ZT')])
                elif tg >= 2:
                    em.dma('pool', AP(tensor=g.GT.tensor, offset=g.GT.offset + (tg - 2) * 512 * g.Lmax + s0,
                                      ap=[[g.Lmax, 128], [128 * g.Lmax, 4], [1, 512]]), sg[:], r=[sgk], w=[('dram', 'GT')])
            sq_, sqk = sq.next()
            em.op('act', lambda E: E.activation(out=sq_[:, 0:2, :], in_=ql[:], func=AF.Square), r=['ql'], w=[sqk])
            em.op('act', lambda E: E.activation(out=sq_[:, 2, :], in_=ck[:], func=AF.Square), r=['ck'], w=[sqk])
            ps, pk = g.psum()
            for c in range(2):
                em.op('pe', lambda E: E.matmul(ps[:], g.onesb[:], sq_[:, c, :], start=(c == 0), stop=(c == 1)), r=[sqk, 'onesb'], w=[pk])
            rs, rsk = rstd.next()
            rsqrt(em, rs[:], ps[:], 256 * EPS, pk, rsk)
            for c in range(2):
                em.op('dve', lambda E: E.scalar_tensor_tensor(qln[:, c, :], ql[:, c, :], gq[:, c:c + 1], rs[:], ALU.mult, ALU.mult),
                      r=['ql', rsk, 'gq'], w=['qln'])
            ps, pk = g.psum()
            em.op('pe', lambda E: E.matmul(ps[:], g.onesb[:], sq_[:, 2, :], start=True, stop=True), r=[sqk, 'onesb'], w=[pk])
            rs, rsk = rstd.next()
            rsqrt(em, rs[:], ps[:], 128 * EPS, pk, rsk)
            em.op('dve', lambda E: E.scalar_tensor_tensor(ckn[:], ck[:], gkv[:, 0:1], rs[:], ALU.mult, ALU.mult),
                  r=['ck', rsk, 'gkv'], w=['ckn'])
            for h in range(4):
                ps, pk = g.psum()
                for c in range(2):
                    em.op('pe', lambda E: E.matmul(ps[:], wqb[:, c, h * 192:h * 192 + 128], qln[:, c, :], start=(c == 0), stop=(c == 1)),
                          r=['wqb', 'qln'], w=[pk])
                sg, sgk = stg.next()
                em.op('act', lambda E: E.activation(out=sg[:, 0, :], in_=ps[:], func=AF.Copy, scale=QSCALE), r=[pk], w=[sgk])
                em.dma('pool', g.QT[h, 0:128, s0:s0 + 512], sg[:, 0, :], r=[sgk], w=[('dram', 'QT')])
                ps, pk = g.psum()
                for c in range(2):
                    em.op('pe', lambda E: E.matmul(ps[0:64, :], wqb[:, c, h * 192 + 128:h * 192 + 192], qln[:, c, :],
                                                   start=(c == 0), stop=(c == 1)), r=['wqb', 'qln'], w=[pk])
                qp, qpk = qpe.next()
                em.op('act', lambda E: E.activation(out=qp[:], in_=ps[0:64, :], func=AF.Copy, scale=QSCALE), r=[pk], w=[qpk])
                rope_store(qp[:], qpk, cs, csk, sn, snk, g.QT[h, 128:192, s0:s0 + 512], ('dram', 'QT'))
                ps, pk = g.psum()
                em.op('pe', lambda E: E.matmul(ps[:], wkn[:, h, :], ckn[:], start=True, stop=True), r=['wkn', 'ckn'], w=[pk])
                em.op('dve', lambda E: E.tensor_copy(sg[:, 1, :], ps[:]), r=[pk], w=[sgk])
                em.dma('pool', g.KTn[h, :, s0:s0 + 512], sg[:, 1, :], r=[sgk], w=[('dram', 'KTn')])
            rope_store(kpe[:], 'kpe', cs, csk, sn, snk, g.KTp[:, s0:s0 + 512], ('dram', 'KTp'))
            sg, sgk = stg.next()
            for tt in range(4):
                ps, pk = g.psum()
                em.op('pe', lambda E: E.matmul(ps[:], ckn[:, tt * 128:(tt + 1) * 128], wv[:].rearrange("p h c -> p (h c)"),
                                               start=True, stop=True), r=['ckn', 'wv'], w=[pk])
                em.op('act' if tt % 2 else 'dve', lambda E: (E.copy if tt % 2 else E.tensor_copy)(sg[:, tt, :], ps[:]), r=[pk], w=[sgk])
            em.dma('pool', AP(tensor=g.VS.tensor, offset=g.VS.offset + s0 * 512, ap=[[512, 128], [128 * 512, 4], [1, 512]]),
                   sg[:], r=[sgk], w=[('dram', 'VS')])
    em.barrier()


def mixer_mla(g, l, si):
    nc, em = g.nc, g.em
    L = g.seq_lens[si]
    Lmax = g.Lmax
    nq, nk = L // 512, L // 128
    NCH = max(1, min(8, nk // 4))
    kpc = nk // NCH
    with ExitStack() as st:
        def sb(name, shape, dt):
            return st.enter_context(nc.sbuf_tensor(g.un(name), list(shape), dt))
        Kp = sb("m_Kp", [64, L], BF16)
        Kn = sb("m_Kn", [128, L], BF16)
        Vh = sb("m_Vh", [128, nk, 128], BF16)
        qn = Rot([sb(f"m_qn{i}", [128, 512], BF16) for i in range(2)], "m_qn")
        qp = Rot([sb(f"m_qp{i}", [64, 512], BF16) for i in range(2)], "m_qp")
        PT = Rot([sb(f"m_PT{i}", [128, 512], BF16) for i in range(6)], "m_PT")
        rs = Rot([sb(f"m_rs{i}", [128, 512], F32) for i in range(2)], "m_rs")
        ob = Rot([sb(f"m_ob{i}", [128, 512], BF16) for i in range(2)], "m_ob")
        acc = Rot([sb(f"m_acc{i}", [128, 512], F32) for i in range(2)], "m_acc")
        acc2 = Rot([sb(f"m_accb{i}", [128, 512], F32) for i in range(2)], "m_accb")
        ones32 = sb("m_ones", [128, 128], F32)
        em.dma('sp', ones32[:], g.ext["c_ones"][:, :], w=['ones32'])
        for ci in range(NCH):
            em.dma('sp', Kp[:, ci * kpc * 128:(ci + 1) * kpc * 128], g.KTp[:, ci * kpc * 128:(ci + 1) * kpc * 128],
                   r=[('dram', 'KTp')], w=[('Kp', ci)])
        nacc = 0
        for h in range(4):
            for ci in range(NCH):
                a, b_ = ci * kpc * 128, (ci + 1) * kpc * 128
                em.dma('sp', Kn[:, a:b_], g.KTn[h, :, a:b_], r=[('dram', 'KTn')], w=[('Kn', ci)])
                em.dma('sp', Vh[:, ci * kpc:(ci + 1) * kpc, :],
                       AP(tensor=g.VS.tensor, offset=g.VS.offset + a * 512 + h * 128, ap=[[512, 128], [128 * 512, kpc], [1, 128]]),
                       r=[('dram', 'VS')], w=[('Vh', ci)])
            for qb in range(nq):
                q0 = qb * 512
                qn_, qnk = qn.next()
                qp_, qpk = qp.next()
                em.dma('sp', qn_[:], g.QT[h, 0:128, q0:q0 + 512], r=[('dram', 'QT')], w=[qnk])
                em.dma('sp', qp_[:], g.QT[h, 128:192, q0:q0 + 512], r=[('dram', 'QT')], w=[qpk])
                po, pok = g.psum([4, 6][nacc % 2:nacc % 2 + 1])
                pz, pzk = g.psum([5, 7][nacc % 2:nacc % 2 + 1])
                ac, ack = acc.next()
                ac2, ac2k = acc2.next()
                nacc += 1

                def s_tile(kt):
                    ci = kt // kpc
                    ps, pk = g.psum(range(4))
                    em.op('pe', lambda E: E.matmul(ps[:], Kn[:, kt * 128:(kt + 1) * 128], qn_[:], start=True, stop=False),
                          r=[('Kn', ci), qnk], w=[pk])
                    em.op('pe', lambda E: E.matmul(ps[:], Kp[:, kt * 128:(kt + 1) * 128], qp_[:], start=False, stop=True),
                          r=[('Kp', ci), qpk], w=[pk])
                    return ps, pk

                cur = s_tile(0)
                for kt in range(nk):
                    ci = kt // kpc
                    nxt = s_tile(kt + 1) if kt + 1 < nk else None
                    ps, pk = cur
                    pt, ptk = PT.next()
                    em.op('act', lambda E: E.activation(out=pt[:], in_=ps[:], func=AF.Exp), r=[pk], w=[ptk])
                    em.op('pe', lambda E: E.matmul(po[:], Vh[:, kt, :], pt[:], start=(kt == 0), stop=(kt == nk - 1)),
                          r=[('Vh', ci), ptk], w=[pok])
                    if kt % 3 != 2:
                        if kt == 0:
                            em.op('dve', lambda E: E.tensor_copy(ac[:], pt[:]), r=[ptk], w=[ack])
                        else:
                            em.op('dve', lambda E: E.tensor_tensor(ac[:], ac[:], pt[:], ALU.add), r=[ptk, ack], w=[ack])
                    else:
                        if kt == 2:
                            em.op('pool', lambda E: E.tensor_copy(ac2[:], pt[:]), r=[ptk], w=[ac2k])
                        else:
                            em.op('pool', lambda E: E.tensor_tensor(ac2[:], ac2[:], pt[:], ALU.add), r=[ptk, ac2k], w=[ac2k])
                    cur = nxt
                em.op('pe', lambda E: E.matmul(pz[:], ones32[:], ac[:], start=True, stop=False), r=['ones32', ack], w=[pzk])
                em.op('pe', lambda E: E.matmul(pz[:], ones32[:], ac2[:], start=False, stop=True), r=['ones32', ac2k], w=[pzk])
                r_, rk = rs.next()
                em.op('dve', lambda E: E.reciprocal(r_[:], pz[:]), r=[pzk], w=[rk])
                o_, ok = ob.next()
                em.op('dve', lambda E: E.tensor_tensor(o_[:], po[:], r_[:], ALU.mult), r=[pok, rk], w=[ok])
                em.dma('pool', g.OT[512 + h * 128:512 + (h + 1) * 128, q0:q0 + 512], o_[:], r=[ok], w=[('dram', 'OT')])
    em.barrier()


def stage_B(g, l, si):
    if g.mixers is None or 'mla' in g.mixers:
        mixer_mla(g, l, si)
    if g.mixers is None or 'gdn' in g.mixers:
        mixer_gdn(g, l, si)
    if g.mixers is None or 'hy' in g.mixers:
        mixer_hyena(g, l, si)


def mixer_gdn(g, l, si):
    nc, em, ext = g.nc, g.em, g.ext
    L = g.seq_lens[si]
    Lmax = g.Lmax
    nt = L // 128
    with ExitStack() as st:
        def sb(name, shape, dt):
            return st.enter_context(nc.sbuf_tensor(g.un(name), list(shape), dt))

        def T(name, shape, dt, n=2):
            return Rot([sb(f"g_{name}{i}", shape, dt) for i in range(n)], "g_" + name)
        cU = {0: sb("g_Ule", [128, 128], F32), 1: sb("g_Uge", [128, 128], F32)}
        cS = {0: sb("g_SL", [128, 128], F32), 1: sb("g_SU", [128, 128], F32)}
        ones32 = sb("g_ones", [128, 128], F32)
        negA = sb("g_negA", [128, 8], F32)
        dtb = sb("g_dtb", [128, 8], F32)
        gno = sb("g_gno", [128, 1], F32)
        em.dma('sp', cU[0][:], ext["c_U_le"][:, :], w=['cU0'])
        em.dma('sp', cU[1][:], ext["c_U_ge"][:, :], w=['cU1'])
        em.dma('sp', cS[0][:], ext["c_SL"][:, :], w=['cS0'])
        em.dma('sp', cS[1][:], ext["c_SU"][:, :], w=['cS1'])
        em.dma('sp', ones32[:], ext["c_ones"][:, :], w=['ones32'])
        em.dma('sp', negA[:], drow_bc(ext["gdn_a_log"], l * 8, 8), w=['negA'])
        em.dma('sp', dtb[:], drow_bc(ext["gdn_dt_bias"], l * 8, 8), w=['dtb'])
        em.dma('sp', gno[:], dcol(ext["gdn_out_norm"], l * 128, 1), w=['gno'], slow=True)
        em.op('act', lambda E: E.activation(out=negA[:], in_=negA[:], func=AF.Exp), r=['negA'], w=['negA'])
        em.op('dve', lambda E: E.tensor_scalar(negA[:], negA[:], -1.0, None, ALU.mult), r=['negA'], w=['negA'])
        em.op('dve', lambda E: E.tensor_scalar(gno[:], gno[:], math.sqrt(128.0), None, ALU.mult), r=['gno'], w=['gno'])
        maskT = {0: (cU[0], 'cU0'), 1: (cU[1], 'cU1')}
        qkv = T("qkv", [128, 1536], BF16)
        bd = T("bd", [128, 16], F32)
        sqf = T("sqf", [128, 1024], F32, 2)
        ss8 = T("ss8", [128, 8], F32)
        qkn = T("qkn", [128, 8, 128], BF16)
        qkT = T("qkT", [128, 8, 128], BF16)
        sm = T("sm", [128, 32], F32)
        gU = T("gU", [128, 4, 128], F32)
        gcc = T("gcc", [128, 8], F32)
        dif = T("dif", [128, 4, 128], F32)
        difT = T("difT", [128, 4, 128], F32)
        DmS = T("DmS", [128, 4, 128], F32)
        Bk = T("B", [128, 4, 128], F32, 6)
        Xk = T("X", [128, 4, 128], F32, 6)
        Pk = T("P", [128, 4, 128], F32, 6)
        IBk = T("IB", [128, 4, 128], F32, 6)
        attnT = T("attnT", [128, 4, 128], BF16)
        vb = T("vb", [128, 4, 128], F32)
        rr = T("rr", [128, 4, 128], F32)
        t1b = T("t1b", [128, 4, 128], F32)
        Erow = T("Erow", [128, 4, 128], F32)
        qdT = T("qdT", [128, 4, 128], BF16)
        kdec = T("kdec", [128, 4, 128], BF16)
        vnew = T("vnew", [128, 4, 128], BF16)
        of32 = T("of32", [128, 4, 128], F32)
        osum = T("osum", [128, 4, 128], F32)
        osq = T("osq", [128, 4, 128], BF16)
        orst = T("orst", [128, 4, 128], F32)
        zT = T("zT", [128, 4, 128], BF16)
        oout = T("oout", [128, 4, 128], BF16)
        identb4 = V(g.ident[:, 0:1], [(0, 4), (1, 128)])

        def f2(t):
            return t[:].rearrange("p h c -> p (h c)")

        def mm4(ps, lhs, lk, rhs, rk, pk, start=True, stop=True):
            for h in range(4):
                em.op('pe', lambda E: E.matmul(ps[:, h * 128:(h + 1) * 128], lhs[:, h, :], rhs[:, h, :], start=start, stop=stop),
                      r=[lk, rk], w=[pk])

        S32d = {d: sb(f"g_S32d{d}", [128, 4, 128], F32) for d in range(2)}
        Sbd = {d: T(f"Sbd{d}", [128, 4, 128], BF16) for d in range(2)}
        Slod = {d: T(f"Slod{d}", [128, 4, 128], BF16) for d in range(2)}
        state = {}
        for d in range(2):
            em.op('pool', lambda E: E.memset(S32d[d][:], 0.0), w=[f'S32_{d}'])
            a_, ak_ = Sbd[d].next()
            em.op('pool', lambda E: E.memset(a_[:], 0.0), w=[ak_])
            b_, bk_ = Slod[d].next()
            em.op('pool', lambda E: E.memset(b_[:], 0.0), w=[bk_])
            state[d] = (a_, ak_, b_, bk_)

        def body(d, n, first):
            U, Uk = cU[d], f'cU{d}'
            MS, MSk = cS[d], f'cS{d}'
            MT, MTk = maskT[d]
            S32_, S32k = S32d[d], f'S32_{d}'
            sb_, sbk, sl_, slk = state[d]
            r0 = n * 128
            qkv_, qk_ = qkv.next()
            em.dma('sp', qkv_[:], g.PG[r0:r0 + 128, :], r=[('dram', 'PG')], w=[qk_])
            bd_, bdk = bd.next()
            em.dma('sp', bd_[:], g.BD[r0:r0 + 128, :], r=[('dram', 'BD')], w=[bdk])
            sq_, sqk = sqf.next()
            em.op('pool', lambda E: E.tensor_tensor(sq_[:], qkv_[:, 0:1024], qkv_[:, 0:1024], ALU.mult), r=[qk_], w=[sqk])
            s8, s8k = ss8.next()
            em.op('dve', lambda E: E.tensor_reduce(s8[:], sq_[:].rearrange("p (h c) -> p h c", h=8), AX.X, ALU.add), r=[sqk], w=[s8k])
            rsqrt(em, s8[:], s8[:], EPS, s8k, s8k)
            em.op('dve', lambda E: E.tensor_scalar(s8[:, 0:4], s8[:, 0:4], 128.0 ** -0.5, None, ALU.mult), r=[s8k], w=[s8k])
            qn_, qnk = qkn.next()
            em.op('dve', lambda E: E.tensor_tensor(qn_[:], qkv_[:, 0:1024].rearrange("p (h c) -> p h c", h=8),
                                                   V(s8[:, 0:1], [(1, 8), (0, 128)]), ALU.mult), r=[qk_, s8k], w=[qnk])
            qT_, qTk = qkT.next()
            for hh in range(2):
                ps, pk = g.psum()
                for h in range(4):
                    em.op('pe', lambda E: E.matmul(ps[:, h * 128:(h + 1) * 128], qn_[:, hh * 4 + h, :], g.identb[:], start=True, stop=True),
                          r=[qnk, 'identb'], w=[pk])
                dst = qT_[:, hh * 4:(hh + 1) * 4, :].rearrange("p h c -> p (h c)")
                if hh == 0:
                    em.op('act', lambda E: E.copy(dst, ps[:]), r=[pk], w=[qTk])
                else:
                    em.op('dve', lambda E: E.tensor_copy(dst, ps[:]), r=[pk], w=[qTk])
            qT4, kT4, kn4 = qT_[:, 0:4], qT_[:, 4:8], qn_[:, 4:8]
            yield
            psG, pGk = g.psum()
            mm4(psG, kT4, qTk, kT4, qTk, pGk)
            psQ, pQk = g.psum()
            mm4(psQ, kT4, qTk, qT4, qTk, pQk)
            if GDN_CUT == 1000 + 1:
                pass
            yield
            sm_, smk = sm.next()
            beta, nbeta, gtok, tmp4, ecol, bg, ed, dec = (sm_[:, i * 4:(i + 1) * 4] for i in range(8))
            em.op('act', lambda E: E.activation(out=beta, in_=bd_[:, d * 4:d * 4 + 4], func=AF.Exp, scale=-1.0), r=[bdk], w=[smk])
            em.op('dve', lambda E: E.tensor_scalar(beta, beta, 1.0, None, ALU.add), r=[smk], w=[smk])
            em.op('dve', lambda E: E.reciprocal(beta, beta), r=[smk], w=[smk])
            em.op('dve', lambda E: E.tensor_scalar(nbeta, beta, -1.0, None, ALU.mult), r=[smk], w=[smk])
            em.op('dve', lambda E: E.tensor_tensor(tmp4, bd_[:, 8 + d * 4:12 + d * 4], dtb[:, d * 4:d * 4 + 4], ALU.add), r=[bdk, 'dtb'], w=[smk])
            em.op('act', lambda E: E.activation(out=tmp4, in_=tmp4, func=AF.Exp), r=[smk], w=[smk])
            em.op('act', lambda E: E.activation(out=tmp4, in_=tmp4, func=AF.Ln, bias=1.0, scale=1.0), r=[smk], w=[smk])
            em.op('dve', lambda E: E.tensor_tensor(gtok, tmp4, negA[:, d * 4:d * 4 + 4], ALU.mult), r=[smk, 'negA'], w=[smk])
            if GDN_CUT == 1000 + 2:
                pass
            yield
            gU_, gUk = gU.next()
            for h in range(4):
                em.op('pool' if h % 2 else 'dve', lambda E: E.tensor_scalar(gU_[:, h, :], U[:], gtok[:, h:h + 1], None, ALU.mult),
                      r=[Uk, smk], w=[gUk])
            psR, pRk = g.psum()
            em.op('pe', lambda E: E.matmul(psR[:], ones32[:], f2(gU_), start=True, stop=True), r=['ones32', gUk], w=[pRk])
            psC, pCk = g.psum()
            em.op('pe', lambda E: E.matmul(psC[:, 0:4], U[:], gtok, start=True, stop=True), r=[Uk, smk], w=[pCk])
            em.op('pe', lambda E: E.matmul(psC[:, 4:8], ones32[:], gtok, start=True, stop=True), r=['ones32', smk], w=[pCk])
            gcc_, gck = gcc.next()
            em.op('dve', lambda E: E.tensor_copy(gcc_[:], psC[:, 0:8]), r=[pCk], w=[gck])
            yield
            dif_, difk = dif.next()
            dT_, dTk = difT.next()
            for h in range(4):
                em.op('dve', lambda E: E.tensor_scalar(dif_[:, h, :], psR[:, h * 128:(h + 1) * 128], gcc_[:, h:h + 1], 0.0,
                                                       ALU.subtract, ALU.max), r=[pRk, gck], w=[difk])
                em.op('dve', lambda E: E.tensor_scalar(dT_[:, h, :], psR[:, h * 128:(h + 1) * 128], gcc_[:, h:h + 1], 0.0,
                                                       ALU.subtract, ALU.min), r=[pRk, gck], w=[dTk])
            em.op('act', lambda E: E.activation(out=dif_[:], in_=dif_[:], func=AF.Exp, scale=-1.0), r=[difk], w=[difk])
            em.op('act', lambda E: E.activation(out=dT_[:], in_=dT_[:], func=AF.Exp), r=[dTk], w=[dTk])
            yield
            er_, erk = Erow.next()
            em.op('act', lambda E: E.activation(out=f2(er_), in_=psR[:], func=AF.Exp), r=[pRk, difk, dTk], w=[erk])
            if GDN_CUT == 1000 + 3:
                pass
            yield
            ds_, dsk = DmS.next()
            for h in range(4):
                em.op('dve', lambda E: E.scalar_tensor_tensor(ds_[:, h, :], dif_[:, h, :], nbeta[:, h:h + 1], MS[:], ALU.mult, ALU.mult),
                      r=[difk, smk, MSk], w=[dsk])
            B0, B0k = Bk.next()
            em.op('dve', lambda E: E.tensor_tensor(f2(B0), psG[:], f2(ds_), ALU.mult), r=[pGk, dsk], w=[B0k])
            if GDN_CUT == 1000 + 31:
                pass
            em.op('pool', lambda E: E.tensor_tensor(dT_[:], dT_[:], V(MT[:, 0:1], [(0, 4), (1, 128)]), ALU.mult), r=[dTk, MTk], w=[dTk])
            at_, atk = attnT.next()
            em.op('dve', lambda E: E.tensor_tensor(f2(at_), psQ[:], f2(dT_), ALU.mult), r=[pQk, dTk], w=[atk])
            if GDN_CUT == 1000 + 32:
                pass
            yield
            psX, pXk = g.psum()
            for h in range(4):
                em.op('pe', lambda E: E.matmul(psX[:, h * 128:(h + 1) * 128], B0[:, h, :], g.ident[:], start=True, stop=True),
                      r=[B0k, 'ident'], w=[pXk])
            if GDN_CUT == 1000 + 33:
                pass
            X0, X0k = Xk.next()
            em.op('dve', lambda E: E.tensor_copy(f2(X0), psX[:]), r=[pXk], w=[X0k])
            if GDN_CUT == 1000 + 34:
                pass
            P0, P0k = Pk.next()
            em.op('dve', lambda E: E.tensor_tensor(P0[:], psX[:].rearrange("p (h c) -> p h c", h=4), identb4, ALU.add),
                  r=[pXk, 'ident'], w=[P0k])
            if GDN_CUT == 1000 + 4:
                pass
            Bc, Bck, Xc, Xck, Pc, Pck = B0, B0k, X0, X0k, P0, P0k
            for k in range(1, 7):
                if k < 6:
                    psX, pXk = g.psum()
                    mm4(psX, Bc, Bck, Xc, Xck, pXk)
                yield
                psB, pBk = g.psum()
                mm4(psB, Xc, Xck, Bc, Bck, pBk)
                if k < 6:
                    Xn, Xnk = Xk.next()
                    em.op('dve', lambda E: E.tensor_copy(f2(Xn), psX[:]), r=[pXk], w=[Xnk])
                    Bn, Bnk = Bk.next()
                    em.op('dve', lambda E: E.tensor_copy(f2(Bn), psB[:]), r=[pBk], w=[Bnk])
                yield
                IB, IBk_ = IBk.next()
                em.op('dve', lambda E: E.tensor_tensor(IB[:], psB[:].rearrange("p (h c) -> p h c", h=4), identb4, ALU.add),
                      r=[pBk, 'ident'], w=[IBk_])
                yield
                psP, pPk = g.psum()
                mm4(psP, IB, IBk_, Pc, Pck, pPk)
                yield
                Pn, Pnk = Pk.next()
                em.op('dve', lambda E: E.tensor_copy(f2(Pn), psP[:]), r=[pPk], w=[Pnk])
                Pc, Pck = Pn, Pnk
                if k < 6:
                    Bc, Bck, Xc, Xck = Bn, Bnk, Xn, Xnk
            TmT, Tk = Pc, Pck
            if GDN_CUT == 1000 + 5:
                pass
            yield
            em.op('act', lambda E: E.activation(out=ecol, in_=gcc_[:, 0:4], func=AF.Exp), r=[gck], w=[smk])
            em.op('dve', lambda E: E.tensor_tensor(bg, beta, ecol, ALU.mult), r=[smk], w=[smk])
            em.op('dve', lambda E: E.tensor_tensor(ed, gcc_[:, 4:8], gcc_[:, 0:4], ALU.subtract), r=[gck], w=[smk])
            em.op('act', lambda E: E.activation(out=ed, in_=ed, func=AF.Exp), r=[smk], w=[smk])
            em.op('act', lambda E: E.activation(out=dec, in_=gcc_[:, 4:8], func=AF.Exp), r=[gck], w=[smk])
            vb_, vbk = vb.next()
            em.op('pool', lambda E: E.tensor_tensor(vb_[:], qkv_[:, 1024:1536].rearrange("p (h c) -> p h c", h=4),
                                                    V(beta[:, 0:1], [(1, 4), (0, 128)]), ALU.mult), r=[qk_, smk], w=[vbk])
            kd_, kdk = kdec.next()
            em.op('pool', lambda E: E.tensor_tensor(kd_[:], kn4, V(ed[:, 0:1], [(1, 4), (0, 128)]), ALU.mult), r=[qnk, smk], w=[kdk])
            qd_, qdk = qdT.next()
            em.op('dve', lambda E: E.tensor_tensor(qd_[:], qT4, er_[:], ALU.mult), r=[qTk, erk], w=[qdk])
            if GDN_CUT == 1000 + 6:
                pass
            yield
            psV, pVk = g.psum()
            for h in range(4):
                em.op('pe', lambda E: E.matmul(psV[:, h * 128:(h + 1) * 128], kT4[:, h, :], sb_[:, h, :], start=True, stop=False),
                      r=[qTk, sbk], w=[pVk])
                em.op('pe', lambda E: E.matmul(psV[:, h * 128:(h + 1) * 128], kT4[:, h, :], sl_[:, h, :], start=False, stop=True),
                      r=[qTk, slk], w=[pVk])
            yield
            t1_, t1k = t1b.next()
            em.op('dve', lambda E: E.tensor_tensor(t1_[:], psV[:].rearrange("p (h c) -> p h c", h=4), V(bg[:, 0:1], [(1, 4), (0, 128)]), ALU.mult),
                  r=[pVk, smk], w=[t1k])
            r_, rk_ = rr.next()
            em.op('pool', lambda E: E.tensor_tensor(r_[:], vb_[:], t1_[:], ALU.subtract), r=[vbk, t1k], w=[rk_])
            yield
            psN2, pN2k = g.psum()
            mm4(psN2, TmT, Tk, r_, rk_, pN2k)
            yield
            vn_, vnk = vnew.next()
            em.op('act', lambda E: E.copy(f2(vn_), psN2[:]), r=[pN2k], w=[vnk])
            yield
            psO, pOk = g.psum()
            for h in range(4):
                em.op('pe', lambda E: E.matmul(psO[:, h * 128:(h + 1) * 128], sb_[:, h, :], qd_[:, h, :], start=True, stop=False),
                      r=[sbk, qdk], w=[pOk])
                em.op('pe', lambda E: E.matmul(psO[:, h * 128:(h + 1) * 128], vn_[:, h, :], at_[:, h, :], start=False, stop=True),
                      r=[vnk, atk], w=[pOk])
            yield
            psS, pSk = g.psum()
            mm4(psS, kd_, kdk, vn_, vnk, pSk)
            em.op('pool', lambda E: E.tensor_tensor(S32_[:], S32_[:], V(dec[:, 0:1], [(1, 4), (0, 128)]), ALU.mult), r=[S32k, smk], w=[S32k])
            em.op('dve', lambda E: E.tensor_tensor(f2(S32_), f2(S32_), psS[:], ALU.add), r=[S32k, pSk], w=[S32k])
            sb_, sbk = Sbd[d].next()
            em.op('act', lambda E: E.copy(sb_[:], S32_[:]), r=[S32k], w=[sbk])
            sl_, slk = Slod[d].next()
            em.op('pool', lambda E: E.tensor_tensor(sl_[:], S32_[:], sb_[:], ALU.subtract), r=[S32k, sbk], w=[slk])
            if GDN_CUT == 1000 + 7:
                pass
            yield
            ofap = AP(tensor=g.OF.tensor, offset=g.OF.offset + r0, ap=[[Lmax, 128], [128 * Lmax, 4], [1, 128]])
            if first:
                of_, ofk = of32.next()
                em.op('dve', lambda E: E.tensor_copy(f2(of_), psO[:]), r=[pOk], w=[ofk])
                em.dma('pool', ofap, of_[:], r=[ofk], w=[('dram', 'OF')])
            else:
                of_, ofk = of32.next()
                em.dma('sp', of_[:], ofap, r=[('dram', 'OF')], w=[ofk])
                z_, zk = zT.next()
                em.dma('sp', z_[:], AP(tensor=g.ZT.tensor, offset=g.ZT.offset + r0, ap=[[Lmax, 128], [128 * Lmax, 4], [1, 128]]),
                       r=[('dram', 'ZT')], w=[zk])
                os_, osk = osum.next()
                em.op('dve', lambda E: E.tensor_tensor(f2(os_), psO[:], f2(of_), ALU.add), r=[pOk, ofk], w=[osk])
                oq_, oqk = osq.next()
                em.op('act', lambda E: E.activation(out=oq_[:], in_=os_[:], func=AF.Square), r=[osk], w=[oqk])
                psN, pNk = g.psum()
                em.op('pe', lambda E: E.matmul(psN[:], g.onesb[:], f2(oq_), start=True, stop=True), r=['onesb', oqk], w=[pNk])
                or_, ork = orst.next()
                rsqrt(em, f2(or_), psN[:], 128 * EPS, pNk, ork)
                em.op('dve', lambda E: E.tensor_tensor(or_[:], or_[:], os_[:], ALU.mult), r=[ork, osk], w=[ork])
                oo_, ook = oout.next()
                em.op('dve', lambda E: E.scalar_tensor_tensor(oo_[:], or_[:], gno[:, 0:1], z_[:], ALU.mult, ALU.mult),
                      r=[ork, 'gno', zk], w=[ook])
                em.dma('pool', AP(tensor=g.OT.tensor, offset=g.OT.offset + 1024 * Lmax + r0, ap=[[Lmax, 128], [128 * Lmax, 4], [1, 128]]),
                       oo_[:], r=[ook], w=[('dram', 'OT')])
            state[d] = (sb_, sbk, sl_, slk)

        for idx in range(nt):
            first = idx < nt - 1 - idx
            gens = [body(0, idx, first), body(1, nt - 1 - idx, first)]
            while gens:
                for gen in list(gens):
                    try:
                        next(gen)
                    except StopIteration:
                        gens.remove(gen)
        em.barrier()
    em.barrier()


def mixer_hyena(g, l, si):
    nc, em, ext = g.nc, g.em, g.ext
    L = g.seq_lens[si]
    Lmax = g.Lmax
    N = 2 * L
    N1 = N // 128
    P1 = L // 128
    R1 = min(N1, HY_RMAX)
    nch1 = max(1, N1 // HY_RMAX)
    CG = min(64, 4096 // N1)
    CS = 64
    ncol = CG * N1
    nchunk = ncol // 512
    cpb = 512 // N1
    nb1 = min(CG, 512 // (2 * N1))
    KF = g.KFq[L]
    KTd = g.KTd[L]
    TWO_PI = 2.0 * math.pi
    with ExitStack() as st:
        def sb(name, shape, dt):
            return st.enter_context(nc.sbuf_tensor(g.un(name), list(shape), dt))

        def T(name, shape, dt, n=2):
            return Rot([sb(f"h_{name}{i}", shape, dt) for i in range(n)], "h_" + name)
        F1 = sb("h_F1", [R1, nch1, 2 * N1], BF16)
        Twr = sb("h_Twr", [128, N1], F32)
        Twi = sb("h_Twi", [128, N1], F32)
        TwTr = sb("h_TwTr", [R1, nch1, 128], F32)
        TwTi = sb("h_TwTi", [R1, nch1, 128], F32)
        G1r = sb("h_G1r", [R1, nch1, P1], BF16)
        G1i = sb("h_G1i", [R1, nch1, P1], BF16)
        F2r = sb("h_F2r", [128, 128], BF16)
        F2i = sb("h_F2i", [128, 128], BF16)
        F2in = sb("h_F2in", [128, 128], BF16)
        G2a = sb("h_G2a", [128, 256], BF16)
        G2b = sb("h_G2b", [128, 256], BF16)
        skip = sb("h_skip", [128, 2, 512], F32)
        for t_, nm in ((F1, f"hyF1_{L}"), (TwTr, f"hyTwTr_{L}"), (TwTi, f"hyTwTi_{L}"), (G1r, f"hyG1r_{L}"), (G1i, f"hyG1i_{L}")):
            em.dma('sp', t_[:], ext["c_" + nm][:, :, :], w=['hconst'])
        for t_, nm in ((Twr, f"hyTwr_{L}"), (Twi, f"hyTwi_{L}"), (F2r, "hyF2r"), (F2i, "hyF2i"), (F2in, "hyF2in"), (G2a, "hyG2a"), (G2b, "hyG2b")):
            em.dma('sp', t_[:], ext["c_" + nm][:, :], w=['hconst'])
        em.dma('sp', skip[:].rearrange("p o c -> p (o c)"), drow_bc(ext["hy_skip"], l * 1024, 1024), w=['hconst'])
        HC = 'hconst'
        Apr = sb("h_Apr", [128, CG, N1], BF16)
        Api = sb("h_Api", [128, CG, N1], BF16)
        Zkr = sb("h_Zkr", [128, CG, N1], BF16)
        Zki = sb("h_Zki", [128, CG, N1], BF16)
        Bpr = sb("h_Bpr", [R1, nch1, CG, 128], BF16)
        Bpi = sb("h_Bpi", [R1, nch1, CG, 128], BF16)
        tP = T("tP", [128, 512], F32, 2)
        tQ = T("tQ", [128, 512], F32, 2)
        kfc = T("kfc", [128, 2, 512], BF16, 2)

        def cmul_tw(ps, pk, M, units, width, tr, ti, outr_fn, outi_fn, okeys):
            n = units * 2 * width
            src = V(ps[0:M, 0:1], [(2 * width, units), (width, 2), (1, width)])
            trb = V(tr, [(0, units), (0, 2), (1, width)])
            tib = V(ti, [(0, units), (0, 2), (1, width)])
            p_, pk_ = tP.next()
            q_, qk_ = tQ.next()
            pv = V(p_[0:M, 0:1], [(2 * width, units), (width, 2), (1, width)])
            qv = V(q_[0:M, 0:1], [(2 * width, units), (width, 2), (1, width)])
            em.op('dve', lambda E: E.tensor_tensor(pv, src, trb, ALU.mult), r=[pk, HC], w=[pk_])
            em.op('dve', lambda E: E.tensor_tensor(qv, src, tib, ALU.mult), r=[pk, HC], w=[qk_])
            pre = V(p_[0:M, 0:1], [(2 * width, units), (1, width)])
            pim = V(p_[0:M, 0:1], [(2 * width, units), (1, width)], off=width)
            qre = V(q_[0:M, 0:1], [(2 * width, units), (1, width)])
            qim = V(q_[0:M, 0:1], [(2 * width, units), (1, width)], off=width)
            em.op('pool', lambda E: E.tensor_tensor(outr_fn, pre, qim, ALU.subtract), r=[pk_, qk_], w=[okeys[0]])
            em.op('pool', lambda E: E.tensor_tensor(outi_fn, qre, pim, ALU.add), r=[pk_, qk_], w=[okeys[1]])

        def fwd_fft(lhs_fn, kchs):
            for c0 in range(0, CG, nb1):
                ps, pk = g.psum()
                for u in range(nb1):
                    for kc in range(kchs):
                        lap, lk = lhs_fn(c0 + u, kc)
                        K_ = lap.shape[0]
                        em.op('pe', lambda E: E.matmul(ps[:, u * 2 * N1:(u + 1) * 2 * N1], lap, F1[0:K_, kc, :],
                                                       start=(kc == 0), stop=(kc == kchs - 1)), r=[lk, HC], w=[pk])
                cmul_tw(ps, pk, 128, nb1, N1, Twr[:, :], Twi[:, :], Apr[:, c0:c0 + nb1, :], Api[:, c0:c0 + nb1, :], ['Apr', 'Api'])

        def stage3(j):
            cols = slice(j * 512, (j + 1) * 512)
            ar = Apr[:].rearrange("p c f -> p (c f)")[:, cols]
            ai = Api[:].rearrange("p c f -> p (c f)")[:, cols]
            pzr, pzrk = g.psum()
            em.op('pe', lambda E: E.matmul(pzr[:], F2r[:], ar, start=True, stop=False), r=['Apr', HC], w=[pzrk])
            em.op('pe', lambda E: E.matmul(pzr[:], F2in[:], ai, start=False, stop=True), r=['Api', HC], w=[pzrk])
            pzi, pzik = g.psum()
            em.op('pe', lambda E: E.matmul(pzi[:], F2i[:], ar, start=True, stop=False), r=['Apr', HC], w=[pzik])
            em.op('pe', lambda E: E.matmul(pzi[:], F2r[:], ai, start=False, stop=True), r=['Api', HC], w=[pzik])
            return pzr, pzrk, pzi, pzik

        if g.filt_done.get(L) != l:
            g.filt_done[L] = l
            with ExitStack() as st2:
                def sb2(name, shape, dt):
                    return st2.enter_context(nc.sbuf_tensor(g.un(name), list(shape), dt))
                w1 = sb2("hf_w1", [33, 64], F32)
                w2 = sb2("hf_w2", [64, 64], F32)
                w3 = sb2("hf_w3", [64, 2048], F32)
                pc = sb2("hf_pc", [64, 8], F32)
                dl = sb2("hf_dl", [128, 512], F32)
                tneg = sb2("hf_tneg", [128, N1], F32)
                ones32 = sb2("hf_ones", [128, 128], F32)
                invn = sb2("hf_invn", [128, 2, 512], F32)
                em.dma('sp', w1[:], ext["hy_ffn_w1"][l], w=['fw'])
                em.dma('sp', w2[:], ext["hy_ffn_w2"][l], w=['fw'])
                em.dma('sp', w3[:], ext["hy_ffn_w3"][l], w=['fw'])
                em.dma('sp', pc[:, 0:1], dcol(ext["hy_ffn_b1"], l * 64, 1, nparts=64), w=['fpc'], slow=True)
                em.dma('sp', pc[:, 1:2], dcol(ext["hy_sin_freq"], l * 64, 1, nparts=64), w=['fpc'], slow=True)
                em.dma('sp', pc[:, 2:3], dcol(ext["hy_ffn_b2"], l * 64, 1, nparts=64), w=['fpc'], slow=True)
                em.dma('sp', dl[:], ext["c_hydelta"][:, :], w=['fw'])
                em.dma('sp', tneg[:], ext[f"c_hytneg_{L}"][:, :], w=['fw'])
                em.dma('sp', ones32[:], ext["c_ones"][:, :], w=['fw'])
                em.op('dve', lambda E: E.tensor_scalar(pc[:, 1:2], pc[:, 1:2], 1.0 / TWO_PI, None, ALU.mult), r=['fpc'], w=['fpc'])
                em.op('pool', lambda E: E.memset(pc[:, 3:4], -math.pi), w=['fpc'])
                ft = Rot([sb2(f"hf_ft{i}", [33, 512], F32) for i in range(2)], "hf_ft")
                hh1 = Rot([sb2(f"hf_h1{i}", [64, 512], F32) for i in range(2)], "hf_h1")
                hh2 = Rot([sb2(f"hf_h2{i}", [64, 512], F32) for i in range(2)], "hf_h2")
                dk = Rot([sb2(f"hf_dk{i}", [128, 512], F32) for i in range(2)], "hf_dk")
                kf32 = Rot([sb2(f"hf_k32{i}", [128, 512], F32) for i in range(2)], "hf_k32")
                kab = Rot([sb2(f"hf_kab{i}", [128, 512], F32) for i in range(2)], "hf_kab")
                kbf = Rot([sb2(f"hf_kbf{i}", [128, 2, 512], BF16) for i in range(2)], "hf_kbf")
                feat = ext[f"c_hyfeat_{L}"]
                msk = sb2("hf_msk", [64, 512], F32)

                def sin_layer(ps, pk, bcol, out, ok):
                    em.op('dve', lambda E: E.tensor_scalar(out, ps, pc[:, bcol:bcol + 1], pc[:, 1:2], ALU.add, ALU.mult), r=[pk, 'fpc'], w=[ok])
                    for _ in range(2):
                        em.op('dve', lambda E: E.tensor_scalar(msk[:], out, 0.5, None, ALU.is_gt), r=[ok], w=['msk'])
                        em.op('dve', lambda E: E.tensor_tensor(out, out, msk[:], ALU.subtract), r=[ok, 'msk'], w=[ok])
                        em.op('dve', lambda E: E.tensor_scalar(msk[:], out, -0.5, None, ALU.is_lt), r=[ok], w=['msk'])
                        em.op('dve', lambda E: E.tensor_tensor(out, out, msk[:], ALU.add), r=[ok, 'msk'], w=[ok])
                    em.op('act', lambda E: E.activation(out=out, in_=out, func=AF.Sin, scale=TWO_PI), r=[ok], w=[ok])

                pn = [g.psum([6]), g.psum([7])]
                ntile_tot = N // 128
                for blk in range(N // 512):
                    f_, fk = ft.next()
                    em.dma('sp', f_[:], feat[:, blk * 512:(blk + 1) * 512], w=[fk])
                    ps, pk = g.psum(range(6))
                    em.op('pe', lambda E: E.matmul(ps[0:64, :], w1[:], f_[:], start=True, stop=True), r=['fw', fk], w=[pk])
                    h1, h1k = hh1.next()
                    sin_layer(ps[0:64, :], pk, 0, h1[:], h1k)
                    ps, pk = g.psum(range(6))
                    em.op('pe', lambda E: E.matmul(ps[0:64, :], w2[:], h1[:], start=True, stop=True), r=['fw', h1k], w=[pk])
                    h2, h2k = hh2.next()
                    sin_layer(ps[0:64, :], pk, 2, h2[:], h2k)
                    for tt in range(4):
                        ti_ = blk * 4 + tt
                        dirn = 0 if ti_ * 128 < L else 1
                        d_, dkk = dk.next()
                        em.op('act', lambda E: E.activation(out=d_[:], in_=dl[:], func=AF.Exp, scale=tneg[:, ti_:ti_ + 1]), r=['fw'], w=[dkk])
                        kb_, kbk = kbf.next()
                        for o in range(2):
                            ps, pk = g.psum(range(6))
                            col0 = o * 1024 + dirn * 512
                            em.op('pe', lambda E: E.matmul(ps[:], h2[:, tt * 128:(tt + 1) * 128], w3[:, col0:col0 + 512], start=True, stop=True),
                                  r=[h2k, 'fw'], w=[pk])
                            k_, kk = kf32.next()
                            em.op('dve', lambda E: E.tensor_tensor(k_[:], ps[:], d_[:], ALU.mult), r=[pk, dkk], w=[kk])
                            if ti_ * 128 == L:
                                em.op('pool', lambda E: E.memset(k_[0:1, :], 0.0), w=[kk])
                            a_, ak = kab.next()
                            em.op('act', lambda E: E.activation(out=a_[:], in_=k_[:], func=AF.Abs), r=[kk], w=[ak])
                            em.op('pe', lambda E: E.matmul(pn[o][0][:], ones32[:], a_[:], start=(ti_ == 0), stop=(ti_ == ntile_tot - 1)),
                                  r=['fw', ak], w=[pn[o][1]])
                            em.op('act', lambda E: E.copy(kb_[:, o, :], k_[:]), r=[kk], w=[kbk])
                        em.dma('pool', KTd[ti_ * 128:(ti_ + 1) * 128, :], kb_[:].rearrange("p o c -> p (o c)"), r=[kbk], w=[('dram', 'KTd')])
                for o in range(2):
                    em.op('dve', lambda E: E.reciprocal(invn[:, o, :], pn[o][0][:]), r=[pn[o][1]], w=['invn'])
                ktd = Rot([sb2(f"hf_ktd{i}", [R1, nch1, 128, CG], BF16) for i in range(2)], "hf_ktd")
                for o in range(2):
                    for gi in range(512 // CG):
                        cbase = gi * CG
                        kt_, ktk = ktd.next()
                        for kc in range(nch1):
                            em.dma('sp', kt_[:, kc], AP(tensor=KTd.tensor, offset=KTd.offset + kc * R1 * 128 * 1024 + o * 512 + cbase,
                                                        ap=[[128 * 1024, R1], [1024, 128], [1, CG]]), r=[('dram', 'KTd')], w=[ktk])
                        fwd_fft(lambda c, kc: (V(kt_[:, kc, 0, 0:1], [(CG, 128)], off=c), ktk), nch1)
                        for j in range(nchunk):
                            pzr, pzrk, pzi, pzik = stage3(j)
                            kc_, kck = kfc.next()
                            inb = V(invn[:, o, cbase + j * cpb:cbase + j * cpb + 1], [(1, cpb), (0, N1)])
                            em.op('dve', lambda E: E.tensor_tensor(kc_[:, 0, :].rearrange("p (c f) -> p c f", c=cpb),
                                                                   pzr[:].rearrange("p (c f) -> p c f", c=cpb), inb, ALU.mult),
                                  r=[pzrk, 'invn'], w=[kck])
                            em.op('dve', lambda E: E.tensor_tensor(kc_[:, 1, :].rearrange("p (c f) -> p c f", c=cpb),
                                                                   pzi[:].rearrange("p (c f) -> p c f", c=cpb), inb, ALU.mult),
                                  r=[pzik, 'invn'], w=[kck])
                            em.dma('pool', KF[o, gi, j].rearrange("p a c -> p (a c)"), kc_[:].rearrange("p a c -> p (a c)"), r=[kck],
                                   w=[('dram', 'KF')])
            em.barrier()
        xs = {nm: T(nm, [P1, 128, CS], BF16, 1) for nm in ("x1s", "x2s", "vs")}
        z1 = sb("h_z1", [P1, 128, CG], BF16)
        z2s = sb("h_z2s", [P1, 128, CS], BF16)
        gt1 = T("gt1", [P1, 128, 4], F32, 2)
        gt2 = T("gt2", [P1, 128, 4], F32, 2)
        for sl in range(512 // CS):
            tiles = {}
            for i, nm in enumerate(("x1s", "x2s", "vs")):
                t_, tk = xs[nm].next()
                src_ = AP(tensor=g.PH.tensor, offset=g.PH.offset + i * 512 + sl * CS, ap=[[128 * 1536, P1], [1536, 128], [1, CS]])
                nsp = 2 if P1 >= 128 else 1
                for sp_ in range(nsp):
                    a0, a1 = sp_ * P1 // nsp, (sp_ + 1) * P1 // nsp
                    em.dma('sp', psplit(t_[:], a0, a1), psplit(src_, a0, a1), r=[('dram', 'PH')], w=[tk])
                tiles[nm] = (t_, tk)
            for sg_ in range(CS // CG):
                gi = sl * (CS // CG) + sg_
                cb = sg_ * CG
                cglob = gi * CG
                zsrc, zk = tiles["vs"]
                zoff, zstride = cb, CS
                for o in range(2):
                    gate, gk = tiles["x1s" if o == 0 else "x2s"]
                    zt, ztk, zo, zs_ = zsrc, zk, zoff, zstride
                    fwd_fft(lambda c, kc: (V(zt[:, 0, 0:1], [(zs_, 128)], off=zo + c), ztk), 1)
                    for j in range(nchunk):
                        pzr, pzrk, pzi, pzik = stage3(j)
                        kc_, kck = kfc.next()
                        em.dma('sp', kc_[:].rearrange("p a c -> p (a c)"), KF[o, gi, j].rearrange("p a c -> p (a c)"), r=[('dram', 'KF')], w=[kck])
                        cols = slice(j * 512, (j + 1) * 512)
                        zr_out = Zkr[:].rearrange("p c f -> p (c f)")[:, cols]
                        zi_out = Zki[:].rearrange("p c f -> p (c f)")[:, cols]
                        a_, ak = tP.next()
                        b_, bk = tQ.next()
                        em.op('dve', lambda E: E.tensor_tensor(a_[:], pzr[:], kc_[:, 0, :], ALU.mult), r=[pzrk, kck], w=[ak])
                        em.op('dve', lambda E: E.tensor_tensor(b_[:], pzi[:], kc_[:, 1, :], ALU.mult), r=[pzik, kck], w=[bk])
                        em.op('pool', lambda E: E.tensor_tensor(zr_out, a_[:], b_[:], ALU.subtract), r=[ak, bk], w=['Zkr'])
                        a_, ak = tP.next()
                        b_, bk = tQ.next()
                        em.op('dve', lambda E: E.tensor_tensor(a_[:], pzr[:], kc_[:, 1, :], ALU.mult), r=[pzrk, kck], w=[ak])
                        em.op('dve', lambda E: E.tensor_tensor(b_[:], pzi[:], kc_[:, 0, :], ALU.mult), r=[pzik, kck], w=[bk])
                        em.op('pool', lambda E: E.tensor_tensor(zi_out, a_[:], b_[:], ALU.add), r=[ak, bk], w=['Zki'])
                    for fc in range(nch1):
                        for c0 in range(0, CG, 2):
                            ps, pk = g.psum()
                            for u in range(2):
                                c = c0 + u
                                em.op('pe', lambda E: E.matmul(ps[0:R1, u * 256:(u + 1) * 256], Zkr[:, c, fc * R1:(fc + 1) * R1], G2a[:],
                                                               start=True, stop=False), r=['Zkr', HC], w=[pk])
                                em.op('pe', lambda E: E.matmul(ps[0:R1, u * 256:(u + 1) * 256], Zki[:, c, fc * R1:(fc + 1) * R1], G2b[:],
                                                               start=False, stop=True), r=['Zki', HC], w=[pk])
                            cmul_tw(ps, pk, R1, 2, 128, TwTr[:, fc, :], TwTi[:, fc, :], Bpr[:, fc, c0:c0 + 2, :], Bpi[:, fc, c0:c0 + 2, :],
                                    ['Bpr', 'Bpi'])
                    zn_, znk = (z1, 'z1') if o == 0 else (z2s, 'z2s')
                    zc0 = 0 if o == 0 else cb
                    for c0 in range(0, CG, 4):
                        ps, pk = g.psum()
                        n = 0
                        for fc in range(nch1):
                            for (gm, bp, bpk) in ((G1r, Bpr, 'Bpr'), (G1i, Bpi, 'Bpi')):
                                em.op('pe', lambda E: E.matmul(ps[0:P1, :], gm[:, fc, :], bp[:, fc, c0:c0 + 4, :].rearrange("p c t -> p (c t)"),
                                                               start=(n == 0), stop=(n == 2 * nch1 - 1)), r=[bpk, HC], w=[pk])
                                n += 1
                        yv = V(ps[0:P1, 0:1], [(1, 128), (128, 4)])
                        zv = V(zt[:, 0, 0:1], [(zs_, 128), (1, 4)], off=zo + c0)
                        gv = gate[:, :, cb + c0:cb + c0 + 4]
                        skb = V(skip[0:P1, o, cglob + c0:cglob + c0 + 1], [(0, 128), (1, 4)])
                        t1_, t1k = gt1.next()
                        em.op('pool', lambda E: E.tensor_tensor(t1_[:], zv, skb, ALU.mult), r=[ztk, HC], w=[t1k])
                        t2_, t2k = gt2.next()
                        em.op('dve', lambda E: E.tensor_tensor(t2_[:], yv, t1_[:], ALU.add), r=[pk, t1k], w=[t2k])
                        em.op('pool', lambda E: E.tensor_tensor(zn_[:, :, zc0 + c0:zc0 + c0 + 4], t2_[:], gv, ALU.mult), r=[t2k, gk], w=[znk])
                    if o == 0:
                        zsrc, zk, zoff, zstride = z1, 'z1', 0, CG
            dst_ = AP(tensor=g.Z2.tensor, offset=g.Z2.offset + sl * CS, ap=[[128 * 512, P1], [512, 128], [1, CS]])
            nsp = 2 if P1 >= 128 else 1
            for sp_ in range(nsp):
                a0, a1 = sp_ * P1 // nsp, (sp_ + 1) * P1 // nsp
                em.dma('pool', psplit(dst_, a0, a1), psplit(z2s[:], a0, a1), r=['z2s'], w=[('dram', 'Z2')])
        em.barrier()
        zl = T("zl", [128, 512], BF16, 2)
        zo_ = T("zo", [128, 4, 128], BF16, 2)
        for t in range(L // 128):
            a_, ak = zl.next()
            em.dma('sp', a_[:], g.Z2[t * 128:(t + 1) * 128, :], r=[('dram', 'Z2')], w=[ak])
            ps, pk = g.psum()
            for c in range(4):
                em.op('pe', lambda E: E.matmul(ps[:, c * 128:(c + 1) * 128], a_[:, c * 128:(c + 1) * 128], g.identb[:], start=True, stop=True),
                      r=[ak, 'identb'], w=[pk])
            o_, ok = zo_.next()
            em.op('act' if t % 2 else 'dve', lambda E: (E.copy if t % 2 else E.tensor_copy)(o_[:].rearrange("p c t -> p (c t)"), ps[:]),
                  r=[pk], w=[ok])
            em.dma('pool', AP(tensor=g.OT.tensor, offset=g.OT.offset + t * 128, ap=[[Lmax, 128], [128 * Lmax, 4], [1, 128]]), o_[:],
                   r=[ok], w=[('dram', 'OT')])
    em.barrier()


def stage_C(g, l, si):
    nc, em, ext = g.nc, g.em, g.ext
    L = g.seq_lens[si]
    t0 = g.seq_off[si]
    Ttot, Lmax = g.Ttot, g.Lmax
    nb = L // 512
    with ExitStack() as st:
        def sb(name, shape, dt):
            return st.enter_context(nc.sbuf_tensor(g.un(name), list(shape), dt))
        g32 = sb("c_g32", [128, 3, KC], F32)
        em.op('dve', lambda E: E.tensor_scalar(g32[:], g.gains[:, 1:4, l, :], 32.0, None, ALU.mult), r=['gains'], w=['g32'])
        xT = sb("c_xT", [128, KC, 512], F32)
        yT = sb("c_yT", [128, KC, 512], F32)
        OTb = sb("c_OTb", [128, 12, 512], BF16)
        Gj = Rot([sb(f"c_Gj{i}", [128, 3, 512], BF16) for i in range(2)], "c_Gj")
        wbr = Rot([sb(f"c_wbr{i}", [128, 12, 128], BF16) for i in range(2)], "c_wbr")
        wout = Rot([sb(f"c_wout{i}", [128, KC, 128], BF16) for i in range(2)], "c_wout")
        wgu = Rot([sb(f"c_wgu{i}", [128, 2, KC, 128], BF16) for i in range(3)], "c_wgu")
        wd = Rot([sb(f"c_wd{i}", [128, NFF, 128], BF16) for i in range(2)], "c_wd")
        tmp = Rot([sb(f"c_tmp{i}", [128, 512], F32) for i in range(4)], "c_tmp")
        sgt = Rot([sb(f"c_sgt{i}", [128, 512], BF16) for i in range(2)], "c_sgt")
        mT = sb("c_mT", [128, KC, 512], BF16)
        sqb = sb("c_sqb", [128, KC, 512], BF16)
        h2T = sb("c_h2T", [128, KC, 512], BF16)
        actT = sb("c_actT", [128, NFF, 512], BF16)
        rstd = Rot([sb(f"c_rstd{i}", [128, 512], F32) for i in range(2)], "c_rstd")
        eng2 = ['dve', 'pool']

        def norm_rstd(sqk_):
            ps, pk = g.psum()
            for kc in range(KC):
                em.op('pe', lambda E: E.matmul(ps[:], g.onesb[:], sqb[:, kc, :], start=(kc == 0), stop=(kc == KC - 1)),
                      r=[sqk_, 'onesb'], w=[pk])
            rs, rsk = rstd.next()
            rsqrt(em, rs[:], ps[:], D * EPS, pk, rsk)
            return rs, rsk

        for b in range(nb):
            c0 = t0 + b * 512
            s0 = b * 512
            em.dma('sp', xT[:], AP(tensor=g.XT.tensor, offset=g.XT.offset + c0, ap=[[Ttot, 128], [128 * Ttot, KC], [1, 512]]),
                   r=[('dram', 'XT')], w=['xT'])
            em.dma('sp', OTb[:], AP(tensor=g.OT.tensor, offset=g.OT.offset + s0, ap=[[Lmax, 128], [128 * Lmax, 12], [1, 512]]),
                   r=[('dram', 'OT')], w=['OTb'])
            for j in range(8):
                w_, wk = wbr.next()
                em.dma('sp', w_[:].rearrange("p k c -> p (k c)"), g.WBR[l][j].rearrange("p k c -> p (k c)"),
                       r=[('dram', f"WBR{l}")], w=[wk])
                gj, gk = Gj.next()
                em.dma('sp', gj[:], AP(tensor=g.GT.tensor, offset=g.GT.offset + j * 128 * Lmax + s0,
                                       ap=[[Lmax, 128], [1024 * Lmax, 3], [1, 512]]), r=[('dram', 'GT')], w=[gk])
                ts = []
                for i in range(3):
                    ps, pk = g.psum()
                    for kc in range(4):
                        em.op('pe', lambda E: E.matmul(ps[:], w_[:, i * 4 + kc, :], OTb[:, i * 4 + kc, :], start=(kc == 0), stop=(kc == 3)),
                              r=[wk, 'OTb'], w=[pk])
                    t_, tk = tmp.next()
                    em.op('dve', lambda E: E.tensor_tensor(t_[:], ps[:], gj[:, i, :], ALU.mult), r=[pk, gk], w=[tk])
                    ts.append((t_, tk))
                em.op('pool', lambda E: E.tensor_tensor(ts[0][0][:], ts[0][0][:], ts[1][0][:], ALU.add), r=[ts[0][1], ts[1][1]], w=[ts[0][1]])
                em.op('pool', lambda E: E.tensor_tensor(mT[:, j, :], ts[0][0][:], ts[2][0][:], ALU.add), r=[ts[0][1], ts[2][1]], w=['mT'])
            if C_CUT == 1:
                continue
            for j in range(8):
                w_, wk = wout.next()
                em.dma('sp', w_[:].rearrange("p k c -> p (k c)"), g.WOUT[l][j].rearrange("p k c -> p (k c)"),
                       r=[('dram', f"WOUT{l}")], w=[wk])
                ps, pk = g.psum()
                for kc in range(KC):
                    em.op('pe', lambda E: E.matmul(ps[:], w_[:, kc, :], mT[:, kc, :], start=(kc == 0), stop=(kc == KC - 1)),
                          r=[wk, 'mT'], w=[pk])
                em.op('dve', lambda E: E.tensor_copy(yT[:, j, :], ps[:]), r=[pk], w=['yT'])
                em.op('act', lambda E: E.activation(out=sqb[:, j, :], in_=yT[:, j, :], func=AF.Square), r=['yT'], w=['sqb'])
            rs, rsk = norm_rstd('sqb')
            for j in range(8):
                t_, tk = tmp.next()
                em.op('dve', lambda E: E.scalar_tensor_tensor(t_[:], yT[:, j, :], g32[:, 0, j:j + 1], rs[:], ALU.mult, ALU.mult),
                      r=['yT', rsk, 'g32'], w=[tk])
                em.op('pool', lambda E: E.tensor_tensor(xT[:, j, :], t_[:], xT[:, j, :], ALU.add), r=[tk, 'xT'], w=['xT'])
            if C_CUT == 2:
                continue
            em.op('act', lambda E: E.activation(out=sqb[:], in_=xT[:], func=AF.Square), r=['xT'], w=['sqb'])
            rs, rsk = norm_rstd('sqb')
            for j in range(8):
                em.op('dve', lambda E: E.scalar_tensor_tensor(h2T[:, j, :], xT[:, j, :], g32[:, 1, j:j + 1], rs[:], ALU.mult, ALU.mult),
                      r=['xT', rsk, 'g32'], w=['h2T'])
            for f in range(NFF):
                w_, wk = wgu.next()
                em.dma('sp', w_[:, 0].rearrange("p k c -> p (k c)"), g.WG[l][f].rearrange("p k c -> p (k c)"),
                       r=[('dram', f"WG{l}")], w=[wk])
                em.dma('sp', w_[:, 1].rearrange("p k c -> p (k c)"), g.WU[l][f].rearrange("p k c -> p (k c)"),
                       r=[('dram', f"WU{l}")], w=[wk])
                psg, pgk = g.psum()
                for kc in range(KC):
                    em.op('pe', lambda E: E.matmul(psg[:], w_[:, 0, kc, :], h2T[:, kc, :], start=(kc == 0), stop=(kc == KC - 1)),
                          r=[wk, 'h2T'], w=[pgk])
                psu, puk = g.psum()
                for kc in range(KC):
                    em.op('pe', lambda E: E.matmul(psu[:], w_[:, 1, kc, :], h2T[:, kc, :], start=(kc == 0), stop=(kc == KC - 1)),
                          r=[wk, 'h2T'], w=[puk])
                sg, sgk = sgt.next()
                em.op('act', lambda E: E.activation(out=sg[:], in_=psg[:], func=AF.Silu), r=[pgk], w=[sgk])
                em.op('dve', lambda E: E.tensor_tensor(actT[:, f, :], psu[:], sg[:], ALU.mult), r=[puk, sgk], w=['actT'])
            for j in range(8):
                w_, wk = wd.next()
                em.dma('sp', w_[:].rearrange("p k c -> p (k c)"), g.WD[l][j].rearrange("p k c -> p (k c)"),
                       r=[('dram', f"WD{l}")], w=[wk])
                ps, pk = g.psum()
                for f in range(NFF):
                    em.op('pe', lambda E: E.matmul(ps[:], w_[:, f, :], actT[:, f, :], start=(f == 0), stop=(f == NFF - 1)),
                          r=[wk, 'actT'], w=[pk])
                em.op('dve', lambda E: E.tensor_copy(yT[:, j, :], ps[:]), r=[pk], w=['yT'])
                em.op('act', lambda E: E.activation(out=sqb[:, j, :], in_=yT[:, j, :], func=AF.Square), r=['yT'], w=['sqb'])
            rs, rsk = norm_rstd('sqb')
            for j in range(8):
                t_, tk = tmp.next()
                em.op('dve', lambda E: E.scalar_tensor_tensor(t_[:], yT[:, j, :], g32[:, 2, j:j + 1], rs[:], ALU.mult, ALU.mult),
                      r=['yT', rsk, 'g32'], w=[tk])
                em.op('pool', lambda E: E.tensor_tensor(xT[:, j, :], t_[:], xT[:, j, :], ALU.add), r=[tk, 'xT'], w=['xT'])
            em.dma('pool', AP(tensor=g.XT.tensor, offset=g.XT.offset + c0, ap=[[Ttot, 128], [128 * Ttot, KC], [1, 512]]), xT[:],
                   r=['xT'], w=[('dram', 'XT')])
    em.barrier()


def stage_out(g):
    nc, em = g.nc, g.em
    with ExitStack() as st:
        def sb(name, shape, dt):
            return st.enter_context(nc.sbuf_tensor(g.un(name), list(shape), dt))
        xin_ = Rot([sb(f"o_x{i}", [128, KC, 128], F32) for i in range(2)], "o_x")
        xo = Rot([sb(f"o_y{i}", [128, D], F32) for i in range(2)], "o_y")
        for t in range(g.Ttot // 128):
            x_, xk = xin_.next()
            em.dma('sp', x_[:], AP(tensor=g.XT.tensor, offset=g.XT.offset + t * 128, ap=[[g.Ttot, 128], [128 * g.Ttot, KC], [1, 128]]),
                   r=[('dram', 'XT')], w=[xk])
            o_, ok = xo.next()
            for hlf in range(2):
                ps, pk = g.psum()
                for c in range(4):
                    kc = hlf * 4 + c
                    em.op('pe', lambda E: E.transpose(ps[:, c * 128:(c + 1) * 128], x_[:, kc, :], g.ident[:]), r=[xk, 'ident'], w=[pk])
                if hlf == 0:
                    em.op('dve', lambda E: E.tensor_copy(o_[:, 0:512], ps[:]), r=[pk], w=[ok])
                else:
                    em.op('dve', lambda E: E.tensor_copy(o_[:, 512:1024], ps[:]), r=[pk], w=[ok])
            em.dma('pool', g.y[t * 128:(t + 1) * 128, :], o_[:], r=[ok], w=[('dram', 'y')])
    em.barrier()


def make_in_map(x, w, consts):
    m = {"x": np.ascontiguousarray(x, np.float32)}
    for k, v in w.items():
        v = np.asarray(v, np.float32)
        if k in ("gdn_a_log", "gdn_dt_bias"):
            v = v.reshape(DEPTH, 8)
        if k == "w_branch":
            v = v.reshape(DEPTH, 1536, D)
        m[k] = np.ascontiguousarray(v)
    for k, v in consts.items():
        m["c_" + k] = v
    return m


_CACHE = {}


def kernel(x_prompt, x_sample, **w):
    x_prompt = np.asarray(x_prompt, np.float32)
    x_sample = np.asarray(x_sample, np.float32)
    n_cores = 8
    Bp, Lp, _ = x_prompt.shape
    Bs, Ls_, _ = x_sample.shape
    per = Bp // n_cores
    seq_lens = [Lp] * per + [Ls_]
    key = tuple(seq_lens)
    if key not in _CACHE:
        _CACHE[key] = build(seq_lens)
    nc, consts = _CACHE[key]
    in_maps = []
    for c in range(n_cores):
        xs = [x_prompt[c * per + j] for j in range(per)] + [x_sample[c % Bs]]
        in_maps.append(make_in_map(np.concatenate(xs, 0), w, consts))
    res = run_bass_kernel_spmd(nc, in_maps, core_ids=list(range(n_cores)))
    y_prompt = np.empty_like(x_prompt)
    y_sample = np.empty_like(x_sample)
    for c in range(n_cores):
        y = np.asarray(res.results[c]["y"], np.float32)
        for j in range(per):
            y_prompt[c * per + j] = y[j * Lp:(j + 1) * Lp]
        if c < Bs:
            y_sample[c] = y[per * Lp:per * Lp + Ls_]
    return (y_prompt, y_sample)
```

```python
import math
from contextlib import ExitStack
import numpy as np
import ml_dtypes
import concourse.bass as bass
import concourse.mybir as mybir
from concourse.bass_utils import run_bass_kernel_spmd

F32 = mybir.dt.float32
BF16 = mybir.dt.bfloat16
AF = mybir.ActivationFunctionType
ALU = mybir.AluOpType
AX = mybir.AxisListType
AP = bass.AP

D = 1024
KC = 8
DEPTH = 2
D_IN = 7120
D_FF = 2816
NFF = 22
EPS = 1e-6
C_HY, C_QL, C_CKV, C_KPE, C_GQKV, C_GZ, C_GB, C_GA, C_GATE = 0, 1536, 1792, 1920, 1984, 3520, 4032, 4040, 4048
NDS = 8
HY_RMAX = 128
GDN_CUT = 0
C_CUT = 0


class Em:
    def __init__(s, nc, es):
        s.nc = nc
        s.E = {'pe': nc.tensor, 'act': nc.scalar, 'dve': nc.vector, 'pool': nc.gpsimd, 'sp': nc.sync}
        s.sem = {}
        s.cnt = {}
        for e in s.E:
            s.sem[e] = es.enter_context(nc.semaphore("c_" + e))
            s.cnt[e] = 0
        s.seen = {e: {} for e in s.E}
        s.W = {}
        s.R = {}
        s.dq = {}
        for q in ('sp', 'pool'):
            s.dq[q] = 0
            for i in range(NDS):
                s.sem[(q, i)] = es.enter_context(nc.semaphore(f"d_{q}{i}"))
                s.cnt[(q, i)] = 0
        s.psn = 0
        s.ninst = 0

    def _wait(s, e, sk, v):
        if v <= 0:
            return
        if s.seen[e].get(sk, 0) < v:
            s.E[e].wait_ge(s.sem[sk], v)
            s.seen[e][sk] = v
            s.ninst += 1

    def _deps(s, e, r, w, pe_acc=False):
        for k in r:
            for sk, v in s.W.get(k, {}).items():
                s._wait(e, sk, v)
        for k in w:
            for sk, v in s.W.get(k, {}).items():
                if pe_acc and sk == 'pe':
                    continue
                s._wait(e, sk, v)
            for sk, v in s.R.get(k, {}).items():
                s._wait(e, sk, v)

    def _mark(s, tok, r, w):
        sk, v = tok
        for k in w:
            s.W.setdefault(k, {})[sk] = v
            s.R[k] = {}
        for k in r:
            s.R.setdefault(k, {})[sk] = v

    def op(s, e, fn, r=(), w=()):
        s._deps(e, r, w, pe_acc=(e == 'pe'))
        ins = fn(s.E[e])
        ins.then_inc(s.sem[e], 1)
        s.cnt[e] += 1
        s.ninst += 1
        s._mark((e, s.cnt[e]), r, w)

    def dma(s, q, out, in_, r=(), w=(), slow=False):
        s._deps(q, r, w)
        i = s.dq[q] % NDS
        s.dq[q] += 1
        sk = (q, i)
        s._wait(q, sk, s.cnt[sk])
        if slow:
            s.E[q].dma_start(out=out, in_=in_, allow_slow_non_contiguous=True).then_inc(s.sem[sk], 16)
        else:
            s.E[q].dma_start(out=out, in_=in_).then_inc(s.sem[sk], 16)
        s.cnt[sk] += 16
        s.ninst += 1
        s._mark((sk, s.cnt[sk]), r, w)

    def barrier(s):
        for e in s.E:
            for sk, v in s.cnt.items():
                if sk != e:
                    s._wait(e, sk, v)
        s.W.clear()
        s.R.clear()


def V(ap, dims, off=0):
    a = ap.ap
    return AP(tensor=ap.tensor, offset=ap.offset + off, ap=[list(a[0])] + [[st, n] for st, n in dims])


def rsqrt(em, out, src, addc, rk, wk):
    em.op('act', lambda E: E.activation(out=out, in_=src, func=AF.Ln, bias=float(addc), scale=1.0), r=[rk], w=[wk])
    em.op('act', lambda E: E.activation(out=out, in_=out, func=AF.Exp, scale=-0.5), r=[wk], w=[wk])


def psplit(ap, p0, p1):
    a = [list(d) for d in ap.ap]
    off = ap.offset + p0 * a[0][0]
    a[0][1] = p1 - p0
    return AP(tensor=ap.tensor, offset=off, ap=a)


def bf(x):
    return np.asarray(x, np.float32).astype(ml_dtypes.bfloat16)


def make_consts(Ls):
    c = {}
    c["ident"] = np.eye(128, dtype=np.float32)
    c["identb"] = bf(np.eye(128))
    c["onesb"] = bf(np.ones((128, 128)))
    c["ones"] = np.ones((128, 128), np.float32)
    i = np.arange(128)
    c["U_le"] = (i[:, None] <= i[None, :]).astype(np.float32)
    c["U_ge"] = (i[:, None] >= i[None, :]).astype(np.float32)
    c["SL"] = (i[None, :] < i[:, None]).astype(np.float32)
    c["SU"] = (i[None, :] > i[:, None]).astype(np.float32)
    Rm = np.zeros((64, 64), np.float32)
    for d in range(32):
        Rm[d + 32, d] = -1.0
        Rm[d, d + 32] = 1.0
    c["rope_R"] = bf(Rm)
    for L in sorted(set(Ls)):
        half = 32
        inv = 10000.0 ** (-np.arange(half, dtype=np.float32) / half)
        ang = np.arange(L, dtype=np.float32)[:, None] * inv[None, :]
        cs = np.cos(ang).astype(np.float32).T
        sn = np.sin(ang).astype(np.float32).T
        c[f"ropecos{L}"] = np.concatenate([cs, cs], 0).astype(np.float32)
        c[f"ropesin{L}"] = np.concatenate([sn, sn], 0).astype(np.float32)
        N = 2 * L
        N1 = N // 128
        P1 = L // 128
        R1 = min(N1, HY_RMAX)
        nch1 = max(1, N1 // HY_RMAX)
        s1 = np.arange(N1, dtype=np.float64)
        f1 = np.arange(N1, dtype=np.float64)
        a = 2 * np.pi * np.outer(s1, f1) / N1
        F1 = np.concatenate([np.cos(a), -np.sin(a)], 1)
        c[f"hyF1_{L}"] = bf(F1.reshape(nch1, R1, 2 * N1).transpose(1, 0, 2))
        s2 = np.arange(128, dtype=np.float64)
        a = 2 * np.pi * np.outer(s2, f1) / N
        c[f"hyTwr_{L}"] = np.cos(a).astype(np.float32)
        c[f"hyTwi_{L}"] = (-np.sin(a)).astype(np.float32)
        aT = a.T
        c[f"hyTwTr_{L}"] = np.cos(aT).reshape(nch1, R1, 128).transpose(1, 0, 2).astype(np.float32)
        c[f"hyTwTi_{L}"] = np.sin(aT).reshape(nch1, R1, 128).transpose(1, 0, 2).astype(np.float32)
        t1 = np.arange(P1, dtype=np.float64)
        a = 2 * np.pi * np.outer(f1, t1) / N1
        c[f"hyG1r_{L}"] = bf((np.cos(a) / N).reshape(nch1, R1, P1).transpose(1, 0, 2))
        c[f"hyG1i_{L}"] = bf((-np.sin(a) / N).reshape(nch1, R1, P1).transpose(1, 0, 2))
        sidx = np.arange(N)
        lag = np.where(sidx < L, sidx, N - sidx)
        lag[L] = 0
        tgrid = np.linspace(0.0, 1.0, L, dtype=np.float32)
        bands = 16
        wpos = (2.0 * math.pi / L) * np.arange(L, dtype=np.float32)
        fr = np.linspace(1e-4, bands - 1, bands, dtype=np.float32)
        feats = np.concatenate([tgrid[:, None], np.cos(fr[None, :] * wpos[:, None]), -np.sin(fr[None, :] * wpos[:, None])], -1)
        c[f"hyfeat_{L}"] = np.ascontiguousarray(feats[lag].T.astype(np.float32))
        c[f"hytneg_{L}"] = np.ascontiguousarray((-tgrid[lag]).reshape(N1, 128).T.astype(np.float32))
    a = 2 * np.pi * np.outer(np.arange(128.0), np.arange(128.0)) / 128
    c["hyF2r"] = bf(np.cos(a))
    c["hyF2i"] = bf(-np.sin(a))
    c["hyF2in"] = bf(np.sin(a))
    c["hyG2a"] = bf(np.concatenate([np.cos(a), np.sin(a)], 1))
    c["hyG2b"] = bf(np.concatenate([-np.sin(a), np.cos(a)], 1))
    deltas = np.abs(np.linspace(math.log(1e-2) / 1.5, math.log(1e-2) / 0.3, 512, dtype=np.float32))
    c["hydelta"] = np.ascontiguousarray(np.broadcast_to(deltas[None, :], (128, 512))).astype(np.float32)
    return c


class Ctx:
    pass


def dcol(ap_dram, off, n, nparts=128, pstride=1, cstride=128):
    return AP(tensor=ap_dram.tensor, offset=ap_dram.offset + off, ap=[[pstride, nparts], [cstride, n], [1, 1]])


def drow_bc(ap_dram, off, n, nparts=128):
    return AP(tensor=ap_dram.tensor, offset=ap_dram.offset + off, ap=[[0, nparts], [1, n]])


def build(seq_lens, debug=False, stages=None, mixers=None):
    nc = bass.Bass("TRN2", target_bir_lowering=False)
    Ttot = sum(seq_lens)
    Lmax = max(seq_lens)
    seq_off = [sum(seq_lens[:i]) for i in range(len(seq_lens))]
    Ls = sorted(set(seq_lens))
    g = Ctx()
    g.nc = nc
    g.uidc = [0]

    def un(name):
        g.uidc[0] += 1
        return f"{name}_{g.uidc[0]}"
    g.un = un
    ext = {}

    def xin(name, shape, dt=F32):
        ext[name] = nc.dram_tensor(name, list(shape), dt, kind="ExternalInput").ap()
        return ext[name]

    xin("x", [Ttot, D])
    for nm, shp in (("norm_mix_pre", [DEPTH, D]), ("norm_mix_post", [DEPTH, D]), ("norm_ffn_pre", [DEPTH, D]),
                    ("norm_ffn_post", [DEPTH, D]), ("w_in", [DEPTH, D, D_IN]), ("hy_conv_w", [DEPTH, 3, 1536]),
                    ("hy_conv_b", [DEPTH, 1536]), ("hy_ffn_w1", [DEPTH, 33, 64]), ("hy_ffn_b1", [DEPTH, 64]),
                    ("hy_sin_freq", [DEPTH, 64]), ("hy_ffn_w2", [DEPTH, 64, 64]), ("hy_ffn_b2", [DEPTH, 64]),
                    ("hy_ffn_w3", [DEPTH, 64, 2048]), ("hy_skip", [DEPTH, 2, 512]), ("mla_q_norm", [DEPTH, 256]),
                    ("mla_wq_b", [DEPTH, 256, 768]), ("mla_kv_norm", [DEPTH, 128]), ("mla_wkv_b", [DEPTH, 128, 1024]),
                    ("gdn_conv_w", [DEPTH, 3, 1536]), ("gdn_a_log", [DEPTH, 8]), ("gdn_dt_bias", [DEPTH, 8]),
                    ("gdn_out_norm", [DEPTH, 128]), ("w_branch", [DEPTH, 1536, D]), ("w_out", [DEPTH, D, D]),
                    ("w_gate", [DEPTH, D, D_FF]), ("w_up", [DEPTH, D, D_FF]), ("w_down", [DEPTH, D_FF, D])):
        xin(nm, shp)
    consts = make_consts(seq_lens)
    for k, v in consts.items():
        xin("c_" + k, v.shape, BF16 if v.dtype == ml_dtypes.bfloat16 else F32)
    y = nc.dram_tensor("y", [Ttot, D], F32, kind="ExternalOutput").ap()

    def scratch(name, shape, dt):
        kind = "ExternalOutput" if debug else "Internal"
        return nc.dram_tensor(name, list(shape), dt, kind=kind).ap()

    XT = scratch("XT", [D, Ttot], F32)
    WTOK = [scratch(f"WTOK{l}", [6, 128, KC, 3, 512], BF16) for l in range(DEPTH)]
    WFM = [scratch(f"WFM{l}", [32, 128, KC, 128], BF16) for l in range(DEPTH)]
    WBR = [scratch(f"WBR{l}", [8, 128, 12, 128], BF16) for l in range(DEPTH)]
    WOUT = [scratch(f"WOUT{l}", [8, 128, KC, 128], BF16) for l in range(DEPTH)]
    WG = [scratch(f"WG{l}", [NFF, 128, KC, 128], BF16) for l in range(DEPTH)]
    WU = [scratch(f"WU{l}", [NFF, 128, KC, 128], BF16) for l in range(DEPTH)]
    WD = [scratch(f"WD{l}", [8, 128, NFF, 128], BF16) for l in range(DEPTH)]
    PH = scratch("PH", [Lmax, 1536], BF16)
    PG = scratch("PG", [Lmax, 1536], BF16)
    BD = scratch("BD", [Lmax, 16], F32)
    QT = scratch("QT", [4, 192, Lmax], BF16)
    KTn = scratch("KTn", [4, 128, Lmax], BF16)
    KTp = scratch("KTp", [64, Lmax], BF16)
    VS = scratch("VS", [Lmax, 512], BF16)
    ZT = scratch("ZT", [512, Lmax], BF16)
    GT = scratch("GT", [3072, Lmax], BF16)
    OT = scratch("OT", [1536, Lmax], BF16)
    OF = scratch("OF", [512, Lmax], F32)
    Z2 = scratch("Z2", [Lmax, 512], BF16)
    KTd = {L_: scratch(f"KTd{L_}", [2 * L_, 1024], BF16) for L_ in Ls}
    KFq = {L_: scratch(f"KF{L_}", [2, 512 // min(64, 4096 // (L_ // 64)), max(1, min(64, 4096 // (L_ // 64)) * (L_ // 64) // 512), 128, 2, 512], BF16)
           for L_ in Ls}
    filt_done = {}

    es = ExitStack()
    with es:
        em = Em(nc, es)

        def sb(name, shape, dt, st=es):
            return st.enter_context(nc.sbuf_tensor(g.un(name), list(shape), dt))

        PS = [es.enter_context(nc.psum_tensor(f"ps{i}", [128, 512], F32)) for i in range(8)]

        def psum(banks=range(8)):
            banks = list(banks)
            b = banks[em.psn % len(banks)]
            em.psn += 1
            return PS[b], ('ps', b)

        ident = sb("ident", [128, 128], F32)
        identb = sb("identb", [128, 128], BF16)
        onesb = sb("onesb", [128, 128], BF16)
        gains = sb("gains", [128, 4, DEPTH, KC], F32)
        em.dma('sp', ident[:], ext["c_ident"][:, :], w=['ident'])
        em.dma('sp', identb[:], ext["c_identb"][:, :], w=['identb'])
        em.dma('sp', onesb[:], ext["c_onesb"][:, :], w=['onesb'])
        for i, nm in enumerate(("norm_mix_pre", "norm_mix_post", "norm_ffn_pre", "norm_ffn_post")):
            for l in range(DEPTH):
                em.dma('sp', gains[:, i, l, :], dcol(ext[nm], l * D, KC), w=['gains'], slow=True)
        em.barrier()

        g.__dict__.update(locals())
        if stages is None or 'prep' in stages:
            stage_prep(g)
        for l in range(DEPTH):
            for si, L in enumerate(seq_lens):
                if stages is None or ('A', l) in stages:
                    stage_A(g, l, si)
                if stages is None or ('B', l) in stages:
                    stage_B(g, l, si)
                if stages is None or ('C', l) in stages:
                    stage_C(g, l, si)
        if stages is None or 'out' in stages:
            stage_out(g)
        em.barrier()
        print("emitted instructions:", em.ninst, {k: v for k, v in em.cnt.items() if isinstance(k, str)})
    return nc, consts


class Rot:
    def __init__(s, tiles, name):
        s.t = tiles
        s.name = name
        s.i = 0

    def next(s):
        j = s.i % len(s.t)
        s.i += 1
        return s.t[j], (s.name, j)


def stage_prep(g):
    nc, em, ext = g.nc, g.em, g.ext
    with ExitStack() as st:
        def sb(name, shape, dt):
            return st.enter_context(nc.sbuf_tensor(g.un(name), list(shape), dt))
        wld = Rot([sb(f"p_wld{i}", [128, 512], F32) for i in range(4)], "p_wld")
        cwt = Rot([sb(f"p_cw{i}", [128, 3, 512], F32) for i in range(2)], "p_cw")
        stg = Rot([sb(f"p_stg{i}", [128, 24 * 512], BF16) for i in range(2)], "p_stg")
        engs = ['dve', 'pool', 'act']
        ei = [0]

        def cast(out, in_, r, w, mul=None):
            e = engs[ei[0] % 3]
            ei[0] += 1
            if mul is not None:
                if e == 'act':
                    e = 'dve'
                em.op(e, lambda E: E.tensor_tensor(out, in_, mul, ALU.mult), r=r, w=w)
            elif e == 'act':
                em.op(e, lambda E: E.copy(out, in_), r=r, w=w)
            else:
                em.op(e, lambda E: E.tensor_copy(out, in_), r=r, w=w)

        def prep_fm(src, row0, nkc, col0, W, dst, t0):
            sg, sk = stg.next()
            for kc in range(nkc):
                wt, wk = wld.next()
                em.dma('sp', wt[:, 0:W], src[row0 + kc * 128: row0 + (kc + 1) * 128, col0:col0 + W], w=[wk])
                cast(sg[:, kc * W:(kc + 1) * W], wt[:, 0:W], [wk], [sk])
            nt = W // 128
            d = dst
            dap = AP(tensor=d.tensor, offset=d.offset + t0 * 128 * nkc * 128,
                     ap=[[nkc * 128, 128], [128, nkc], [128 * nkc * 128, nt], [1, 128]])
            sap = V(sg[:, 0:1], [(W, nkc), (128, nt), (1, 128)])
            em.dma('pool', dap, sap, r=[sk], w=[('dram', d.tensor.name)])

        for l in range(DEPTH):
            w_in = ext["w_in"][l]
            for grp in range(6):
                col0 = (C_HY + 512 * grp) if grp < 3 else (C_GQKV + 512 * (grp - 3))
                cwn = "hy_conv_w" if grp < 3 else "gdn_conv_w"
                cw, cwk = cwt.next()
                cw_src = ext[cwn]
                em.dma('sp', cw[:], AP(tensor=cw_src.tensor, offset=cw_src.offset + l * 3 * 1536 + 512 * (grp % 3),
                                       ap=[[0, 128], [1536, 3], [1, 512]]), w=[cwk])
                sg, sk = stg.next()
                for kc in range(KC):
                    wt, wk = wld.next()
                    em.dma('sp', wt[:], w_in[kc * 128:(kc + 1) * 128, col0:col0 + 512], w=[wk])
                    for s_ in range(3):
                        o = (kc * 3 + s_) * 512
                        cast(sg[:, o:o + 512], wt[:], [wk, cwk], [sk], mul=cw[:, s_, :])
                em.dma('pool', g.WTOK[l][grp].rearrange("p k s c -> p (k s c)"), sg[:, 0:KC * 3 * 512], r=[sk],
                       w=[('dram', f"WTOK{l}")])
            prep_fm(w_in, 0, KC, C_QL, 512, g.WFM[l], 0)
            prep_fm(w_in, 0, KC, C_GZ, 512, g.WFM[l], 4)
            for i in range(6):
                prep_fm(w_in, 0, KC, C_GATE + 512 * i, 512, g.WFM[l], 8 + 4 * i)
            for i in range(2):
                prep_fm(ext["w_branch"][l], 0, 12, 512 * i, 512, g.WBR[l], 4 * i)
                prep_fm(ext["w_out"][l], 0, KC, 512 * i, 512, g.WOUT[l], 4 * i)
                prep_fm(ext["w_down"][l], 0, NFF, 512 * i, 512, g.WD[l], 4 * i)
            for i in range(11):
                prep_fm(ext["w_gate"][l], 0, KC, 256 * i, 256, g.WG[l], 2 * i)
                prep_fm(ext["w_up"][l], 0, KC, 256 * i, 256, g.WU[l], 2 * i)
        xld = Rot([sb(f"p_x{i}", [128, D], F32) for i in range(2)], "p_x")
        xts = Rot([sb(f"p_xt{i}", [128, KC, 128], F32) for i in range(2)], "p_xt")
        for t in range(g.Ttot // 128):
            xt_, xk = xld.next()
            em.dma('sp', xt_[:], ext["x"][t * 128:(t + 1) * 128, :], w=[xk])
            xo, xok = xts.next()
            for hlf in range(2):
                ps, pk = g.psum()
                for c in range(4):
                    kc = hlf * 4 + c
                    em.op('pe', lambda E: E.transpose(ps[:, c * 128:(c + 1) * 128], xt_[:, kc * 128:(kc + 1) * 128],
                                                      g.ident[:]), r=[xk, 'ident'], w=[pk])
                e = 'dve' if hlf == 0 else 'act'
                dst = xo[:, hlf * 4:(hlf + 1) * 4, :]
                src = ps[:].rearrange("p (c t) -> p c t", c=4)
                if e == 'dve':
                    em.op(e, lambda E: E.tensor_copy(dst, src), r=[pk], w=[xok])
                else:
                    em.op(e, lambda E: E.copy(dst, src), r=[pk], w=[xok])
            dap = AP(tensor=g.XT.tensor, offset=g.XT.offset + t * 128, ap=[[g.Ttot, 128], [128 * g.Ttot, KC], [1, 128]])
            em.dma('pool', dap, xo[:], r=[xok], w=[('dram', 'XT')])
    em.barrier()


def stage_A(g, l, si):
    nc, em, ext = g.nc, g.em, g.ext
    L = g.seq_lens[si]
    t0 = g.seq_off[si]
    Ttot = g.Ttot
    nb = L // 512
    QSCALE = 192.0 ** -0.5
    with ExitStack() as st:
        def sb(name, shape, dt):
            return st.enter_context(nc.sbuf_tensor(g.un(name), list(shape), dt))
        g32 = sb("a_g32", [128, KC], F32)
        hb = sb("a_hb", [128, 1536], F32)
        tmpf = sb("a_tmpf", [128, 2 * 768], F32)
        wbd = sb("a_wbd", [128, KC, 16], BF16)
        wqb = sb("a_wqb", [128, 2, 768], BF16)
        wkn = sb("a_wkn", [128, 4, 128], BF16)
        wv = sb("a_wv", [128, 4, 128], BF16)
        gq = sb("a_gq", [128, 2], F32)
        gkv = sb("a_gkv", [128, 1], F32)
        ropeR = sb("a_ropeR", [64, 64], BF16)
        em.op('dve', lambda E: E.tensor_scalar(g32[:], g.gains[:, 0, l, :], 32.0, None, ALU.mult), r=['gains'], w=['g32'])
        em.dma('sp', hb[:], drow_bc(ext["hy_conv_b"], l * 1536, 1536), w=['hb'])
        em.dma('sp', ropeR[:], ext["c_rope_R"][:, :], w=['ropeR'])
        w_in = ext["w_in"][l]
        em.dma('sp', V(tmpf[:, 0:1], [(16, KC), (1, 16)]),
               AP(tensor=w_in.tensor, offset=w_in.offset + C_GB, ap=[[D_IN, 128], [128 * D_IN, KC], [1, 16]]), w=['tmpf'])
        em.op('dve', lambda E: E.tensor_copy(wbd[:].rearrange("p k c -> p (k c)"), tmpf[:, 0:KC * 16]), r=['tmpf'], w=['wbd'])
        wq = ext["mla_wq_b"][l]
        em.dma('sp', V(tmpf[:, 0:1], [(768, 2), (1, 768)]),
               AP(tensor=wq.tensor, offset=wq.offset, ap=[[768, 128], [128 * 768, 2], [1, 768]]), w=['tmpf'], r=['tmpf'])
        em.op('dve', lambda E: E.tensor_copy(wqb[:].rearrange("p k c -> p (k c)"), tmpf[:, 0:1536]), r=['tmpf'], w=['wqb'])
        wkv = ext["mla_wkv_b"][l]
        em.dma('sp', tmpf[:, 0:1024], wkv[:, :], w=['tmpf'], r=['tmpf'])
        em.op('dve', lambda E: E.tensor_copy(wkn[:], V(tmpf[:, 0:1], [(256, 4), (1, 128)])), r=['tmpf'], w=['wkn'])
        em.op('dve', lambda E: E.tensor_copy(wv[:], V(tmpf[:, 0:1], [(256, 4), (1, 128)], off=128)), r=['tmpf'], w=['wv'])
        em.dma('sp', gq[:], dcol(ext["mla_q_norm"], l * 256, 2), w=['gq'], slow=True)
        em.dma('sp', gkv[:], dcol(ext["mla_kv_norm"], l * 128, 1), w=['gkv'], slow=True)
        em.op('dve', lambda E: E.tensor_scalar(gq[:], gq[:], 16.0, None, ALU.mult), r=['gq'], w=['gq'])
        em.op('dve', lambda E: E.tensor_scalar(gkv[:], gkv[:], math.sqrt(128.0), None, ALU.mult), r=['gkv'], w=['gkv'])
        xT = Rot([sb(f"a_xT{i}", [128, KC, 512], F32) for i in range(2)], "a_xT")
        xh = Rot([sb(f"a_xh{i}", [128, KC, 2], F32) for i in range(2)], "a_xh")
        sq = Rot([sb(f"a_sq{i}", [128, KC, 512], BF16) for i in range(1)], "a_sq")
        sqh = sb("a_sqh", [128, KC, 2], BF16)
        rstd = Rot([sb(f"a_rstd{i}", [128, 512], F32) for i in range(2)], "a_rstd")
        rsth = sb("a_rsth", [128, 2], F32)
        xhg = sb("a_xhg", [128, KC, 2], F32)
        hT = Rot([sb(f"a_hT{i}", [128, KC, 514], BF16) for i in range(2)], "a_hT")
        wtk = Rot([sb(f"a_wtk{i}", [128, KC, 3, 512], BF16) for i in range(2)], "a_wtk")
        wfm = Rot([sb(f"a_wfm{i}", [128, 4, KC, 128], BF16) for i in range(2)], "a_wfm")
        stg = Rot([sb(f"a_stg{i}", [128, 4, 512], BF16) for i in range(3)], "a_stg")
        bds = Rot([sb(f"a_bds{i}", [128, 4, 16], F32) for i in range(2)], "a_bds")
        ql = sb("a_ql", [128, 2, 512], F32)
        ck = sb("a_ck", [128, 512], F32)
        kpe = sb("a_kpe", [64, 512], BF16)
        qln = sb("a_qln", [128, 2, 512], BF16)
        ckn = sb("a_ckn", [128, 512], BF16)
        qpe = Rot([sb(f"a_qpe{i}", [64, 512], BF16) for i in range(2)], "a_qpe")
        rt1 = Rot([sb(f"a_rt1{i}", [64, 512], F32) for i in range(2)], "a_rt1")
        rt2 = Rot([sb(f"a_rt2{i}", [64, 512], F32) for i in range(2)], "a_rt2")
        cosT = Rot([sb(f"a_cos{i}", [64, 512], F32) for i in range(2)], "a_cos")
        sinT = Rot([sb(f"a_sin{i}", [64, 512], F32) for i in range(2)], "a_sin")
        rcos, rsin = ext[f"c_ropecos{L}"], ext[f"c_ropesin{L}"]
        eng2 = ['dve', 'pool']

        def rope_store(src_bf, srck, cs, csk, sn, snk, dst_ap, dkey):
            ps, pk = g.psum()
            em.op('pe', lambda E: E.matmul(ps[0:64, :], ropeR[:], src_bf, start=True, stop=True), r=[srck, 'ropeR'], w=[pk])
            a, ak = rt1.next()
            b_, bk = rt2.next()
            em.op('pool', lambda E: E.tensor_tensor(a[:], src_bf, cs[:], ALU.mult), r=[srck, csk], w=[ak])
            em.op('dve', lambda E: E.tensor_tensor(b_[:], ps[0:64, :], sn[:], ALU.mult), r=[pk, snk], w=[bk])
            o, ok = qpe.next()
            em.op('dve', lambda E: E.tensor_tensor(o[:], a[:], b_[:], ALU.add), r=[ak, bk], w=[ok])
            em.dma('pool', dst_ap, o[:], r=[ok], w=[dkey])

        for b in range(nb):
            c0 = t0 + b * 512
            s0 = b * 512
            x_, xk = xT.next()
            em.dma('sp', x_[:], AP(tensor=g.XT.tensor, offset=g.XT.offset + c0, ap=[[Ttot, 128], [128 * Ttot, KC], [1, 512]]),
                   r=[('dram', 'XT')], w=[xk])
            xh_, xhk = xh.next()
            for side, col in ((0, c0 - 1), (1, c0 + 512)):
                if (side == 0 and b == 0) or (side == 1 and b == nb - 1):
                    em.op('pool', lambda E: E.memset(xh_[:, :, side:side + 1], 0.0), w=[xhk])
                else:
                    em.dma('sp', xh_[:, :, side:side + 1],
                           AP(tensor=g.XT.tensor, offset=g.XT.offset + col, ap=[[Ttot, 128], [128 * Ttot, KC], [1, 1]]),
                           r=[('dram', 'XT')], w=[xhk], slow=True)
            cs, csk = cosT.next()
            sn, snk = sinT.next()
            em.dma('sp', cs[:], rcos[:, s0:s0 + 512], w=[csk])
            em.dma('sp', sn[:], rsin[:, s0:s0 + 512], w=[snk])
            sq_, sqk = sq.next()
            em.op('act', lambda E: E.activation(out=sq_[:], in_=x_[:], func=AF.Square), r=[xk], w=[sqk])
            ps, pk = g.psum()
            for kc in range(KC):
                em.op('pe', lambda E: E.matmul(ps[:], g.onesb[:], sq_[:, kc, :], start=(kc == 0), stop=(kc == KC - 1)),
                      r=[sqk, 'onesb'], w=[pk])
            rs, rsk = rstd.next()
            rsqrt(em, rs[:], ps[:], D * EPS, pk, rsk)
            h_, hk = hT.next()
            for kc in range(KC):
                em.op('dve', lambda E: E.scalar_tensor_tensor(h_[:, kc, 1:513], x_[:, kc, :], g32[:, kc:kc + 1], rs[:],
                                                                     ALU.mult, ALU.mult), r=[xk, rsk, 'g32'], w=[hk])
            em.op('act', lambda E: E.activation(out=sqh[:], in_=xh_[:], func=AF.Square), r=[xhk], w=['sqh'])
            ps2, pk2 = g.psum()
            for kc in range(KC):
                em.op('pe', lambda E: E.matmul(ps2[:, 0:2], g.onesb[:], sqh[:, kc, :], start=(kc == 0), stop=(kc == KC - 1)),
                      r=['sqh', 'onesb'], w=[pk2])
            rsqrt(em, rsth[:], ps2[:, 0:2], D * EPS, pk2, 'rsth')
            em.op('pool', lambda E: E.tensor_tensor(xhg[:], xh_[:], V(g32[:, 0:1], [(1, KC), (0, 2)]), ALU.mult),
                  r=[xhk, 'g32'], w=['xhg'])
            em.op('pool', lambda E: E.tensor_tensor(V(h_[:, 0, 0:1], [(514, KC), (513, 2)]), xhg[:],
                                                    V(rsth[:, 0:1], [(0, KC), (1, 2)]), ALU.mult),
                  r=['xhg', 'rsth'], w=[hk])
            for grp in range(6):
                w_, wk = wtk.next()
                em.dma('sp', w_[:].rearrange("p k s c -> p (k s c)"), g.WTOK[l][grp].rearrange("p k s c -> p (k s c)"),
                       r=[('dram', f"WTOK{l}")], w=[wk])
                sg, sgk = stg.next()
                for tt in range(4):
                    ps, pk = g.psum()
                    n = 0
                    for kc in range(KC):
                        for s_ in range(3):
                            lo = 1 + tt * 128 + (s_ - 1)
                            em.op('pe', lambda E: E.matmul(ps[:], h_[:, kc, lo:lo + 128], w_[:, kc, s_, :],
                                                           start=(n == 0), stop=(n == 23)), r=[hk, wk], w=[pk])
                            n += 1
                    if grp < 3:
                        em.op('dve', lambda E: E.tensor_tensor(sg[:, tt, :], ps[:], hb[:, grp * 512:(grp + 1) * 512], ALU.add),
                              r=[pk, 'hb'], w=[sgk])
                    else:
                        em.op('act', lambda E: E.activation(out=sg[:, tt, :], in_=ps[:], func=AF.Silu), r=[pk], w=[sgk])
                dst = g.PH if grp < 3 else g.PG
                em.dma('pool', AP(tensor=dst.tensor, offset=dst.offset + s0 * 1536 + (grp % 3) * 512,
                                  ap=[[1536, 128], [128 * 1536, 4], [1, 512]]), sg[:], r=[sgk],
                       w=[('dram', 'PH' if grp < 3 else 'PG')])
            ps, pk = g.psum()
            for tt in range(4):
                for kc in range(KC):
                    em.op('pe', lambda E: E.matmul(ps[:, tt * 16:(tt + 1) * 16], h_[:, kc, 1 + tt * 128:1 + (tt + 1) * 128],
                                                   wbd[:, kc, :], start=(kc == 0), stop=(kc == KC - 1)), r=[hk, 'wbd'], w=[pk])
            bd_, bdk = bds.next()
            em.op('dve', lambda E: E.tensor_copy(bd_[:].rearrange("p t c -> p (t c)"), ps[:, 0:64]), r=[pk], w=[bdk])
            em.dma('pool', AP(tensor=g.BD.tensor, offset=g.BD.offset + s0 * 16, ap=[[16, 128], [128 * 16, 4], [1, 16]]),
                   bd_[:], r=[bdk], w=[('dram', 'BD')])
            for tg in range(8):
                wf, wfk = wfm.next()
                em.dma('sp', wf[:].rearrange("p t k c -> p t (k c)"),
                       AP(tensor=g.WFM[l].tensor, offset=g.WFM[l].offset + tg * 4 * 128 * KC * 128,
                          ap=[[KC * 128, 128], [128 * KC * 128, 4], [1, KC * 128]]), r=[('dram', f"WFM{l}")], w=[wfk])
                if tg >= 1:
                    sg, sgk = stg.next()
                for ti in range(4):
                    M = 64 if (tg == 0 and ti == 3) else 128
                    ps, pk = g.psum()
                    for kc in range(KC):
                        em.op('pe', lambda E: E.matmul(ps[0:M, :], wf[:, ti, kc, 0:M], h_[:, kc, 1:513],
                                                       start=(kc == 0), stop=(kc == KC - 1)), r=[hk, wfk], w=[pk])
                    if tg == 0:
                        if ti < 2:
                            em.op('dve', lambda E: E.tensor_copy(ql[:, ti, :], ps[:]), r=[pk], w=['ql'])
                        elif ti == 2:
                            em.op('dve', lambda E: E.tensor_copy(ck[:], ps[:]), r=[pk], w=['ck'])
                        else:
                            em.op('act', lambda E: E.copy(kpe[:], ps[0:64, :]), r=[pk], w=['kpe'])
                    elif tg == 1:
                        em.op('act', lambda E: E.activation(out=sg[:, ti, :], in_=ps[:], func=AF.Silu), r=[pk], w=[sgk])
                    else:
                        em.op('act', lambda E: E.activation(out=sg[:, ti, :], in_=ps[:], func=AF.Sigmoid), r=[pk], w=[sgk])
                if tg == 1:
                    em.dma('pool', AP(tensor=g.ZT.tensor, offset=g.ZT.offset + s0, ap=[[g.Lmax, 128], [128 * g.Lmax, 4], [1, 512]]),
                           sg[:], r=[sgk], w=[('dram', 'ZT')])
                elif tg >= 2:
                    em.dma('pool', AP(tensor=g.GT.tensor, offset=g.GT.offset + (tg - 2) * 512 * g.Lmax + s0,
                                      ap=[[g.Lmax, 128], [128 * g.Lmax, 4], [1, 512]]), sg[:], r=[sgk], w=[('dram', 'GT')])
            sq_, sqk = sq.next()
            em.op('act', lambda E: E.activation(out=sq_[:, 0:2, :], in_=ql[:], func=AF.Square), r=['ql'], w=[sqk])
            em.op('act', lambda E: E.activation(out=sq_[:, 2, :], in_=ck[:], func=AF.Square), r=['ck'], w=[sqk])
            ps, pk = g.psum()
            for c in range(2):
                em.op('pe', lambda E: E.matmul(ps[:], g.onesb[:], sq_[:, c, :], start=(c == 0), stop=(c == 1)), r=[sqk, 'onesb'], w=[pk])
            rs, rsk = rstd.next()
            rsqrt(em, rs[:], ps[:], 256 * EPS, pk, rsk)
            for c in range(2):
                em.op('dve', lambda E: E.scalar_tensor_tensor(qln[:, c, :], ql[:, c, :], gq[:, c:c + 1], rs[:], ALU.mult, ALU.mult),
                      r=['ql', rsk, 'gq'], w=['qln'])
            ps, pk = g.psum()
            em.op('pe', lambda E: E.matmul(ps[:], g.onesb[:], sq_[:, 2, :], start=True, stop=True), r=[sqk, 'onesb'], w=[pk])
            rs, rsk = rstd.next()
            rsqrt(em, rs[:], ps[:], 128 * EPS, pk, rsk)
            em.op('dve', lambda E: E.scalar_tensor_tensor(ckn[:], ck[:], gkv[:, 0:1], rs[:], ALU.mult, ALU.mult),
                  r=['ck', rsk, 'gkv'], w=['ckn'])
            for h in range(4):
                ps, pk = g.psum()
                for c in range(2):
                    em.op('pe', lambda E: E.matmul(ps[:], wqb[:, c, h * 192:h * 192 + 128], qln[:, c, :], start=(c == 0), stop=(c == 1)),
                          r=['wqb', 'qln'], w=[pk])
                sg, sgk = stg.next()
                em.op('act', lambda E: E.activation(out=sg[:, 0, :], in_=ps[:], func=AF.Copy, scale=QSCALE), r=[pk], w=[sgk])
                em.dma('pool', g.QT[h, 0:128, s0:s0 + 512], sg[:, 0, :], r=[sgk], w=[('dram', 'QT')])
                ps, pk = g.psum()
                for c in range(2):
                    em.op('pe', lambda E: E.matmul(ps[0:64, :], wqb[:, c, h * 192 + 128:h * 192 + 192], qln[:, c, :],
                                                   start=(c == 0), stop=(c == 1)), r=['wqb', 'qln'], w=[pk])
                qp, qpk = qpe.next()
                em.op('act', lambda E: E.activation(out=qp[:], in_=ps[0:64, :], func=AF.Copy, scale=QSCALE), r=[pk], w=[qpk])
                rope_store(qp[:], qpk, cs, csk, sn, snk, g.QT[h, 128:192, s0:s0 + 512], ('dram', 'QT'))
                ps, pk = g.psum()
                em.op('pe', lambda E: E.matmul(ps[:], wkn[:, h, :], ckn[:], start=True, stop=True), r=['wkn', 'ckn'], w=[pk])
                em.op('dve', lambda E: E.tensor_copy(sg[:, 1, :], ps[:]), r=[pk], w=[sgk])
                em.dma('pool', g.KTn[h, :, s0:s0 + 512], sg[:, 1, :], r=[sgk], w=[('dram', 'KTn')])
            rope_store(kpe[:], 'kpe', cs, csk, sn, snk, g.KTp[:, s0:s0 + 512], ('dram', 'KTp'))
            sg, sgk = stg.next()
            for tt in range(4):
                ps, pk = g.psum()
                em.op('pe', lambda E: E.matmul(ps[:], ckn[:, tt * 128:(tt + 1) * 128], wv[:].rearrange("p h c -> p (h c)"),
                                               start=True, stop=True), r=['ckn', 'wv'], w=[pk])
                em.op('act' if tt % 2 else 'dve', lambda E: (E.copy if tt % 2 else E.tensor_copy)(sg[:, tt, :], ps[:]), r=[pk], w=[sgk])
            em.dma('pool', AP(tensor=g.VS.tensor, offset=g.VS.offset + s0 * 512, ap=[[512, 128], [128 * 512, 4], [1, 512]]),
                   sg[:], r=[sgk], w=[('dram', 'VS')])
    em.barrier()


def mixer_mla(g, l, si):
    nc, em = g.nc, g.em
    L = g.seq_lens[si]
    Lmax = g.Lmax
    nq, nk = L // 512, L // 128
    NCH = max(1, min(8, nk // 4))
    kpc = nk // NCH
    with ExitStack() as st:
        def sb(name, shape, dt):
            return st.enter_context(nc.sbuf_tensor(g.un(name), list(shape), dt))
        Kp = sb("m_Kp", [64, L], BF16)
        Kn = sb("m_Kn", [128, L], BF16)
        Vh = sb("m_Vh", [128, nk, 128], BF16)
        qn = Rot([sb(f"m_qn{i}", [128, 512], BF16) for i in range(2)], "m_qn")
        qp = Rot([sb(f"m_qp{i}", [64, 512], BF16) for i in range(2)], "m_qp")
        PT = Rot([sb(f"m_PT{i}", [128, 512], BF16) for i in range(6)], "m_PT")
        rs = Rot([sb(f"m_rs{i}", [128, 512], F32) for i in range(2)], "m_rs")
        ob = Rot([sb(f"m_ob{i}", [128, 512], BF16) for i in range(2)], "m_ob")
        acc = Rot([sb(f"m_acc{i}", [128, 512], F32) for i in range(2)], "m_acc")
        acc2 = Rot([sb(f"m_accb{i}", [128, 512], F32) for i in range(2)], "m_accb")
        ones32 = sb("m_ones", [128, 128], F32)
        em.dma('sp', ones32[:], g.ext["c_ones"][:, :], w=['ones32'])
        for ci in range(NCH):
            em.dma('sp', Kp[:, ci * kpc * 128:(ci + 1) * kpc * 128], g.KTp[:, ci * kpc * 128:(ci + 1) * kpc * 128],
                   r=[('dram', 'KTp')], w=[('Kp', ci)])
        nacc = 0
        for h in range(4):
            for ci in range(NCH):
                a, b_ = ci * kpc * 128, (ci + 1) * kpc * 128
                em.dma('sp', Kn[:, a:b_], g.KTn[h, :, a:b_], r=[('dram', 'KTn')], w=[('Kn', ci)])
                em.dma('sp', Vh[:, ci * kpc:(ci + 1) * kpc, :],
                       AP(tensor=g.VS.tensor, offset=g.VS.offset + a * 512 + h * 128, ap=[[512, 128], [128 * 512, kpc], [1, 128]]),
                       r=[('dram', 'VS')], w=[('Vh', ci)])
            for qb in range(nq):
                q0 = qb * 512
                qn_, qnk = qn.next()
                qp_, qpk = qp.next()
                em.dma('sp', qn_[:], g.QT[h, 0:128, q0:q0 + 512], r=[('dram', 'QT')], w=[qnk])
                em.dma('sp', qp_[:], g.QT[h, 128:192, q0:q0 + 512], r=[('dram', 'QT')], w=[qpk])
                po, pok = g.psum([4, 6][nacc % 2:nacc % 2 + 1])
                pz, pzk = g.psum([5, 7][nacc % 2:nacc % 2 + 1])
                ac, ack = acc.next()
                ac2, ac2k = acc2.next()
                nacc += 1

                def s_tile(kt):
                    ci = kt // kpc
                    ps, pk = g.psum(range(4))
                    em.op('pe', lambda E: E.matmul(ps[:], Kn[:, kt * 128:(kt + 1) * 128], qn_[:], start=True, stop=False),
                          r=[('Kn', ci), qnk], w=[pk])
                    em.op('pe', lambda E: E.matmul(ps[:], Kp[:, kt * 128:(kt + 1) * 128], qp_[:], start=False, stop=True),
                          r=[('Kp', ci), qpk], w=[pk])
                    return ps, pk

                cur = s_tile(0)
                for kt in range(nk):
                    ci = kt // kpc
                    nxt = s_tile(kt + 1) if kt + 1 < nk else None
                    ps, pk = cur
                    pt, ptk = PT.next()
                    em.op('act', lambda E: E.activation(out=pt[:], in_=ps[:], func=AF.Exp), r=[pk], w=[ptk])
                    em.op('pe', lambda E: E.matmul(po[:], Vh[:, kt, :], pt[:], start=(kt == 0), stop=(kt == nk - 1)),
                          r=[('Vh', ci), ptk], w=[pok])
                    if kt % 3 != 2:
                        if kt == 0:
                            em.op('dve', lambda E: E.tensor_copy(ac[:], pt[:]), r=[ptk], w=[ack])
                        else:
                            em.op('dve', lambda E: E.tensor_tensor(ac[:], ac[:], pt[:], ALU.add), r=[ptk, ack], w=[ack])
                    else:
                        if kt == 2:
                            em.op('pool', lambda E: E.tensor_copy(ac2[:], pt[:]), r=[ptk], w=[ac2k])
                        else:
                            em.op('pool', lambda E: E.tensor_tensor(ac2[:], ac2[:], pt[:], ALU.add), r=[ptk, ac2k], w=[ac2k])
                    cur = nxt
                em.op('pe', lambda E: E.matmul(pz[:], ones32[:], ac[:], start=True, stop=False), r=['ones32', ack], w=[pzk])
                em.op('pe', lambda E: E.matmul(pz[:], ones32[:], ac2[:], start=False, stop=True), r=['ones32', ac2k], w=[pzk])
                r_, rk = rs.next()
                em.op('dve', lambda E: E.reciprocal(r_[:], pz[:]), r=[pzk], w=[rk])
                o_, ok = ob.next()
                em.op('dve', lambda E: E.tensor_tensor(o_[:], po[:], r_[:], ALU.mult), r=[pok, rk], w=[ok])
                em.dma('pool', g.OT[512 + h * 128:512 + (h + 1) * 128, q0:q0 + 512], o_[:], r=[ok], w=[('dram', 'OT')])
    em.barrier()


def stage_B(g, l, si):
    if g.mixers is None or 'mla' in g.mixers:
        mixer_mla(g, l, si)
    if g.mixers is None or 'gdn' in g.mixers:
        mixer_gdn(g, l, si)
    if g.mixers is None or 'hy' in g.mixers:
        mixer_hyena(g, l, si)


def mixer_gdn(g, l, si):
    nc, em, ext = g.nc, g.em, g.ext
    L = g.seq_lens[si]
    Lmax = g.Lmax
    nt = L // 128
    with ExitStack() as st:
        def sb(name, shape, dt):
            return st.enter_context(nc.sbuf_tensor(g.un(name), list(shape), dt))

        def T(name, shape, dt, n=2):
            return Rot([sb(f"g_{name}{i}", shape, dt) for i in range(n)], "g_" + name)
        cU = {0: sb("g_Ule", [128, 128], F32), 1: sb("g_Uge", [128, 128], F32)}
        cS = {0: sb("g_SL", [128, 128], F32), 1: sb("g_SU", [128, 128], F32)}
        ones32 = sb("g_ones", [128, 128], F32)
        negA = sb("g_negA", [128, 8], F32)
        dtb = sb("g_dtb", [128, 8], F32)
        gno = sb("g_gno", [128, 1], F32)
        em.dma('sp', cU[0][:], ext["c_U_le"][:, :], w=['cU0'])
        em.dma('sp', cU[1][:], ext["c_U_ge"][:, :], w=['cU1'])
        em.dma('sp', cS[0][:], ext["c_SL"][:, :], w=['cS0'])
        em.dma('sp', cS[1][:], ext["c_SU"][:, :], w=['cS1'])
        em.dma('sp', ones32[:], ext["c_ones"][:, :], w=['ones32'])
        em.dma('sp', negA[:], drow_bc(ext["gdn_a_log"], l * 8, 8), w=['negA'])
        em.dma('sp', dtb[:], drow_bc(ext["gdn_dt_bias"], l * 8, 8), w=['dtb'])
        em.dma('sp', gno[:], dcol(ext["gdn_out_norm"], l * 128, 1), w=['gno'], slow=True)
        em.op('act', lambda E: E.activation(out=negA[:], in_=negA[:], func=AF.Exp), r=['negA'], w=['negA'])
        em.op('dve', lambda E: E.tensor_scalar(negA[:], negA[:], -1.0, None, ALU.mult), r=['negA'], w=['negA'])
        em.op('dve', lambda E: E.tensor_scalar(gno[:], gno[:], math.sqrt(128.0), None, ALU.mult), r=['gno'], w=['gno'])
        maskT = {0: (cU[0], 'cU0'), 1: (cU[1], 'cU1')}
        qkv = T("qkv", [128, 1536], BF16)
        bd = T("bd", [128, 16], F32)
        sqf = T("sqf", [128, 1024], F32, 2)
        ss8 = T("ss8", [128, 8], F32)
        qkn = T("qkn", [128, 8, 128], BF16)
        qkT = T("qkT", [128, 8, 128], BF16)
        sm = T("sm", [128, 32], F32)
        gU = T("gU", [128, 4, 128], F32)
        gcc = T("gcc", [128, 8], F32)
        dif = T("dif", [128, 4, 128], F32)
        difT = T("difT", [128, 4, 128], F32)
        DmS = T("DmS", [128, 4, 128], F32)
        Bk = T("B", [128, 4, 128], F32, 6)
        Xk = T("X", [128, 4, 128], F32, 6)
        Pk = T("P", [128, 4, 128], F32, 6)
        IBk = T("IB", [128, 4, 128], F32, 6)
        attnT = T("attnT", [128, 4, 128], BF16)
        vb = T("vb", [128, 4, 128], F32)
        rr = T("rr", [128, 4, 128], F32)
        t1b = T("t1b", [128, 4, 128], F32)
        Erow = T("Erow", [128, 4, 128], F32)
        qdT = T("qdT", [128, 4, 128], BF16)
        kdec = T("kdec", [128, 4, 128], BF16)
        vnew = T("vnew", [128, 4, 128], BF16)
        of32 = T("of32", [128, 4, 128], F32)
        osum = T("osum", [128, 4, 128], F32)
        osq = T("osq", [128, 4, 128], BF16)
        orst = T("orst", [128, 4, 128], F32)
        zT = T("zT", [128, 4, 128], BF16)
        oout = T("oout", [128, 4, 128], BF16)
        identb4 = V(g.ident[:, 0:1], [(0, 4), (1, 128)])

        def f2(t):
            return t[:].rearrange("p h c -> p (h c)")

        def mm4(ps, lhs, lk, rhs, rk, pk, start=True, stop=True):
            for h in range(4):
                em.op('pe', lambda E: E.matmul(ps[:, h * 128:(h + 1) * 128], lhs[:, h, :], rhs[:, h, :], start=start, stop=stop),
                      r=[lk, rk], w=[pk])

        S32d = {d: sb(f"g_S32d{d}", [128, 4, 128], F32) for d in range(2)}
        Sbd = {d: T(f"Sbd{d}", [128, 4, 128], BF16) for d in range(2)}
        Slod = {d: T(f"Slod{d}", [128, 4, 128], BF16) for d in range(2)}
        state = {}
        for d in range(2):
            em.op('pool', lambda E: E.memset(S32d[d][:], 0.0), w=[f'S32_{d}'])
            a_, ak_ = Sbd[d].next()
            em.op('pool', lambda E: E.memset(a_[:], 0.0), w=[ak_])
            b_, bk_ = Slod[d].next()
            em.op('pool', lambda E: E.memset(b_[:], 0.0), w=[bk_])
            state[d] = (a_, ak_, b_, bk_)

        def body(d, n, first):
            U, Uk = cU[d], f'cU{d}'
            MS, MSk = cS[d], f'cS{d}'
            MT, MTk = maskT[d]
            S32_, S32k = S32d[d], f'S32_{d}'
            sb_, sbk, sl_, slk = state[d]
            r0 = n * 128
            qkv_, qk_ = qkv.next()
            em.dma('sp', qkv_[:], g.PG[r0:r0 + 128, :], r=[('dram', 'PG')], w=[qk_])
            bd_, bdk = bd.next()
            em.dma('sp', bd_[:], g.BD[r0:r0 + 128, :], r=[('dram', 'BD')], w=[bdk])
            sq_, sqk = sqf.next()
            em.op('pool', lambda E: E.tensor_tensor(sq_[:], qkv_[:, 0:1024], qkv_[:, 0:1024], ALU.mult), r=[qk_], w=[sqk])
            s8, s8k = ss8.next()
            em.op('dve', lambda E: E.tensor_reduce(s8[:], sq_[:].rearrange("p (h c) -> p h c", h=8), AX.X, ALU.add), r=[sqk], w=[s8k])
            rsqrt(em, s8[:], s8[:], EPS, s8k, s8k)
            em.op('dve', lambda E: E.tensor_scalar(s8[:, 0:4], s8[:, 0:4], 128.0 ** -0.5, None, ALU.mult), r=[s8k], w=[s8k])
            qn_, qnk = qkn.next()
            em.op('dve', lambda E: E.tensor_tensor(qn_[:], qkv_[:, 0:1024].rearrange("p (h c) -> p h c", h=8),
                                                   V(s8[:, 0:1], [(1, 8), (0, 128)]), ALU.mult), r=[qk_, s8k], w=[qnk])
            qT_, qTk = qkT.next()
            for hh in range(2):
                ps, pk = g.psum()
                for h in range(4):
                    em.op('pe', lambda E: E.matmul(ps[:, h * 128:(h + 1) * 128], qn_[:, hh * 4 + h, :], g.identb[:], start=True, stop=True),
                          r=[qnk, 'identb'], w=[pk])
                dst = qT_[:, hh * 4:(hh + 1) * 4, :].rearrange("p h c -> p (h c)")
                if hh == 0:
                    em.op('act', lambda E: E.copy(dst, ps[:]), r=[pk], w=[qTk])
                else:
                    em.op('dve', lambda E: E.tensor_copy(dst, ps[:]), r=[pk], w=[qTk])
            qT4, kT4, kn4 = qT_[:, 0:4], qT_[:, 4:8], qn_[:, 4:8]
            yield
            psG, pGk = g.psum()
            mm4(psG, kT4, qTk, kT4, qTk, pGk)
            psQ, pQk = g.psum()
            mm4(psQ, kT4, qTk, qT4, qTk, pQk)
            if GDN_CUT == 1000 + 1:
                pass
            yield
            sm_, smk = sm.next()
            beta, nbeta, gtok, tmp4, ecol, bg, ed, dec = (sm_[:, i * 4:(i + 1) * 4] for i in range(8))
            em.op('act', lambda E: E.activation(out=beta, in_=bd_[:, d * 4:d * 4 + 4], func=AF.Exp, scale=-1.0), r=[bdk], w=[smk])
            em.op('dve', lambda E: E.tensor_scalar(beta, beta, 1.0, None, ALU.add), r=[smk], w=[smk])
            em.op('dve', lambda E: E.reciprocal(beta, beta), r=[smk], w=[smk])
            em.op('dve', lambda E: E.tensor_scalar(nbeta, beta, -1.0, None, ALU.mult), r=[smk], w=[smk])
            em.op('dve', lambda E: E.tensor_tensor(tmp4, bd_[:, 8 + d * 4:12 + d * 4], dtb[:, d * 4:d * 4 + 4], ALU.add), r=[bdk, 'dtb'], w=[smk])
            em.op('act', lambda E: E.activation(out=tmp4, in_=tmp4, func=AF.Exp), r=[smk], w=[smk])
            em.op('act', lambda E: E.activation(out=tmp4, in_=tmp4, func=AF.Ln, bias=1.0, scale=1.0), r=[smk], w=[smk])
            em.op('dve', lambda E: E.tensor_tensor(gtok, tmp4, negA[:, d * 4:d * 4 + 4], ALU.mult), r=[smk, 'negA'], w=[smk])
            if GDN_CUT == 1000 + 2:
                pass
            yield
            gU_, gUk = gU.next()
            for h in range(4):
                em.op('pool' if h % 2 else 'dve', lambda E: E.tensor_scalar(gU_[:, h, :], U[:], gtok[:, h:h + 1], None, ALU.mult),
                      r=[Uk, smk], w=[gUk])
            psR, pRk = g.psum()
            em.op('pe', lambda E: E.matmul(psR[:], ones32[:], f2(gU_), start=True, stop=True), r=['ones32', gUk], w=[pRk])
            psC, pCk = g.psum()
            em.op('pe', lambda E: E.matmul(psC[:, 0:4], U[:], gtok, start=True, stop=True), r=[Uk, smk], w=[pCk])
            em.op('pe', lambda E: E.matmul(psC[:, 4:8], ones32[:], gtok, start=True, stop=True), r=['ones32', smk], w=[pCk])
            gcc_, gck = gcc.next()
            em.op('dve', lambda E: E.tensor_copy(gcc_[:], psC[:, 0:8]), r=[pCk], w=[gck])
            yield
            dif_, difk = dif.next()
            dT_, dTk = difT.next()
            for h in range(4):
                em.op('dve', lambda E: E.tensor_scalar(dif_[:, h, :], psR[:, h * 128:(h + 1) * 128], gcc_[:, h:h + 1], 0.0,
                                                       ALU.subtract, ALU.max), r=[pRk, gck], w=[difk])
                em.op('dve', lambda E: E.tensor_scalar(dT_[:, h, :], psR[:, h * 128:(h + 1) * 128], gcc_[:, h:h + 1], 0.0,
                                                       ALU.subtract, ALU.min), r=[pRk, gck], w=[dTk])
            em.op('act', lambda E: E.activation(out=dif_[:], in_=dif_[:], func=AF.Exp, scale=-1.0), r=[difk], w=[difk])
            em.op('act', lambda E: E.activation(out=dT_[:], in_=dT_[:], func=AF.Exp), r=[dTk], w=[dTk])
            yield
            er_, erk = Erow.next()
            em.op('act', lambda E: E.activation(out=f2(er_), in_=psR[:], func=AF.Exp), r=[pRk, difk, dTk], w=[erk])
            if GDN_CUT == 1000 + 3:
                pass
            yield
            ds_, dsk = DmS.next()
            for h in range(4):
                em.op('dve', lambda E: E.scalar_tensor_tensor(ds_[:, h, :], dif_[:, h, :], nbeta[:, h:h + 1], MS[:], ALU.mult, ALU.mult),
                      r=[difk, smk, MSk], w=[dsk])
            B0, B0k = Bk.next()
            em.op('dve', lambda E: E.tensor_tensor(f2(B0), psG[:], f2(ds_), ALU.mult), r=[pGk, dsk], w=[B0k])
            if GDN_CUT == 1000 + 31:
                pass
            em.op('pool', lambda E: E.tensor_tensor(dT_[:], dT_[:], V(MT[:, 0:1], [(0, 4), (1, 128)]), ALU.mult), r=[dTk, MTk], w=[dTk])
            at_, atk = attnT.next()
            em.op('dve', lambda E: E.tensor_tensor(f2(at_), psQ[:], f2(dT_), ALU.mult), r=[pQk, dTk], w=[atk])
            if GDN_CUT == 1000 + 32:
                pass
            yield
            psX, pXk = g.psum()
            for h in range(4):
                em.op('pe', lambda E: E.matmul(psX[:, h * 128:(h + 1) * 128], B0[:, h, :], g.ident[:], start=True, stop=True),
                      r=[B0k, 'ident'], w=[pXk])
            if GDN_CUT == 1000 + 33:
                pass
            X0, X0k = Xk.next()
            em.op('dve', lambda E: E.tensor_copy(f2(X0), psX[:]), r=[pXk], w=[X0k])
            if GDN_CUT == 1000 + 34:
                pass
            P0, P0k = Pk.next()
            em.op('dve', lambda E: E.tensor_tensor(P0[:], psX[:].rearrange("p (h c) -> p h c", h=4), identb4, ALU.add),
                  r=[pXk, 'ident'], w=[P0k])
            if GDN_CUT == 1000 + 4:
                pass
            Bc, Bck, Xc, Xck, Pc, Pck = B0, B0k, X0, X0k, P0, P0k
            for k in range(1, 7):
                if k < 6:
                    psX, pXk = g.psum()
                    mm4(psX, Bc, Bck, Xc, Xck, pXk)
                yield
                psB, pBk = g.psum()
                mm4(psB, Xc, Xck, Bc, Bck, pBk)
                if k < 6:
                    Xn, Xnk = Xk.next()
                    em.op('act', lambda E: E.activation(out=f2(Xn), in_=psX[:], func=AF.Copy), r=[pXk], w=[Xnk])
                    Bn, Bnk = Bk.next()
                    em.op('dve', lambda E: E.tensor_copy(f2(Bn), psB[:]), r=[pBk], w=[Bnk])
                yield
                IB, IBk_ = IBk.next()
                em.op('dve', lambda E: E.tensor_tensor(IB[:], psB[:].rearrange("p (h c) -> p h c", h=4), identb4, ALU.add),
                      r=[pBk, 'ident'], w=[IBk_])
                yield
                psP, pPk = g.psum()
                mm4(psP, IB, IBk_, Pc, Pck, pPk)
                yield
                Pn, Pnk = Pk.next()
                em.op('act', lambda E: E.activation(out=f2(Pn), in_=psP[:], func=AF.Copy), r=[pPk], w=[Pnk])
                Pc, Pck = Pn, Pnk
                if k < 6:
                    Bc, Bck, Xc, Xck = Bn, Bnk, Xn, Xnk
            TmT, Tk = Pc, Pck
            if GDN_CUT == 1000 + 5:
                pass
            yield
            em.op('act', lambda E: E.activation(out=ecol, in_=gcc_[:, 0:4], func=AF.Exp), r=[gck], w=[smk])
            em.op('dve', lambda E: E.tensor_tensor(bg, beta, ecol, ALU.mult), r=[smk], w=[smk])
            em.op('dve', lambda E: E.tensor_tensor(ed, gcc_[:, 4:8], gcc_[:, 0:4], ALU.subtract), r=[gck], w=[smk])
            em.op('act', lambda E: E.activation(out=ed, in_=ed, func=AF.Exp), r=[smk], w=[smk])
            em.op('act', lambda E: E.activation(out=dec, in_=gcc_[:, 4:8], func=AF.Exp), r=[gck], w=[smk])
            vb_, vbk = vb.next()
            em.op('pool', lambda E: E.tensor_tensor(vb_[:], qkv_[:, 1024:1536].rearrange("p (h c) -> p h c", h=4),
                                                    V(beta[:, 0:1], [(1, 4), (0, 128)]), ALU.mult), r=[qk_, smk], w=[vbk])
            kd_, kdk = kdec.next()
            em.op('pool', lambda E: E.tensor_tensor(kd_[:], kn4, V(ed[:, 0:1], [(1, 4), (0, 128)]), ALU.mult), r=[qnk, smk], w=[kdk])
            qd_, qdk = qdT.next()
            em.op('dve', lambda E: E.tensor_tensor(qd_[:], qT4, er_[:], ALU.mult), r=[qTk, erk], w=[qdk])
            if GDN_CUT == 1000 + 6:
                pass
            yield
            psV, pVk = g.psum()
            for h in range(4):
                em.op('pe', lambda E: E.matmul(psV[:, h * 128:(h + 1) * 128], kT4[:, h, :], sb_[:, h, :], start=True, stop=False),
                      r=[qTk, sbk], w=[pVk])
                em.op('pe', lambda E: E.matmul(psV[:, h * 128:(h + 1) * 128], kT4[:, h, :], sl_[:, h, :], start=False, stop=True),
                      r=[qTk, slk], w=[pVk])
            yield
            t1_, t1k = t1b.next()
            em.op('dve', lambda E: E.tensor_tensor(t1_[:], psV[:].rearrange("p (h c) -> p h c", h=4), V(bg[:, 0:1], [(1, 4), (0, 128)]), ALU.mult),
                  r=[pVk, smk], w=[t1k])
            r_, rk_ = rr.next()
            em.op('pool', lambda E: E.tensor_tensor(r_[:], vb_[:], t1_[:], ALU.subtract), r=[vbk, t1k], w=[rk_])
            yield
            psN2, pN2k = g.psum()
            mm4(psN2, TmT, Tk, r_, rk_, pN2k)
            yield
            vn_, vnk = vnew.next()
            em.op('act', lambda E: E.copy(f2(vn_), psN2[:]), r=[pN2k], w=[vnk])
            yield
            psO, pOk = g.psum()
            for h in range(4):
                em.op('pe', lambda E: E.matmul(psO[:, h * 128:(h + 1) * 128], sb_[:, h, :], qd_[:, h, :], start=True, stop=False),
                      r=[sbk, qdk], w=[pOk])
                em.op('pe', lambda E: E.matmul(psO[:, h * 128:(h + 1) * 128], vn_[:, h, :], at_[:, h, :], start=False, stop=True),
                      r=[vnk, atk], w=[pOk])
            yield
            psS, pSk = g.psum()
            mm4(psS, kd_, kdk, vn_, vnk, pSk)
            em.op('pool', lambda E: E.tensor_tensor(S32_[:], S32_[:], V(dec[:, 0:1], [(1, 4), (0, 128)]), ALU.mult), r=[S32k, smk], w=[S32k])
            em.op('dve', lambda E: E.tensor_tensor(f2(S32_), f2(S32_), psS[:], ALU.add), r=[S32k, pSk], w=[S32k])
            sb_, sbk = Sbd[d].next()
            em.op('act', lambda E: E.copy(sb_[:], S32_[:]), r=[S32k], w=[sbk])
            sl_, slk = Slod[d].next()
            em.op('pool', lambda E: E.tensor_tensor(sl_[:], S32_[:], sb_[:], ALU.subtract), r=[S32k, sbk], w=[slk])
            if GDN_CUT == 1000 + 7:
                pass
            yield
            ofap = AP(tensor=g.OF.tensor, offset=g.OF.offset + r0, ap=[[Lmax, 128], [128 * Lmax, 4], [1, 128]])
            if first:
                of_, ofk = of32.next()
                em.op('dve', lambda E: E.tensor_copy(f2(of_), psO[:]), r=[pOk], w=[ofk])
                em.dma('pool', ofap, of_[:], r=[ofk], w=[('dram', 'OF')])
            else:
                of_, ofk = of32.next()
                em.dma('sp', of_[:], ofap, r=[('dram', 'OF')], w=[ofk])
                z_, zk = zT.next()
                em.dma('sp', z_[:], AP(tensor=g.ZT.tensor, offset=g.ZT.offset + r0, ap=[[Lmax, 128], [128 * Lmax, 4], [1, 128]]),
                       r=[('dram', 'ZT')], w=[zk])
                os_, osk = osum.next()
                em.op('dve', lambda E: E.tensor_tensor(f2(os_), psO[:], f2(of_), ALU.add), r=[pOk, ofk], w=[osk])
                oq_, oqk = osq.next()
                em.op('act', lambda E: E.activation(out=oq_[:], in_=os_[:], func=AF.Square), r=[osk], w=[oqk])
                psN, pNk = g.psum()
                em.op('pe', lambda E: E.matmul(psN[:], g.onesb[:], f2(oq_), start=True, stop=True), r=['onesb', oqk], w=[pNk])
                or_, ork = orst.next()
                rsqrt(em, f2(or_), psN[:], 128 * EPS, pNk, ork)
                em.op('dve', lambda E: E.tensor_tensor(or_[:], or_[:], os_[:], ALU.mult), r=[ork, osk], w=[ork])
                oo_, ook = oout.next()
                em.op('dve', lambda E: E.scalar_tensor_tensor(oo_[:], or_[:], gno[:, 0:1], z_[:], ALU.mult, ALU.mult),
                      r=[ork, 'gno', zk], w=[ook])
                em.dma('pool', AP(tensor=g.OT.tensor, offset=g.OT.offset + 1024 * Lmax + r0, ap=[[Lmax, 128], [128 * Lmax, 4], [1, 128]]),
                       oo_[:], r=[ook], w=[('dram', 'OT')])
            state[d] = (sb_, sbk, sl_, slk)

        for idx in range(nt):
            first = idx < nt - 1 - idx
            gens = [body(0, idx, first), body(1, nt - 1 - idx, first)]
            while gens:
                for gen in list(gens):
                    try:
                        next(gen)
                    except StopIteration:
                        gens.remove(gen)
        em.barrier()
    em.barrier()


def mixer_hyena(g, l, si):
    nc, em, ext = g.nc, g.em, g.ext
    L = g.seq_lens[si]
    Lmax = g.Lmax
    N = 2 * L
    N1 = N // 128
    P1 = L // 128
    R1 = min(N1, HY_RMAX)
    nch1 = max(1, N1 // HY_RMAX)
    CG = min(64, 4096 // N1)
    CS = 64
    ncol = CG * N1
    nchunk = ncol // 512
    cpb = 512 // N1
    nb1 = min(CG, 512 // (2 * N1))
    KF = g.KFq[L]
    KTd = g.KTd[L]
    TWO_PI = 2.0 * math.pi
    with ExitStack() as st:
        def sb(name, shape, dt):
            return st.enter_context(nc.sbuf_tensor(g.un(name), list(shape), dt))

        def T(name, shape, dt, n=2):
            return Rot([sb(f"h_{name}{i}", shape, dt) for i in range(n)], "h_" + name)
        F1 = sb("h_F1", [R1, nch1, 2 * N1], BF16)
        Twr = sb("h_Twr", [128, N1], F32)
        Twi = sb("h_Twi", [128, N1], F32)
        TwTr = sb("h_TwTr", [R1, nch1, 128], F32)
        TwTi = sb("h_TwTi", [R1, nch1, 128], F32)
        G1r = sb("h_G1r", [R1, nch1, P1], BF16)
        G1i = sb("h_G1i", [R1, nch1, P1], BF16)
        F2r = sb("h_F2r", [128, 128], BF16)
        F2i = sb("h_F2i", [128, 128], BF16)
        F2in = sb("h_F2in", [128, 128], BF16)
        G2a = sb("h_G2a", [128, 256], BF16)
        G2b = sb("h_G2b", [128, 256], BF16)
        skip = sb("h_skip", [128, 2, 512], F32)
        for t_, nm in ((F1, f"hyF1_{L}"), (TwTr, f"hyTwTr_{L}"), (TwTi, f"hyTwTi_{L}"), (G1r, f"hyG1r_{L}"), (G1i, f"hyG1i_{L}")):
            em.dma('sp', t_[:], ext["c_" + nm][:, :, :], w=['hconst'])
        for t_, nm in ((Twr, f"hyTwr_{L}"), (Twi, f"hyTwi_{L}"), (F2r, "hyF2r"), (F2i, "hyF2i"), (F2in, "hyF2in"), (G2a, "hyG2a"), (G2b, "hyG2b")):
            em.dma('sp', t_[:], ext["c_" + nm][:, :], w=['hconst'])
        em.dma('sp', skip[:].rearrange("p o c -> p (o c)"), drow_bc(ext["hy_skip"], l * 1024, 1024), w=['hconst'])
        HC = 'hconst'
        Apr = sb("h_Apr", [128, CG, N1], BF16)
        Api = sb("h_Api", [128, CG, N1], BF16)
        Zkr = sb("h_Zkr", [128, CG, N1], BF16)
        Zki = sb("h_Zki", [128, CG, N1], BF16)
        Bpr = sb("h_Bpr", [R1, nch1, CG, 128], BF16)
        Bpi = sb("h_Bpi", [R1, nch1, CG, 128], BF16)
        tP = T("tP", [128, 512], F32, 2)
        tQ = T("tQ", [128, 512], F32, 2)
        kfc = T("kfc", [128, 2, 512], BF16, 2)

        def cmul_tw(ps, pk, M, units, width, tr, ti, outr_fn, outi_fn, okeys):
            n = units * 2 * width
            src = V(ps[0:M, 0:1], [(2 * width, units), (width, 2), (1, width)])
            trb = V(tr, [(0, units), (0, 2), (1, width)])
            tib = V(ti, [(0, units), (0, 2), (1, width)])
            p_, pk_ = tP.next()
            q_, qk_ = tQ.next()
            pv = V(p_[0:M, 0:1], [(2 * width, units), (width, 2), (1, width)])
            qv = V(q_[0:M, 0:1], [(2 * width, units), (width, 2), (1, width)])
            em.op('dve', lambda E: E.tensor_tensor(pv, src, trb, ALU.mult), r=[pk, HC], w=[pk_])
            em.op('dve', lambda E: E.tensor_tensor(qv, src, tib, ALU.mult), r=[pk, HC], w=[qk_])
            pre = V(p_[0:M, 0:1], [(2 * width, units), (1, width)])
            pim = V(p_[0:M, 0:1], [(2 * width, units), (1, width)], off=width)
            qre = V(q_[0:M, 0:1], [(2 * width, units), (1, width)])
            qim = V(q_[0:M, 0:1], [(2 * width, units), (1, width)], off=width)
            em.op('pool', lambda E: E.tensor_tensor(outr_fn, pre, qim, ALU.subtract), r=[pk_, qk_], w=[okeys[0]])
            em.op('pool', lambda E: E.tensor_tensor(outi_fn, qre, pim, ALU.add), r=[pk_, qk_], w=[okeys[1]])

        def fwd_fft(lhs_fn, kchs):
            for c0 in range(0, CG, nb1):
                ps, pk = g.psum()
                for u in range(nb1):
                    for kc in range(kchs):
                        lap, lk = lhs_fn(c0 + u, kc)
                        K_ = lap.shape[0]
                        em.op('pe', lambda E: E.matmul(ps[:, u * 2 * N1:(u + 1) * 2 * N1], lap, F1[0:K_, kc, :],
                                                       start=(kc == 0), stop=(kc == kchs - 1)), r=[lk, HC], w=[pk])
                cmul_tw(ps, pk, 128, nb1, N1, Twr[:, :], Twi[:, :], Apr[:, c0:c0 + nb1, :], Api[:, c0:c0 + nb1, :], ['Apr', 'Api'])

        def stage3(j):
            cols = slice(j * 512, (j + 1) * 512)
            ar = Apr[:].rearrange("p c f -> p (c f)")[:, cols]
            ai = Api[:].rearrange("p c f -> p (c f)")[:, cols]
            pzr, pzrk = g.psum()
            em.op('pe', lambda E: E.matmul(pzr[:], F2r[:], ar, start=True, stop=False), r=['Apr', HC], w=[pzrk])
            em.op('pe', lambda E: E.matmul(pzr[:], F2in[:], ai, start=False, stop=True), r=['Api', HC], w=[pzrk])
            pzi, pzik = g.psum()
            em.op('pe', lambda E: E.matmul(pzi[:], F2i[:], ar, start=True, stop=False), r=['Apr', HC], w=[pzik])
            em.op('pe', lambda E: E.matmul(pzi[:], F2r[:], ai, start=False, stop=True), r=['Api', HC], w=[pzik])
            return pzr, pzrk, pzi, pzik

        if g.filt_done.get(L) != l:
            g.filt_done[L] = l
            with ExitStack() as st2:
                def sb2(name, shape, dt):
                    return st2.enter_context(nc.sbuf_tensor(g.un(name), list(shape), dt))
                w1 = sb2("hf_w1", [33, 64], F32)
                w2 = sb2("hf_w2", [64, 64], F32)
                w3 = sb2("hf_w3", [64, 2048], F32)
                pc = sb2("hf_pc", [64, 8], F32)
                dl = sb2("hf_dl", [128, 512], F32)
                tneg = sb2("hf_tneg", [128, N1], F32)
                ones32 = sb2("hf_ones", [128, 128], F32)
                invn = sb2("hf_invn", [128, 2, 512], F32)
                em.dma('sp', w1[:], ext["hy_ffn_w1"][l], w=['fw'])
                em.dma('sp', w2[:], ext["hy_ffn_w2"][l], w=['fw'])
                em.dma('sp', w3[:], ext["hy_ffn_w3"][l], w=['fw'])
                em.dma('sp', pc[:, 0:1], dcol(ext["hy_ffn_b1"], l * 64, 1, nparts=64), w=['fpc'], slow=True)
                em.dma('sp', pc[:, 1:2], dcol(ext["hy_sin_freq"], l * 64, 1, nparts=64), w=['fpc'], slow=True)
                em.dma('sp', pc[:, 2:3], dcol(ext["hy_ffn_b2"], l * 64, 1, nparts=64), w=['fpc'], slow=True)
                em.dma('sp', dl[:], ext["c_hydelta"][:, :], w=['fw'])
                em.dma('sp', tneg[:], ext[f"c_hytneg_{L}"][:, :], w=['fw'])
                em.dma('sp', ones32[:], ext["c_ones"][:, :], w=['fw'])
                em.op('dve', lambda E: E.tensor_scalar(pc[:, 1:2], pc[:, 1:2], 1.0 / TWO_PI, None, ALU.mult), r=['fpc'], w=['fpc'])
                em.op('pool', lambda E: E.memset(pc[:, 3:4], -math.pi), w=['fpc'])
                ft = Rot([sb2(f"hf_ft{i}", [33, 512], F32) for i in range(2)], "hf_ft")
                hh1 = Rot([sb2(f"hf_h1{i}", [64, 512], F32) for i in range(2)], "hf_h1")
                hh2 = Rot([sb2(f"hf_h2{i}", [64, 512], F32) for i in range(2)], "hf_h2")
                dk = Rot([sb2(f"hf_dk{i}", [128, 512], F32) for i in range(2)], "hf_dk")
                kf32 = Rot([sb2(f"hf_k32{i}", [128, 512], F32) for i in range(2)], "hf_k32")
                kab = Rot([sb2(f"hf_kab{i}", [128, 512], F32) for i in range(2)], "hf_kab")
                kbf = Rot([sb2(f"hf_kbf{i}", [128, 2, 512], BF16) for i in range(2)], "hf_kbf")
                feat = ext[f"c_hyfeat_{L}"]
                msk = sb2("hf_msk", [64, 512], F32)

                def sin_layer(ps, pk, bcol, out, ok):
                    em.op('dve', lambda E: E.tensor_scalar(out, ps, pc[:, bcol:bcol + 1], pc[:, 1:2], ALU.add, ALU.mult), r=[pk, 'fpc'], w=[ok])
                    for _ in range(2):
                        em.op('dve', lambda E: E.tensor_scalar(msk[:], out, 0.5, None, ALU.is_gt), r=[ok], w=['msk'])
                        em.op('dve', lambda E: E.tensor_tensor(out, out, msk[:], ALU.subtract), r=[ok, 'msk'], w=[ok])
                        em.op('dve', lambda E: E.tensor_scalar(msk[:], out, -0.5, None, ALU.is_lt), r=[ok], w=['msk'])
                        em.op('dve', lambda E: E.tensor_tensor(out, out, msk[:], ALU.add), r=[ok, 'msk'], w=[ok])
                    em.op('act', lambda E: E.activation(out=out, in_=out, func=AF.Sin, scale=TWO_PI), r=[ok], w=[ok])

                pn = [g.psum([6]), g.psum([7])]
                ntile_tot = N // 128
                for blk in range(N // 512):
                    f_, fk = ft.next()
                    em.dma('sp', f_[:], feat[:, blk * 512:(blk + 1) * 512], w=[fk])
                    ps, pk = g.psum(range(6))
                    em.op('pe', lambda E: E.matmul(ps[0:64, :], w1[:], f_[:], start=True, stop=True), r=['fw', fk], w=[pk])
                    h1, h1k = hh1.next()
                    sin_layer(ps[0:64, :], pk, 0, h1[:], h1k)
                    ps, pk = g.psum(range(6))
                    em.op('pe', lambda E: E.matmul(ps[0:64, :], w2[:], h1[:], start=True, stop=True), r=['fw', h1k], w=[pk])
                    h2, h2k = hh2.next()
                    sin_layer(ps[0:64, :], pk, 2, h2[:], h2k)
                    for tt in range(4):
                        ti_ = blk * 4 + tt
                        dirn = 0 if ti_ * 128 < L else 1
                        d_, dkk = dk.next()
                        em.op('act', lambda E: E.activation(out=d_[:], in_=dl[:], func=AF.Exp, scale=tneg[:, ti_:ti_ + 1]), r=['fw'], w=[dkk])
                        kb_, kbk = kbf.next()
                        for o in range(2):
                            ps, pk = g.psum(range(6))
                            col0 = o * 1024 + dirn * 512
                            em.op('pe', lambda E: E.matmul(ps[:], h2[:, tt * 128:(tt + 1) * 128], w3[:, col0:col0 + 512], start=True, stop=True),
                                  r=[h2k, 'fw'], w=[pk])
                            k_, kk = kf32.next()
                            em.op('dve', lambda E: E.tensor_tensor(k_[:], ps[:], d_[:], ALU.mult), r=[pk, dkk], w=[kk])
                            if ti_ * 128 == L:
                                em.op('pool', lambda E: E.memset(k_[0:1, :], 0.0), w=[kk])
                            a_, ak = kab.next()
                            em.op('act', lambda E: E.activation(out=a_[:], in_=k_[:], func=AF.Abs), r=[kk], w=[ak])
                            em.op('pe', lambda E: E.matmul(pn[o][0][:], ones32[:], a_[:], start=(ti_ == 0), stop=(ti_ == ntile_tot - 1)),
                                  r=['fw', ak], w=[pn[o][1]])
                            em.op('act', lambda E: E.copy(kb_[:, o, :], k_[:]), r=[kk], w=[kbk])
                        em.dma('pool', KTd[ti_ * 128:(ti_ + 1) * 128, :], kb_[:].rearrange("p o c -> p (o c)"), r=[kbk], w=[('dram', 'KTd')])
                for o in range(2):
                    em.op('dve', lambda E: E.reciprocal(invn[:, o, :], pn[o][0][:]), r=[pn[o][1]], w=['invn'])
                ktd = Rot([sb2(f"hf_ktd{i}", [R1, nch1, 128, CG], BF16) for i in range(2)], "hf_ktd")
                for o in range(2):
                    for gi in range(512 // CG):
                        cbase = gi * CG
                        kt_, ktk = ktd.next()
                        for kc in range(nch1):
                            em.dma('sp', kt_[:, kc], AP(tensor=KTd.tensor, offset=KTd.offset + kc * R1 * 128 * 1024 + o * 512 + cbase,
                                                        ap=[[128 * 1024, R1], [1024, 128], [1, CG]]), r=[('dram', 'KTd')], w=[ktk])
                        fwd_fft(lambda c, kc: (V(kt_[:, kc, 0, 0:1], [(CG, 128)], off=c), ktk), nch1)
                        for j in range(nchunk):
                            pzr, pzrk, pzi, pzik = stage3(j)
                            kc_, kck = kfc.next()
                            inb = V(invn[:, o, cbase + j * cpb:cbase + j * cpb + 1], [(1, cpb), (0, N1)])
                            em.op('dve', lambda E: E.tensor_tensor(kc_[:, 0, :].rearrange("p (c f) -> p c f", c=cpb),
                                                                   pzr[:].rearrange("p (c f) -> p c f", c=cpb), inb, ALU.mult),
                                  r=[pzrk, 'invn'], w=[kck])
                            em.op('dve', lambda E: E.tensor_tensor(kc_[:, 1, :].rearrange("p (c f) -> p c f", c=cpb),
                                                                   pzi[:].rearrange("p (c f) -> p c f", c=cpb), inb, ALU.mult),
                                  r=[pzik, 'invn'], w=[kck])
                            em.dma('pool', KF[o, gi, j].rearrange("p a c -> p (a c)"), kc_[:].rearrange("p a c -> p (a c)"), r=[kck],
                                   w=[('dram', 'KF')])
            em.barrier()
        xs = {nm: T(nm, [P1, 128, CS], BF16, 1) for nm in ("x1s", "x2s", "vs")}
        z1 = sb("h_z1", [P1, 128, CG], BF16)
        z2s = sb("h_z2s", [P1, 128, CS], BF16)
        gt1 = T("gt1", [P1, 128, 4], F32, 2)
        gt2 = T("gt2", [P1, 128, 4], F32, 2)
        for sl in range(512 // CS):
            tiles = {}
            for i, nm in enumerate(("x1s", "x2s", "vs")):
                t_, tk = xs[nm].next()
                src_ = AP(tensor=g.PH.tensor, offset=g.PH.offset + i * 512 + sl * CS, ap=[[128 * 1536, P1], [1536, 128], [1, CS]])
                nsp = 2 if P1 >= 128 else 1
                for sp_ in range(nsp):
                    a0, a1 = sp_ * P1 // nsp, (sp_ + 1) * P1 // nsp
                    em.dma('sp', psplit(t_[:], a0, a1), psplit(src_, a0, a1), r=[('dram', 'PH')], w=[tk])
                tiles[nm] = (t_, tk)
            for sg_ in range(CS // CG):
                gi = sl * (CS // CG) + sg_
                cb = sg_ * CG
                cglob = gi * CG
                zsrc, zk = tiles["vs"]
                zoff, zstride = cb, CS
                for o in range(2):
                    gate, gk = tiles["x1s" if o == 0 else "x2s"]
                    zt, ztk, zo, zs_ = zsrc, zk, zoff, zstride
                    fwd_fft(lambda c, kc: (V(zt[:, 0, 0:1], [(zs_, 128)], off=zo + c), ztk), 1)
                    for j in range(nchunk):
                        pzr, pzrk, pzi, pzik = stage3(j)
                        kc_, kck = kfc.next()
                        em.dma('sp', kc_[:].rearrange("p a c -> p (a c)"), KF[o, gi, j].rearrange("p a c -> p (a c)"), r=[('dram', 'KF')], w=[kck])
                        cols = slice(j * 512, (j + 1) * 512)
                        zr_out = Zkr[:].rearrange("p c f -> p (c f)")[:, cols]
                        zi_out = Zki[:].rearrange("p c f -> p (c f)")[:, cols]
                        a_, ak = tP.next()
                        b_, bk = tQ.next()
                        em.op('dve', lambda E: E.tensor_tensor(a_[:], pzr[:], kc_[:, 0, :], ALU.mult), r=[pzrk, kck], w=[ak])
                        em.op('dve', lambda E: E.tensor_tensor(b_[:], pzi[:], kc_[:, 1, :], ALU.mult), r=[pzik, kck], w=[bk])
                        em.op('pool', lambda E: E.tensor_tensor(zr_out, a_[:], b_[:], ALU.subtract), r=[ak, bk], w=['Zkr'])
                        a_, ak = tP.next()
                        b_, bk = tQ.next()
                        em.op('dve', lambda E: E.tensor_tensor(a_[:], pzr[:], kc_[:, 1, :], ALU.mult), r=[pzrk, kck], w=[ak])
                        em.op('dve', lambda E: E.tensor_tensor(b_[:], pzi[:], kc_[:, 0, :], ALU.mult), r=[pzik, kck], w=[bk])
                        em.op('pool', lambda E: E.tensor_tensor(zi_out, a_[:], b_[:], ALU.add), r=[ak, bk], w=['Zki'])
                    for fc in range(nch1):
                        for c0 in range(0, CG, 2):
                            ps, pk = g.psum()
                            for u in range(2):
                                c = c0 + u
                                em.op('pe', lambda E: E.matmul(ps[0:R1, u * 256:(u + 1) * 256], Zkr[:, c, fc * R1:(fc + 1) * R1], G2a[:],
                                                               start=True, stop=False), r=['Zkr', HC], w=[pk])
                                em.op('pe', lambda E: E.matmul(ps[0:R1, u * 256:(u + 1) * 256], Zki[:, c, fc * R1:(fc + 1) * R1], G2b[:],
                                                               start=False, stop=True), r=['Zki', HC], w=[pk])
                            cmul_tw(ps, pk, R1, 2, 128, TwTr[:, fc, :], TwTi[:, fc, :], Bpr[:, fc, c0:c0 + 2, :], Bpi[:, fc, c0:c0 + 2, :],
                                    ['Bpr', 'Bpi'])
                    zn_, znk = (z1, 'z1') if o == 0 else (z2s, 'z2s')
                    zc0 = 0 if o == 0 else cb
                    for c0 in range(0, CG, 4):
                        ps, pk = g.psum()
                        n = 0
                        for fc in range(nch1):
                            for (gm, bp, bpk) in ((G1r, Bpr, 'Bpr'), (G1i, Bpi, 'Bpi')):
                                em.op('pe', lambda E: E.matmul(ps[0:P1, :], gm[:, fc, :], bp[:, fc, c0:c0 + 4, :].rearrange("p c t -> p (c t)"),
                                                               start=(n == 0), stop=(n == 2 * nch1 - 1)), r=[bpk, HC], w=[pk])
                                n += 1
                        yv = V(ps[0:P1, 0:1], [(1, 128), (128, 4)])
                        zv = V(zt[:, 0, 0:1], [(zs_, 128), (1, 4)], off=zo + c0)
                        gv = gate[:, :, cb + c0:cb + c0 + 4]
                        skb = V(skip[0:P1, o, cglob + c0:cglob + c0 + 1], [(0, 128), (1, 4)])
                        t1_, t1k = gt1.next()
                        em.op('pool', lambda E: E.tensor_tensor(t1_[:], zv, skb, ALU.mult), r=[ztk, HC], w=[t1k])
                        t2_, t2k = gt2.next()
                        em.op('dve', lambda E: E.tensor_tensor(t2_[:], yv, t1_[:], ALU.add), r=[pk, t1k], w=[t2k])
                        em.op('pool', lambda E: E.tensor_tensor(zn_[:, :, zc0 + c0:zc0 + c0 + 4], t2_[:], gv, ALU.mult), r=[t2k, gk], w=[znk])
                    if o == 0:
                        zsrc, zk, zoff, zstride = z1, 'z1', 0, CG
            dst_ = AP(tensor=g.Z2.tensor, offset=g.Z2.offset + sl * CS, ap=[[128 * 512, P1], [512, 128], [1, CS]])
            nsp = 2 if P1 >= 128 else 1
            for sp_ in range(nsp):
                a0, a1 = sp_ * P1 // nsp, (sp_ + 1) * P1 // nsp
                em.dma('pool', psplit(dst_, a0, a1), psplit(z2s[:], a0, a1), r=['z2s'], w=[('dram', 'Z2')])
        em.barrier()
        zl = T("zl", [128, 512], BF16, 2)
        zo_ = T("zo", [128, 4, 128], BF16, 2)
        for t in range(L // 128):
            a_, ak = zl.next()
            em.dma('sp', a_[:], g.Z2[t * 128:(t + 1) * 128, :], r=[('dram', 'Z2')], w=[ak])
            ps, pk = g.psum()
            for c in range(4):
                em.op('pe', lambda E: E.matmul(ps[:, c * 128:(c + 1) * 128], a_[:, c * 128:(c + 1) * 128], g.identb[:], start=True, stop=True),
                      r=[ak, 'identb'], w=[pk])
            o_, ok = zo_.next()
            em.op('act' if t % 2 else 'dve', lambda E: (E.copy if t % 2 else E.tensor_copy)(o_[:].rearrange("p c t -> p (c t)"), ps[:]),
                  r=[pk], w=[ok])
            em.dma('pool', AP(tensor=g.OT.tensor, offset=g.OT.offset + t * 128, ap=[[Lmax, 128], [128 * Lmax, 4], [1, 128]]), o_[:],
                   r=[ok], w=[('dram', 'OT')])
    em.barrier()


def stage_C(g, l, si):
    nc, em, ext = g.nc, g.em, g.ext
    L = g.seq_lens[si]
    t0 = g.seq_off[si]
    Ttot, Lmax = g.Ttot, g.Lmax
    nb = L // 512
    with ExitStack() as st:
        def sb(name, shape, dt):
            return st.enter_context(nc.sbuf_tensor(g.un(name), list(shape), dt))
        g32 = sb("c_g32", [128, 3, KC], F32)
        em.op('dve', lambda E: E.tensor_scalar(g32[:], g.gains[:, 1:4, l, :], 32.0, None, ALU.mult), r=['gains'], w=['g32'])
        xT = sb("c_xT", [128, KC, 512], F32)
        yT = sb("c_yT", [128, KC, 512], F32)
        OTb = sb("c_OTb", [128, 12, 512], BF16)
        Gj = Rot([sb(f"c_Gj{i}", [128, 3, 512], BF16) for i in range(2)], "c_Gj")
        wbr = Rot([sb(f"c_wbr{i}", [128, 12, 128], BF16) for i in range(2)], "c_wbr")
        wout = Rot([sb(f"c_wout{i}", [128, KC, 128], BF16) for i in range(2)], "c_wout")
        wgu = Rot([sb(f"c_wgu{i}", [128, 2, KC, 128], BF16) for i in range(3)], "c_wgu")
        wd = Rot([sb(f"c_wd{i}", [128, NFF, 128], BF16) for i in range(2)], "c_wd")
        tmp = Rot([sb(f"c_tmp{i}", [128, 512], F32) for i in range(4)], "c_tmp")
        sgt = Rot([sb(f"c_sgt{i}", [128, 512], BF16) for i in range(2)], "c_sgt")
        mT = sb("c_mT", [128, KC, 512], BF16)
        sqb = sb("c_sqb", [128, KC, 512], BF16)
        h2T = sb("c_h2T", [128, KC, 512], BF16)
        actT = sb("c_actT", [128, NFF, 512], BF16)
        rstd = Rot([sb(f"c_rstd{i}", [128, 512], F32) for i in range(2)], "c_rstd")
        eng2 = ['dve', 'pool']

        def norm_rstd(sqk_):
            ps, pk = g.psum()
            for kc in range(KC):
                em.op('pe', lambda E: E.matmul(ps[:], g.onesb[:], sqb[:, kc, :], start=(kc == 0), stop=(kc == KC - 1)),
                      r=[sqk_, 'onesb'], w=[pk])
            rs, rsk = rstd.next()
            rsqrt(em, rs[:], ps[:], D * EPS, pk, rsk)
            return rs, rsk

        for b in range(nb):
            c0 = t0 + b * 512
            s0 = b * 512
            em.dma('sp', xT[:], AP(tensor=g.XT.tensor, offset=g.XT.offset + c0, ap=[[Ttot, 128], [128 * Ttot, KC], [1, 512]]),
                   r=[('dram', 'XT')], w=['xT'])
            em.dma('sp', OTb[:], AP(tensor=g.OT.tensor, offset=g.OT.offset + s0, ap=[[Lmax, 128], [128 * Lmax, 12], [1, 512]]),
                   r=[('dram', 'OT')], w=['OTb'])
            for j in range(8):
                w_, wk = wbr.next()
                em.dma('sp', w_[:].rearrange("p k c -> p (k c)"), g.WBR[l][j].rearrange("p k c -> p (k c)"),
                       r=[('dram', f"WBR{l}")], w=[wk])
                gj, gk = Gj.next()
                em.dma('sp', gj[:], AP(tensor=g.GT.tensor, offset=g.GT.offset + j * 128 * Lmax + s0,
                                       ap=[[Lmax, 128], [1024 * Lmax, 3], [1, 512]]), r=[('dram', 'GT')], w=[gk])
                ts = []
                for i in range(3):
                    ps, pk = g.psum()
                    for kc in range(4):
                        em.op('pe', lambda E: E.matmul(ps[:], w_[:, i * 4 + kc, :], OTb[:, i * 4 + kc, :], start=(kc == 0), stop=(kc == 3)),
                              r=[wk, 'OTb'], w=[pk])
                    t_, tk = tmp.next()
                    em.op('dve', lambda E: E.tensor_tensor(t_[:], ps[:], gj[:, i, :], ALU.mult), r=[pk, gk], w=[tk])
                    ts.append((t_, tk))
                em.op('pool', lambda E: E.tensor_tensor(ts[0][0][:], ts[0][0][:], ts[1][0][:], ALU.add), r=[ts[0][1], ts[1][1]], w=[ts[0][1]])
                em.op('pool', lambda E: E.tensor_tensor(mT[:, j, :], ts[0][0][:], ts[2][0][:], ALU.add), r=[ts[0][1], ts[2][1]], w=['mT'])
            if C_CUT == 1:
                continue
            for j in range(8):
                w_, wk = wout.next()
                em.dma('sp', w_[:].rearrange("p k c -> p (k c)"), g.WOUT[l][j].rearrange("p k c -> p (k c)"),
                       r=[('dram', f"WOUT{l}")], w=[wk])
                ps, pk = g.psum()
                for kc in range(KC):
                    em.op('pe', lambda E: E.matmul(ps[:], w_[:, kc, :], mT[:, kc, :], start=(kc == 0), stop=(kc == KC - 1)),
                          r=[wk, 'mT'], w=[pk])
                em.op('dve', lambda E: E.tensor_copy(yT[:, j, :], ps[:]), r=[pk], w=['yT'])
                em.op('act', lambda E: E.activation(out=sqb[:, j, :], in_=yT[:, j, :], func=AF.Square), r=['yT'], w=['sqb'])
            rs, rsk = norm_rstd('sqb')
            for j in range(8):
                t_, tk = tmp.next()
                em.op('dve', lambda E: E.scalar_tensor_tensor(t_[:], yT[:, j, :], g32[:, 0, j:j + 1], rs[:], ALU.mult, ALU.mult),
                      r=['yT', rsk, 'g32'], w=[tk])
                em.op('pool', lambda E: E.tensor_tensor(xT[:, j, :], t_[:], xT[:, j, :], ALU.add), r=[tk, 'xT'], w=['xT'])
            if C_CUT == 2:
                continue
            em.op('act', lambda E: E.activation(out=sqb[:], in_=xT[:], func=AF.Square), r=['xT'], w=['sqb'])
            rs, rsk = norm_rstd('sqb')
            for j in range(8):
                em.op('dve', lambda E: E.scalar_tensor_tensor(h2T[:, j, :], xT[:, j, :], g32[:, 1, j:j + 1], rs[:], ALU.mult, ALU.mult),
                      r=['xT', rsk, 'g32'], w=['h2T'])
            for f in range(NFF):
                w_, wk = wgu.next()
                em.dma('sp', w_[:, 0].rearrange("p k c -> p (k c)"), g.WG[l][f].rearrange("p k c -> p (k c)"),
                       r=[('dram', f"WG{l}")], w=[wk])
                em.dma('sp', w_[:, 1].rearrange("p k c -> p (k c)"), g.WU[l][f].rearrange("p k c -> p (k c)"),
                       r=[('dram', f"WU{l}")], w=[wk])
                psg, pgk = g.psum()
                for kc in range(KC):
                    em.op('pe', lambda E: E.matmul(psg[:], w_[:, 0, kc, :], h2T[:, kc, :], start=(kc == 0), stop=(kc == KC - 1)),
                          r=[wk, 'h2T'], w=[pgk])
                psu, puk = g.psum()
                for kc in range(KC):
                    em.op('pe', lambda E: E.matmul(psu[:], w_[:, 1, kc, :], h2T[:, kc, :], start=(kc == 0), stop=(kc == KC - 1)),
                          r=[wk, 'h2T'], w=[puk])
                sg, sgk = sgt.next()
                em.op('act', lambda E: E.activation(out=sg[:], in_=psg[:], func=AF.Silu), r=[pgk], w=[sgk])
                em.op('dve', lambda E: E.tensor_tensor(actT[:, f, :], psu[:], sg[:], ALU.mult), r=[puk, sgk], w=['actT'])
            for j in range(8):
                w_, wk = wd.next()
                em.dma('sp', w_[:].rearrange("p k c -> p (k c)"), g.WD[l][j].rearrange("p k c -> p (k c)"),
                       r=[('dram', f"WD{l}")], w=[wk])
                ps, pk = g.psum()
                for f in range(NFF):
                    em.op('pe', lambda E: E.matmul(ps[:], w_[:, f, :], actT[:, f, :], start=(f == 0), stop=(f == NFF - 1)),
                          r=[wk, 'actT'], w=[pk])
                em.op('dve', lambda E: E.tensor_copy(yT[:, j, :], ps[:]), r=[pk], w=['yT'])
                em.op('act', lambda E: E.activation(out=sqb[:, j, :], in_=yT[:, j, :], func=AF.Square), r=['yT'], w=['sqb'])
            rs, rsk = norm_rstd('sqb')
            for j in range(8):
                t_, tk = tmp.next()
                em.op('dve', lambda E: E.scalar_tensor_tensor(t_[:], yT[:, j, :], g32[:, 2, j:j + 1], rs[:], ALU.mult, ALU.mult),
                      r=['yT', rsk, 'g32'], w=[tk])
                em.op('pool', lambda E: E.tensor_tensor(xT[:, j, :], t_[:], xT[:, j, :], ALU.add), r=[tk, 'xT'], w=['xT'])
            em.dma('pool', AP(tensor=g.XT.tensor, offset=g.XT.offset + c0, ap=[[Ttot, 128], [128 * Ttot, KC], [1, 512]]), xT[:],
                   r=['xT'], w=[('dram', 'XT')])
    em.barrier()


def stage_out(g):
    nc, em = g.nc, g.em
    with ExitStack() as st:
        def sb(name, shape, dt):
            return st.enter_context(nc.sbuf_tensor(g.un(name), list(shape), dt))
        xin_ = Rot([sb(f"o_x{i}", [128, KC, 128], F32) for i in range(2)], "o_x")
        xo = Rot([sb(f"o_y{i}", [128, D], F32) for i in range(2)], "o_y")
        for t in range(g.Ttot // 128):
            x_, xk = xin_.next()
            em.dma('sp', x_[:], AP(tensor=g.XT.tensor, offset=g.XT.offset + t * 128, ap=[[g.Ttot, 128], [128 * g.Ttot, KC], [1, 128]]),
                   r=[('dram', 'XT')], w=[xk])
            o_, ok = xo.next()
            for hlf in range(2):
                ps, pk = g.psum()
                for c in range(4):
                    kc = hlf * 4 + c
                    em.op('pe', lambda E: E.transpose(ps[:, c * 128:(c + 1) * 128], x_[:, kc, :], g.ident[:]), r=[xk, 'ident'], w=[pk])
                if hlf == 0:
                    em.op('dve', lambda E: E.tensor_copy(o_[:, 0:512], ps[:]), r=[pk], w=[ok])
                else:
                    em.op('dve', lambda E: E.tensor_copy(o_[:, 512:1024], ps[:]), r=[pk], w=[ok])
            em.dma('pool', g.y[t * 128:(t + 1) * 128, :], o_[:], r=[ok], w=[('dram', 'y')])
    em.barrier()


def make_in_map(x, w, consts):
    m = {"x": np.ascontiguousarray(x, np.float32)}
    for k, v in w.items():
        v = np.asarray(v, np.float32)
        if k in ("gdn_a_log", "gdn_dt_bias"):
            v = v.reshape(DEPTH, 8)
        if k == "w_branch":
            v = v.reshape(DEPTH, 1536, D)
        m[k] = np.ascontiguousarray(v)
    for k, v in consts.items():
        m["c_" + k] = v
    return m


_CACHE = {}


def kernel(x_prompt, x_sample, **w):
    x_prompt = np.asarray(x_prompt, np.float32)
    x_sample = np.asarray(x_sample, np.float32)
    n_cores = 8
    Bp, Lp, _ = x_prompt.shape
    Bs, Ls_, _ = x_sample.shape
    per = Bp // n_cores
    seq_lens = [Lp] * per + [Ls_]
    key = tuple(seq_lens)
    if key not in _CACHE:
        _CACHE[key] = build(seq_lens)
    nc, consts = _CACHE[key]
    in_maps = []
    for c in range(n_cores):
        xs = [x_prompt[c * per + j] for j in range(per)] + [x_sample[c % Bs]]
        in_maps.append(make_in_map(np.concatenate(xs, 0), w, consts))
    res = run_bass_kernel_spmd(nc, in_maps, core_ids=list(range(n_cores)))
    y_prompt = np.empty_like(x_prompt)
    y_sample = np.empty_like(x_sample)
    for c in range(n_cores):
        y = np.asarray(res.results[c]["y"], np.float32)
        for j in range(per):
            y_prompt[c * per + j] = y[j * Lp:(j + 1) * Lp]
        if c < Bs:
            y_sample[c] = y[per * Lp:per * Lp + Ls_]
    return (y_prompt, y_sample)
```

```python
import math
from contextlib import ExitStack
import numpy as np
import ml_dtypes
import concourse.bass as bass
import concourse.mybir as mybir
from concourse.bass_utils import run_bass_kernel_spmd

F32 = mybir.dt.float32
BF16 = mybir.dt.bfloat16
AF = mybir.ActivationFunctionType
ALU = mybir.AluOpType
AX = mybir.AxisListType
AP = bass.AP

D = 1024
KC = 8
DEPTH = 2
D_IN = 7120
D_FF = 2816
NFF = 22
EPS = 1e-6
C_HY, C_QL, C_CKV, C_KPE, C_GQKV, C_GZ, C_GB, C_GA, C_GATE = 0, 1536, 1792, 1920, 1984, 3520, 4032, 4040, 4048
NDS = 8
HY_RMAX = 128
GDN_CUT = 0
C_CUT = 0


class Em:
    def __init__(s, nc, es):
        s.nc = nc
        s.E = {'pe': nc.tensor, 'act': nc.scalar, 'dve': nc.vector, 'pool': nc.gpsimd, 'sp': nc.sync}
        s.sem = {}
        s.cnt = {}
        for e in s.E:
            s.sem[e] = es.enter_context(nc.semaphore("c_" + e))
            s.cnt[e] = 0
        s.seen = {e: {} for e in s.E}
        s.W = {}
        s.R = {}
        s.dq = {}
        for q in ('sp', 'pool'):
            s.dq[q] = 0
            for i in range(NDS):
                s.sem[(q, i)] = es.enter_context(nc.semaphore(f"d_{q}{i}"))
                s.cnt[(q, i)] = 0
        s.psn = 0
        s.ninst = 0

    def _wait(s, e, sk, v):
        if v <= 0:
            return
        if s.seen[e].get(sk, 0) < v:
            s.E[e].wait_ge(s.sem[sk], v)
            s.seen[e][sk] = v
            s.ninst += 1

    def _deps(s, e, r, w, pe_acc=False):
        for k in r:
            for sk, v in s.W.get(k, {}).items():
                s._wait(e, sk, v)
        for k in w:
            for sk, v in s.W.get(k, {}).items():
                if pe_acc and sk == 'pe':
                    continue
                s._wait(e, sk, v)
            for sk, v in s.R.get(k, {}).items():
                s._wait(e, sk, v)

    def _mark(s, tok, r, w):
        sk, v = tok
        for k in w:
            s.W.setdefault(k, {})[sk] = v
            s.R[k] = {}
        for k in r:
            s.R.setdefault(k, {})[sk] = v

    def op(s, e, fn, r=(), w=()):
        s._deps(e, r, w, pe_acc=(e == 'pe'))
        ins = fn(s.E[e])
        ins.then_inc(s.sem[e], 1)
        s.cnt[e] += 1
        s.ninst += 1
        s._mark((e, s.cnt[e]), r, w)

    def dma(s, q, out, in_, r=(), w=(), slow=False):
        s._deps(q, r, w)
        i = s.dq[q] % NDS
        s.dq[q] += 1
        sk = (q, i)
        s._wait(q, sk, s.cnt[sk])
        if slow:
            s.E[q].dma_start(out=out, in_=in_, allow_slow_non_contiguous=True).then_inc(s.sem[sk], 16)
        else:
            s.E[q].dma_start(out=out, in_=in_).then_inc(s.sem[sk], 16)
        s.cnt[sk] += 16
        s.ninst += 1
        s._mark((sk, s.cnt[sk]), r, w)

    def barrier(s):
        for e in s.E:
            for sk, v in s.cnt.items():
                if sk != e:
                    s._wait(e, sk, v)
        s.W.clear()
        s.R.clear()


def V(ap, dims, off=0):
    a = ap.ap
    return AP(tensor=ap.tensor, offset=ap.offset + off, ap=[list(a[0])] + [[st, n] for st, n in dims])


def rsqrt(em, out, src, addc, rk, wk):
    em.op('act', lambda E: E.activation(out=out, in_=src, func=AF.Ln, bias=float(addc), scale=1.0), r=[rk], w=[wk])
    em.op('act', lambda E: E.activation(out=out, in_=out, func=AF.Exp, scale=-0.5), r=[wk], w=[wk])


def psplit(ap, p0, p1):
    a = [list(d) for d in ap.ap]
    off = ap.offset + p0 * a[0][0]
    a[0][1] = p1 - p0
    return AP(tensor=ap.tensor, offset=off, ap=a)


def bf(x):
    return np.asarray(x, np.float32).astype(ml_dtypes.bfloat16)


def make_consts(Ls):
    c = {}
    c["ident"] = np.eye(128, dtype=np.float32)
    c["identb"] = bf(np.eye(128))
    c["onesb"] = bf(np.ones((128, 128)))
    c["ones"] = np.ones((128, 128), np.float32)
    i = np.arange(128)
    c["U_le"] = (i[:, None] <= i[None, :]).astype(np.float32)
    c["U_ge"] = (i[:, None] >= i[None, :]).astype(np.float32)
    c["SL"] = (i[None, :] < i[:, None]).astype(np.float32)
    c["SU"] = (i[None, :] > i[:, None]).astype(np.float32)
    Rm = np.zeros((64, 64), np.float32)
    for d in range(32):
        Rm[d + 32, d] = -1.0
        Rm[d, d + 32] = 1.0
    c["rope_R"] = bf(Rm)
    for L in sorted(set(Ls)):
        half = 32
        inv = 10000.0 ** (-np.arange(half, dtype=np.float32) / half)
        ang = np.arange(L, dtype=np.float32)[:, None] * inv[None, :]
        cs = np.cos(ang).astype(np.float32).T
        sn = np.sin(ang).astype(np.float32).T
        c[f"ropecos{L}"] = np.concatenate([cs, cs], 0).astype(np.float32)
        c[f"ropesin{L}"] = np.concatenate([sn, sn], 0).astype(np.float32)
        N = 2 * L
        N1 = N // 128
        P1 = L // 128
        R1 = min(N1, HY_RMAX)
        nch1 = max(1, N1 // HY_RMAX)
        s1 = np.arange(N1, dtype=np.float64)
        f1 = np.arange(N1, dtype=np.float64)
        a = 2 * np.pi * np.outer(s1, f1) / N1
        F1 = np.concatenate([np.cos(a), -np.sin(a)], 1)
        c[f"hyF1_{L}"] = bf(F1.reshape(nch1, R1, 2 * N1).transpose(1, 0, 2))
        s2 = np.arange(128, dtype=np.float64)
        a = 2 * np.pi * np.outer(s2, f1) / N
        c[f"hyTwr_{L}"] = np.cos(a).astype(np.float32)
        c[f"hyTwi_{L}"] = (-np.sin(a)).astype(np.float32)
        aT = a.T
        c[f"hyTwTr_{L}"] = np.cos(aT).reshape(nch1, R1, 128).transpose(1, 0, 2).astype(np.float32)
        c[f"hyTwTi_{L}"] = np.sin(aT).reshape(nch1, R1, 128).transpose(1, 0, 2).astype(np.float32)
        t1 = np.arange(P1, dtype=np.float64)
        a = 2 * np.pi * np.outer(f1, t1) / N1
        c[f"hyG1r_{L}"] = bf((np.cos(a) / N).reshape(nch1, R1, P1).transpose(1, 0, 2))
        c[f"hyG1i_{L}"] = bf((-np.sin(a) / N).reshape(nch1, R1, P1).transpose(1, 0, 2))
        sidx = np.arange(N)
        lag = np.where(sidx < L, sidx, N - sidx)
        lag[L] = 0
        tgrid = np.linspace(0.0, 1.0, L, dtype=np.float32)
        bands = 16
        wpos = (2.0 * math.pi / L) * np.arange(L, dtype=np.float32)
        fr = np.linspace(1e-4, bands - 1, bands, dtype=np.float32)
        feats = np.concatenate([tgrid[:, None], np.cos(fr[None, :] * wpos[:, None]), -np.sin(fr[None, :] * wpos[:, None])], -1)
        c[f"hyfeat_{L}"] = np.ascontiguousarray(feats[lag].T.astype(np.float32))
        c[f"hytneg_{L}"] = np.ascontiguousarray((-tgrid[lag]).reshape(N1, 128).T.astype(np.float32))
    a = 2 * np.pi * np.outer(np.arange(128.0), np.arange(128.0)) / 128
    c["hyF2r"] = bf(np.cos(a))
    c["hyF2i"] = bf(-np.sin(a))
    c["hyF2in"] = bf(np.sin(a))
    c["hyG2a"] = bf(np.concatenate([np.cos(a), np.sin(a)], 1))
    c["hyG2b"] = bf(np.concatenate([-np.sin(a), np.cos(a)], 1))
    deltas = np.abs(np.linspace(math.log(1e-2) / 1.5, math.log(1e-2) / 0.3, 512, dtype=np.float32))
    c["hydelta"] = np.ascontiguousarray(np.broadcast_to(deltas[None, :], (128, 512))).astype(np.float32)
    return c


class Ctx:
    pass


def dcol(ap_dram, off, n, nparts=128, pstride=1, cstride=128):
    return AP(tensor=ap_dram.tensor, offset=ap_dram.offset + off, ap=[[pstride, nparts], [cstride, n], [1, 1]])


def drow_bc(ap_dram, off, n, nparts=128):
    return AP(tensor=ap_dram.tensor, offset=ap_dram.offset + off, ap=[[0, nparts], [1, n]])


def build(seq_lens, debug=False, stages=None, mixers=None):
    nc = bass.Bass("TRN2", target_bir_lowering=False)
    Ttot = sum(seq_lens)
    Lmax = max(seq_lens)
    seq_off = [sum(seq_lens[:i]) for i in range(len(seq_lens))]
    Ls = sorted(set(seq_lens))
    g = Ctx()
    g.nc = nc
    g.uidc = [0]

    def un(name):
        g.uidc[0] += 1
        return f"{name}_{g.uidc[0]}"
    g.un = un
    ext = {}

    def xin(name, shape, dt=F32):
        ext[name] = nc.dram_tensor(name, list(shape), dt, kind="ExternalInput").ap()
        return ext[name]

    xin("x", [Ttot, D])
    for nm, shp in (("norm_mix_pre", [DEPTH, D]), ("norm_mix_post", [DEPTH, D]), ("norm_ffn_pre", [DEPTH, D]),
                    ("norm_ffn_post", [DEPTH, D]), ("w_in", [DEPTH, D, D_IN]), ("hy_conv_w", [DEPTH, 3, 1536]),
                    ("hy_conv_b", [DEPTH, 1536]), ("hy_ffn_w1", [DEPTH, 33, 64]), ("hy_ffn_b1", [DEPTH, 64]),
                    ("hy_sin_freq", [DEPTH, 64]), ("hy_ffn_w2", [DEPTH, 64, 64]), ("hy_ffn_b2", [DEPTH, 64]),
                    ("hy_ffn_w3", [DEPTH, 64, 2048]), ("hy_skip", [DEPTH, 2, 512]), ("mla_q_norm", [DEPTH, 256]),
                    ("mla_wq_b", [DEPTH, 256, 768]), ("mla_kv_norm", [DEPTH, 128]), ("mla_wkv_b", [DEPTH, 128, 1024]),
                    ("gdn_conv_w", [DEPTH, 3, 1536]), ("gdn_a_log", [DEPTH, 8]), ("gdn_dt_bias", [DEPTH, 8]),
                    ("gdn_out_norm", [DEPTH, 128]), ("w_branch", [DEPTH, 1536, D]), ("w_out", [DEPTH, D, D]),
                    ("w_gate", [DEPTH, D, D_FF]), ("w_up", [DEPTH, D, D_FF]), ("w_down", [DEPTH, D_FF, D])):
        xin(nm, shp)
    consts = make_consts(seq_lens)
    for k, v in consts.items():
        xin("c_" + k, v.shape, BF16 if v.dtype == ml_dtypes.bfloat16 else F32)
    y = nc.dram_tensor("y", [Ttot, D], F32, kind="ExternalOutput").ap()

    def scratch(name, shape, dt):
        kind = "ExternalOutput" if debug else "Internal"
        return nc.dram_tensor(name, list(shape), dt, kind=kind).ap()

    XT = scratch("XT", [D, Ttot], F32)
    WTOK = [scratch(f"WTOK{l}", [6, 128, KC, 3, 512], BF16) for l in range(DEPTH)]
    WFM = [scratch(f"WFM{l}", [32, 128, KC, 128], BF16) for l in range(DEPTH)]
    WBR = [scratch(f"WBR{l}", [8, 128, 12, 128], BF16) for l in range(DEPTH)]
    WOUT = [scratch(f"WOUT{l}", [8, 128, KC, 128], BF16) for l in range(DEPTH)]
    WG = [scratch(f"WG{l}", [NFF, 128, KC, 128], BF16) for l in range(DEPTH)]
    WU = [scratch(f"WU{l}", [NFF, 128, KC, 128], BF16) for l in range(DEPTH)]
    WD = [scratch(f"WD{l}", [8, 128, NFF, 128], BF16) for l in range(DEPTH)]
    PH = scratch("PH", [Lmax, 1536], BF16)
    PG = scratch("PG", [Lmax, 1536], BF16)
    BD = scratch("BD", [Lmax, 16], F32)
    QT = scratch("QT", [4, 192, Lmax], BF16)
    KTn = scratch("KTn", [4, 128, Lmax], BF16)
    KTp = scratch("KTp", [64, Lmax], BF16)
    VS = scratch("VS", [Lmax, 512], BF16)
    ZT = scratch("ZT", [512, Lmax], BF16)
    GT = scratch("GT", [3072, Lmax], BF16)
    OT = scratch("OT", [1536, Lmax], BF16)
    OF = scratch("OF", [512, Lmax], F32)
    Z2 = scratch("Z2", [Lmax, 512], BF16)
    KTd = {L_: scratch(f"KTd{L_}", [2 * L_, 1024], BF16) for L_ in Ls}
    KFq = {L_: scratch(f"KF{L_}", [2, 512 // min(64, 4096 // (L_ // 64)), max(1, min(64, 4096 // (L_ // 64)) * (L_ // 64) // 512), 128, 2, 512], BF16)
           for L_ in Ls}
    filt_done = {}

    es = ExitStack()
    with es:
        em = Em(nc, es)

        def sb(name, shape, dt, st=es):
            return st.enter_context(nc.sbuf_tensor(g.un(name), list(shape), dt))

        PS = [es.enter_context(nc.psum_tensor(f"ps{i}", [128, 512], F32)) for i in range(8)]

        def psum(banks=range(8)):
            banks = list(banks)
            b = banks[em.psn % len(banks)]
            em.psn += 1
            return PS[b], ('ps', b)

        ident = sb("ident", [128, 128], F32)
        identb = sb("identb", [128, 128], BF16)
        onesb = sb("onesb", [128, 128], BF16)
        gains = sb("gains", [128, 4, DEPTH, KC], F32)
        em.dma('sp', ident[:], ext["c_ident"][:, :], w=['ident'])
        em.dma('sp', identb[:], ext["c_identb"][:, :], w=['identb'])
        em.dma('sp', onesb[:], ext["c_onesb"][:, :], w=['onesb'])
        for i, nm in enumerate(("norm_mix_pre", "norm_mix_post", "norm_ffn_pre", "norm_ffn_post")):
            for l in range(DEPTH):
                em.dma('sp', gains[:, i, l, :], dcol(ext[nm], l * D, KC), w=['gains'], slow=True)
        em.barrier()

        g.__dict__.update(locals())
        if stages is None or 'prep' in stages:
            stage_prep(g)
        for l in range(DEPTH):
            for si, L in enumerate(seq_lens):
                if stages is None or ('A', l) in stages:
                    stage_A(g, l, si)
                if stages is None or ('B', l) in stages:
                    stage_B(g, l, si)
                if stages is None or ('C', l) in stages:
                    stage_C(g, l, si)
        if stages is None or 'out' in stages:
            stage_out(g)
        em.barrier()
        print("emitted instructions:", em.ninst, {k: v for k, v in em.cnt.items() if isinstance(k, str)})
    return nc, consts


class Rot:
    def __init__(s, tiles, name):
        s.t = tiles
        s.name = name
        s.i = 0

    def next(s):
        j = s.i % len(s.t)
        s.i += 1
        return s.t[j], (s.name, j)


def stage_prep(g):
    nc, em, ext = g.nc, g.em, g.ext
    with ExitStack() as st:
        def sb(name, shape, dt):
            return st.enter_context(nc.sbuf_tensor(g.un(name), list(shape), dt))
        wld = Rot([sb(f"p_wld{i}", [128, 512], F32) for i in range(4)], "p_wld")
        cwt = Rot([sb(f"p_cw{i}", [128, 3, 512], F32) for i in range(2)], "p_cw")
        stg = Rot([sb(f"p_stg{i}", [128, 24 * 512], BF16) for i in range(2)], "p_stg")
        engs = ['dve', 'pool', 'act']
        ei = [0]

        def cast(out, in_, r, w, mul=None):
            e = engs[ei[0] % 3]
            ei[0] += 1
            if mul is not None:
                if e == 'act':
                    e = 'dve'
                em.op(e, lambda E: E.tensor_tensor(out, in_, mul, ALU.mult), r=r, w=w)
            elif e == 'act':
                em.op(e, lambda E: E.copy(out, in_), r=r, w=w)
            else:
                em.op(e, lambda E: E.tensor_copy(out, in_), r=r, w=w)

        def prep_fm(src, row0, nkc, col0, W, dst, t0):
            sg, sk = stg.next()
            for kc in range(nkc):
                wt, wk = wld.next()
                em.dma('sp', wt[:, 0:W], src[row0 + kc * 128: row0 + (kc + 1) * 128, col0:col0 + W], w=[wk])
                cast(sg[:, kc * W:(kc + 1) * W], wt[:, 0:W], [wk], [sk])
            nt = W // 128
            d = dst
            dap = AP(tensor=d.tensor, offset=d.offset + t0 * 128 * nkc * 128,
                     ap=[[nkc * 128, 128], [128, nkc], [128 * nkc * 128, nt], [1, 128]])
            sap = V(sg[:, 0:1], [(W, nkc), (128, nt), (1, 128)])
            em.dma('pool', dap, sap, r=[sk], w=[('dram', d.tensor.name)])

        for l in range(DEPTH):
            w_in = ext["w_in"][l]
            for grp in range(6):
                col0 = (C_HY + 512 * grp) if grp < 3 else (C_GQKV + 512 * (grp - 3))
                cwn = "hy_conv_w" if grp < 3 else "gdn_conv_w"
                cw, cwk = cwt.next()
                cw_src = ext[cwn]
                em.dma('sp', cw[:], AP(tensor=cw_src.tensor, offset=cw_src.offset + l * 3 * 1536 + 512 * (grp % 3),
                                       ap=[[0, 128], [1536, 3], [1, 512]]), w=[cwk])
                sg, sk = stg.next()
                for kc in range(KC):
                    wt, wk = wld.next()
                    em.dma('sp', wt[:], w_in[kc * 128:(kc + 1) * 128, col0:col0 + 512], w=[wk])
                    for s_ in range(3):
                        o = (kc * 3 + s_) * 512
                        cast(sg[:, o:o + 512], wt[:], [wk, cwk], [sk], mul=cw[:, s_, :])
                em.dma('pool', g.WTOK[l][grp].rearrange("p k s c -> p (k s c)"), sg[:, 0:KC * 3 * 512], r=[sk],
                       w=[('dram', f"WTOK{l}")])
            prep_fm(w_in, 0, KC, C_QL, 512, g.WFM[l], 0)
            prep_fm(w_in, 0, KC, C_GZ, 512, g.WFM[l], 4)
            for i in range(6):
                prep_fm(w_in, 0, KC, C_GATE + 512 * i, 512, g.WFM[l], 8 + 4 * i)
            for i in range(2):
                prep_fm(ext["w_branch"][l], 0, 12, 512 * i, 512, g.WBR[l], 4 * i)
                prep_fm(ext["w_out"][l], 0, KC, 512 * i, 512, g.WOUT[l], 4 * i)
                prep_fm(ext["w_down"][l], 0, NFF, 512 * i, 512, g.WD[l], 4 * i)
            for i in range(11):
                prep_fm(ext["w_gate"][l], 0, KC, 256 * i, 256, g.WG[l], 2 * i)
                prep_fm(ext["w_up"][l], 0, KC, 256 * i, 256, g.WU[l], 2 * i)
        xld = Rot([sb(f"p_x{i}", [128, D], F32) for i in range(2)], "p_x")
        xts = Rot([sb(f"p_xt{i}", [128, KC, 128], F32) for i in range(2)], "p_xt")
        for t in range(g.Ttot // 128):
            xt_, xk = xld.next()
            em.dma('sp', xt_[:], ext["x"][t * 128:(t + 1) * 128, :], w=[xk])
            xo, xok = xts.next()
            for hlf in range(2):
                ps, pk = g.psum()
                for c in range(4):
                    kc = hlf * 4 + c
                    em.op('pe', lambda E: E.transpose(ps[:, c * 128:(c + 1) * 128], xt_[:, kc * 128:(kc + 1) * 128],
                                                      g.ident[:]), r=[xk, 'ident'], w=[pk])
                e = 'dve' if hlf == 0 else 'act'
                dst = xo[:, hlf * 4:(hlf + 1) * 4, :]
                src = ps[:].rearrange("p (c t) -> p c t", c=4)
                if e == 'dve':
                    em.op(e, lambda E: E.tensor_copy(dst, src), r=[pk], w=[xok])
                else:
                    em.op(e, lambda E: E.copy(dst, src), r=[pk], w=[xok])
            dap = AP(tensor=g.XT.tensor, offset=g.XT.offset + t * 128, ap=[[g.Ttot, 128], [128 * g.Ttot, KC], [1, 128]])
            em.dma('pool', dap, xo[:], r=[xok], w=[('dram', 'XT')])
    em.barrier()


def stage_A(g, l, si):
    nc, em, ext = g.nc, g.em, g.ext
    L = g.seq_lens[si]
    t0 = g.seq_off[si]
    Ttot = g.Ttot
    nb = L // 512
    QSCALE = 192.0 ** -0.5
    with ExitStack() as st:
        def sb(name, shape, dt):
            return st.enter_context(nc.sbuf_tensor(g.un(name), list(shape), dt))
        g32 = sb("a_g32", [128, KC], F32)
        hb = sb("a_hb", [128, 1536], F32)
        tmpf = sb("a_tmpf", [128, 2 * 768], F32)
        wbd = sb("a_wbd", [128, KC, 16], BF16)
        wqb = sb("a_wqb", [128, 2, 768], BF16)
        wkn = sb("a_wkn", [128, 4, 128], BF16)
        wv = sb("a_wv", [128, 4, 128], BF16)
        gq = sb("a_gq", [128, 2], F32)
        gkv = sb("a_gkv", [128, 1], F32)
        ropeR = sb("a_ropeR", [64, 64], BF16)
        em.op('dve', lambda E: E.tensor_scalar(g32[:], g.gains[:, 0, l, :], 32.0, None, ALU.mult), r=['gains'], w=['g32'])
        em.dma('sp', hb[:], drow_bc(ext["hy_conv_b"], l * 1536, 1536), w=['hb'])
        em.dma('sp', ropeR[:], ext["c_rope_R"][:, :], w=['ropeR'])
        w_in = ext["w_in"][l]
        em.dma('sp', V(tmpf[:, 0:1], [(16, KC), (1, 16)]),
               AP(tensor=w_in.tensor, offset=w_in.offset + C_GB, ap=[[D_IN, 128], [128 * D_IN, KC], [1, 16]]), w=['tmpf'])
        em.op('dve', lambda E: E.tensor_copy(wbd[:].rearrange("p k c -> p (k c)"), tmpf[:, 0:KC * 16]), r=['tmpf'], w=['wbd'])
        wq = ext["mla_wq_b"][l]
        em.dma('sp', V(tmpf[:, 0:1], [(768, 2), (1, 768)]),
               AP(tensor=wq.tensor, offset=wq.offset, ap=[[768, 128], [128 * 768, 2], [1, 768]]), w=['tmpf'], r=['tmpf'])
        em.op('dve', lambda E: E.tensor_copy(wqb[:].rearrange("p k c -> p (k c)"), tmpf[:, 0:1536]), r=['tmpf'], w=['wqb'])
        wkv = ext["mla_wkv_b"][l]
        em.dma('sp', tmpf[:, 0:1024], wkv[:, :], w=['tmpf'], r=['tmpf'])
        em.op('dve', lambda E: E.tensor_copy(wkn[:], V(tmpf[:, 0:1], [(256, 4), (1, 128)])), r=['tmpf'], w=['wkn'])
        em.op('dve', lambda E: E.tensor_copy(wv[:], V(tmpf[:, 0:1], [(256, 4), (1, 128)], off=128)), r=['tmpf'], w=['wv'])
        em.dma('sp', gq[:], dcol(ext["mla_q_norm"], l * 256, 2), w=['gq'], slow=True)
        em.dma('sp', gkv[:], dcol(ext["mla_kv_norm"], l * 128, 1), w=['gkv'], slow=True)
        em.op('dve', lambda E: E.tensor_scalar(gq[:], gq[:], 16.0, None, ALU.mult), r=['gq'], w=['gq'])
        em.op('dve', lambda E: E.tensor_scalar(gkv[:], gkv[:], math.sqrt(128.0), None, ALU.mult), r=['gkv'], w=['gkv'])
        xT = Rot([sb(f"a_xT{i}", [128, KC, 512], F32) for i in range(2)], "a_xT")
        xh = Rot([sb(f"a_xh{i}", [128, KC, 2], F32) for i in range(2)], "a_xh")
        sq = Rot([sb(f"a_sq{i}", [128, KC, 512], BF16) for i in range(1)], "a_sq")
        sqh = sb("a_sqh", [128, KC, 2], BF16)
        rstd = Rot([sb(f"a_rstd{i}", [128, 512], F32) for i in range(2)], "a_rstd")
        rsth = sb("a_rsth", [128, 2], F32)
        xhg = sb("a_xhg", [128, KC, 2], F32)
        hT = Rot([sb(f"a_hT{i}", [128, KC, 514], BF16) for i in range(2)], "a_hT")
        wtk = Rot([sb(f"a_wtk{i}", [128, KC, 3, 512], BF16) for i in range(2)], "a_wtk")
        wfm = Rot([sb(f"a_wfm{i}", [128, 4, KC, 128], BF16) for i in range(2)], "a_wfm")
        stg = Rot([sb(f"a_stg{i}", [128, 4, 512], BF16) for i in range(3)], "a_stg")
        bds = Rot([sb(f"a_bds{i}", [128, 4, 16], F32) for i in range(2)], "a_bds")
        ql = sb("a_ql", [128, 2, 512], F32)
        ck = sb("a_ck", [128, 512], F32)
        kpe = sb("a_kpe", [64, 512], BF16)
        qln = sb("a_qln", [128, 2, 512], BF16)
        ckn = sb("a_ckn", [128, 512], BF16)
        qpe = Rot([sb(f"a_qpe{i}", [64, 512], BF16) for i in range(2)], "a_qpe")
        rt1 = Rot([sb(f"a_rt1{i}", [64, 512], F32) for i in range(2)], "a_rt1")
        rt2 = Rot([sb(f"a_rt2{i}", [64, 512], F32) for i in range(2)], "a_rt2")
        cosT = Rot([sb(f"a_cos{i}", [64, 512], F32) for i in range(2)], "a_cos")
        sinT = Rot([sb(f"a_sin{i}", [64, 512], F32) for i in range(2)], "a_sin")
        rcos, rsin = ext[f"c_ropecos{L}"], ext[f"c_ropesin{L}"]
        eng2 = ['dve', 'pool']

        def rope_store(src_bf, srck, cs, csk, sn, snk, dst_ap, dkey):
            ps, pk = g.psum()
            em.op('pe', lambda E: E.matmul(ps[0:64, :], ropeR[:], src_bf, start=True, stop=True), r=[srck, 'ropeR'], w=[pk])
            a, ak = rt1.next()
            b_, bk = rt2.next()
            em.op('pool', lambda E: E.tensor_tensor(a[:], src_bf, cs[:], ALU.mult), r=[srck, csk], w=[ak])
            em.op('dve', lambda E: E.tensor_tensor(b_[:], ps[0:64, :], sn[:], ALU.mult), r=[pk, snk], w=[bk])
            o, ok = qpe.next()
            em.op('dve', lambda E: E.tensor_tensor(o[:], a[:], b_[:], ALU.add), r=[ak, bk], w=[ok])
            em.dma('pool', dst_ap, o[:], r=[ok], w=[dkey])

        for b in range(nb):
            c0 = t0 + b * 512
            s0 = b * 512
            x_, xk = xT.next()
            em.dma('sp', x_[:], AP(tensor=g.XT.tensor, offset=g.XT.offset + c0, ap=[[Ttot, 128], [128 * Ttot, KC], [1, 512]]),
                   r=[('dram', 'XT')], w=[xk])
            xh_, xhk = xh.next()
            for side, col in ((0, c0 - 1), (1, c0 + 512)):
                if (side == 0 and b == 0) or (side == 1 and b == nb - 1):
                    em.op('pool', lambda E: E.memset(xh_[:, :, side:side + 1], 0.0), w=[xhk])
                else:
                    em.dma('sp', xh_[:, :, side:side + 1],
                           AP(tensor=g.XT.tensor, offset=g.XT.offset + col, ap=[[Ttot, 128], [128 * Ttot, KC], [1, 1]]),
                           r=[('dram', 'XT')], w=[xhk], slow=True)
            cs, csk = cosT.next()
            sn, snk = sinT.next()
            em.dma('sp', cs[:], rcos[:, s0:s0 + 512], w=[csk])
            em.dma('sp', sn[:], rsin[:, s0:s0 + 512], w=[snk])
            sq_, sqk = sq.next()
            em.op('act', lambda E: E.activation(out=sq_[:], in_=x_[:], func=AF.Square), r=[xk], w=[sqk])
            ps, pk = g.psum()
            for kc in range(KC):
                em.op('pe', lambda E: E.matmul(ps[:], g.onesb[:], sq_[:, kc, :], start=(kc == 0), stop=(kc == KC - 1)),
                      r=[sqk, 'onesb'], w=[pk])
            rs, rsk = rstd.next()
            rsqrt(em, rs[:], ps[:], D * EPS, pk, rsk)
            h_, hk = hT.next()
            for kc in range(KC):
                em.op('dve', lambda E: E.scalar_tensor_tensor(h_[:, kc, 1:513], x_[:, kc, :], g32[:, kc:kc + 1], rs[:],
                                                                     ALU.mult, ALU.mult), r=[xk, rsk, 'g32'], w=[hk])
            em.op('act', lambda E: E.activation(out=sqh[:], in_=xh_[:], func=AF.Square), r=[xhk], w=['sqh'])
            ps2, pk2 = g.psum()
            for kc in range(KC):
                em.op('pe', lambda E: E.matmul(ps2[:, 0:2], g.onesb[:], sqh[:, kc, :], start=(kc == 0), stop=(kc == KC - 1)),
                      r=['sqh', 'onesb'], w=[pk2])
            rsqrt(em, rsth[:], ps2[:, 0:2], D * EPS, pk2, 'rsth')
            em.op('pool', lambda E: E.tensor_tensor(xhg[:], xh_[:], V(g32[:, 0:1], [(1, KC), (0, 2)]), ALU.mult),
                  r=[xhk, 'g32'], w=['xhg'])
            em.op('pool', lambda E: E.tensor_tensor(V(h_[:, 0, 0:1], [(514, KC), (513, 2)]), xhg[:],
                                                    V(rsth[:, 0:1], [(0, KC), (1, 2)]), ALU.mult),
                  r=['xhg', 'rsth'], w=[hk])
            for grp in range(6):
                w_, wk = wtk.next()
                em.dma('sp', w_[:].rearrange("p k s c -> p (k s c)"), g.WTOK[l][grp].rearrange("p k s c -> p (k s c)"),
                       r=[('dram', f"WTOK{l}")], w=[wk])
                sg, sgk = stg.next()
                for tt in range(4):
                    ps, pk = g.psum()
                    n = 0
                    for kc in range(KC):
                        for s_ in range(3):
                            lo = 1 + tt * 128 + (s_ - 1)
                            em.op('pe', lambda E: E.matmul(ps[:], h_[:, kc, lo:lo + 128], w_[:, kc, s_, :],
                                                           start=(n == 0), stop=(n == 23)), r=[hk, wk], w=[pk])
                            n += 1
                    if grp < 3:
                        em.op('dve', lambda E: E.tensor_tensor(sg[:, tt, :], ps[:], hb[:, grp * 512:(grp + 1) * 512], ALU.add),
                              r=[pk, 'hb'], w=[sgk])
                    else:
                        em.op('act', lambda E: E.activation(out=sg[:, tt, :], in_=ps[:], func=AF.Silu), r=[pk], w=[sgk])
                dst = g.PH if grp < 3 else g.PG
                em.dma('pool', AP(tensor=dst.tensor, offset=dst.offset + s0 * 1536 + (grp % 3) * 512,
                                  ap=[[1536, 128], [128 * 1536, 4], [1, 512]]), sg[:], r=[sgk],
                       w=[('dram', 'PH' if grp < 3 else 'PG')])
            ps, pk = g.psum()
            for tt in range(4):
                for kc in range(KC):
                    em.op('pe', lambda E: E.matmul(ps[:, tt * 16:(tt + 1) * 16], h_[:, kc, 1 + tt * 128:1 + (tt + 1) * 128],
                                                   wbd[:, kc, :], start=(kc == 0), stop=(kc == KC - 1)), r=[hk, 'wbd'], w=[pk])
            bd_, bdk = bds.next()
            em.op('dve', lambda E: E.tensor_copy(bd_[:].rearrange("p t c -> p (t c)"), ps[:, 0:64]), r=[pk], w=[bdk])
            em.dma('pool', AP(tensor=g.BD.tensor, offset=g.BD.offset + s0 * 16, ap=[[16, 128], [128 * 16, 4], [1, 16]]),
                   bd_[:], r=[bdk], w=[('dram', 'BD')])
            for tg in range(8):
                wf, wfk = wfm.next()
                em.dma('sp', wf[:].rearrange("p t k c -> p t (k c)"),
                       AP(tensor=g.WFM[l].tensor, offset=g.WFM[l].offset + tg * 4 * 128 * KC * 128,
                          ap=[[KC * 128, 128], [128 * KC * 128, 4], [1, KC * 128]]), r=[('dram', f"WFM{l}")], w=[wfk])
                if tg >= 1:
                    sg, sgk = stg.next()
                for ti in range(4):
                    M = 64 if (tg == 0 and ti == 3) else 128
                    ps, pk = g.psum()
                    for kc in range(KC):
                        em.op('pe', lambda E: E.matmul(ps[0:M, :], wf[:, ti, kc, 0:M], h_[:, kc, 1:513],
                                                       start=(kc == 0), stop=(kc == KC - 1)), r=[hk, wfk], w=[pk])
                    if tg == 0:
                        if ti < 2:
                            em.op('dve', lambda E: E.tensor_copy(ql[:, ti, :], ps[:]), r=[pk], w=['ql'])
                        elif ti == 2:
                            em.op('dve', lambda E: E.tensor_copy(ck[:], ps[:]), r=[pk], w=['ck'])
                        else:
                            em.op('act', lambda E: E.copy(kpe[:], ps[0:64, :]), r=[pk], w=['kpe'])
                    elif tg == 1:
                        em.op('act', lambda E: E.activation(out=sg[:, ti, :], in_=ps[:], func=AF.Silu), r=[pk], w=[sgk])
                    else:
                        em.op('act', lambda E: E.activation(out=sg[:, ti, :], in_=ps[:], func=AF.Sigmoid), r=[pk], w=[sgk])
                if tg == 1:
                    em.dma('pool', AP(tensor=g.ZT.tensor, offset=g.ZT.offset + s0, ap=[[g.Lmax, 128], [128 * g.Lmax, 4], [1, 512]]),
                           sg[:], r=[sgk], w=[('dram', 'ZT')])
                elif tg >= 2:
                    em.dma('pool', AP(tensor=g.GT.tensor, offset=g.GT.offset + (tg - 2) * 512 * g.Lmax + s0,
                                      ap=[[g.Lmax, 128], [128 * g.Lmax, 4], [1, 512]]), sg[:], r=[sgk], w=[('dram', 'GT')])
            sq_, sqk = sq.next()
            em.op('act', lambda E: E.activation(out=sq_[:, 0:2, :], in_=ql[:], func=AF.Square), r=['ql'], w=[sqk])
            em.op('act', lambda E: E.activation(out=sq_[:, 2, :], in_=ck[:], func=AF.Square), r=['ck'], w=[sqk])
            ps, pk = g.psum()
            for c in range(2):
                em.op('pe', lambda E: E.matmul(ps[:], g.onesb[:], sq_[:, c, :], start=(c == 0), stop=(c == 1)), r=[sqk, 'onesb'], w=[pk])
            rs, rsk = rstd.next()
            rsqrt(em, rs[:], ps[:], 256 * EPS, pk, rsk)
            for c in range(2):
                em.op('dve', lambda E: E.scalar_tensor_tensor(qln[:, c, :], ql[:, c, :], gq[:, c:c + 1], rs[:], ALU.mult, ALU.mult),
                      r=['ql', rsk, 'gq'], w=['qln'])
            ps, pk = g.psum()
            em.op('pe', lambda E: E.matmul(ps[:], g.onesb[:], sq_[:, 2, :], start=True, stop=True), r=[sqk, 'onesb'], w=[pk])
            rs, rsk = rstd.next()
            rsqrt(em, rs[:], ps[:], 128 * EPS, pk, rsk)
            em.op('dve', lambda E: E.scalar_tensor_tensor(ckn[:], ck[:], gkv[:, 0:1], rs[:], ALU.mult, ALU.mult),
                  r=['ck', rsk, 'gkv'], w=['ckn'])
            for h in range(4):
                ps, pk = g.psum()
                for c in range(2):
                    em.op('pe', lambda E: E.matmul(ps[:], wqb[:, c, h * 192:h * 192 + 128], qln[:, c, :], start=(c == 0), stop=(c == 1)),
                          r=['wqb', 'qln'], w=[pk])
                sg, sgk = stg.next()
                em.op('act', lambda E: E.activation(out=sg[:, 0, :], in_=ps[:], func=AF.Copy, scale=QSCALE), r=[pk], w=[sgk])
                em.dma('pool', g.QT[h, 0:128, s0:s0 + 512], sg[:, 0, :], r=[sgk], w=[('dram', 'QT')])
                ps, pk = g.psum()
                for c in range(2):
                    em.op('pe', lambda E: E.matmul(ps[0:64, :], wqb[:, c, h * 192 + 128:h * 192 + 192], qln[:, c, :],
                                                   start=(c == 0), stop=(c == 1)), r=['wqb', 'qln'], w=[pk])
                qp, qpk = qpe.next()
                em.op('act', lambda E: E.activation(out=qp[:], in_=ps[0:64, :], func=AF.Copy, scale=QSCALE), r=[pk], w=[qpk])
                rope_store(qp[:], qpk, cs, csk, sn, snk, g.QT[h, 128:192, s0:s0 + 512], ('dram', 'QT'))
                ps, pk = g.psum()
                em.op('pe', lambda E: E.matmul(ps[:], wkn[:, h, :], ckn[:], start=True, stop=True), r=['wkn', 'ckn'], w=[pk])
                em.op('dve', lambda E: E.tensor_copy(sg[:, 1, :], ps[:]), r=[pk], w=[sgk])
                em.dma('pool', g.KTn[h, :, s0:s0 + 512], sg[:, 1, :], r=[sgk], w=[('dram', 'KTn')])
            rope_store(kpe[:], 'kpe', cs, csk, sn, snk, g.KTp[:, s0:s0 + 512], ('dram', 'KTp'))
            sg, sgk = stg.next()
            for tt in range(4):
                ps, pk = g.psum()
                em.op('pe', lambda E: E.matmul(ps[:], ckn[:, tt * 128:(tt + 1) * 128], wv[:].rearrange("p h c -> p (h c)"),
                                               start=True, stop=True), r=['ckn', 'wv'], w=[pk])
                em.op('act' if tt % 2 else 'dve', lambda E: (E.copy if tt % 2 else E.tensor_copy)(sg[:, tt, :], ps[:]), r=[pk], w=[sgk])
            em.dma('pool', AP(tensor=g.VS.tensor, offset=g.VS.offset + s0 * 512, ap=[[512, 128], [128 * 512, 4], [1, 512]]),
                   sg[:], r=[sgk], w=[('dram', 'VS')])
    em.barrier()


def mixer_mla(g, l, si):
    nc, em = g.nc, g.em
    L = g.seq_lens[si]
    Lmax = g.Lmax
    nq, nk = L // 512, L // 128
    NCH = max(1, min(8, nk // 4))
    kpc = nk // NCH
    with ExitStack() as st:
        def sb(name, shape, dt):
            return st.enter_context(nc.sbuf_tensor(g.un(name), list(shape), dt))
        Kp = sb("m_Kp", [64, L], BF16)
        Kn = sb("m_Kn", [128, L], BF16)
        Vh = sb("m_Vh", [128, nk, 128], BF16)
        qn = Rot([sb(f"m_qn{i}", [128, 512], BF16) for i in range(2)], "m_qn")
        qp = Rot([sb(f"m_qp{i}", [64, 512], BF16) for i in range(2)], "m_qp")
        PT = Rot([sb(f"m_PT{i}", [128, 512], BF16) for i in range(6)], "m_PT")
        rs = Rot([sb(f"m_rs{i}", [128, 512], F32) for i in range(2)], "m_rs")
        ob = Rot([sb(f"m_ob{i}", [128, 512], BF16) for i in range(2)], "m_ob")
        acc = Rot([sb(f"m_acc{i}", [128, 512], F32) for i in range(2)], "m_acc")
        acc2 = Rot([sb(f"m_accb{i}", [128, 512], F32) for i in range(2)], "m_accb")
        ones32 = sb("m_ones", [128, 128], F32)
        em.dma('sp', ones32[:], g.ext["c_ones"][:, :], w=['ones32'])
        for ci in range(NCH):
            em.dma('sp', Kp[:, ci * kpc * 128:(ci + 1) * kpc * 128], g.KTp[:, ci * kpc * 128:(ci + 1) * kpc * 128],
                   r=[('dram', 'KTp')], w=[('Kp', ci)])
        nacc = 0
        for h in range(4):
            for ci in range(NCH):
                a, b_ = ci * kpc * 128, (ci + 1) * kpc * 128
                em.dma('sp', Kn[:, a:b_], g.KTn[h, :, a:b_], r=[('dram', 'KTn')], w=[('Kn', ci)])
                em.dma('sp', Vh[:, ci * kpc:(ci + 1) * kpc, :],
                       AP(tensor=g.VS.tensor, offset=g.VS.offset + a * 512 + h * 128, ap=[[512, 128], [128 * 512, kpc], [1, 128]]),
                       r=[('dram', 'VS')], w=[('Vh', ci)])
            for qb in range(nq):
                q0 = qb * 512
                qn_, qnk = qn.next()
                qp_, qpk = qp.next()
                em.dma('sp', qn_[:], g.QT[h, 0:128, q0:q0 + 512], r=[('dram', 'QT')], w=[qnk])
                em.dma('sp', qp_[:], g.QT[h, 128:192, q0:q0 + 512], r=[('dram', 'QT')], w=[qpk])
                po, pok = g.psum([4, 6][nacc % 2:nacc % 2 + 1])
                pz, pzk = g.psum([5, 7][nacc % 2:nacc % 2 + 1])
                ac, ack = acc.next()
                ac2, ac2k = acc2.next()
                nacc += 1

                def s_tile(kt):
                    ci = kt // kpc
                    ps, pk = g.psum(range(4))
                    em.op('pe', lambda E: E.matmul(ps[:], Kn[:, kt * 128:(kt + 1) * 128], qn_[:], start=True, stop=False),
                          r=[('Kn', ci), qnk], w=[pk])
                    em.op('pe', lambda E: E.matmul(ps[:], Kp[:, kt * 128:(kt + 1) * 128], qp_[:], start=False, stop=True),
                          r=[('Kp', ci), qpk], w=[pk])
                    return ps, pk

                cur = s_tile(0)
                for kt in range(nk):
                    ci = kt // kpc
                    nxt = s_tile(kt + 1) if kt + 1 < nk else None
                    ps, pk = cur
                    pt, ptk = PT.next()
                    em.op('act', lambda E: E.activation(out=pt[:], in_=ps[:], func=AF.Exp), r=[pk], w=[ptk])
                    em.op('pe', lambda E: E.matmul(po[:], Vh[:, kt, :], pt[:], start=(kt == 0), stop=(kt == nk - 1)),
                          r=[('Vh', ci), ptk], w=[pok])
                    if kt % 3 != 2:
                        if kt == 0:
                            em.op('dve', lambda E: E.tensor_copy(ac[:], pt[:]), r=[ptk], w=[ack])
                        else:
                            em.op('dve', lambda E: E.tensor_tensor(ac[:], ac[:], pt[:], ALU.add), r=[ptk, ack], w=[ack])
                    else:
                        if kt == 2:
                            em.op('pool', lambda E: E.tensor_copy(ac2[:], pt[:]), r=[ptk], w=[ac2k])
                        else:
                            em.op('pool', lambda E: E.tensor_tensor(ac2[:], ac2[:], pt[:], ALU.add), r=[ptk, ac2k], w=[ac2k])
                    cur = nxt
                em.op('pe', lambda E: E.matmul(pz[:], ones32[:], ac[:], start=True, stop=False), r=['ones32', ack], w=[pzk])
                em.op('pe', lambda E: E.matmul(pz[:], ones32[:], ac2[:], start=False, stop=True), r=['ones32', ac2k], w=[pzk])
                r_, rk = rs.next()
                em.op('dve', lambda E: E.reciprocal(r_[:], pz[:]), r=[pzk], w=[rk])
                o_, ok = ob.next()
                em.op('dve', lambda E: E.tensor_tensor(o_[:], po[:], r_[:], ALU.mult), r=[pok, rk], w=[ok])
                em.dma('pool', g.OT[512 + h * 128:512 + (h + 1) * 128, q0:q0 + 512], o_[:], r=[ok], w=[('dram', 'OT')])
    em.barrier()


def stage_B(g, l, si):
    if g.mixers is None or 'mla' in g.mixers:
        mixer_mla(g, l, si)
    if g.mixers is None or 'gdn' in g.mixers:
        mixer_gdn(g, l, si)
    if g.mixers is None or 'hy' in g.mixers:
        mixer_hyena(g, l, si)


def mixer_gdn(g, l, si):
    nc, em, ext = g.nc, g.em, g.ext
    L = g.seq_lens[si]
    Lmax = g.Lmax
    nt = L // 128
    with ExitStack() as st:
        def sb(name, shape, dt):
            return st.enter_context(nc.sbuf_tensor(g.un(name), list(shape), dt))

        def T(name, shape, dt, n=2):
            return Rot([sb(f"g_{name}{i}", shape, dt) for i in range(n)], "g_" + name)
        cU = {0: sb("g_Ule", [128, 128], F32), 1: sb("g_Uge", [128, 128], F32)}
        cS = {0: sb("g_SL", [128, 128], F32), 1: sb("g_SU", [128, 128], F32)}
        ones32 = sb("g_ones", [128, 128], F32)
        negA = sb("g_negA", [128, 8], F32)
        dtb = sb("g_dtb", [128, 8], F32)
        gno = sb("g_gno", [128, 1], F32)
        em.dma('sp', cU[0][:], ext["c_U_le"][:, :], w=['cU0'])
        em.dma('sp', cU[1][:], ext["c_U_ge"][:, :], w=['cU1'])
        em.dma('sp', cS[0][:], ext["c_SL"][:, :], w=['cS0'])
        em.dma('sp', cS[1][:], ext["c_SU"][:, :], w=['cS1'])
        em.dma('sp', ones32[:], ext["c_ones"][:, :], w=['ones32'])
        em.dma('sp', negA[:], drow_bc(ext["gdn_a_log"], l * 8, 8), w=['negA'])
        em.dma('sp', dtb[:], drow_bc(ext["gdn_dt_bias"], l * 8, 8), w=['dtb'])
        em.dma('sp', gno[:], dcol(ext["gdn_out_norm"], l * 128, 1), w=['gno'], slow=True)
        em.op('act', lambda E: E.activation(out=negA[:], in_=negA[:], func=AF.Exp), r=['negA'], w=['negA'])
        em.op('dve', lambda E: E.tensor_scalar(negA[:], negA[:], -1.0, None, ALU.mult), r=['negA'], w=['negA'])
        em.op('dve', lambda E: E.tensor_scalar(gno[:], gno[:], math.sqrt(128.0), None, ALU.mult), r=['gno'], w=['gno'])
        maskT = {0: (cU[0], 'cU0'), 1: (cU[1], 'cU1')}
        qkv = T("qkv", [128, 1536], BF16)
        bd = T("bd", [128, 16], F32)
        sqf = T("sqf", [128, 1024], F32, 2)
        ss8 = T("ss8", [128, 8], F32)
        qkn = T("qkn", [128, 8, 128], BF16)
        qkT = T("qkT", [128, 8, 128], BF16)
        sm = T("sm", [128, 32], F32)
        gU = T("gU", [128, 4, 128], F32)
        gcc = T("gcc", [128, 8], F32)
        dif = T("dif", [128, 4, 128], F32)
        difT = T("difT", [128, 4, 128], F32)
        DmS = T("DmS", [128, 4, 128], F32)
        Bk = T("B", [128, 4, 128], F32, 6)
        Xk = T("X", [128, 4, 128], F32, 6)
        Pk = T("P", [128, 4, 128], F32, 6)
        IBk = T("IB", [128, 4, 128], F32, 6)
        attnT = T("attnT", [128, 4, 128], BF16)
        vb = T("vb", [128, 4, 128], F32)
        rr = T("rr", [128, 4, 128], F32)
        t1b = T("t1b", [128, 4, 128], F32)
        Erow = T("Erow", [128, 4, 128], F32)
        qdT = T("qdT", [128, 4, 128], BF16)
        kdec = T("kdec", [128, 4, 128], BF16)
        vnew = T("vnew", [128, 4, 128], BF16)
        of32 = T("of32", [128, 4, 128], F32)
        osum = T("osum", [128, 4, 128], F32)
        osq = T("osq", [128, 4, 128], BF16)
        orst = T("orst", [128, 4, 128], F32)
        zT = T("zT", [128, 4, 128], BF16)
        oout = T("oout", [128, 4, 128], BF16)
        identb4 = V(g.ident[:, 0:1], [(0, 4), (1, 128)])

        def f2(t):
            return t[:].rearrange("p h c -> p (h c)")

        def mm4(ps, lhs, lk, rhs, rk, pk, start=True, stop=True):
            for h in range(4):
                em.op('pe', lambda E: E.matmul(ps[:, h * 128:(h + 1) * 128], lhs[:, h, :], rhs[:, h, :], start=start, stop=stop),
                      r=[lk, rk], w=[pk])

        S32d = {d: sb(f"g_S32d{d}", [128, 4, 128], F32) for d in range(2)}
        Sbd = {d: T(f"Sbd{d}", [128, 4, 128], BF16) for d in range(2)}
        Slod = {d: T(f"Slod{d}", [128, 4, 128], BF16) for d in range(2)}
        state = {}
        for d in range(2):
            em.op('pool', lambda E: E.memset(S32d[d][:], 0.0), w=[f'S32_{d}'])
            a_, ak_ = Sbd[d].next()
            em.op('pool', lambda E: E.memset(a_[:], 0.0), w=[ak_])
            b_, bk_ = Slod[d].next()
            em.op('pool', lambda E: E.memset(b_[:], 0.0), w=[bk_])
            state[d] = (a_, ak_, b_, bk_)

        def body(d, n, first):
            U, Uk = cU[d], f'cU{d}'
            MS, MSk = cS[d], f'cS{d}'
            MT, MTk = maskT[d]
            S32_, S32k = S32d[d], f'S32_{d}'
            sb_, sbk, sl_, slk = state[d]
            r0 = n * 128
            qkv_, qk_ = qkv.next()
            em.dma('sp', qkv_[:], g.PG[r0:r0 + 128, :], r=[('dram', 'PG')], w=[qk_])
            bd_, bdk = bd.next()
            em.dma('sp', bd_[:], g.BD[r0:r0 + 128, :], r=[('dram', 'BD')], w=[bdk])
            sq_, sqk = sqf.next()
            em.op('pool', lambda E: E.tensor_tensor(sq_[:], qkv_[:, 0:1024], qkv_[:, 0:1024], ALU.mult), r=[qk_], w=[sqk])
            s8, s8k = ss8.next()
            em.op('dve', lambda E: E.tensor_reduce(s8[:], sq_[:].rearrange("p (h c) -> p h c", h=8), AX.X, ALU.add), r=[sqk], w=[s8k])
            rsqrt(em, s8[:], s8[:], EPS, s8k, s8k)
            em.op('dve', lambda E: E.tensor_scalar(s8[:, 0:4], s8[:, 0:4], 128.0 ** -0.5, None, ALU.mult), r=[s8k], w=[s8k])
            qn_, qnk = qkn.next()
            em.op('dve', lambda E: E.tensor_tensor(qn_[:], qkv_[:, 0:1024].rearrange("p (h c) -> p h c", h=8),
                                                   V(s8[:, 0:1], [(1, 8), (0, 128)]), ALU.mult), r=[qk_, s8k], w=[qnk])
            qT_, qTk = qkT.next()
            for hh in range(2):
                ps, pk = g.psum()
                for h in range(4):
                    em.op('pe', lambda E: E.matmul(ps[:, h * 128:(h + 1) * 128], qn_[:, hh * 4 + h, :], g.identb[:], start=True, stop=True),
                          r=[qnk, 'identb'], w=[pk])
                dst = qT_[:, hh * 4:(hh + 1) * 4, :].rearrange("p h c -> p (h c)")
                if hh == 0:
                    em.op('act', lambda E: E.copy(dst, ps[:]), r=[pk], w=[qTk])
                else:
                    em.op('dve', lambda E: E.tensor_copy(dst, ps[:]), r=[pk], w=[qTk])
            qT4, kT4, kn4 = qT_[:, 0:4], qT_[:, 4:8], qn_[:, 4:8]
            yield
            psG, pGk = g.psum()
            mm4(psG, kT4, qTk, kT4, qTk, pGk)
            psQ, pQk = g.psum()
            mm4(psQ, kT4, qTk, qT4, qTk, pQk)
            if GDN_CUT == 1000 + 1:
                pass
            yield
            sm_, smk = sm.next()
            beta, nbeta, gtok, tmp4, ecol, bg, ed, dec = (sm_[:, i * 4:(i + 1) * 4] for i in range(8))
            em.op('act', lambda E: E.activation(out=beta, in_=bd_[:, d * 4:d * 4 + 4], func=AF.Exp, scale=-1.0), r=[bdk], w=[smk])
            em.op('dve', lambda E: E.tensor_scalar(beta, beta, 1.0, None, ALU.add), r=[smk], w=[smk])
            em.op('dve', lambda E: E.reciprocal(beta, beta), r=[smk], w=[smk])
            em.op('dve', lambda E: E.tensor_scalar(nbeta, beta, -1.0, None, ALU.mult), r=[smk], w=[smk])
            em.op('dve', lambda E: E.tensor_tensor(tmp4, bd_[:, 8 + d * 4:12 + d * 4], dtb[:, d * 4:d * 4 + 4], ALU.add), r=[bdk, 'dtb'], w=[smk])
            em.op('act', lambda E: E.activation(out=tmp4, in_=tmp4, func=AF.Exp), r=[smk], w=[smk])
            em.op('act', lambda E: E.activation(out=tmp4, in_=tmp4, func=AF.Ln, bias=1.0, scale=1.0), r=[smk], w=[smk])
            em.op('dve', lambda E: E.tensor_tensor(gtok, tmp4, negA[:, d * 4:d * 4 + 4], ALU.mult), r=[smk, 'negA'], w=[smk])
            if GDN_CUT == 1000 + 2:
                pass
            yield
            gU_, gUk = gU.next()
            for h in range(4):
                em.op('pool' if h % 2 else 'dve', lambda E: E.tensor_scalar(gU_[:, h, :], U[:], gtok[:, h:h + 1], None, ALU.mult),
                      r=[Uk, smk], w=[gUk])
            psR, pRk = g.psum()
            em.op('pe', lambda E: E.matmul(psR[:], ones32[:], f2(gU_), start=True, stop=True), r=['ones32', gUk], w=[pRk])
            psC, pCk = g.psum()
            em.op('pe', lambda E: E.matmul(psC[:, 0:4], U[:], gtok, start=True, stop=True), r=[Uk, smk], w=[pCk])
            em.op('pe', lambda E: E.matmul(psC[:, 4:8], ones32[:], gtok, start=True, stop=True), r=['ones32', smk], w=[pCk])
            gcc_, gck = gcc.next()
            em.op('dve', lambda E: E.tensor_copy(gcc_[:], psC[:, 0:8]), r=[pCk], w=[gck])
            yield
            dif_, difk = dif.next()
            dT_, dTk = difT.next()
            for h in range(4):
                em.op('dve', lambda E: E.tensor_scalar(dif_[:, h, :], psR[:, h * 128:(h + 1) * 128], gcc_[:, h:h + 1], 0.0,
                                                       ALU.subtract, ALU.max), r=[pRk, gck], w=[difk])
                em.op('dve', lambda E: E.tensor_scalar(dT_[:, h, :], psR[:, h * 128:(h + 1) * 128], gcc_[:, h:h + 1], 0.0,
                                                       ALU.subtract, ALU.min), r=[pRk, gck], w=[dTk])
            em.op('act', lambda E: E.activation(out=dif_[:], in_=dif_[:], func=AF.Exp, scale=-1.0), r=[difk], w=[difk])
            em.op('act', lambda E: E.activation(out=dT_[:], in_=dT_[:], func=AF.Exp), r=[dTk], w=[dTk])
            yield
            er_, erk = Erow.next()
            em.op('act', lambda E: E.activation(out=f2(er_), in_=psR[:], func=AF.Exp), r=[pRk, difk, dTk], w=[erk])
            if GDN_CUT == 1000 + 3:
                pass
            yield
            ds_, dsk = DmS.next()
            for h in range(4):
                em.op('dve', lambda E: E.scalar_tensor_tensor(ds_[:, h, :], dif_[:, h, :], nbeta[:, h:h + 1], MS[:], ALU.mult, ALU.mult),
                      r=[difk, smk, MSk], w=[dsk])
            B0, B0k = Bk.next()
            em.op('dve', lambda E: E.tensor_tensor(f2(B0), psG[:], f2(ds_), ALU.mult), r=[pGk, dsk], w=[B0k])
            if GDN_CUT == 1000 + 31:
                pass
            em.op('pool', lambda E: E.tensor_tensor(dT_[:], dT_[:], V(MT[:, 0:1], [(0, 4), (1, 128)]), ALU.mult), r=[dTk, MTk], w=[dTk])
            at_, atk = attnT.next()
            em.op('dve', lambda E: E.tensor_tensor(f2(at_), psQ[:], f2(dT_), ALU.mult), r=[pQk, dTk], w=[atk])
            if GDN_CUT == 1000 + 32:
                pass
            yield
            psX, pXk = g.psum()
            for h in range(4):
                em.op('pe', lambda E: E.matmul(psX[:, h * 128:(h + 1) * 128], B0[:, h, :], g.ident[:], start=True, stop=True),
                      r=[B0k, 'ident'], w=[pXk])
            if GDN_CUT == 1000 + 33:
                pass
            X0, X0k = Xk.next()
            em.op('dve', lambda E: E.tensor_copy(f2(X0), psX[:]), r=[pXk], w=[X0k])
            if GDN_CUT == 1000 + 34:
                pass
            P0, P0k = Pk.next()
            em.op('dve', lambda E: E.tensor_tensor(P0[:], psX[:].rearrange("p (h c) -> p h c", h=4), identb4, ALU.add),
                  r=[pXk, 'ident'], w=[P0k])
            if GDN_CUT == 1000 + 4:
                pass
            Bc, Bck, Xc, Xck, Pc, Pck = B0, B0k, X0, X0k, P0, P0k
            for k in range(1, 7):
                if k < 6:
                    psX, pXk = g.psum()
                    mm4(psX, Bc, Bck, Xc, Xck, pXk)
                yield
                psB, pBk = g.psum()
                mm4(psB, Xc, Xck, Bc, Bck, pBk)
                if k < 6:
                    Xn, Xnk = Xk.next()
                    em.op('dve', lambda E: E.tensor_copy(f2(Xn), psX[:]), r=[pXk], w=[Xnk])
                    Bn, Bnk = Bk.next()
                    em.op('dve', lambda E: E.tensor_copy(f2(Bn), psB[:]), r=[pBk], w=[Bnk])
                yield
                IB, IBk_ = IBk.next()
                em.op('dve', lambda E: E.tensor_tensor(IB[:], psB[:].rearrange("p (h c) -> p h c", h=4), identb4, ALU.add),
                      r=[pBk, 'ident'], w=[IBk_])
                yield
                psP, pPk = g.psum()
                mm4(psP, IB, IBk_, Pc, Pck, pPk)
                yield
                Pn, Pnk = Pk.next()
                em.op('dve', lambda E: E.tensor_copy(f2(Pn), psP[:]), r=[pPk], w=[Pnk])
                Pc, Pck = Pn, Pnk
                if k < 6:
                    Bc, Bck, Xc, Xck = Bn, Bnk, Xn, Xnk
            TmT, Tk = Pc, Pck
            if GDN_CUT == 1000 + 5:
                pass
            yield
            em.op('act', lambda E: E.activation(out=ecol, in_=gcc_[:, 0:4], func=AF.Exp), r=[gck], w=[smk])
            em.op('dve', lambda E: E.tensor_tensor(bg, beta, ecol, ALU.mult), r=[smk], w=[smk])
            em.op('dve', lambda E: E.tensor_tensor(ed, gcc_[:, 4:8], gcc_[:, 0:4], ALU.subtract), r=[gck], w=[smk])
            em.op('act', lambda E: E.activation(out=ed, in_=ed, func=AF.Exp), r=[smk], w=[smk])
            em.op('act', lambda E: E.activation(out=dec, in_=gcc_[:, 4:8], func=AF.Exp), r=[gck], w=[smk])
            vb_, vbk = vb.next()
            em.op('pool', lambda E: E.tensor_tensor(vb_[:], qkv_[:, 1024:1536].rearrange("p (h c) -> p h c", h=4),
                                                    V(beta[:, 0:1], [(1, 4), (0, 128)]), ALU.mult), r=[qk_, smk], w=[vbk])
            kd_, kdk = kdec.next()
            em.op('pool', lambda E: E.tensor_tensor(kd_[:], kn4, V(ed[:, 0:1], [(1, 4), (0, 128)]), ALU.mult), r=[qnk, smk], w=[kdk])
            qd_, qdk = qdT.next()
            em.op('dve', lambda E: E.tensor_tensor(qd_[:], qT4, er_[:], ALU.mult), r=[qTk, erk], w=[qdk])
            if GDN_CUT == 1000 + 6:
                pass
            yield
            psV, pVk = g.psum()
            for h in range(4):
                em.op('pe', lambda E: E.matmul(psV[:, h * 128:(h + 1) * 128], kT4[:, h, :], sb_[:, h, :], start=True, stop=False),
                      r=[qTk, sbk], w=[pVk])
                em.op('pe', lambda E: E.matmul(psV[:, h * 128:(h + 1) * 128], kT4[:, h, :], sl_[:, h, :], start=False, stop=True),
                      r=[qTk, slk], w=[pVk])
            yield
            t1_, t1k = t1b.next()
            em.op('dve', lambda E: E.tensor_tensor(t1_[:], psV[:].rearrange("p (h c) -> p h c", h=4), V(bg[:, 0:1], [(1, 4), (0, 128)]), ALU.mult),
                  r=[pVk, smk], w=[t1k])
            r_, rk_ = rr.next()
            em.op('pool', lambda E: E.tensor_tensor(r_[:], vb_[:], t1_[:], ALU.subtract), r=[vbk, t1k], w=[rk_])
            yield
            psN2, pN2k = g.psum()
            mm4(psN2, TmT, Tk, r_, rk_, pN2k)
            yield
            vn_, vnk = vnew.next()
            em.op('act', lambda E: E.copy(f2(vn_), psN2[:]), r=[pN2k], w=[vnk])
            yield
            psO, pOk = g.psum()
            for h in range(4):
                em.op('pe', lambda E: E.matmul(psO[:, h * 128:(h + 1) * 128], sb_[:, h, :], qd_[:, h, :], start=True, stop=False),
                      r=[sbk, qdk], w=[pOk])
                em.op('pe', lambda E: E.matmul(psO[:, h * 128:(h + 1) * 128], vn_[:, h, :], at_[:, h, :], start=False, stop=True),
                      r=[vnk, atk], w=[pOk])
            yield
            psS, pSk = g.psum()
            mm4(psS, kd_, kdk, vn_, vnk, pSk)
            em.op('pool', lambda E: E.tensor_tensor(S32_[:], S32_[:], V(dec[:, 0:1], [(1, 4), (0, 128)]), ALU.mult), r=[S32k, smk], w=[S32k])
            em.op('dve', lambda E: E.tensor_tensor(f2(S32_), f2(S32_), psS[:], ALU.add), r=[S32k, pSk], w=[S32k])
            sb_, sbk = Sbd[d].next()
            em.op('act', lambda E: E.copy(sb_[:], S32_[:]), r=[S32k], w=[sbk])
            sl_, slk = Slod[d].next()
            em.op('pool', lambda E: E.tensor_tensor(sl_[:], S32_[:], sb_[:], ALU.subtract), r=[S32k, sbk], w=[slk])
            if GDN_CUT == 1000 + 7:
                pass
            yield
            ofap = AP(tensor=g.OF.tensor, offset=g.OF.offset + r0, ap=[[Lmax, 128], [128 * Lmax, 4], [1, 128]])
            if first:
                of_, ofk = of32.next()
                em.op('dve', lambda E: E.tensor_copy(f2(of_), psO[:]), r=[pOk], w=[ofk])
                em.dma('pool', ofap, of_[:], r=[ofk], w=[('dram', 'OF')])
            else:
                of_, ofk = of32.next()
                em.dma('sp', of_[:], ofap, r=[('dram', 'OF')], w=[ofk])
                z_, zk = zT.next()
                em.dma('sp', z_[:], AP(tensor=g.ZT.tensor, offset=g.ZT.offset + r0, ap=[[Lmax, 128], [128 * Lmax, 4], [1, 128]]),
                       r=[('dram', 'ZT')], w=[zk])
                os_, osk = osum.next()
                em.op('dve', lambda E: E.tensor_tensor(f2(os_), psO[:], f2(of_), ALU.add), r=[pOk, ofk], w=[osk])
                oq_, oqk = osq.next()
                em.op('act', lambda E: E.activation(out=oq_[:], in_=os_[:], func=AF.Square), r=[osk], w=[oqk])
                psN, pNk = g.psum()
                em.op('pe', lambda E: E.matmul(psN[:], g.onesb[:], f2(oq_), start=True, stop=True), r=['onesb', oqk], w=[pNk])
                or_, ork = orst.next()
                rsqrt(em, f2(or_), psN[:], 128 * EPS, pNk, ork)
                em.op('dve', lambda E: E.tensor_tensor(or_[:], or_[:], os_[:], ALU.mult), r=[ork, osk], w=[ork])
                oo_, ook = oout.next()
                em.op('dve', lambda E: E.scalar_tensor_tensor(oo_[:], or_[:], gno[:, 0:1], z_[:], ALU.mult, ALU.mult),
                      r=[ork, 'gno', zk], w=[ook])
                em.dma('pool', AP(tensor=g.OT.tensor, offset=g.OT.offset + 1024 * Lmax + r0, ap=[[Lmax, 128], [128 * Lmax, 4], [1, 128]]),
                       oo_[:], r=[ook], w=[('dram', 'OT')])
            state[d] = (sb_, sbk, sl_, slk)

        for idx in range(nt):
            first = idx < nt - 1 - idx
            gens = [body(0, idx, first), body(1, nt - 1 - idx, first)]
            while gens:
                for gen in list(gens):
                    try:
                        next(gen)
                    except StopIteration:
                        gens.remove(gen)
        em.barrier()
    em.barrier()


def mixer_hyena(g, l, si):
    nc, em, ext = g.nc, g.em, g.ext
    L = g.seq_lens[si]
    Lmax = g.Lmax
    N = 2 * L
    N1 = N // 128
    P1 = L // 128
    R1 = min(N1, HY_RMAX)
    nch1 = max(1, N1 // HY_RMAX)
    CG = min(64, 4096 // N1)
    CS = 64
    ncol = CG * N1
    nchunk = ncol // 512
    cpb = 512 // N1
    nb1 = min(CG, 512 // (2 * N1))
    KF = g.KFq[L]
    KTd = g.KTd[L]
    TWO_PI = 2.0 * math.pi
    with ExitStack() as st:
        def sb(name, shape, dt):
            return st.enter_context(nc.sbuf_tensor(g.un(name), list(shape), dt))

        def T(name, shape, dt, n=2):
            return Rot([sb(f"h_{name}{i}", shape, dt) for i in range(n)], "h_" + name)
        F1 = sb("h_F1", [R1, nch1, 2 * N1], BF16)
        Twr = sb("h_Twr", [128, N1], F32)
        Twi = sb("h_Twi", [128, N1], F32)
        TwTr = sb("h_TwTr", [R1, nch1, 128], F32)
        TwTi = sb("h_TwTi", [R1, nch1, 128], F32)
        G1r = sb("h_G1r", [R1, nch1, P1], BF16)
        G1i = sb("h_G1i", [R1, nch1, P1], BF16)
        F2r = sb("h_F2r", [128, 128], BF16)
        F2i = sb("h_F2i", [128, 128], BF16)
        F2in = sb("h_F2in", [128, 128], BF16)
        G2a = sb("h_G2a", [128, 256], BF16)
        G2b = sb("h_G2b", [128, 256], BF16)
        skip = sb("h_skip", [128, 2, 512], F32)
        for t_, nm in ((F1, f"hyF1_{L}"), (TwTr, f"hyTwTr_{L}"), (TwTi, f"hyTwTi_{L}"), (G1r, f"hyG1r_{L}"), (G1i, f"hyG1i_{L}")):
            em.dma('sp', t_[:], ext["c_" + nm][:, :, :], w=['hconst'])
        for t_, nm in ((Twr, f"hyTwr_{L}"), (Twi, f"hyTwi_{L}"), (F2r, "hyF2r"), (F2i, "hyF2i"), (F2in, "hyF2in"), (G2a, "hyG2a"), (G2b, "hyG2b")):
            em.dma('sp', t_[:], ext["c_" + nm][:, :], w=['hconst'])
        em.dma('sp', skip[:].rearrange("p o c -> p (o c)"), drow_bc(ext["hy_skip"], l * 1024, 1024), w=['hconst'])
        HC = 'hconst'
        Apr = sb("h_Apr", [128, CG, N1], BF16)
        Api = sb("h_Api", [128, CG, N1], BF16)
        Zkr = sb("h_Zkr", [128, CG, N1], BF16)
        Zki = sb("h_Zki", [128, CG, N1], BF16)
        Bpr = sb("h_Bpr", [R1, nch1, CG, 128], BF16)
        Bpi = sb("h_Bpi", [R1, nch1, CG, 128], BF16)
        tP = T("tP", [128, 512], F32, 2)
        tQ = T("tQ", [128, 512], F32, 2)
        kfc = T("kfc", [128, 2, 512], BF16, 2)

        def cmul_tw(ps, pk, M, units, width, tr, ti, outr_fn, outi_fn, okeys):
            n = units * 2 * width
            src = V(ps[0:M, 0:1], [(2 * width, units), (width, 2), (1, width)])
            trb = V(tr, [(0, units), (0, 2), (1, width)])
            tib = V(ti, [(0, units), (0, 2), (1, width)])
            p_, pk_ = tP.next()
            q_, qk_ = tQ.next()
            pv = V(p_[0:M, 0:1], [(2 * width, units), (width, 2), (1, width)])
            qv = V(q_[0:M, 0:1], [(2 * width, units), (width, 2), (1, width)])
            em.op('dve', lambda E: E.tensor_tensor(pv, src, trb, ALU.mult), r=[pk, HC], w=[pk_])
            em.op('dve', lambda E: E.tensor_tensor(qv, src, tib, ALU.mult), r=[pk, HC], w=[qk_])
            pre = V(p_[0:M, 0:1], [(2 * width, units), (1, width)])
            pim = V(p_[0:M, 0:1], [(2 * width, units), (1, width)], off=width)
            qre = V(q_[0:M, 0:1], [(2 * width, units), (1, width)])
            qim = V(q_[0:M, 0:1], [(2 * width, units), (1, width)], off=width)
            em.op('pool', lambda E: E.tensor_tensor(outr_fn, pre, qim, ALU.subtract), r=[pk_, qk_], w=[okeys[0]])
            em.op('pool', lambda E: E.tensor_tensor(outi_fn, qre, pim, ALU.add), r=[pk_, qk_], w=[okeys[1]])

        def fwd_fft(lhs_fn, kchs):
            for c0 in range(0, CG, nb1):
                ps, pk = g.psum()
                for u in range(nb1):
                    for kc in range(kchs):
                        lap, lk = lhs_fn(c0 + u, kc)
                        K_ = lap.shape[0]
                        em.op('pe', lambda E: E.matmul(ps[:, u * 2 * N1:(u + 1) * 2 * N1], lap, F1[0:K_, kc, :],
                                                       start=(kc == 0), stop=(kc == kchs - 1)), r=[lk, HC], w=[pk])
                cmul_tw(ps, pk, 128, nb1, N1, Twr[:, :], Twi[:, :], Apr[:, c0:c0 + nb1, :], Api[:, c0:c0 + nb1, :], ['Apr', 'Api'])

        def stage3(j):
            cols = slice(j * 512, (j + 1) * 512)
            ar = Apr[:].rearrange("p c f -> p (c f)")[:, cols]
            ai = Api[:].rearrange("p c f -> p (c f)")[:, cols]
            pzr, pzrk = g.psum()
            em.op('pe', lambda E: E.matmul(pzr[:], F2r[:], ar, start=True, stop=False), r=['Apr', HC], w=[pzrk])
            em.op('pe', lambda E: E.matmul(pzr[:], F2in[:], ai, start=False, stop=True), r=['Api', HC], w=[pzrk])
            pzi, pzik = g.psum()
            em.op('pe', lambda E: E.matmul(pzi[:], F2i[:], ar, start=True, stop=False), r=['Apr', HC], w=[pzik])
            em.op('pe', lambda E: E.matmul(pzi[:], F2r[:], ai, start=False, stop=True), r=['Api', HC], w=[pzik])
            return pzr, pzrk, pzi, pzik

        if g.filt_done.get(L) != l:
            g.filt_done[L] = l
            with ExitStack() as st2:
                def sb2(name, shape, dt):
                    return st2.enter_context(nc.sbuf_tensor(g.un(name), list(shape), dt))
                w1 = sb2("hf_w1", [33, 64], F32)
                w2 = sb2("hf_w2", [64, 64], F32)
                w3 = sb2("hf_w3", [64, 2048], F32)
                pc = sb2("hf_pc", [64, 8], F32)
                dl = sb2("hf_dl", [128, 512], F32)
                tneg = sb2("hf_tneg", [128, N1], F32)
                ones32 = sb2("hf_ones", [128, 128], F32)
                invn = sb2("hf_invn", [128, 2, 512], F32)
                em.dma('sp', w1[:], ext["hy_ffn_w1"][l], w=['fw'])
                em.dma('sp', w2[:], ext["hy_ffn_w2"][l], w=['fw'])
                em.dma('sp', w3[:], ext["hy_ffn_w3"][l], w=['fw'])
                em.dma('sp', pc[:, 0:1], dcol(ext["hy_ffn_b1"], l * 64, 1, nparts=64), w=['fpc'], slow=True)
                em.dma('sp', pc[:, 1:2], dcol(ext["hy_sin_freq"], l * 64, 1, nparts=64), w=['fpc'], slow=True)
                em.dma('sp', pc[:, 2:3], dcol(ext["hy_ffn_b2"], l * 64, 1, nparts=64), w=['fpc'], slow=True)
                em.dma('sp', dl[:], ext["c_hydelta"][:, :], w=['fw'])
                em.dma('sp', tneg[:], ext[f"c_hytneg_{L}"][:, :], w=['fw'])
                em.dma('sp', ones32[:], ext["c_ones"][:, :], w=['fw'])
                em.op('dve', lambda E: E.tensor_scalar(pc[:, 1:2], pc[:, 1:2], 1.0 / TWO_PI, None, ALU.mult), r=['fpc'], w=['fpc'])
                em.op('pool', lambda E: E.memset(pc[:, 3:4], -math.pi), w=['fpc'])
                ft = Rot([sb2(f"hf_ft{i}", [33, 512], F32) for i in range(2)], "hf_ft")
                hh1 = Rot([sb2(f"hf_h1{i}", [64, 512], F32) for i in range(2)], "hf_h1")
                hh2 = Rot([sb2(f"hf_h2{i}", [64, 512], F32) for i in range(2)], "hf_h2")
                dk = Rot([sb2(f"hf_dk{i}", [128, 512], F32) for i in range(2)], "hf_dk")
                kf32 = Rot([sb2(f"hf_k32{i}", [128, 512], F32) for i in range(2)], "hf_k32")
                kab = Rot([sb2(f"hf_kab{i}", [128, 512], F32) for i in range(2)], "hf_kab")
                kbf = Rot([sb2(f"hf_kbf{i}", [128, 2, 512], BF16) for i in range(2)], "hf_kbf")
                feat = ext[f"c_hyfeat_{L}"]
                msk = sb2("hf_msk", [64, 512], F32)

                def sin_layer(ps, pk, bcol, out, ok):
                    em.op('dve', lambda E: E.tensor_scalar(out, ps, pc[:, bcol:bcol + 1], pc[:, 1:2], ALU.add, ALU.mult), r=[pk, 'fpc'], w=[ok])
                    for _ in range(2):
                        em.op('dve', lambda E: E.tensor_scalar(msk[:], out, 0.5, None, ALU.is_gt), r=[ok], w=['msk'])
                        em.op('dve', lambda E: E.tensor_tensor(out, out, msk[:], ALU.subtract), r=[ok, 'msk'], w=[ok])
                        em.op('dve', lambda E: E.tensor_scalar(msk[:], out, -0.5, None, ALU.is_lt), r=[ok], w=['msk'])
                        em.op('dve', lambda E: E.tensor_tensor(out, out, msk[:], ALU.add), r=[ok, 'msk'], w=[ok])
                    em.op('act', lambda E: E.activation(out=out, in_=out, func=AF.Sin, scale=TWO_PI), r=[ok], w=[ok])

                pn = [g.psum([6]), g.psum([7])]
                ntile_tot = N // 128
                for blk in range(N // 512):
                    f_, fk = ft.next()
                    em.dma('sp', f_[:], feat[:, blk * 512:(blk + 1) * 512], w=[fk])
                    ps, pk = g.psum(range(6))
                    em.op('pe', lambda E: E.matmul(ps[0:64, :], w1[:], f_[:], start=True, stop=True), r=['fw', fk], w=[pk])
                    h1, h1k = hh1.next()
                    sin_layer(ps[0:64, :], pk, 0, h1[:], h1k)
                    ps, pk = g.psum(range(6))
                    em.op('pe', lambda E: E.matmul(ps[0:64, :], w2[:], h1[:], start=True, stop=True), r=['fw', h1k], w=[pk])
                    h2, h2k = hh2.next()
                    sin_layer(ps[0:64, :], pk, 2, h2[:], h2k)
                    for tt in range(4):
                        ti_ = blk * 4 + tt
                        dirn = 0 if ti_ * 128 < L else 1
                        d_, dkk = dk.next()
                        em.op('act', lambda E: E.activation(out=d_[:], in_=dl[:], func=AF.Exp, scale=tneg[:, ti_:ti_ + 1]), r=['fw'], w=[dkk])
                        kb_, kbk = kbf.next()
                        for o in range(2):
                            ps, pk = g.psum(range(6))
                            col0 = o * 1024 + dirn * 512
                            em.op('pe', lambda E: E.matmul(ps[:], h2[:, tt * 128:(tt + 1) * 128], w3[:, col0:col0 + 512], start=True, stop=True),
                                  r=[h2k, 'fw'], w=[pk])
                            k_, kk = kf32.next()
                            em.op('dve', lambda E: E.tensor_tensor(k_[:], ps[:], d_[:], ALU.mult), r=[pk, dkk], w=[kk])
                            if ti_ * 128 == L:
                                em.op('pool', lambda E: E.memset(k_[0:1, :], 0.0), w=[kk])
                            a_, ak = kab.next()
                            em.op('act', lambda E: E.activation(out=a_[:], in_=k_[:], func=AF.Abs), r=[kk], w=[ak])
                            em.op('pe', lambda E: E.matmul(pn[o][0][:], ones32[:], a_[:], start=(ti_ == 0), stop=(ti_ == ntile_tot - 1)),
                                  r=['fw', ak], w=[pn[o][1]])
                            em.op('act', lambda E: E.copy(kb_[:, o, :], k_[:]), r=[kk], w=[kbk])
                        em.dma('pool', KTd[ti_ * 128:(ti_ + 1) * 128, :], kb_[:].rearrange("p o c -> p (o c)"), r=[kbk], w=[('dram', 'KTd')])
                for o in range(2):
                    em.op('dve', lambda E: E.reciprocal(invn[:, o, :], pn[o][0][:]), r=[pn[o][1]], w=['invn'])
                ktd = Rot([sb2(f"hf_ktd{i}", [R1, nch1, 128, CG], BF16) for i in range(2)], "hf_ktd")
                for o in range(2):
                    for gi in range(512 // CG):
                        cbase = gi * CG
                        kt_, ktk = ktd.next()
                        for kc in range(nch1):
                            em.dma('sp', kt_[:, kc], AP(tensor=KTd.tensor, offset=KTd.offset + kc * R1 * 128 * 1024 + o * 512 + cbase,
                                                        ap=[[128 * 1024, R1], [1024, 128], [1, CG]]), r=[('dram', 'KTd')], w=[ktk])
                        fwd_fft(lambda c, kc: (V(kt_[:, kc, 0, 0:1], [(CG, 128)], off=c), ktk), nch1)
                        for j in range(nchunk):
                            pzr, pzrk, pzi, pzik = stage3(j)
                            kc_, kck = kfc.next()
                            inb = V(invn[:, o, cbase + j * cpb:cbase + j * cpb + 1], [(1, cpb), (0, N1)])
                            em.op('dve', lambda E: E.tensor_tensor(kc_[:, 0, :].rearrange("p (c f) -> p c f", c=cpb),
                                                                   pzr[:].rearrange("p (c f) -> p c f", c=cpb), inb, ALU.mult),
                                  r=[pzrk, 'invn'], w=[kck])
                            em.op('dve', lambda E: E.tensor_tensor(kc_[:, 1, :].rearrange("p (c f) -> p c f", c=cpb),
                                                                   pzi[:].rearrange("p (c f) -> p c f", c=cpb), inb, ALU.mult),
                                  r=[pzik, 'invn'], w=[kck])
                            em.dma('pool', KF[o, gi, j].rearrange("p a c -> p (a c)"), kc_[:].rearrange("p a c -> p (a c)"), r=[kck],
                                   w=[('dram', 'KF')])
            em.barrier()
        xs = {nm: T(nm, [P1, 128, CS], BF16, 1) for nm in ("x1s", "x2s", "vs")}
        z1 = sb("h_z1", [P1, 128, CG], BF16)
        z2s = sb("h_z2s", [P1, 128, CS], BF16)
        gt1 = T("gt1", [P1, 128, 4], F32, 2)
        gt2 = T("gt2", [P1, 128, 4], F32, 2)
        for sl in range(512 // CS):
            tiles = {}
            for i, nm in enumerate(("x1s", "x2s", "vs")):
                t_, tk = xs[nm].next()
                src_ = AP(tensor=g.PH.tensor, offset=g.PH.offset + i * 512 + sl * CS, ap=[[128 * 1536, P1], [1536, 128], [1, CS]])
                nsp = 2 if P1 >= 128 else 1
                for sp_ in range(nsp):
                    a0, a1 = sp_ * P1 // nsp, (sp_ + 1) * P1 // nsp
                    em.dma('sp', psplit(t_[:], a0, a1), psplit(src_, a0, a1), r=[('dram', 'PH')], w=[tk])
                tiles[nm] = (t_, tk)
            for sg_ in range(CS // CG):
                gi = sl * (CS // CG) + sg_
                cb = sg_ * CG
                cglob = gi * CG
                zsrc, zk = tiles["vs"]
                zoff, zstride = cb, CS
                for o in range(2):
                    gate, gk = tiles["x1s" if o == 0 else "x2s"]
                    zt, ztk, zo, zs_ = zsrc, zk, zoff, zstride
                    fwd_fft(lambda c, kc: (V(zt[:, 0, 0:1], [(zs_, 128)], off=zo + c), ztk), 1)
                    for j in range(nchunk):
                        pzr, pzrk, pzi, pzik = stage3(j)
                        kc_, kck = kfc.next()
                        em.dma('sp', kc_[:].rearrange("p a c -> p (a c)"), KF[o, gi, j].rearrange("p a c -> p (a c)"), r=[('dram', 'KF')], w=[kck])
                        cols = slice(j * 512, (j + 1) * 512)
                        zr_out = Zkr[:].rearrange("p c f -> p (c f)")[:, cols]
                        zi_out = Zki[:].rearrange("p c f -> p (c f)")[:, cols]
                        a_, ak = tP.next()
                        b_, bk = tQ.next()
                        em.op('dve', lambda E: E.tensor_tensor(a_[:], pzr[:], kc_[:, 0, :], ALU.mult), r=[pzrk, kck], w=[ak])
                        em.op('dve', lambda E: E.tensor_tensor(b_[:], pzi[:], kc_[:, 1, :], ALU.mult), r=[pzik, kck], w=[bk])
                        em.op('pool', lambda E: E.tensor_tensor(zr_out, a_[:], b_[:], ALU.subtract), r=[ak, bk], w=['Zkr'])
                        a_, ak = tP.next()
                        b_, bk = tQ.next()
                        em.op('dve', lambda E: E.tensor_tensor(a_[:], pzr[:], kc_[:, 1, :], ALU.mult), r=[pzrk, kck], w=[ak])
                        em.op('dve', lambda E: E.tensor_tensor(b_[:], pzi[:], kc_[:, 0, :], ALU.mult), r=[pzik, kck], w=[bk])
                        em.op('pool', lambda E: E.tensor_tensor(zi_out, a_[:], b_[:], ALU.add), r=[ak, bk], w=['Zki'])
                    for fc in range(nch1):
                        for c0 in range(0, CG, 2):
                            ps, pk = g.psum()
                            for u in range(2):
                                c = c0 + u
                                em.op('pe', lambda E: E.matmul(ps[0:R1, u * 256:(u + 1) * 256], Zkr[:, c, fc * R1:(fc + 1) * R1], G2a[:],
                                                               start=True, stop=False), r=['Zkr', HC], w=[pk])
                                em.op('pe', lambda E: E.matmul(ps[0:R1, u * 256:(u + 1) * 256], Zki[:, c, fc * R1:(fc + 1) * R1], G2b[:],
                                                               start=False, stop=True), r=['Zki', HC], w=[pk])
                            cmul_tw(ps, pk, R1, 2, 128, TwTr[:, fc, :], TwTi[:, fc, :], Bpr[:, fc, c0:c0 + 2, :], Bpi[:, fc, c0:c0 + 2, :],
                                    ['Bpr', 'Bpi'])
                    zn_, znk = (z1, 'z1') if o == 0 else (z2s, 'z2s')
                    zc0 = 0 if o == 0 else cb
                    for c0 in range(0, CG, 4):
                        ps, pk = g.psum()
                        n = 0
                        for fc in range(nch1):
                            for (gm, bp, bpk) in ((G1r, Bpr, 'Bpr'), (G1i, Bpi, 'Bpi')):
                                em.op('pe', lambda E: E.matmul(ps[0:P1, :], gm[:, fc, :], bp[:, fc, c0:c0 + 4, :].rearrange("p c t -> p (c t)"),
                                                               start=(n == 0), stop=(n == 2 * nch1 - 1)), r=[bpk, HC], w=[pk])
                                n += 1
                        yv = V(ps[0:P1, 0:1], [(1, 128), (128, 4)])
                        zv = V(zt[:, 0, 0:1], [(zs_, 128), (1, 4)], off=zo + c0)
                        gv = gate[:, :, cb + c0:cb + c0 + 4]
                        skb = V(skip[0:P1, o, cglob + c0:cglob + c0 + 1], [(0, 128), (1, 4)])
                        t1_, t1k = gt1.next()
                        em.op('pool', lambda E: E.tensor_tensor(t1_[:], zv, skb, ALU.mult), r=[ztk, HC], w=[t1k])
                        t2_, t2k = gt2.next()
                        em.op('dve', lambda E: E.tensor_tensor(t2_[:], yv, t1_[:], ALU.add), r=[pk, t1k], w=[t2k])
                        em.op('pool', lambda E: E.tensor_tensor(zn_[:, :, zc0 + c0:zc0 + c0 + 4], t2_[:], gv, ALU.mult), r=[t2k, gk], w=[znk])
                    if o == 0:
                        zsrc, zk, zoff, zstride = z1, 'z1', 0, CG
            dst_ = AP(tensor=g.Z2.tensor, offset=g.Z2.offset + sl * CS, ap=[[128 * 512, P1], [512, 128], [1, CS]])
            nsp = 2 if P1 >= 128 else 1
            for sp_ in range(nsp):
                a0, a1 = sp_ * P1 // nsp, (sp_ + 1) * P1 // nsp
                em.dma('pool', psplit(dst_, a0, a1), psplit(z2s[:], a0, a1), r=['z2s'], w=[('dram', 'Z2')])
        em.barrier()
        zl = T("zl", [128, 512], BF16, 2)
        zo_ = T("zo", [128, 4, 128], BF16, 2)
        for t in range(L // 128):
            a_, ak = zl.next()
            em.dma('sp', a_[:], g.Z2[t * 128:(t + 1) * 128, :], r=[('dram', 'Z2')], w=[ak])
            ps, pk = g.psum()
            for c in range(4):
                em.op('pe', lambda E: E.matmul(ps[:, c * 128:(c + 1) * 128], a_[:, c * 128:(c + 1) * 128], g.identb[:], start=True, stop=True),
                      r=[ak, 'identb'], w=[pk])
            o_, ok = zo_.next()
            em.op('act' if t % 2 else 'dve', lambda E: (E.copy if t % 2 else E.tensor_copy)(o_[:].rearrange("p c t -> p (c t)"), ps[:]),
                  r=[pk], w=[ok])
            em.dma('pool', AP(tensor=g.OT.tensor, offset=g.OT.offset + t * 128, ap=[[Lmax, 128], [128 * Lmax, 4], [1, 128]]), o_[:],
                   r=[ok], w=[('dram', 'OT')])
    em.barrier()


def stage_C(g, l, si):
    nc, em, ext = g.nc, g.em, g.ext
    L = g.seq_lens[si]
    t0 = g.seq_off[si]
    Ttot, Lmax = g.Ttot, g.Lmax
    nb = L // 512
    with ExitStack() as st:
        def sb(name, shape, dt):
            return st.enter_context(nc.sbuf_tensor(g.un(name), list(shape), dt))
        g32 = sb("c_g32", [128, 3, KC], F32)
        em.op('dve', lambda E: E.tensor_scalar(g32[:], g.gains[:, 1:4, l, :], 32.0, None, ALU.mult), r=['gains'], w=['g32'])
        xT = sb("c_xT", [128, KC, 512], F32)
        yT = sb("c_yT", [128, KC, 512], F32)
        OTb = sb("c_OTb", [128, 12, 512], BF16)
        Gj = Rot([sb(f"c_Gj{i}", [128, 3, 512], BF16) for i in range(4)], "c_Gj")
        wbr = Rot([sb(f"c_wbr{i}", [128, 12, 128], BF16) for i in range(4)], "c_wbr")
        wout = Rot([sb(f"c_wout{i}", [128, KC, 128], BF16) for i in range(4)], "c_wout")
        wgu = Rot([sb(f"c_wgu{i}", [128, 2, KC, 128], BF16) for i in range(5)], "c_wgu")
        wd = Rot([sb(f"c_wd{i}", [128, NFF, 128], BF16) for i in range(3)], "c_wd")
        tmp = Rot([sb(f"c_tmp{i}", [128, 512], F32) for i in range(4)], "c_tmp")
        sgt = Rot([sb(f"c_sgt{i}", [128, 512], BF16) for i in range(2)], "c_sgt")
        mT = sb("c_mT", [128, KC, 512], BF16)
        sqb = sb("c_sqb", [128, KC, 512], BF16)
        h2T = sb("c_h2T", [128, KC, 512], BF16)
        actT = sb("c_actT", [128, NFF, 512], BF16)
        rstd = Rot([sb(f"c_rstd{i}", [128, 512], F32) for i in range(2)], "c_rstd")
        eng2 = ['dve', 'pool']

        def norm_rstd(sqk_):
            ps, pk = g.psum()
            for kc in range(KC):
                em.op('pe', lambda E: E.matmul(ps[:], g.onesb[:], sqb[:, kc, :], start=(kc == 0), stop=(kc == KC - 1)),
                      r=[sqk_, 'onesb'], w=[pk])
            rs, rsk = rstd.next()
            rsqrt(em, rs[:], ps[:], D * EPS, pk, rsk)
            return rs, rsk

        for b in range(nb):
            c0 = t0 + b * 512
            s0 = b * 512
            em.dma('sp', xT[:], AP(tensor=g.XT.tensor, offset=g.XT.offset + c0, ap=[[Ttot, 128], [128 * Ttot, KC], [1, 512]]),
                   r=[('dram', 'XT')], w=['xT'])
            em.dma('sp', OTb[:], AP(tensor=g.OT.tensor, offset=g.OT.offset + s0, ap=[[Lmax, 128], [128 * Lmax, 12], [1, 512]]),
                   r=[('dram', 'OT')], w=['OTb'])
            for j in range(8):
                w_, wk = wbr.next()
                em.dma('sp', w_[:].rearrange("p k c -> p (k c)"), g.WBR[l][j].rearrange("p k c -> p (k c)"),
                       r=[('dram', f"WBR{l}")], w=[wk])
                gj, gk = Gj.next()
                em.dma('sp', gj[:], AP(tensor=g.GT.tensor, offset=g.GT.offset + j * 128 * Lmax + s0,
                                       ap=[[Lmax, 128], [1024 * Lmax, 3], [1, 512]]), r=[('dram', 'GT')], w=[gk])
                ts = []
                for i in range(3):
                    ps, pk = g.psum()
                    for kc in range(4):
                        em.op('pe', lambda E: E.matmul(ps[:], w_[:, i * 4 + kc, :], OTb[:, i * 4 + kc, :], start=(kc == 0), stop=(kc == 3)),
                              r=[wk, 'OTb'], w=[pk])
                    t_, tk = tmp.next()
                    em.op('dve', lambda E: E.tensor_tensor(t_[:], ps[:], gj[:, i, :], ALU.mult), r=[pk, gk], w=[tk])
                    ts.append((t_, tk))
                em.op('pool', lambda E: E.tensor_tensor(ts[0][0][:], ts[0][0][:], ts[1][0][:], ALU.add), r=[ts[0][1], ts[1][1]], w=[ts[0][1]])
                em.op('pool', lambda E: E.tensor_tensor(mT[:, j, :], ts[0][0][:], ts[2][0][:], ALU.add), r=[ts[0][1], ts[2][1]], w=['mT'])
            if C_CUT == 1:
                continue
            for j in range(8):
                w_, wk = wout.next()
                em.dma('sp', w_[:].rearrange("p k c -> p (k c)"), g.WOUT[l][j].rearrange("p k c -> p (k c)"),
                       r=[('dram', f"WOUT{l}")], w=[wk])
                ps, pk = g.psum()
                for kc in range(KC):
                    em.op('pe', lambda E: E.matmul(ps[:], w_[:, kc, :], mT[:, kc, :], start=(kc == 0), stop=(kc == KC - 1)),
                          r=[wk, 'mT'], w=[pk])
                em.op('dve', lambda E: E.tensor_copy(yT[:, j, :], ps[:]), r=[pk], w=['yT'])
                em.op('act', lambda E: E.activation(out=sqb[:, j, :], in_=yT[:, j, :], func=AF.Square), r=['yT'], w=['sqb'])
            rs, rsk = norm_rstd('sqb')
            for j in range(8):
                t_, tk = tmp.next()
                em.op('dve', lambda E: E.scalar_tensor_tensor(t_[:], yT[:, j, :], g32[:, 0, j:j + 1], rs[:], ALU.mult, ALU.mult),
                      r=['yT', rsk, 'g32'], w=[tk])
                em.op('pool', lambda E: E.tensor_tensor(xT[:, j, :], t_[:], xT[:, j, :], ALU.add), r=[tk, 'xT'], w=['xT'])
            if C_CUT == 2:
                continue
            em.op('act', lambda E: E.activation(out=sqb[:], in_=xT[:], func=AF.Square), r=['xT'], w=['sqb'])
            rs, rsk = norm_rstd('sqb')
            for j in range(8):
                em.op('dve', lambda E: E.scalar_tensor_tensor(h2T[:, j, :], xT[:, j, :], g32[:, 1, j:j + 1], rs[:], ALU.mult, ALU.mult),
                      r=['xT', rsk, 'g32'], w=['h2T'])
            for f in range(NFF):
                w_, wk = wgu.next()
                em.dma('sp', w_[:, 0].rearrange("p k c -> p (k c)"), g.WG[l][f].rearrange("p k c -> p (k c)"),
                       r=[('dram', f"WG{l}")], w=[wk])
                em.dma('sp', w_[:, 1].rearrange("p k c -> p (k c)"), g.WU[l][f].rearrange("p k c -> p (k c)"),
                       r=[('dram', f"WU{l}")], w=[wk])
                psg, pgk = g.psum()
                for kc in range(KC):
                    em.op('pe', lambda E: E.matmul(psg[:], w_[:, 0, kc, :], h2T[:, kc, :], start=(kc == 0), stop=(kc == KC - 1)),
                          r=[wk, 'h2T'], w=[pgk])
                psu, puk = g.psum()
                for kc in range(KC):
                    em.op('pe', lambda E: E.matmul(psu[:], w_[:, 1, kc, :], h2T[:, kc, :], start=(kc == 0), stop=(kc == KC - 1)),
                          r=[wk, 'h2T'], w=[puk])
                sg, sgk = sgt.next()
                em.op('act', lambda E: E.activation(out=sg[:], in_=psg[:], func=AF.Silu), r=[pgk], w=[sgk])
                em.op('dve', lambda E: E.tensor_tensor(actT[:, f, :], psu[:], sg[:], ALU.mult), r=[puk, sgk], w=['actT'])
            for j in range(8):
                w_, wk = wd.next()
                em.dma('sp', w_[:].rearrange("p k c -> p (k c)"), g.WD[l][j].rearrange("p k c -> p (k c)"),
                       r=[('dram', f"WD{l}")], w=[wk])
                ps, pk = g.psum()
                for f in range(NFF):
                    em.op('pe', lambda E: E.matmul(ps[:], w_[:, f, :], actT[:, f, :], start=(f == 0), stop=(f == NFF - 1)),
                          r=[wk, 'actT'], w=[pk])
                em.op('dve', lambda E: E.tensor_copy(yT[:, j, :], ps[:]), r=[pk], w=['yT'])
                em.op('act', lambda E: E.activation(out=sqb[:, j, :], in_=yT[:, j, :], func=AF.Square), r=['yT'], w=['sqb'])
            rs, rsk = norm_rstd('sqb')
            for j in range(8):
                t_, tk = tmp.next()
                em.op('dve', lambda E: E.scalar_tensor_tensor(t_[:], yT[:, j, :], g32[:, 2, j:j + 1], rs[:], ALU.mult, ALU.mult),
                      r=['yT', rsk, 'g32'], w=[tk])
                em.op('pool', lambda E: E.tensor_tensor(xT[:, j, :], t_[:], xT[:, j, :], ALU.add), r=[tk, 'xT'], w=['xT'])
            em.dma('pool', AP(tensor=g.XT.tensor, offset=g.XT.offset + c0, ap=[[Ttot, 128], [128 * Ttot, KC], [1, 512]]), xT[:],
                   r=['xT'], w=[('dram', 'XT')])
    em.barrier()


def stage_out(g):
    nc, em = g.nc, g.em
    with ExitStack() as st:
        def sb(name, shape, dt):
            return st.enter_context(nc.sbuf_tensor(g.un(name), list(shape), dt))
        xin_ = Rot([sb(f"o_x{i}", [128, KC, 128], F32) for i in range(2)], "o_x")
        xo = Rot([sb(f"o_y{i}", [128, D], F32) for i in range(2)], "o_y")
        for t in range(g.Ttot // 128):
            x_, xk = xin_.next()
            em.dma('sp', x_[:], AP(tensor=g.XT.tensor, offset=g.XT.offset + t * 128, ap=[[g.Ttot, 128], [128 * g.Ttot, KC], [1, 128]]),
                   r=[('dram', 'XT')], w=[xk])
            o_, ok = xo.next()
            for hlf in range(2):
                ps, pk = g.psum()
                for c in range(4):
                    kc = hlf * 4 + c
                    em.op('pe', lambda E: E.transpose(ps[:, c * 128:(c + 1) * 128], x_[:, kc, :], g.ident[:]), r=[xk, 'ident'], w=[pk])
                if hlf == 0:
                    em.op('dve', lambda E: E.tensor_copy(o_[:, 0:512], ps[:]), r=[pk], w=[ok])
                else:
                    em.op('dve', lambda E: E.tensor_copy(o_[:, 512:1024], ps[:]), r=[pk], w=[ok])
            em.dma('pool', g.y[t * 128:(t + 1) * 128, :], o_[:], r=[ok], w=[('dram', 'y')])
    em.barrier()


def make_in_map(x, w, consts):
    m = {"x": np.ascontiguousarray(x, np.float32)}
    for k, v in w.items():
        v = np.asarray(v, np.float32)
        if k in ("gdn_a_log", "gdn_dt_bias"):
            v = v.reshape(DEPTH, 8)
        if k == "w_branch":
            v = v.reshape(DEPTH, 1536, D)
        m[k] = np.ascontiguousarray(v)
    for k, v in consts.items():
        m["c_" + k] = v
    return m


_CACHE = {}


def kernel(x_prompt, x_sample, **w):
    x_prompt = np.asarray(x_prompt, np.float32)
    x_sample = np.asarray(x_sample, np.float32)
    n_cores = 8
    Bp, Lp, _ = x_prompt.shape
    Bs, Ls_, _ = x_sample.shape
    per = Bp // n_cores
    seq_lens = [Lp] * per + [Ls_]
    key = tuple(seq_lens)
    if key not in _CACHE:
        _CACHE[key] = build(seq_lens)
    nc, consts = _CACHE[key]
    in_maps = []
    for c in range(n_cores):
        xs = [x_prompt[c * per + j] for j in range(per)] + [x_sample[c % Bs]]
        in_maps.append(make_in_map(np.concatenate(xs, 0), w, consts))
    res = run_bass_kernel_spmd(nc, in_maps, core_ids=list(range(n_cores)))
    y_prompt = np.empty_like(x_prompt)
    y_sample = np.empty_like(x_sample)
    for c in range(n_cores):
        y = np.asarray(res.results[c]["y"], np.float32)
        for j in range(per):
            y_prompt[c * per + j] = y[j * Lp:(j + 1) * Lp]
        if c < Bs:
            y_sample[c] = y[per * Lp:per * Lp + Ls_]
    return (y_prompt, y_sample)
```
